# Optimizing a Trainium2 kernel written in Bass

```python
import jax, jax.numpy as jnp
from jax import lax
import numpy as np

D_MODEL = 1024
BATCH = 16
SEQ = 4096
DEPTH = 4
DEC_BATCH = 2
DEC_SEQ = 16384
PAST_LEN = 128

N_EVEN = (DEPTH + 1) // 2
N_ODD = DEPTH // 2
D_FF = 2816
EPS = 1e-6
H_A = 4
DK_A = 64
DV_A = 128
ALPHA_RANK = 16
GATE_NORM = 16.0
GLA_CHUNK = 64
H_B = 8
Q_RANK = 256
KV_RANK = 128
NOPE_B = 64
ROPE_B = 32
V_B = 64
ROPE_BASE = 10000.0
Q_BLOCK = 128
H_C = 4
CG_C = 128
H_D = 4
DG_D = 128
SGU_CHUNK = 128
EV_SIZES = (H_A * DK_A, H_A * DK_A, H_A * DV_A, H_A * DV_A, 2 * ALPHA_RANK, Q_RANK, KV_RANK, ROPE_B)
EV_IN = sum(EV_SIZES)
D_MIX_EV = H_A * DV_A + H_B * V_B
OD_SIZES = (H_C * CG_C, 2 * H_D * DG_D)
OD_IN = sum(OD_SIZES)
D_MIX_OD = H_C * CG_C + H_D * DG_D

kernel_name = 'hybrid_bidir_gla_mla_fnet_sgu_encoder'


def _split_cols(z, sizes):
    out, off = [], 0
    for s in sizes:
        out.append(z[..., off:off + s])
        off += s
    return out


def _rmsnorm(x, g):
    xf = x.astype(jnp.float32)
    y = xf * lax.rsqrt(jnp.mean(xf * xf, axis=-1, keepdims=True) + EPS)
    return (y * g.astype(jnp.float32)).astype(x.dtype)


def _rope(x, cos, sin):
    xf = x.astype(jnp.float32)
    x1, x2 = jnp.split(xf, 2, axis=-1)
    return jnp.concatenate([x1 * cos - x2 * sin, x1 * sin + x2 * cos], axis=-1).astype(x.dtype)


def _swiglu(h, w13, w2):
    a, b = jnp.split(h @ w13, 2, axis=-1)
    return (jax.nn.silu(a) * b) @ w2


def _sublayer(x, f, m, g_pre, g_post, w_res):
    shift, scale, gate = m[:, 0], m[:, 1], m[:, 2]
    h = _rmsnorm(x, g_pre) * (1.0 + scale[:, None, :]) + shift[:, None, :]
    return x + (w_res * (1.0 + gate))[:, None, :] * _rmsnorm(f(h), g_post)


def _gla_chunked(q, k, v, lg, strict):
    B, H, S, dk = q.shape
    dv = v.shape[-1]
    n = S // GLA_CHUNK
    r = lambda t: t.reshape(B, H, n, GLA_CHUNK, t.shape[-1])
    q, k, v, lg = r(q), r(k), r(v), r(lg)
    b = jnp.cumsum(lg, axis=-2)
    bq = b - lg if strict else b
    qt = q * jnp.exp(bq)
    kt = k * jnp.exp(-b)
    kd = k * jnp.exp(b[..., -1:, :] - b)
    mask = jnp.tril(jnp.ones((GLA_CHUNK, GLA_CHUNK), dtype=bool), -1 if strict else 0)
    a = jnp.where(mask, jnp.einsum('bhnid,bhnjd->bhnij', qt, kt), 0.0)
    intra = jnp.einsum('bhnij,bhnjv->bhniv', a, v)
    decay = jnp.exp(b[..., -1, :])
    kv = jnp.einsum('bhnjd,bhnjv->bhndv', kd, v)

    def step(state, inp):
        dec, kvc = inp
        return dec[..., None] * state + kvc, state

    _, s_prev = lax.scan(step, jnp.zeros((B, H, dk, dv), jnp.float32),
                         (jnp.moveaxis(decay, 2, 0), jnp.moveaxis(kv, 2, 0)))
    s_prev = jnp.moveaxis(s_prev, 0, 2)
    inter = jnp.einsum('bhnid,bhndv->bhniv', qt, s_prev)
    return (intra + inter).reshape(B, H, S, dv)


def _gla_mixer(q, k, v, g, a_lr, w_alpha, b_alpha, gain):
    B, S, _ = q.shape
    f32 = jnp.float32

    def heads(t, d):
        return t.astype(f32).reshape(B, S, H_A, d).transpose(0, 2, 1, 3)

    qh = heads(q, DK_A) * DK_A ** -0.5
    kh, vh = heads(k, DK_A), heads(v, DV_A)
    a_f, a_b = jnp.split(a_lr.astype(f32), 2, axis=-1)
    lg_f = heads(jax.nn.log_sigmoid(a_f @ w_alpha[0].astype(f32) + b_alpha[0].astype(f32)) / GATE_NORM, DK_A)
    lg_b = heads(jax.nn.log_sigmoid(a_b @ w_alpha[1].astype(f32) + b_alpha[1].astype(f32)) / GATE_NORM, DK_A)
    flip = lambda t: jnp.flip(t, axis=2)
    o = _gla_chunked(qh, kh, vh, lg_f, False) + flip(_gla_chunked(flip(qh), flip(kh), flip(vh), flip(lg_b), True))
    o = _rmsnorm(o.transpose(0, 2, 1, 3), gain).reshape(B, S, H_A * DV_A)
    return (o * jax.nn.silu(g.astype(f32))).astype(q.dtype)


def _mla_mixer(cq, ckv, kr, q_norm, w_q_b, kv_norm, w_kv_b, cos, sin):
    B, S, _ = cq.shape
    q = (_rmsnorm(cq, q_norm) @ w_q_b).reshape(B, S, H_B, NOPE_B + ROPE_B)
    q_nope = q[..., :NOPE_B]
    q_rope = _rope(q[..., NOPE_B:], cos[None, :, None, :], sin[None, :, None, :])
    kv = (_rmsnorm(ckv, kv_norm) @ w_kv_b).reshape(B, S, H_B, NOPE_B + V_B)
    k_nope, v = kv[..., :NOPE_B], kv[..., NOPE_B:]
    k_rope = _rope(kr, cos[None], sin[None])
    scale = (NOPE_B + ROPE_B) ** -0.5
    nq = S // Q_BLOCK
    blocks = lambda t: jnp.swapaxes(t.reshape(B, nq, Q_BLOCK, *t.shape[2:]), 0, 1)

    def attend(qs):
        qn, qr = qs
        s = (jnp.einsum('bqhd,bkhd->bhqk', qn, k_nope, preferred_element_type=jnp.float32)
             + jnp.einsum('bqhr,bkr->bhqk', qr, k_rope, preferred_element_type=jnp.float32)) * scale
        p = jax.nn.softmax(s, axis=-1)
        return jnp.einsum('bhqk,bkhd->bqhd', p.astype(v.dtype), v)

    o = lax.map(attend, (blocks(q_nope), blocks(q_rope)))
    return jnp.swapaxes(o, 0, 1).reshape(B, S, H_B * V_B)


def _even_mixer(h, w_in, w_out, w_alpha, b_alpha, gla_gain, q_norm, w_q_b, kv_norm, w_kv_b, cos, sin):
    q, k, v, g, a_lr, cq, ckv, kr = _split_cols(h @ w_in, EV_SIZES)
    oa = _gla_mixer(q, k, v, g, a_lr, w_alpha, b_alpha, gla_gain)
    ob = _mla_mixer(cq, ckv, kr, q_norm, w_q_b, kv_norm, w_kv_b, cos, sin)
    return jnp.concatenate([oa, ob], axis=-1) @ w_out


def _odd_mixer(h, w_in, w_out, v_norm, w_s, b_s):
    B, S, _ = h.shape
    zc, zd = _split_cols(h @ w_in, OD_SIZES)
    fc = jnp.fft.fft2(zc.astype(jnp.float32).reshape(B, S, H_C, CG_C), axes=(1, 3), norm='ortho').real
    fc = fc.reshape(B, S, H_C * CG_C).astype(h.dtype)
    u, v = jnp.split(jax.nn.gelu(zd, approximate=False), 2, axis=-1)
    v = _rmsnorm(v, v_norm).reshape(B, S // SGU_CHUNK, SGU_CHUNK, H_D, DG_D)
    sv = jnp.einsum('hij,bnjhc->bnihc', w_s, v) + jnp.transpose(b_s)[None, None, :, :, None]
    od = u * sv.reshape(B, S, H_D * DG_D)
    return jnp.concatenate([fc, od], axis=-1) @ w_out


def setup_inputs(seed: int = 0) -> dict:
    key = jax.random.key(seed)
    ks = jax.random.split(key, 24)
    nrm = lambda k, shape, s: s * jax.random.normal(k, shape, jnp.float32)
    D = D_MODEL
    return {
        'x_prompt': nrm(ks[0], (BATCH, SEQ, D), 1.0),
        'x_sample': nrm(ks[1], (DEC_BATCH, DEC_SEQ, D), 1.0),
        'c_prompt': nrm(ks[2], (BATCH, D), 1.0),
        'c_sample': nrm(ks[3], (DEC_BATCH, D), 1.0),
        'ada_w': nrm(ks[4], (DEPTH, D, 9 * D), 0.5 * D ** -0.5),
        'ada_b': nrm(ks[5], (DEPTH, 9 * D), 0.01),
        'norm_pre': 1.0 + nrm(ks[6], (DEPTH, 3, D), 0.05),
        'norm_post': 1.0 + nrm(ks[7], (DEPTH, 3, D), 0.05),
        'ffn_w13': nrm(ks[8], (DEPTH, 2, D, 2 * D_FF), D ** -0.5),
        'ffn_w2': nrm(ks[9], (DEPTH, 2, D_FF, D), D_FF ** -0.5),
        'ev_w_in': nrm(ks[10], (N_EVEN, D, EV_IN), D ** -0.5),
        'ev_w_out': nrm(ks[11], (N_EVEN, D_MIX_EV, D), D_MIX_EV ** -0.5),
        'gla_w_alpha': nrm(ks[12], (N_EVEN, 2, ALPHA_RANK, H_A * DK_A), ALPHA_RANK ** -0.5),
        'gla_b_alpha': nrm(ks[13], (N_EVEN, 2, H_A * DK_A), 0.1),
        'gla_norm': 1.0 + nrm(ks[14], (N_EVEN, DV_A), 0.05),
        'mla_q_norm': 1.0 + nrm(ks[15], (N_EVEN, Q_RANK), 0.05),
        'mla_w_q_b': nrm(ks[16], (N_EVEN, Q_RANK, H_B * (NOPE_B + ROPE_B)), Q_RANK ** -0.5),
        'mla_kv_norm': 1.0 + nrm(ks[17], (N_EVEN, KV_RANK), 0.05),
        'mla_w_kv_b': nrm(ks[18], (N_EVEN, KV_RANK, H_B * (NOPE_B + V_B)), KV_RANK ** -0.5),
        'od_w_in': nrm(ks[19], (N_ODD, D, OD_IN), D ** -0.5),
        'od_w_out': nrm(ks[20], (N_ODD, D_MIX_OD, D), D_MIX_OD ** -0.5),
        'sgu_norm': 1.0 + nrm(ks[21], (N_ODD, H_D * DG_D), 0.05),
        'sgu_w_s': nrm(ks[22], (N_ODD, H_D, SGU_CHUNK, SGU_CHUNK), 0.5 * SGU_CHUNK ** -0.5),
        'sgu_b': 1.0 + nrm(ks[23], (N_ODD, H_D, SGU_CHUNK), 0.1),
    }


def reference(x_prompt, x_sample, c_prompt, c_sample, ada_w, ada_b, norm_pre, norm_post, ffn_w13, ffn_w2,
              ev_w_in, ev_w_out, gla_w_alpha, gla_b_alpha, gla_norm, mla_q_norm, mla_w_q_b, mla_kv_norm,
              mla_w_kv_b, od_w_in, od_w_out, sgu_norm, sgu_w_s, sgu_b):
    def run(x, c):
        B, S, _ = x.shape
        half = ROPE_B // 2
        inv = ROPE_BASE ** (-jnp.arange(half, dtype=jnp.float32) / half)
        ang = jnp.arange(S, dtype=jnp.float32)[:, None] * inv[None, :]
        cos, sin = jnp.cos(ang), jnp.sin(ang)
        cc = jax.nn.silu(c)
        for l in range(DEPTH):
            m = (cc @ ada_w[l] + ada_b[l]).reshape(B, 3, 3, D_MODEL)
            x = _sublayer(x, lambda h: _swiglu(h, ffn_w13[l, 0], ffn_w2[l, 0]), m[:, 0],
                          norm_pre[l, 0], norm_post[l, 0], 0.5)
            i = l // 2
            if l % 2 == 0:
                mix = lambda h: _even_mixer(h, ev_w_in[i], ev_w_out[i], gla_w_alpha[i], gla_b_alpha[i], gla_norm[i],
                                            mla_q_norm[i], mla_w_q_b[i], mla_kv_norm[i], mla_w_kv_b[i], cos, sin)
            else:
                mix = lambda h: _odd_mixer(h, od_w_in[i], od_w_out[i], sgu_norm[i], sgu_w_s[i], sgu_b[i])
            x = _sublayer(x, mix, m[:, 1], norm_pre[l, 1], norm_post[l, 1], 1.0)
            x = _sublayer(x, lambda h: _swiglu(h, ffn_w13[l, 1], ffn_w2[l, 1]), m[:, 2],
                          norm_pre[l, 2], norm_post[l, 2], 0.5)
        return x

    y_prompt = run(x_prompt, c_prompt)
    y_sample = run(x_sample, c_sample)
    return (y_prompt, y_sample)
```

```python
import numpy as np
import concourse.bass as bass
import concourse.mybir as mybir
from concourse.bass_utils import run_bass_kernel_spmd
from contextlib import ExitStack

F32 = mybir.dt.float32
BF16 = mybir.dt.bfloat16
I32 = mybir.dt.int32
ACT = mybir.ActivationFunctionType
ALU = mybir.AluOpType

D = 1024
KC = 8
DFF = 2816
NFC = 22
EPS = 1e-6


class Buf:
    __slots__ = ("name", "w", "r")

    def __init__(self, name=""):
        self.name = name
        self.w = None
        self.r = {}


class EngW:
    def __init__(self, name, eng, sid, sem, inorder=False):
        self.name = name
        self.eng = eng
        self.sid = sid
        self.sem = sem
        self.cnt = 0
        self.known = {}
        self.inorder = inorder
        self.ring = []
        self.ring_pos = 0


class KB:
    def __init__(self, nc, nring=20):
        self.nc = nc
        self.es = ExitStack()
        self.sems = []
        self.semcnt = []
        self.engs = {}
        for name, eng, inorder in (("pe", nc.tensor, True), ("act", nc.scalar, False), ("dve", nc.vector, False),
                                   ("pool", nc.gpsimd, False), ("sp", nc.sync, False)):
            sid = self._newsem("c_" + name)
            self.engs[name] = EngW(name, eng, sid, self.sems[sid], inorder)
        for q in ("sp", "pool", "act"):
            E = self.engs[q]
            for i in range(nring):
                E.ring.append(self._newsem("d_%s%d" % (q, i)))
        self.pe, self.act, self.dve, self.pool, self.sp = (self.engs[n] for n in ("pe", "act", "dve", "pool", "sp"))

    def _newsem(self, name):
        s = self.es.enter_context(self.nc.semaphore(name))
        self.sems.append(s)
        self.semcnt.append(0)
        return len(self.sems) - 1

    def _waits(self, E, reads, writes, extra=()):
        need = {}
        for b in reads:
            if b.w is not None and need.get(b.w[0], 0) < b.w[1]:
                need[b.w[0]] = b.w[1]
        for b in writes:
            if b.w is not None and need.get(b.w[0], 0) < b.w[1]:
                need[b.w[0]] = b.w[1]
            for sid, val in b.r.items():
                if need.get(sid, 0) < val:
                    need[sid] = val
        for sid, val in extra:
            if need.get(sid, 0) < val:
                need[sid] = val
        for sid, val in need.items():
            if sid == E.sid and E.inorder:
                continue
            if E.known.get(sid, 0) >= val:
                continue
            E.eng.wait_ge(self.sems[sid], val)
            E.known[sid] = val

    def op(self, E, emit, reads=(), writes=()):
        self._waits(E, reads, writes)
        ins = emit(E.eng)
        E.cnt += 1
        ins.then_inc(E.sem, 1)
        self.semcnt[E.sid] = E.cnt
        for b in reads:
            if b.r.get(E.sid, 0) < E.cnt:
                b.r[E.sid] = E.cnt
        for b in writes:
            b.w = (E.sid, E.cnt)
            b.r = {}

    def dma(self, Q, out, in_, reads=(), writes=(), **kw):
        sid = Q.ring[Q.ring_pos]
        Q.ring_pos = (Q.ring_pos + 1) % len(Q.ring)
        prev = self.semcnt[sid]
        self._waits(Q, reads, writes, extra=((sid, prev),) if prev else ())
        ins = Q.eng.dma_start(out=out, in_=in_, **kw)
        self.semcnt[sid] = prev + 16
        ins.then_inc(self.sems[sid], 16)
        val = prev + 16
        for b in reads:
            if b.r.get(sid, 0) < val:
                b.r[sid] = val
        for b in writes:
            b.w = (sid, val)
            b.r = {}

    def barrier(self):
        for E in self.engs.values():
            for sid in range(len(self.sems)):
                val = self.semcnt[sid]
                if val and sid != E.sid and E.known.get(sid, 0) < val:
                    E.eng.wait_ge(self.sems[sid], val)
                    E.known[sid] = val
            if E.cnt and not E.inorder and E.known.get(E.sid, 0) < E.cnt:
                E.eng.wait_ge(E.sem, E.cnt)
                E.known[E.sid] = E.cnt

    def mm(self, out, lhsT, rhs, start, stop, reads, writes, **kw):
        self.op(self.pe, lambda e: e.matmul(out, lhsT, rhs, start=start, stop=stop, **kw), reads, writes)

    def actf(self, out, in_, func, reads, writes, **kw):
        self.op(self.act, lambda e: e.activation(out, in_, func, **kw), reads, writes)

    def tt(self, E, out, in0, in1, op, reads, writes):
        self.op(E, lambda e: e.tensor_tensor(out, in0, in1, op), reads, writes)

    def ts(self, E, out, in0, s1, s2, op0, op1, reads, writes):
        if op1 is None:
            self.op(E, lambda e: e.tensor_scalar(out, in0, s1, None, op0), reads, writes)
        else:
            self.op(E, lambda e: e.tensor_scalar(out, in0, s1, s2, op0, op1), reads, writes)

    def stt(self, out, in0, scalar, in1, op0, op1, reads, writes):
        self.op(self.dve, lambda e: e.scalar_tensor_tensor(out, in0, scalar, in1, op0, op1), reads, writes)

    def copy(self, E, out, in_, reads, writes):
        if E is self.act:
            self.op(E, lambda e: e.copy(out, in_), reads, writes)
        else:
            self.op(E, lambda e: e.tensor_copy(out, in_), reads, writes)


class Cfg:
    def __init__(self, seg_tokens=(4096, 4096, 4096), depth=4, do_mixer=True, n_cores=8, group=4):
        self.seg_tokens = tuple(seg_tokens)
        self.ntok = sum(seg_tokens)
        self.depth = depth
        self.do_mixer = 3 if do_mixer is True else int(do_mixer)
        self.nffn = depth * 2
        self.n_cores = n_cores
        self.ev_stop = 0
        self.a2_stop = 0
        self.no_xg = 0
        self.cc_max = 4 * 1024 * 1024
        self.group = group
        self.replica_groups = [list(range(g * group, (g + 1) * group)) for g in range(n_cores // group)]


def build(cfg):
    nc = bass.Bass("TRN2", target_bir_lowering=False)
    L = cfg.depth
    NF = cfg.nffn
    NT = cfg.ntok

    def din(name, shape, dt=F32):
        return nc.dram_tensor(name, list(shape), dt, kind="ExternalInput").ap()

    def dscr(name, shape, dt):
        return nc.dram_tensor(name, list(shape), dt, kind="Internal").ap()

    xin = din("xin", [NT, D])
    c3 = din("c3", [128, KC, 4])
    consts = din("consts", [128, 256])
    ada_w = din("ada_w", [L * 72, 128, KC, 128])
    ada_b = din("ada_b", [128, L * 72])
    npre = din("npre", [128, L * 3 * KC])
    npost = din("npost", [128, L * 3 * KC])
    w13 = din("w13", [NF, NFC, 128, KC * 256])
    w2 = din("w2", [NF, KC, 128, NFC * 128])
    yout = nc.dram_tensor("yout", [NT, D], F32, kind="ExternalOutput").ap()
    NOD = L // 2
    NEV = (L + 1) // 2
    SQ = cfg.seg_tokens[2]
    GRP = cfg.group
    iconst = din("iconst", [128, 8], I32)
    if NOD:
        od_win = din("od_win", [NOD, 128, KC * 1536])
        od_wout = din("od_wout", [NOD, 128, KC * 1024])
        sgu_wsT = din("sgu_wsT", [NOD, 128, 512])
        sgu_b = din("sgu_b", [NOD, 1, 512])
        sgu_nrm = din("sgu_nrm", [NOD, 128, 512])
        od_win_b = dscr("od_win_b", [NOD, 128, KC * 1536], BF16)
        od_wout_b = dscr("od_wout_b", [NOD, 128, KC * 1024], BF16)
        sgu_wsT_b = dscr("sgu_wsT_b", [NOD, 128, 512], BF16)
        sgu_b_b = dscr("sgu_b_b", [NOD, 1, 512], BF16)
    SMAXL = max(cfg.seg_tokens)
    tconst = din("tconst", [128, 516])
    fconst = din("fconst", [128, 64])
    if NEV:
        ev_win1 = din("ev_win1", [NEV, 128, KC * 1056])
        ev_win2 = din("ev_win2", [NEV, 128, KC * 1088])
        ev_woutg = din("ev_woutg", [NEV, 128, 4 * 1024])
        ev_woutm = din("ev_woutm", [NEV, 64, 8 * 1024])
        gla_wal = din("gla_wal", [NEV, 33, 512])
        gla_nrm = din("gla_nrm", [NEV, 128, 1])
        mla_qn = din("mla_qn", [NEV, 128, 2])
        mla_kvn = din("mla_kvn", [NEV, 128, 1])
        mla_wqb = din("mla_wqb", [NEV, 128, 2 * 1536])
        mla_wkvb = din("mla_wkvb", [NEV, 128, 1024])
        ev_win1_b = dscr("ev_win1_b", [NEV, 128, KC * 1056], BF16)
        ev_win2_b = dscr("ev_win2_b", [NEV, 128, KC * 1088], BF16)
        ev_woutg_b = dscr("ev_woutg_b", [NEV, 128, 4 * 1024], BF16)
        ev_woutm_b = dscr("ev_woutm_b", [NEV, 64, 8 * 1024], BF16)
        gla_wal_b = dscr("gla_wal_b", [NEV, 33, 512], BF16)
        mla_wqb_b = dscr("mla_wqb_b", [NEV, 128, 2 * 1536], BF16)
        mla_wkvb_b = dscr("mla_wkvb_b", [NEV, 128, 1024], BF16)
    NCH = SMAXL // 64
    gq = dscr("gq", [4, 2, 128, SMAXL], BF16)
    gvt = dscr("gvt", [SMAXL // 128, 128, 512], BF16)
    gkv = dscr("gkv", [2, NCH, 2, 128, 128], F32)
    gs = dscr("gs", [2, NCH, 2, 128, 128], BF16)
    gg = dscr("gg", [512, SMAXL], BF16)
    gsum = dscr("gsum", [4 * 128, 129], F32)
    gsum_all = dscr("gsum_all", [GRP * 4 * 128, 129], F32)
    Qd = dscr("Qd", [8, 96, SMAXL], BF16)
    Kd = dscr("Kd", [8 * 96, SMAXL], BF16)
    Kall = dscr("Kall", [8, GRP * 96, SQ], BF16)
    Vd = dscr("Vd", [8 * 128, (SMAXL // 128) * 65], BF16)
    Vall = dscr("Vall", [8, GRP * 128, (SQ // 128) * 65], BF16)
    mixm = dscr("mixm", [8, 64, SMAXL], BF16)
    Ud = dscr("Ud", [SMAXL, 1024], BF16)
    CC_MAX = cfg.cc_max
    RCU = min(SQ, max(128, (CC_MAX // (GRP * 2048)) // 128 * 128))
    NUC = SQ // RCU
    Uall = dscr("Uall", [NUC, GRP * RCU, 1024], BF16)
    mixo = dscr("mixo", [1024, SMAXL], BF16)

    w13b = dscr("w13b", [NF, NFC, 128, KC * 256], BF16)
    w2b = dscr("w2b", [NF, KC, 128, NFC * 128], BF16)
    modv = dscr("modv", [3, 128, L * 3 * 3 * KC], F32)

    K = KB(nc)
    es = K.es
    pe, act, dve, pool, sp = K.pe, K.act, K.dve, K.pool, K.sp

    uid = [0]

    def sb(name, shape, dt, stack=es):
        uid[0] += 1
        return stack.enter_context(nc.sbuf_tensor("%s_u%d" % (name, uid[0]), list(shape), dt))

    psb = [es.enter_context(nc.psum_tensor("ps%d" % i, [128, 512], F32)) for i in range(8)]
    PB = [Buf("ps%d" % i) for i in range(8)]

    SMAX = max(cfg.seg_tokens)
    xT = sb("xT", [128, KC, SMAX], F32)
    XB = [Buf("x%d" % i) for i in range(SMAX // 512)]
    cst = sb("cst", [128, 256], F32)
    onesb = sb("onesb", [128, 128], BF16)
    mv = sb("mv", [128, L * 3 * 3 * KC], F32)
    Bcst, Bones, Bmv = Buf("cst"), Buf("ones"), Buf("mv")
    ident = cst[:, 0:128]

    K.dma(sp, cst[:], consts, writes=[Bcst])
    K.copy(dve, onesb[:], cst[:, 128:256], [Bcst], [Bones])

    WB13 = [Buf("w13b%d" % f) for f in range(NF)]
    WB2 = [Buf("w2b%d" % f) for f in range(NF)]
    for f in range(NF):
        K.dma(pool, w13b[f], w13[f], writes=[WB13[f]], max_dma_last_dim=4096)
        K.dma(pool, w2b[f], w2[f], writes=[WB2[f]], max_dma_last_dim=4096)

    Bodw = Buf("odw")
    if NOD:
        for src, dst in ((od_win, od_win_b), (od_wout, od_wout_b), (sgu_wsT, sgu_wsT_b), (sgu_b, sgu_b_b)):
            K.dma(pool, dst, src, writes=[Bodw], max_dma_last_dim=4096)
    Bevw = Buf("evw")
    if NEV:
        for src, dst in ((ev_win1, ev_win1_b), (ev_win2, ev_win2_b), (ev_woutg, ev_woutg_b), (ev_woutm, ev_woutm_b),
                         (gla_wal, gla_wal_b), (mla_wqb, mla_wqb_b), (mla_wkvb, mla_wkvb_b)):
            K.dma(pool, dst, src, writes=[Bevw], max_dma_last_dim=4096)
    icst = sb("icst", [128, 8], I32)
    Bic = Buf("icst")
    K.dma(sp, icst[:], iconst, writes=[Bic])

    Bmodv = Buf("modv")
    with ExitStack() as ps:
        ccT = sb("ccT", [128, KC, 4], F32, ps)
        adab = sb("adab", [128, L * 72], F32, ps)
        gpre = sb("gpre", [128, L * 3 * KC], F32, ps)
        gpost = sb("gpost", [128, L * 3 * KC], F32, ps)
        mfm = sb("mfm", [128, L * 72, 4], F32, ps)
        mvall = sb("mvall", [128, 3, L * 3 * 3 * KC], F32, ps)
        NAW = 4
        awt = [sb("awt%d" % i, [128, KC, 128], F32, ps) for i in range(NAW)]
        Bcc, Badab, Bgpre, Bgpost, Bmfm, Bmvall = (Buf(n) for n in ("cc", "adab", "gpre", "gpost", "mfm", "mvall"))
        Bawt = [Buf("awt%d" % i) for i in range(NAW)]
        K.dma(sp, ccT[:], c3, writes=[Bcc])
        K.dma(sp, adab[:], ada_b, writes=[Badab])
        K.dma(sp, gpre[:], npre, writes=[Bgpre])
        K.dma(sp, gpost[:], npost, writes=[Bgpost])
        K.actf(ccT[:], ccT[:], ACT.Silu, [Bcc], [Bcc])
        for t in range(L * 72):
            wt, Bw = awt[t % NAW], Bawt[t % NAW]
            K.dma(sp, wt[:], ada_w[t], writes=[Bw])
            bank = 7 - (t % 2)
            for kc in range(KC):
                K.mm(psb[bank][:, 0:4], wt[:, kc, :], ccT[:, kc, :], kc == 0, kc == KC - 1, [Bw, Bcc], [PB[bank]])
            K.ts(dve, mfm[:, t, :], psb[bank][:, 0:4], adab[:, t:t + 1], None, ALU.add, None,
                 [PB[bank], Badab], [Bmfm])
        gp4 = gpost[:].rearrange("p (l j c) -> p l j c", l=L, j=3)
        for j in (0, 2):
            K.ts(dve, gp4[:, :, j, :], gp4[:, :, j, :], 0.5, None, ALU.mult, None, [Bgpost], [Bgpost])
        mf5 = mfm[:].rearrange("p (l j t c) b -> p l j t c b", l=L, j=3, t=3)
        mv5 = mvall[:].rearrange("p b (l j v c) -> p b l j v c", l=L, j=3, v=3)
        gpr4 = gpre[:].rearrange("p (l j c) -> p l j c", l=L, j=3)
        for b in range(3):
            for l in range(L):
                for j in range(3):
                    K.stt(mv5[:, b, l, j, 0, :], mf5[:, l, j, 1, :, b], 1.0, gpr4[:, l, j, :], ALU.add, ALU.mult,
                          [Bmfm, Bgpre], [Bmvall])
                    K.copy(dve, mv5[:, b, l, j, 1, :], mf5[:, l, j, 0, :, b], [Bmfm], [Bmvall])
                    K.stt(mv5[:, b, l, j, 2, :], mf5[:, l, j, 2, :, b], 1.0, gp4[:, l, j, :], ALU.add, ALU.mult,
                          [Bmfm, Bgpost], [Bmvall])
        K.dma(sp, modv.rearrange("b p n -> p b n"), mvall[:], reads=[Bmvall], writes=[Bmodv])
        K.barrier()

    def vec(l, j, v):
        o = ((l * 3 + j) * 3 + v) * KC
        return mv[:, o:o + KC]

    def load_segment(tok0, S, stack):
        xtok = [sb("xtok%d" % i, [128, D], F32, stack) for i in range(2)]
        Bxt = [Buf("xtok%d" % i) for i in range(2)]
        for i in range(S // 128):
            xt_, Bx = xtok[i % 2], Bxt[i % 2]
            K.dma(sp, xt_[:], xin[tok0 + i * 128: tok0 + (i + 1) * 128, :], writes=[Bx])
            for hh in range(2):
                bank = (2 * i + hh) % 4
                for q in range(4):
                    kc = hh * 4 + q
                    K.op(pe, lambda e, kc=kc, q=q, bank=bank: e.transpose(psb[bank][:, q * 128:(q + 1) * 128],
                                                                           xt_[:, kc * 128:(kc + 1) * 128], ident),
                         [Bx, Bcst], [PB[bank]])
                dst = xT[:, hh * 4:(hh + 1) * 4, i * 128:(i + 1) * 128]
                src = psb[bank][:, :].rearrange("p (q t) -> p q t", q=4)
                K.copy(act if hh == 0 else dve, dst, src, [PB[bank]], [XB[i // 4]])

    def store_segment(tok0, S, stack):
        yt = [sb("ytok%d" % i, [128, D], F32, stack) for i in range(2)]
        Byt = [Buf("ytok%d" % i) for i in range(2)]
        for i in range(S // 128):
            y_, By = yt[i % 2], Byt[i % 2]
            for hh in range(2):
                bank = (2 * i + hh) % 4
                for q in range(4):
                    kc = hh * 4 + q
                    K.op(pe, lambda e, kc=kc, q=q, bank=bank: e.transpose(psb[bank][:, q * 128:(q + 1) * 128],
                                                                           xT[:, kc, i * 128:(i + 1) * 128], ident),
                         [XB[i // 4], Bcst], [PB[bank]])
                K.copy(act if hh == 0 else dve, y_[:, hh * 512:(hh + 1) * 512], psb[bank][:, :], [PB[bank]], [By])
            K.dma(sp, yout[tok0 + i * 128: tok0 + (i + 1) * 128, :], y_[:], reads=[By])

    class FfnBufs:
        pass

    def ffn_alloc(stack):
        fb = FfnBufs()
        fb.h = sb("f_h", [128, KC, 512], BF16, stack)
        fb.g = sb("f_g", [128, NFC, 512], BF16, stack)
        fb.y = sb("f_y", [128, KC, 512], F32, stack)
        fb.s = sb("f_s", [128, 512], F32, stack)
        fb.w13 = [sb("f_w13_%d" % i, [128, KC, 256], BF16, stack) for i in range(3)]
        fb.w2 = [sb("f_w2_%d" % i, [128, 11, 128], BF16, stack) for i in range(3)]
        fb.rstd = [sb("f_rstd%d" % i, [128, 512], F32, stack) for i in range(2)]
        fb.sq = [sb("f_sq%d" % i, [128, 512], BF16, stack) for i in range(1)]
        fb.tmp = [sb("f_tmp%d" % i, [128, 512], F32, stack) for i in range(1)]
        fb.Bh, fb.By, fb.Bs = Buf("h"), Buf("y"), Buf("s")
        fb.Bg = [Buf("g%d" % i) for i in range(NFC)]
        fb.Bw13 = [Buf() for _ in range(3)]
        fb.Bw2 = [Buf() for _ in range(3)]
        fb.Brstd = [Buf(), Buf()]
        fb.Bsq = [Buf(), Buf()]
        fb.Btmp = [Buf(), Buf()]
        fb.n13 = 0
        fb.n2 = 0
        fb.nsq = 0
        fb.ntmp = 0
        return fb

    def rstd_from_ss(fb, ri, bank):
        K.actf(fb.rstd[ri][:], psb[bank][:, :], ACT.Sqrt, [PB[bank]], [fb.Brstd[ri]], scale=1.0 / D, bias=EPS)
        K.op(dve, lambda e: e.reciprocal(fb.rstd[ri][:], fb.rstd[ri][:]), [fb.Brstd[ri]], [fb.Brstd[ri]])

    def prenorm_steps(fb, l, j, tt, hdst, Bh):
        tsl = slice(tt * 512, (tt + 1) * 512)
        A, Bv = vec(l, j, 0), vec(l, j, 1)
        steps = []

        def p0():
            for kc in range(KC):
                i = fb.nsq % len(fb.sq)
                fb.nsq += 1
                K.actf(fb.sq[i][:], xT[:, kc, tsl], ACT.Square, [XB[tt]], [fb.Bsq[i]])
                K.mm(psb[6][:, :], onesb[:], fb.sq[i][:], kc == 0, kc == KC - 1, [Bones, fb.Bsq[i]], [PB[6]])
        steps.append(p0)
        steps.append(lambda: rstd_from_ss(fb, 0, 6))
        for kc in range(KC):
            def pk(kc=kc):
                i = fb.ntmp % len(fb.tmp)
                fb.ntmp += 1
                K.tt(dve, fb.tmp[i][:], xT[:, kc, tsl], fb.rstd[0][:], ALU.mult, [XB[tt], fb.Brstd[0]], [fb.Btmp[i]])
                K.actf(hdst[:, kc, :], fb.tmp[i][:], ACT.Identity, [fb.Btmp[i], Bmv], [Bh],
                       scale=A[:, kc:kc + 1], bias=Bv[:, kc:kc + 1])
            steps.append(pk)
        return steps

    def yphase(fb, tt, Cg, mm_oc, nxt, pending_tail):
        tsl = slice(tt * 512, (tt + 1) * 512)
        prev_sq = None
        for oc in range(KC):
            bank = 4 + oc % 2
            mm_oc(oc, bank)
            if prev_sq is not None:
                po, pi = prev_sq
                K.mm(psb[7][:, :], onesb[:], fb.sq[pi][:], po == 0, False, [Bones, fb.Bsq[pi]], [PB[7]])
            K.copy(act, fb.y[:, oc, :], psb[bank][:, :], [PB[bank]], [fb.By])
            i = fb.nsq % len(fb.sq)
            fb.nsq += 1
            K.actf(fb.sq[i][:], psb[bank][:, :], ACT.Square, [PB[bank]], [fb.Bsq[i]])
            prev_sq = (oc, i)
            if nxt:
                nxt.pop(0)()
        po, pi = prev_sq
        K.mm(psb[7][:, :], onesb[:], fb.sq[pi][:], False, True, [Bones, fb.Bsq[pi]], [PB[7]])
        while nxt:
            nxt.pop(0)()
        pending_tail.append(lambda: rstd_from_ss(fb, 1, 7))
        for oc in range(KC):
            def tl(oc=oc):
                i = fb.ntmp % len(fb.tmp)
                fb.ntmp += 1
                K.tt(dve, fb.tmp[i][:], fb.y[:, oc, :], fb.rstd[1][:], ALU.mult, [fb.By, fb.Brstd[1]],
                     [fb.Btmp[i]])
                K.stt(xT[:, oc, tsl], fb.tmp[i][:], Cg[:, oc:oc + 1], xT[:, oc, tsl], ALU.mult, ALU.add,
                      [fb.Btmp[i], Bmv, XB[tt]], [XB[tt]])
            pending_tail.append(tl)

    def ffn_sublayer(fb, l, j, S):
        f = l * 2 + (0 if j == 0 else 1)
        ntile = S // 512
        Cg = vec(l, j, 2)
        pending_tail = []
        for st in prenorm_steps(fb, l, j, 0, fb.h, fb.Bh):
            st()
        for tt in range(ntile):
            tsl = slice(tt * 512, (tt + 1) * 512)
            for fc in range(NFC):
                r = fb.n13 % 3
                fb.n13 += 1
                K.dma(sp, fb.w13[r][:], w13b[f, fc].rearrange("p (k n) -> p k n", k=KC), reads=[WB13[f]],
                      writes=[fb.Bw13[r]])
                ba, bb = fc % 2, 2 + fc % 2
                for half, bank in ((0, ba), (1, bb)):
                    for kc in range(KC):
                        K.mm(psb[bank][:, :], fb.w13[r][:, kc, half * 128:(half + 1) * 128], fb.h[:, kc, :],
                             kc == 0, kc == KC - 1, [fb.Bw13[r], fb.Bh], [PB[bank]])
                K.actf(fb.s[:], psb[ba][:, :], ACT.Silu, [PB[ba]], [fb.Bs])
                K.tt(dve, fb.g[:, fc, :], fb.s[:], psb[bb][:, :], ALU.mult, [fb.Bs, PB[bb]], [fb.Bg[fc]])
                if pending_tail:
                    pending_tail.pop(0)()
            while pending_tail:
                pending_tail.pop(0)()
            nxt = prenorm_steps(fb, l, j, tt + 1, fb.h, fb.Bh) if tt + 1 < ntile else []
            if nxt:
                nxt.pop(0)()
            def mm_oc(oc, bank, f=f):
                for hf in range(2):
                    r = fb.n2 % 3
                    fb.n2 += 1
                    K.dma(sp, fb.w2[r][:],
                          w2b[f, oc].rearrange("p (k n) -> p k n", k=NFC)[:, hf * 11:(hf + 1) * 11, :],
                          reads=[WB2[f]], writes=[fb.Bw2[r]])
                    for q in range(11):
                        fc = hf * 11 + q
                        K.mm(psb[bank][:, :], fb.w2[r][:, q, :], fb.g[:, fc, :], fc == 0, fc == NFC - 1,
                             [fb.Bw2[r], fb.Bg[fc]], [PB[bank]])
            yphase(fb, tt, Cg, mm_oc, nxt, pending_tail)
        while pending_tail:
            pending_tail.pop(0)()


    def mx_alloc(stack, with_h=True, with_y=False, nrstd=2):
        fb = FfnBufs()
        if with_h:
            fb.h = sb("m_h", [128, KC, 512], BF16, stack)
        if with_y:
            fb.y = sb("m_y", [128, KC, 512], F32, stack)
        fb.rstd = [sb("m_rstd%d" % i, [128, 512], F32, stack) for i in range(nrstd)]
        fb.sq = [sb("m_sq%d" % i, [128, 512], BF16, stack) for i in range(2)]
        fb.tmp = [sb("m_tmp%d" % i, [128, 512], F32, stack) for i in range(2)]
        fb.Bh, fb.By = Buf("h"), Buf("y")
        fb.Brstd = [Buf(), Buf()]
        fb.Bsq = [Buf(), Buf()]
        fb.Btmp = [Buf(), Buf()]
        fb.nsq = 0
        fb.ntmp = 0
        return fb

    class Gen:
        pass

    def gen_alloc(stack, mask_col, use_pjx, nA=2, blocks=True):
        G = Gen()
        G.PJ = sb("g_pj", [128, 512], I32, stack)
        G.A = [sb("g_a%d" % i, [128, 512], I32, stack) for i in range(nA)]
        G.BPJ = Buf("pj")
        G.BA = [Buf() for _ in range(nA)]
        G.n = 0
        G.mask = icst[:, mask_col:mask_col + 1]
        K.op(pool, lambda e: e.iota(G.PJ[:], [[1, 512]], base=0, channel_multiplier=0), [], [G.BPJ])
        K.op(pool, lambda e: e.iota(G.A[0][:], [[0, 512]], base=0, channel_multiplier=1), [], [G.BA[0]])
        if blocks:
            G.Jf = sb("g_jf", [128, 512], I32, stack)
            G.BJf = Buf()
            K.copy(dve, G.Jf[:], G.PJ[:], [G.BPJ], [G.BJf])
        K.op(pool, lambda e: e.tensor_tensor(G.PJ[:], G.PJ[:], G.A[0][:], ALU.mult), [G.BPJ, G.BA[0]], [G.BPJ])
        if blocks:
            G.Pf = sb("g_pf", [128, 512], I32, stack)
            G.PJ0 = sb("g_pj0", [128, 512], I32, stack)
            G.BPf, G.BPJ0 = Buf(), Buf()
            K.copy(dve, G.Pf[:], G.A[0][:], [G.BA[0]], [G.BPf])
        if use_pjx:
            K.op(pool, lambda e: e.iota(G.A[0][:], [[0, 512]], base=0, channel_multiplier=0), [], [G.BA[0]])
            K.op(pool, lambda e: e.tensor_scalar(G.A[0][:], G.A[0][:], icst[:, 4:5], None, ALU.add),
                 [G.BA[0], Bic], [G.BA[0]])
            K.op(pool, lambda e: e.tensor_tensor(G.PJ[:], G.PJ[:], G.A[0][:], ALU.add), [G.BPJ, G.BA[0]], [G.BPJ])
        if blocks:
            K.copy(dve, G.PJ0[:], G.PJ[:], [G.BPJ], [G.BPJ0])
        return G

    def gen_block(G, S_tot, sp0):
        K.op(pool, lambda e: e.tensor_scalar(G.PJ[:], G.Pf[:], int(sp0 % S_tot), None, ALU.mult), [G.BPf], [G.BPJ])
        K.op(pool, lambda e: e.tensor_tensor(G.PJ[:], G.PJ[:], G.PJ0[:], ALU.add), [G.BPJ, G.BPJ0], [G.BPJ])

    def gen_tile(G, dst, Bdst, S_tot, s0, sp0, off):
        base = (s0 * sp0 + off) % S_tot
        step = s0 % S_tot
        i = G.n % len(G.A)
        G.n += 1
        A = G.A[i]
        if step == 0:
            K.op(pool, lambda e: e.iota(A[:], [[0, 512]], base=base, channel_multiplier=0), [], [G.BA[i]])
        else:
            K.op(pool, lambda e: e.tensor_scalar(A[:], G.Jf[:], int(step), int(base), ALU.mult, ALU.add), [G.BJf],
                 [G.BA[i]])
        K.op(pool, lambda e: e.tensor_tensor(A[:], A[:], G.PJ[:], ALU.add), [G.BA[i], G.BPJ], [G.BA[i]])
        K.op(dve, lambda e: e.tensor_scalar(A[:], A[:], G.mask, None, ALU.bitwise_and), [G.BA[i], Bic], [G.BA[i]])
        K.actf(dst, A[:], ACT.Sin, [G.BA[i]], [Bdst], scale=2.0 * np.pi / S_tot, bias=-np.pi)

    BUd, BUall, Bmixo = Buf("Ud"), Buf("Uall"), Buf("mixo")

    def odd_phase_a(l, S, stack):
        i_od = l // 2
        fb = mx_alloc(stack, nrstd=1)
        win = sb("o_win", [128, KC, 1536], BF16, stack)
        wsT = sb("o_wsT", [128, 512], BF16, stack)
        sgb = sb("o_sgb", [1, 512], BF16, stack)
        nrm = sb("o_nrm", [128, 512], F32, stack)
        ccsc = sb("o_ccsc", [128, 256], BF16, stack)
        gtmp = sb("o_gtmp", [128, 512], BF16, stack)
        zcT = sb("o_zcT", [128, 4, 512], BF16, stack)
        usb = [sb("o_usb%d" % i, [128, 1024], BF16, stack) for i in range(2)]
        uT = sb("o_uT", [128, 4, 512], F32, stack)
        gv = sb("o_gv", [128, 512], F32, stack)
        vtok = [sb("o_vtok%d" % i, [128, 512], BF16, stack) for i in range(2)]
        odT = sb("o_odT", [128, 4, 512], BF16, stack)
        ssq = sb("o_ssq", [128, 2], F32, stack)
        Bwin, BwsT, Bsgb, Bnrm, Bccsc, Bgtmp, BzcT, BuT, Bgv, BodT, Bssq = (Buf() for _ in range(11))
        Busb = [Buf(), Buf()]
        Bvtok = [Buf(), Buf()]
        K.dma(sp, win[:], od_win_b[i_od].rearrange("p (k n) -> p k n", k=KC), reads=[Bodw], writes=[Bwin])
        K.dma(sp, wsT[:], sgu_wsT_b[i_od], reads=[Bodw], writes=[BwsT])
        K.dma(sp, sgb[:], sgu_b_b[i_od], reads=[Bodw], writes=[Bsgb])
        K.dma(sp, nrm[:], sgu_nrm[i_od], writes=[Bnrm])
        G = gen_alloc(stack, 2, False, nA=1, blocks=False)
        gen_tile(G, gtmp[:], Bgtmp, 128, 0, 0, 96)
        K.copy(dve, ccsc[:, 0:128], gtmp[:, 0:128], [Bgtmp], [Bccsc])
        gen_tile(G, gtmp[:], Bgtmp, 128, 0, 0, 0)
        K.copy(dve, ccsc[:, 128:256], gtmp[:, 0:128], [Bgtmp], [Bccsc])
        mixo_v = mixo.rearrange("(c p) s -> p c s", p=128)
        nb = [0]

        def bank2():
            nb[0] += 1
            return nb[0] % 2

        for tt in range(S // 512):
            for st in prenorm_steps(fb, l, 1, tt, fb.h, fb.Bh):
                st()
            for g in range(4):
                bank = bank2()
                for kc in range(KC):
                    K.mm(psb[bank][:, :], win[:, kc, g * 128:(g + 1) * 128], fb.h[:, kc, :], kc == 0, kc == KC - 1,
                         [Bwin, fb.Bh], [PB[bank]])
                K.copy(act if g % 2 == 0 else dve, zcT[:, g, :], psb[bank][:, :], [PB[bank]], [BzcT])
            for g in range(4):
                bank = bank2()
                for kc in range(KC):
                    K.mm(psb[bank][:, :], win[:, kc, 512 + g * 128:512 + (g + 1) * 128], fb.h[:, kc, :], kc == 0,
                         kc == KC - 1, [Bwin, fb.Bh], [PB[bank]])
                K.actf(uT[:, g, :], psb[bank][:, :], ACT.Gelu, [PB[bank]], [BuT])
            for ts in range(4):
                tk = slice(ts * 128, (ts + 1) * 128)
                ub, Bub = usb[ts % 2], Busb[ts % 2]
                for gp in range(2):
                    bank = 2 + gp
                    for gg in range(2):
                        g = gp * 2 + gg
                        K.mm(psb[bank][:, gg * 256:(gg + 1) * 256], zcT[:, g, tk], ccsc[:], True, True,
                             [BzcT, Bccsc], [PB[bank]])
                    K.copy(act if gp == 0 else dve, ub[:, gp * 512:(gp + 1) * 512], psb[bank][:, :], [PB[bank]],
                           [Bub])
                K.dma(sp, Ud[tt * 512 + ts * 128: tt * 512 + (ts + 1) * 128, :], ub[:], reads=[Bub], writes=[BUd])
                bank = bank2()
                for kc in range(KC):
                    K.mm(psb[bank][:, :], fb.h[:, kc, tk], win[:, kc, 1024:1536], kc == 0, kc == KC - 1,
                         [Bwin, fb.Bh], [PB[bank]])
                K.actf(gv[:], psb[bank][:, :], ACT.Gelu, [PB[bank]], [Bgv])
                vt, Bvt = vtok[ts % 2], Bvtok[ts % 2]
                K.actf(vt[:], gv[:], ACT.Square, [Bgv], [Bvt, Bssq], accum_out=ssq[:, 0:1])
                K.actf(ssq[:, 1:2], ssq[:, 0:1], ACT.Sqrt, [Bssq], [Bssq], scale=1.0 / 512, bias=EPS)
                K.op(dve, lambda e: e.reciprocal(ssq[:, 1:2], ssq[:, 1:2]), [Bssq], [Bssq])
                K.stt(vt[:], gv[:], ssq[:, 1:2], nrm[:], ALU.mult, ALU.mult, [Bgv, Bssq, Bnrm], [Bvt])
                bank = 6
                for hd in range(4):
                    hs = slice(hd * 128, (hd + 1) * 128)
                    K.mm(psb[bank][:, hs], vt[:, hs], wsT[:, hs], True, False, [Bvt, BwsT], [PB[bank]])
                    K.mm(psb[bank][:, hs], onesb[0:1, :], sgb[0:1, hs], False, True, [Bones, Bsgb], [PB[bank]])
                K.tt(dve, odT[:, :, tk], uT[:, :, tk], psb[bank][:, :].rearrange("p (h i) -> p h i", h=4), ALU.mult,
                     [BuT, PB[bank]], [BodT])
            K.dma(sp, mixo_v[:, 4:8, tt * 512:(tt + 1) * 512], odT[:], reads=[BodT], writes=[Bmixo])

    def odd_phase_b(S, S_keys, Usrc, BUsrc, is_sample, stack):
        G = gen_alloc(stack, 1 if is_sample else 0, is_sample)
        ct = [sb("b_ct%d" % i, [128, 512], BF16, stack) for i in range(2)]
        stl = [sb("b_st%d" % i, [128, 512], BF16, stack) for i in range(2)]
        ut = [sb("b_ut%d" % i, [128, 1024], BF16, stack) for i in range(3)]
        fcs = sb("b_fcs", [128, 4, 512], BF16, stack)
        Bct, Bst = [Buf(), Buf()], [Buf(), Buf()]
        But = [Buf(), Buf(), Buf()]
        Bfcs = Buf()
        mixo_v = mixo.rearrange("(c p) s -> p c s", p=128)
        scale = 1.0 / float(np.sqrt(S_keys * 128.0))
        na = S_keys // 128
        n = 0
        for bq in range(S // 512):
            sp0 = bq * 512
            gen_block(G, S_keys, sp0)
            for a in range(na):
                s0 = a * 128
                i2, i3 = n % 2, n % 3
                n += 1
                gen_tile(G, ct[i2][:], Bct[i2], S_keys, s0, sp0, (3 * S_keys) // 4)
                gen_tile(G, stl[i2][:], Bst[i2], S_keys, s0, sp0, S_keys // 2)
                K.dma(sp, ut[i3][:], Usrc(s0), reads=[BUsrc], writes=[But[i3]])
                for g in range(4):
                    K.mm(psb[g][:, :], ut[i3][:, g * 256:g * 256 + 128], ct[i2][:], a == 0, False,
                         [But[i3], Bct[i2]], [PB[g]])
                    K.mm(psb[g][:, :], ut[i3][:, g * 256 + 128:g * 256 + 256], stl[i2][:], False, a == na - 1,
                         [But[i3], Bst[i2]], [PB[g]])
            for g in range(4):
                if g % 2 == 0:
                    K.actf(fcs[:, g, :], psb[g][:, :], ACT.Copy, [PB[g]], [Bfcs], scale=scale)
                else:
                    K.ts(dve, fcs[:, g, :], psb[g][:, :], scale, None, ALU.mult, None, [PB[g]], [Bfcs])
            K.dma(sp, mixo_v[:, 0:4, bq * 512:(bq + 1) * 512], fcs[:], reads=[Bfcs], writes=[Bmixo])

    def mixer_phase_c(l, S, wout_dram, Bw_dram, stack):
        fb = mx_alloc(stack, with_h=False, with_y=True)
        wo = sb("c_wo", [128, KC, 1024], BF16, stack)
        ot = [sb("c_ot%d" % i, [128, KC, 512], BF16, stack) for i in range(2)]
        Bwo = Buf()
        Bot = [Buf(), Buf()]
        K.dma(sp, wo[:], wout_dram.rearrange("p (k n) -> p k n", k=KC), reads=[Bw_dram], writes=[Bwo])
        mixo_v = mixo.rearrange("(c p) s -> p c s", p=128)
        Cg = vec(l, 1, 2)
        pending = []
        for tt in range(S // 512):
            o_, Bo = ot[tt % 2], Bot[tt % 2]
            K.dma(sp, o_[:], mixo_v[:, :, tt * 512:(tt + 1) * 512], reads=[Bmixo], writes=[Bo])

            def mm_oc(oc, bank, o_=o_, Bo=Bo):
                for ic in range(KC):
                    K.mm(psb[bank][:, :], wo[:, ic, oc * 128:(oc + 1) * 128], o_[:, ic, :], ic == 0, ic == KC - 1,
                         [Bwo, Bo], [PB[bank]])
            yphase(fb, tt, Cg, mm_oc, [], pending)
            while pending:
                pending.pop(0)()

    def odd_mixer(l, S, is_sample):
        i_od = l // 2
        with ExitStack() as st:
            odd_phase_a(l, S, st)
            K.barrier()
        if is_sample and GRP > 1 and not cfg.no_xg:
            for ci in range(NUC):
                K.op(pool, lambda e, ci=ci: e.collective_compute(
                    "AllGather", ALU.bypass, replica_groups=cfg.replica_groups,
                    ins=[Ud[ci * RCU:(ci + 1) * RCU, :]], outs=[Uall[ci]]), [BUd], [BUall])
            K.barrier()

            def usrc(s0):
                g, i = s0 // S, s0 % S
                ci, w = i // RCU, i % RCU
                return Uall[ci, g * RCU + w:g * RCU + w + 128, :]
            Usrc, BUsrc, S_keys = usrc, BUall, GRP * S
        else:
            Usrc, BUsrc, S_keys = (lambda s0: Ud[s0:s0 + 128, :]), BUd, S
        with ExitStack() as st:
            odd_phase_b(S, S_keys, Usrc, BUsrc, is_sample and GRP > 1 and not cfg.no_xg, st)
            K.barrier()
        with ExitStack() as st:
            mixer_phase_c(l, S, od_wout_b[i_od], Bodw, st)
            K.barrier()

    fcst = sb("fcst", [128, 64], F32)
    Bfc = Buf("fcst")
    K.dma(sp, fcst[:], fconst, writes=[Bfc])
    NCHL = SMAXL // 64
    decs = sb("decs", [128, 2, 2, NCHL], F32)
    Bdecs = Buf("decs")
    Bgq, Bgvt, Bgkv, Bgs, Bgg, Bgsum, Bgsall = (Buf() for _ in range(7))
    BQd, BKd, BKall, BVd, BVall, Bmixm = (Buf() for _ in range(6))
    rr = [0]

    def rbank(lo=0, n=2):
        rr[0] += 1
        return lo + rr[0] % n

    def even_a1(l, S, stack):
        i_ev = l // 2
        fb = mx_alloc(stack)
        win = sb("a_win", [128, KC, 1056], BF16, stack)
        wal = sb("a_wal", [33, 512], BF16, stack)
        tc = sb("a_tc", [128, 516], F32, stack)
        alr = sb("a_alr", [33, 512], BF16, stack)
        qk = sb("a_qk", [128, 4, 512], F32, stack)
        spt = sb("a_spt", [128, 512], F32, stack)
        E = sb("a_E", [128, 4, 2, 128], F32, stack)
        ekd = sb("a_ekd", [128, 512], F32, stack)
        kd = sb("a_kd", [128, 512], BF16, stack)
        vtok = [sb("a_vtok%d" % i, [128, 512], BF16, stack) for i in range(2)]
        kvst = sb("a_kvst", [128, 2, 4, 128], F32, stack)
        qst = sb("a_qst", [128, 4, 2, 512], BF16, stack)
        Bwin, Bwal, Btc, Balr, Bqk, Bspt, BE, Bekd, Bkd, Bkvst, Bqst = (Buf() for _ in range(11))
        Bvtok = [Buf(), Buf()]
        K.dma(sp, win[:], ev_win1_b[i_ev].rearrange("p (k n) -> p k n", k=KC), reads=[Bevw], writes=[Bwin])
        K.dma(sp, wal[:], gla_wal_b[i_ev], reads=[Bevw], writes=[Bwal])
        K.dma(sp, tc[:], tconst, writes=[Btc])
        K.op(dve, lambda e: e.memset(alr[32:33, :], 1.0), [], [Balr])
        gq_v = gq.rearrange("k r p s -> p k r s")
        for tt in range(S // 512):
            for st in prenorm_steps(fb, l, 1, tt, fb.h, fb.Bh):
                st()
            for c4 in range(4):
                bank = rbank()
                for kc in range(KC):
                    K.mm(psb[bank][:, :], win[:, kc, c4 * 128:(c4 + 1) * 128], fb.h[:, kc, :], kc == 0, kc == KC - 1,
                         [Bwin, fb.Bh], [PB[bank]])
                K.copy(act if c4 % 2 == 0 else dve, qk[:, c4, :], psb[bank][:, :], [PB[bank]], [Bqk])
            bank = rbank()
            for kc in range(KC):
                K.mm(psb[bank][0:32, :], win[:, kc, 1024:1056], fb.h[:, kc, :], kc == 0, kc == KC - 1,
                     [Bwin, fb.Bh], [PB[bank]])
            K.copy(act, alr[0:32, :], psb[bank][0:32, :], [PB[bank]], [Balr])
            for ts in range(4):
                tk = slice(ts * 128, (ts + 1) * 128)
                n = tt * 4 + ts
                vt, Bvt = vtok[n % 2], Bvtok[n % 2]
                for kc in range(KC):
                    K.mm(psb[2][:, 0:256], fb.h[:, kc, tk], win[:, kc, 256:512], kc == 0, kc == KC - 1,
                         [Bwin, fb.Bh], [PB[2]])
                for kc in range(KC):
                    K.mm(psb[3][:, :], fb.h[:, kc, tk], win[:, kc, 512:1024], kc == 0, kc == KC - 1,
                         [Bwin, fb.Bh], [PB[3]])
                K.copy(act, vt[:], psb[3][:, :], [PB[3]], [Bvt])
                K.dma(sp, gvt[n], vt[:], reads=[Bvt], writes=[Bgvt])
                K.mm(psb[4][:, :], alr[0:33, tk], wal[0:33, :], True, True, [Balr, Bwal], [PB[4]])
                K.actf(spt[:], psb[4][:, :], ACT.Exp, [PB[4]], [Bspt], scale=-1.0)
                K.actf(spt[:], spt[:], ACT.Ln, [Bspt], [Bspt], bias=1.0)
                for pr in range(2):
                    K.mm(psb[5][:, pr * 130:pr * 130 + 130], spt[:, pr * 128:(pr + 1) * 128], tc[:, 0:130], True, True,
                         [Bspt, Btc], [PB[5]])
                for pr in range(2):
                    K.mm(psb[6 + pr][:, 0:258], spt[:, 256 + pr * 128:256 + (pr + 1) * 128], tc[:, 130:388], True,
                         True, [Bspt, Btc], [PB[6 + pr]])
                K.mm(psb[4][:, 0:256], tc[:, 130:258], spt[:, 0:256], True, True, [Bspt, Btc], [PB[4]])
                K.mm(psb[4][:, 256:512], tc[:, 388:516], spt[:, 256:512], True, True, [Bspt, Btc], [PB[4]])
                sc = 1.0 / 16.0
                for pr in range(2):
                    K.actf(E[:, 0, pr, :], psb[5][:, pr * 130:pr * 130 + 128], ACT.Exp, [PB[5]], [BE], scale=-sc)
                    K.actf(E[:, 1, pr, :], psb[5][:, pr * 130:pr * 130 + 128], ACT.Exp, [PB[5]], [BE], scale=sc)
                    K.actf(decs[:, 0, pr, 2 * n:2 * n + 2], psb[5][:, pr * 130 + 128:pr * 130 + 130], ACT.Exp,
                           [PB[5]], [Bdecs], scale=-sc)
                    K.actf(E[:, 2, pr, :], psb[6 + pr][:, 0:128], ACT.Exp, [PB[6 + pr]], [BE], scale=-sc)
                    K.actf(E[:, 3, pr, :], psb[6 + pr][:, 128:256], ACT.Exp, [PB[6 + pr]], [BE], scale=sc)
                    K.actf(decs[:, 1, pr, 2 * n:2 * n + 2], psb[6 + pr][:, 256:258], ACT.Exp, [PB[6 + pr]], [Bdecs],
                           scale=-sc)
                K.actf(ekd[:], psb[4][:, :], ACT.Exp, [PB[4]], [Bekd], scale=-sc)
                K.tt(dve, kd[:, 0:256], psb[2][:, 0:256], ekd[:, 0:256], ALU.mult, [PB[2], Bekd], [Bkd])
                K.tt(dve, kd[:, 256:512], psb[2][:, 0:256], ekd[:, 256:512], ALU.mult, [PB[2], Bekd], [Bkd])
                for pr in range(2):
                    K.stt(qst[:, 0, pr, tk], qk[:, pr, tk], 0.125, E[:, 0, pr, :], ALU.mult, ALU.mult, [Bqk, BE], [Bqst])
                    K.stt(qst[:, 1, pr, tk], qk[:, pr, tk], 0.125, E[:, 2, pr, :], ALU.mult, ALU.mult, [Bqk, BE], [Bqst])
                    K.tt(pool, qst[:, 2, pr, tk], qk[:, 2 + pr, tk], E[:, 1, pr, :], ALU.mult, [Bqk, BE], [Bqst])
                    K.tt(pool, qst[:, 3, pr, tk], qk[:, 2 + pr, tk], E[:, 3, pr, :], ALU.mult, [Bqk, BE], [Bqst])
                for c in range(2):
                    for dr in range(2):
                        for h in range(4):
                            hb = (h % 2) * 64
                            col = (dr * 2 + h // 2) * 128
                            K.mm(psb[c][hb:hb + 64, col:col + 128],
                                 kd[c * 64:(c + 1) * 64, dr * 256 + h * 64:dr * 256 + (h + 1) * 64],
                                 vt[c * 64:(c + 1) * 64, h * 128:(h + 1) * 128], True, True, [Bkd, Bvt], [PB[c]])
                    K.copy(act if c == 0 else dve, kvst[:, :, c * 2:c * 2 + 2, :],
                           psb[c][:, :].rearrange("p (d r v) -> p d r v", d=2, r=2), [PB[c]], [Bkvst])
                for dr in range(2):
                    K.dma(sp, gkv[dr, 2 * n:2 * n + 2].rearrange("c r p v -> p c r v"),
                          kvst[:, dr, :, :].rearrange("p (c r) v -> p c r v", c=2), reads=[Bkvst], writes=[Bgkv])
            K.dma(sp, gq_v[:, :, :, tt * 512:(tt + 1) * 512], qst[:], reads=[Bqst], writes=[Bgq])

    def even_r(S, stack, store, Sin=None):
        nch = S // 64
        CB = min(8, nch)
        St = [[sb("r_st%d%d" % (d_, p_), [128, 128], F32, stack) for p_ in range(2)] for d_ in range(2)]
        BSt = [[Buf(), Buf()], [Buf(), Buf()]]
        kvb = [sb("r_kvb%d" % i, [128, CB, 2, 128], F32, stack) for i in range(2)]
        stb = [sb("r_stb%d" % i, [128, CB, 2, 128], BF16, stack) for i in range(2)]
        Bkvb, Bstb = [Buf(), Buf()], [Buf(), Buf()]
        nb = 0
        for dr in range(2):
            for pr in range(2):
                if Sin is None:
                    K.op(dve, lambda e, dr=dr, pr=pr: e.memset(St[dr][pr][:], 0.0), [], [BSt[dr][pr]])
                else:
                    K.copy(dve, St[dr][pr][:], Sin[dr][pr][0][:], [Sin[dr][pr][1]], [BSt[dr][pr]])
            batches = list(range(0, nch, CB))
            if dr == 1:
                batches = batches[::-1]
            for c0 in batches:
                kb, Bk = kvb[nb % 2], Bkvb[nb % 2]
                sbf, Bs_ = stb[nb % 2], Bstb[nb % 2]
                nb += 1
                K.dma(sp, kb[:], gkv[dr, c0:c0 + CB].rearrange("c r p v -> p c r v"), reads=[Bgkv], writes=[Bk])
                cis = list(range(CB))
                if dr == 1:
                    cis = cis[::-1]
                for ci in cis:
                    c = c0 + ci
                    for pr in range(2):
                        if store:
                            K.copy(act, sbf[:, ci, pr, :], St[dr][pr][:], [BSt[dr][pr]], [Bs_])
                        K.stt(St[dr][pr][:], St[dr][pr][:], decs[:, dr, pr, c:c + 1], kb[:, ci, pr, :], ALU.mult,
                              ALU.add, [BSt[dr][pr], Bdecs, Bk], [BSt[dr][pr]])
                if store:
                    K.dma(sp, gs[dr, c0:c0 + CB].rearrange("c r p v -> p c r v"), sbf[:], reads=[Bs_], writes=[Bgs])
        return St, BSt

    def even_exchange(S, stack):
        nch = S // 64
        St, BSt = even_r(S, stack, False)
        pk = sb("x_pk", [128, 4, 129], F32, stack)
        Bpk = Buf()
        for dr in range(2):
            for pr in range(2):
                k4 = dr * 2 + pr
                K.copy(dve, pk[:, k4, 0:128], St[dr][pr][:], [BSt[dr][pr]], [Bpk])
                K.copy(dve, pk[:, k4, 128:129], decs[:, dr, pr, 0:1], [Bdecs], [Bpk])
                for c in range(1, nch):
                    K.tt(dve, pk[:, k4, 128:129], pk[:, k4, 128:129], decs[:, dr, pr, c:c + 1], ALU.mult,
                         [Bpk, Bdecs], [Bpk])
        K.dma(sp, gsum.rearrange("(k p) v -> p k v", p=128), pk[:], reads=[Bpk], writes=[Bgsum])
        K.barrier()
        K.op(pool, lambda e: e.collective_compute("AllGather", ALU.bypass, replica_groups=cfg.replica_groups,
                                                  ins=[gsum], outs=[gsum_all]), [Bgsum], [Bgsall])
        K.barrier()
        pa = sb("x_pa", [128, GRP, 4, 129], F32, stack)
        Bpa = Buf()
        K.dma(sp, pa[:], gsum_all.rearrange("(g k p) v -> p g k v", g=GRP, p=128), reads=[Bgsall], writes=[Bpa])
        Sin = [[None, None], [None, None]]
        cf = sb("x_cf", [128, 2], F32, stack)
        Bcf = Buf()
        for dr in range(2):
            for pr in range(2):
                k4 = dr * 2 + pr
                t_ = sb("x_sin%d" % k4, [128, 128], F32, stack)
                Bt = Buf()
                K.op(dve, lambda e, t_=t_: e.memset(t_[:], 0.0), [], [Bt])
                for r1 in range(GRP):
                    K.copy(dve, cf[:, 0:1], fcst[:, 8 + dr * 4 + r1:9 + dr * 4 + r1], [Bfc], [Bcf])
                    for r2 in range(GRP):
                        ic_ = 16 + dr * 16 + r1 * 4 + r2
                        K.ts(dve, cf[:, 1:2], pa[:, r2, k4, 128:129], -1.0, fcst[:, ic_:ic_ + 1], ALU.add, ALU.mult,
                             [Bpa, Bfc], [Bcf])
                        K.stt(cf[:, 0:1], cf[:, 1:2], 1.0, cf[:, 0:1], ALU.add, ALU.mult, [Bcf], [Bcf])
                    K.stt(t_[:], pa[:, r1, k4, 0:128], cf[:, 0:1], t_[:], ALU.mult, ALU.add, [Bpa, Bcf, Bt], [Bt])
                Sin[dr][pr] = (t_, Bt)
        return Sin

    def even_o(l, S, stack):
        i_ev = l // 2
        tcm = sb("o_tcm", [128, 256], F32, stack)
        gn = sb("o_gn", [128, 1], F32, stack)
        qt = [sb("o_qt%d" % i, [128, 4, 2, 512], BF16, stack) for i in range(2)]
        vtl = [sb("o_vt%d" % i, [128, 4, 512], BF16, stack) for i in range(2)]
        gt = [sb("o_gt%d" % i, [128, 4, 512], BF16, stack) for i in range(2)]
        sf = [sb("o_sf%d" % i, [128, 8, 2, 128], BF16, stack) for i in range(2)]
        sbw = [sb("o_sb%d" % i, [128, 8, 2, 128], BF16, stack) for i in range(2)]
        am = [sb("o_am%d" % i, [128, 256], BF16, stack) for i in range(2)]
        sq = sb("o_sq", [128, 512], BF16, stack)
        rs = sb("o_rs", [128, 512], F32, stack)
        on = sb("o_on", [128, 512], F32, stack)
        ost = sb("o_ost", [128, 4, 512], BF16, stack)
        Btcm, Bgn, Bsq, Brs, Bon, Bost = (Buf() for _ in range(6))
        Bqt, Bvtl, Bgt, Bsf, Bsbw, Bam = ([Buf(), Buf()] for _ in range(6))
        K.dma(sp, tcm[:, 0:128], tconst[:, 0:128], writes=[Btcm])
        K.dma(sp, tcm[:, 128:256], tconst[:, 130:258], writes=[Btcm])
        K.dma(sp, gn[:], gla_nrm[i_ev], writes=[Bgn])
        gq_v = gq.rearrange("k r p s -> p k r s")
        gg_v = gg.rearrange("(c p) s -> p c s", p=128)
        mixo_v = mixo.rearrange("(c p) s -> p c s", p=128)
        na = 0
        for tt in range(S // 512):
            i2 = tt % 2
            K.dma(sp, qt[i2][:], gq_v[:, :, :, tt * 512:(tt + 1) * 512], reads=[Bgq], writes=[Bqt[i2]])
            K.dma(sp, vtl[i2][:], gvt[tt * 4:(tt + 1) * 4].rearrange("n p v -> p n v"), reads=[Bgvt], writes=[Bvtl[i2]])
            K.dma(sp, gt[i2][:], gg_v[:, :, tt * 512:(tt + 1) * 512], reads=[Bgg], writes=[Bgt[i2]])
            K.dma(sp, sf[i2][:], gs[0, tt * 8:(tt + 1) * 8].rearrange("c r p v -> p c r v"), reads=[Bgs],
                  writes=[Bsf[i2]])
            K.dma(sp, sbw[i2][:], gs[1, tt * 8:(tt + 1) * 8].rearrange("c r p v -> p c r v"), reads=[Bgs],
                  writes=[Bsbw[i2]])
            q_ = qt[i2]
            for ts in range(4):
                tk = slice(ts * 128, (ts + 1) * 128)
                for h in range(4):
                    pr, hb = h // 2, (h % 2) * 64
                    rows = slice(hb, hb + 64)
                    ab = h % 2
                    ob = 2 + h % 2
                    K.mm(psb[ab][:, 0:128], q_[rows, 2, pr, tk], q_[rows, 0, pr, tk], True, True, [Bqt[i2]], [PB[ab]])
                    K.mm(psb[ab][:, 128:256], q_[rows, 3, pr, tk], q_[rows, 1, pr, tk], True, True, [Bqt[i2]],
                         [PB[ab]])
                    a_, Ba = am[na % 2], Bam[na % 2]
                    na += 1
                    K.tt(dve, a_[:], psb[ab][:, 0:256], tcm[:], ALU.mult, [PB[ab], Btcm], [Ba])
                    o0 = pr * 128
                    oc_ = slice(o0, o0 + 128)
                    K.mm(psb[ob][:, oc_], vtl[i2][:, ts, h * 128:(h + 1) * 128], a_[:, 0:128], True, False,
                         [Bvtl[i2], Ba], [PB[ob]])
                    K.mm(psb[ob][:, oc_], vtl[i2][:, ts, h * 128:(h + 1) * 128], a_[:, 128:256], False, False,
                         [Bvtl[i2], Ba], [PB[ob]])
                    for c in range(2):
                        ci = ts * 2 + c
                        cs = slice(o0 + c * 64, o0 + (c + 1) * 64)
                        tks = slice(ts * 128 + c * 64, ts * 128 + (c + 1) * 64)
                        K.mm(psb[ob][:, cs], sf[i2][rows, ci, pr, :], q_[rows, 0, pr, tks], False, False,
                             [Bsf[i2], Bqt[i2]], [PB[ob]])
                        K.mm(psb[ob][:, cs], sbw[i2][rows, ci, pr, :], q_[rows, 1, pr, tks], False, c == 1,
                             [Bsbw[i2], Bqt[i2]], [PB[ob]])
                for par in range(2):
                    ob = 2 + par
                    hsel = slice(par, 4, 2)
                    K.actf(sq[:, 0:256], psb[ob][:, 0:256], ACT.Square, [PB[ob]], [Bsq])
                    K.mm(psb[4][:, 0:256], onesb[:], sq[:, 0:256], True, True, [Bones, Bsq], [PB[4]])
                    K.actf(rs[:, 0:256], psb[4][:, 0:256], ACT.Sqrt, [PB[4]], [Brs], scale=1.0 / 128, bias=EPS)
                    K.op(dve, lambda e: e.reciprocal(rs[:, 0:256], rs[:, 0:256]), [Brs], [Brs])
                    K.tt(dve, on[:, 0:256], psb[ob][:, 0:256], rs[:, 0:256], ALU.mult, [PB[ob], Brs], [Bon])
                    K.stt(ost[:, hsel, tk], on[:, 0:256].rearrange("p (h i) -> p h i", h=2), gn[:, 0:1],
                          gt[i2][:, hsel, tk], ALU.mult, ALU.mult, [Bon, Bgn, Bgt[i2]], [Bost])
            K.dma(sp, mixo_v[:, 0:4, tt * 512:(tt + 1) * 512], ost[:], reads=[Bost], writes=[Bmixo])

    def even_a2(l, S, is_sample, stack):
        i_ev = l // 2
        fb = mx_alloc(stack)
        win = sb("m_win", [128, KC, 1088], BF16, stack)
        wqb = sb("m_wqb", [128, 2, 1536], BF16, stack)
        wkv = sb("m_wkv", [128, 1024], BF16, stack)
        qn = sb("m_qn", [128, 2], F32, stack)
        kvn = sb("m_kvn", [128, 1], F32, stack)
        cq = sb("m_cq", [128, 2, 512], F32, stack)
        cqn = sb("m_cqn", [128, 2, 512], BF16, stack)
        ckvn = sb("m_ckvn", [128, 512], BF16, stack)
        cf_ = sb("m_cf", [128, 4], F32, stack)
        zb = sb("m_zb", [128, 1], F32, stack)
        Bzb = Buf()
        K.op(dve, lambda e: e.memset(zb[:], 0.0), [], [Bzb])
        ai = sb("m_ai", [128, 512], I32, stack)
        pos = sb("m_pos", [128, 512], F32, stack)
        tab = sb("m_tab", [128, 2, 512], F32, stack)
        gst = [sb("m_gst%d" % i, [128, 512], BF16, stack) for i in range(2)]
        qh = [sb("m_qh%d" % i, [96, 512], BF16, stack) for i in range(2)]
        kst = sb("m_kst", [96, 8, 512], BF16, stack)
        kro = sb("m_kro", [96, 512], BF16, stack)
        vaug = [sb("m_vaug%d" % i, [128, 8, 65], BF16, stack) for i in range(2)]
        (Bwin, Bwqb, Bwkv, Bqn, Bkvn, Bcq, Bcqn, Bckvn, Bcf, Bai, Bpos, Btab, Bkst, Bkro) = (Buf() for _ in range(14))
        Bgst, Bqh, Bvaug = ([Buf(), Buf()] for _ in range(3))
        K.dma(sp, win[:], ev_win2_b[i_ev].rearrange("p (k n) -> p k n", k=KC), reads=[Bevw], writes=[Bwin])
        K.dma(sp, wqb[:], mla_wqb_b[i_ev].rearrange("p (k n) -> p k n", k=2), reads=[Bevw], writes=[Bwqb])
        K.dma(sp, wkv[:], mla_wkvb_b[i_ev], reads=[Bevw], writes=[Bwkv])
        K.dma(sp, qn[:], mla_qn[i_ev], writes=[Bqn])
        K.dma(sp, kvn[:], mla_kvn[i_ev], writes=[Bkvn])
        for i in range(2):
            K.op(dve, lambda e, i=i: e.memset(vaug[i][:], 1.0), [], [Bvaug[i]])
        K.op(pool, lambda e: e.iota(ai[:], [[0, 512]], base=0, channel_multiplier=1), [], [Bai])
        K.op(dve, lambda e: e.tensor_scalar(ai[:, 0:1], ai[:, 0:1], icst[:, 6:7], None, ALU.bitwise_and), [Bai, Bic],
             [Bai])
        K.copy(dve, cf_[:, 1:2], ai[:, 0:1], [Bai], [Bcf])
        K.actf(cf_[:, 0:1], cf_[:, 1:2], ACT.Exp, [Bcf], [Bcf], scale=-float(np.log(10000.0)) / 16.0)
        K.ts(dve, cf_[:, 0:1], cf_[:, 0:1], 65536.0 / (2.0 * np.pi), None, ALU.mult, None, [Bcf], [Bcf])
        jpos = sb("m_jpos", [128, 512], I32, stack)
        Bjpos = Buf()
        K.op(pool, lambda e: e.iota(jpos[:], [[1, 512]], base=0, channel_multiplier=0), [], [Bjpos])
        if is_sample:
            K.op(pool, lambda e: e.tensor_scalar(jpos[:], jpos[:], icst[:, 5:6], None, ALU.add), [Bjpos, Bic], [Bjpos])
        gg_v = gg.rearrange("(c p) s -> p c s", p=128)
        Kd_v = Kd.rearrange("(h r) s -> r h s", h=8)
        Vd_v = Vd.rearrange("(h p) (k e) -> p h k e", h=8, e=65)
        qs = float(96.0 ** -0.5)
        R = slice(64, 96)
        ng = 0
        a2s = cfg.a2_stop
        if a2s == 1:
            return
        for tt in range(S // 512):
            for st in prenorm_steps(fb, l, 1, tt, fb.h, fb.Bh):
                st()
            for c4 in range(4):
                bank = rbank()
                for kc in range(KC):
                    K.mm(psb[bank][:, :], win[:, kc, c4 * 128:(c4 + 1) * 128], fb.h[:, kc, :], kc == 0, kc == KC - 1,
                         [Bwin, fb.Bh], [PB[bank]])
                g_, Bg_ = gst[ng % 2], Bgst[ng % 2]
                ng += 1
                K.actf(g_[:], psb[bank][:, :], ACT.Silu, [PB[bank]], [Bg_])
                K.dma(sp, gg_v[:, c4, tt * 512:(tt + 1) * 512], g_[:], reads=[Bg_], writes=[Bgg])
            if a2s == 2:
                continue
            K.ts(dve, pos[:], jpos[:], float(tt * 512), None, ALU.add, None, [Bjpos], [Bpos])
            for k2 in range(2):
                K.ts(dve, ai[:], pos[:], cf_[:, 0:1], fcst[:, k2:k2 + 1], ALU.mult, ALU.add, [Bpos, Bcf, Bfc], [Bai])
                K.op(dve, lambda e: e.tensor_scalar(ai[:], ai[:], icst[:, 3:4], None, ALU.bitwise_and), [Bai, Bic],
                     [Bai])
                K.actf(tab[:, k2, :], ai[:], ACT.Sin, [Bai], [Btab], scale=2.0 * np.pi / 65536.0, bias=-np.pi)
            if a2s == 3:
                continue
            for c2 in range(2):
                bank = rbank()
                for kc in range(KC):
                    K.mm(psb[bank][:, :], win[:, kc, 512 + c2 * 128:512 + (c2 + 1) * 128], fb.h[:, kc, :], kc == 0,
                         kc == KC - 1, [Bwin, fb.Bh], [PB[bank]])
                K.copy(act, cq[:, c2, :], psb[bank][:, :], [PB[bank]], [Bcq])
                i = fb.nsq % len(fb.sq)
                fb.nsq += 1
                K.actf(fb.sq[i][:], psb[bank][:, :], ACT.Square, [PB[bank]], [fb.Bsq[i]])
                K.mm(psb[6][:, :], onesb[:], fb.sq[i][:], c2 == 0, c2 == 1, [Bones, fb.Bsq[i]], [PB[6]])
            K.actf(fb.rstd[1][:], psb[6][:, :], ACT.Sqrt, [PB[6]], [fb.Brstd[1]], scale=1.0 / 256, bias=EPS)
            K.op(dve, lambda e: e.reciprocal(fb.rstd[1][:], fb.rstd[1][:]), [fb.Brstd[1]], [fb.Brstd[1]])
            for c2 in range(2):
                i = fb.ntmp % len(fb.tmp)
                fb.ntmp += 1
                K.stt(fb.tmp[i][:], cq[:, c2, :], qs, fb.rstd[1][:], ALU.mult, ALU.mult, [Bcq, fb.Brstd[1]],
                      [fb.Btmp[i]])
                K.actf(cqn[:, c2, :], fb.tmp[i][:], ACT.Identity, [fb.Btmp[i], Bqn, Bzb], [Bcqn],
                       scale=qn[:, c2:c2 + 1], bias=zb[:, 0:1])
            bank = rbank()
            for kc in range(KC):
                K.mm(psb[bank][:, :], win[:, kc, 768:896], fb.h[:, kc, :], kc == 0, kc == KC - 1, [Bwin, fb.Bh],
                     [PB[bank]])
            i = fb.nsq % len(fb.sq)
            fb.nsq += 1
            K.actf(fb.sq[i][:], psb[bank][:, :], ACT.Square, [PB[bank]], [fb.Bsq[i]])
            K.mm(psb[6][:, :], onesb[:], fb.sq[i][:], True, True, [Bones, fb.Bsq[i]], [PB[6]])
            K.actf(fb.rstd[1][:], psb[6][:, :], ACT.Sqrt, [PB[6]], [fb.Brstd[1]], scale=1.0 / 128, bias=EPS)
            K.op(dve, lambda e: e.reciprocal(fb.rstd[1][:], fb.rstd[1][:]), [fb.Brstd[1]], [fb.Brstd[1]])
            i = fb.ntmp % len(fb.tmp)
            fb.ntmp += 1
            K.tt(dve, fb.tmp[i][:], psb[bank][:, :], fb.rstd[1][:], ALU.mult, [PB[bank], fb.Brstd[1]], [fb.Btmp[i]])
            K.actf(ckvn[:], fb.tmp[i][:], ACT.Identity, [fb.Btmp[i], Bkvn, Bzb], [Bckvn], scale=kvn[:, 0:1],
                   bias=zb[:, 0:1])
            if a2s == 4:
                continue
            for kc in range(KC):
                K.mm(psb[2][0:96, :], win[:, kc, 896:992], fb.h[:, kc, :], kc == 0, kc == KC - 1, [Bwin, fb.Bh], [PB[2]])
            for kc in range(KC):
                K.mm(psb[3][0:96, :], win[:, kc, 992:1088], fb.h[:, kc, :], kc == 0, kc == KC - 1, [Bwin, fb.Bh],
                     [PB[3]])
            i = fb.ntmp % len(fb.tmp)
            fb.ntmp += 1
            K.tt(dve, fb.tmp[i][R, :], psb[2][R, :], tab[R, 0, :], ALU.mult, [PB[2], Btab], [fb.Btmp[i]])
            i2 = fb.ntmp % len(fb.tmp)
            fb.ntmp += 1
            K.tt(dve, fb.tmp[i2][R, :], psb[3][R, :], tab[R, 1, :], ALU.mult, [PB[3], Btab], [fb.Btmp[i2]])
            K.tt(dve, kro[R, :], fb.tmp[i][R, :], fb.tmp[i2][R, :], ALU.add, [fb.Btmp[i], fb.Btmp[i2]], [Bkro])
            if a2s == 5:
                continue
            for h in range(8):
                bank = rbank()
                K.mm(psb[bank][0:64, :], wkv[:, h * 64:(h + 1) * 64], ckvn[:], True, True, [Bwkv, Bckvn], [PB[bank]])
                K.copy(act, kst[0:64, h, :], psb[bank][0:64, :], [PB[bank]], [Bkst])
                K.copy(dve, kst[R, h, :], kro[R, :], [Bkro], [Bkst])
                for kc in range(2):
                    K.mm(psb[2][0:96, :], wqb[:, kc, h * 96:(h + 1) * 96], cqn[:, kc, :], kc == 0, kc == 1,
                         [Bwqb, Bcqn], [PB[2]])
                for kc in range(2):
                    K.mm(psb[3][0:96, :], wqb[:, kc, 768 + h * 96:768 + (h + 1) * 96], cqn[:, kc, :], kc == 0, kc == 1,
                         [Bwqb, Bcqn], [PB[3]])
                q_, Bq_ = qh[h % 2], Bqh[h % 2]
                K.copy(dve, q_[0:64, :], psb[2][0:64, :], [PB[2]], [Bq_])
                i = fb.ntmp % len(fb.tmp)
                fb.ntmp += 1
                K.tt(dve, fb.tmp[i][R, :], psb[2][R, :], tab[R, 0, :], ALU.mult, [PB[2], Btab], [fb.Btmp[i]])
                i2 = fb.ntmp % len(fb.tmp)
                fb.ntmp += 1
                K.tt(dve, fb.tmp[i2][R, :], psb[3][R, :], tab[R, 1, :], ALU.mult, [PB[3], Btab], [fb.Btmp[i2]])
                K.tt(dve, q_[R, :], fb.tmp[i][R, :], fb.tmp[i2][R, :], ALU.add, [fb.Btmp[i], fb.Btmp[i2]], [Bq_])
                K.dma(sp, Qd[h, :, tt * 512:(tt + 1) * 512], q_[:], reads=[Bq_], writes=[BQd])
            K.dma(sp, Kd_v[:, :, tt * 512:(tt + 1) * 512], kst[:], reads=[Bkst], writes=[BKd])
            if a2s == 6:
                continue
            for ts in range(4):
                tk = slice(ts * 128, (ts + 1) * 128)
                kt = tt * 4 + ts
                va, Bva = vaug[kt % 2], Bvaug[kt % 2]
                bank = rbank()
                K.mm(psb[bank][:, :], ckvn[:, tk], wkv[:, 512:1024], True, True, [Bckvn, Bwkv], [PB[bank]])
                K.copy(act if ts % 2 == 0 else dve, va[:, :, 0:64], psb[bank][:, :].rearrange("p (h e) -> p h e", h=8),
                       [PB[bank]], [Bva])
                K.dma(sp, Vd_v[:, :, kt, :], va[:], reads=[Bva], writes=[BVd])

    def even_b(S, nrank, Ksrc, BKs, Vsrc, BVs, stack):
        SK = S
        KB_ = min(1024, SK)
        nkt = KB_ // 128
        LOOK = 2
        NP = 4
        qt = [sb("b_qt%d" % i, [96, 512], BF16, stack) for i in range(2)]
        ktl = [sb("b_kt%d" % i, [96, KB_], BF16, stack) for i in range(3)]
        vtl = [sb("b_vt%d" % i, [128, nkt, 65], BF16, stack) for i in range(3)]
        pt = [sb("b_pt%d" % i, [128, 512], BF16, stack) for i in range(NP)]
        rc = sb("b_rc", [128, 512], F32, stack)
        osb = sb("b_osb", [64, 512], F32, stack)
        onb = [sb("b_on%d" % i, [64, 512], BF16, stack) for i in range(2)]
        Brc, Bosb = Buf(), Buf()
        Bqt, Bonb = ([Buf(), Buf()] for _ in range(2))
        Bktl, Bvtl = ([Buf(), Buf(), Buf()] for _ in range(2))
        Bpt = [Buf() for _ in range(NP)]
        nq = nk = ns = 0
        pend = []
        tails = []

        def flush_one():
            pend.pop(0)()

        for qb in range(S // 512):
            for h in range(8):
                q_, Bq_ = qt[nq % 2], Bqt[nq % 2]
                ob = 4 + nq % 2
                nq += 1
                K.dma(sp, q_[:], Qd[h, :, qb * 512:(qb + 1) * 512], reads=[BQd], writes=[Bq_])
                nblk = nrank * (SK // KB_)
                nstep = nblk * nkt
                si = 0
                for g in range(nrank):
                    for k0 in range(0, SK, KB_):
                        k_, Bk_ = ktl[nk % 3], Bktl[nk % 3]
                        v_, Bv_ = vtl[nk % 3], Bvtl[nk % 3]
                        nk += 1
                        K.dma(sp, k_[:], Ksrc(g, h, k0, KB_), reads=[BKs], writes=[Bk_])
                        K.dma(sp, v_[:], Vsrc(g, h, k0 // 128, nkt), reads=[BVs], writes=[Bv_])
                        for kt in range(nkt):
                            sbk = ns % 4
                            p_, Bp_ = pt[ns % NP], Bpt[ns % NP]
                            ns += 1
                            K.mm(psb[sbk][:, :], k_[:, kt * 128:(kt + 1) * 128], q_[:], True, True, [Bk_, Bq_],
                                 [PB[sbk]])
                            K.actf(p_[:], psb[sbk][:, :], ACT.Exp, [PB[sbk]], [Bp_])

                            def pv(v_=v_, Bv_=Bv_, p_=p_, Bp_=Bp_, kt=kt, first=(si == 0), last=(si == nstep - 1),
                                   ob=ob):
                                K.mm(psb[ob][0:65, :], v_[:, kt, :], p_[:], first, last, [Bv_, Bp_], [PB[ob]])
                            pend.append(pv)
                            si += 1
                            if len(pend) > LOOK:
                                flush_one()
                            if si == 4 and tails:
                                tails.pop(0)()
                while pend:
                    flush_one()
                while tails:
                    tails.pop(0)()
                K.op(dve, lambda e, ob=ob: e.reciprocal(rc[64:65, :], psb[ob][64:65, :]), [PB[ob]], [Brc])
                K.copy(dve, osb[:], psb[ob][0:64, :], [PB[ob]], [Bosb])

                def tail(h=h, qb=qb):
                    K.mm(psb[6][0:64, :], cst[64:65, 128:192], rc[64:65, :], True, True, [Bcst, Brc], [PB[6]])
                    o_, Bo_ = onb[h % 2], Bonb[h % 2]
                    K.tt(dve, o_[:], osb[:], psb[6][0:64, :], ALU.mult, [Bosb, PB[6]], [Bo_])
                    K.dma(sp, mixm[h, :, qb * 512:(qb + 1) * 512], o_[:], reads=[Bo_], writes=[Bmixm])
                tails.append(tail)
        while tails:
            tails.pop(0)()

    def even_phase_c(l, S, stack):
        i_ev = l // 2
        fb = mx_alloc(stack, with_h=False, with_y=True)
        wg = sb("c_wg", [128, 4, 1024], BF16, stack)
        wm = sb("c_wm", [64, 8, 1024], BF16, stack)
        og = [sb("c_og%d" % i, [128, 4, 512], BF16, stack) for i in range(2)]
        om = [sb("c_om%d" % i, [64, 8, 512], BF16, stack) for i in range(2)]
        Bwg, Bwm = Buf(), Buf()
        Bog, Bom = [Buf(), Buf()], [Buf(), Buf()]
        K.dma(sp, wg[:], ev_woutg_b[i_ev].rearrange("p (k n) -> p k n", k=4), reads=[Bevw], writes=[Bwg])
        K.dma(sp, wm[:], ev_woutm_b[i_ev].rearrange("p (k n) -> p k n", k=8), reads=[Bevw], writes=[Bwm])
        mixo_v = mixo.rearrange("(c p) s -> p c s", p=128)
        mixm_v = mixm.rearrange("h e s -> e h s")
        Cg = vec(l, 1, 2)
        pending = []
        for tt in range(S // 512):
            g_, Bg_ = og[tt % 2], Bog[tt % 2]
            m_, Bm_ = om[tt % 2], Bom[tt % 2]
            K.dma(sp, g_[:], mixo_v[:, 0:4, tt * 512:(tt + 1) * 512], reads=[Bmixo], writes=[Bg_])
            K.dma(sp, m_[:], mixm_v[:, :, tt * 512:(tt + 1) * 512], reads=[Bmixm], writes=[Bm_])

            def mm_oc(oc, bank, g_=g_, m_=m_, Bg_=Bg_, Bm_=Bm_):
                for ic in range(4):
                    K.mm(psb[bank][:, :], wg[:, ic, oc * 128:(oc + 1) * 128], g_[:, ic, :], ic == 0, False,
                         [Bwg, Bg_], [PB[bank]])
                for hh in range(8):
                    K.mm(psb[bank][:, :], wm[:, hh, oc * 128:(oc + 1) * 128], m_[:, hh, :], False, hh == 7,
                         [Bwm, Bm_], [PB[bank]])
            yphase(fb, tt, Cg, mm_oc, [], pending)
            while pending:
                pending.pop(0)()

    def even_mixer(l, S, is_sample):
        xg = is_sample and GRP > 1
        stop = cfg.ev_stop
        with ExitStack() as st:
            even_a1(l, S, st)
            K.barrier()
        if stop == 1:
            return
        with ExitStack() as st:
            Sin = even_exchange(S, st) if xg else None
            even_r(S, st, True, Sin)
            K.barrier()
        if stop == 2:
            return
        with ExitStack() as st:
            even_a2(l, S, is_sample, st)
            K.barrier()
        if stop == 3:
            return
        if xg:
            for h in range(8):
                K.op(pool, lambda e, h=h: e.collective_compute(
                    "AllGather", ALU.bypass, replica_groups=cfg.replica_groups,
                    ins=[Kd[h * 96:(h + 1) * 96, 0:S]], outs=[Kall[h]]), [BKd], [BKall])
                K.op(pool, lambda e, h=h: e.collective_compute(
                    "AllGather", ALU.bypass, replica_groups=cfg.replica_groups,
                    ins=[Vd[h * 128:(h + 1) * 128, 0:(S // 128) * 65]], outs=[Vall[h]]), [BVd], [BVall])
        with ExitStack() as st:
            even_o(l, S, st)
            K.barrier()
        if stop == 4:
            return
        if xg:
            K.barrier()
            kget = lambda g, h, k0, n: Kall[h, g * 96:(g + 1) * 96, k0:k0 + n]
            vget = lambda g, h, kt0, n: Vall[h].rearrange("(g p) (k e) -> g p k e", p=128, e=65)[g, :, kt0:kt0 + n, :]
            srcs = (GRP, kget, BKall, vget, BVall)
        else:
            kget = lambda g, h, k0, n: Kd[h * 96:(h + 1) * 96, k0:k0 + n]
            vget = lambda g, h, kt0, n: Vd[h * 128:(h + 1) * 128, :].rearrange("p (k e) -> p k e", e=65)[:, kt0:kt0 + n, :]
            srcs = (1, kget, BKd, vget, BVd)
        with ExitStack() as st:
            even_b(S, *srcs, st)
            K.barrier()
        if stop == 5:
            return
        with ExitStack() as st:
            even_phase_c(l, S, st)
            K.barrier()

    tok0 = 0
    for si, S in enumerate(cfg.seg_tokens):
        K.dma(sp, mv[:], modv[si], reads=[Bmodv], writes=[Bmv])
        with ExitStack() as st:
            load_segment(tok0, S, st)
            K.barrier()
        for l in range(L):
            with ExitStack() as st:
                fb = ffn_alloc(st)
                ffn_sublayer(fb, l, 0, S)
                K.barrier()
            if cfg.do_mixer and l % 2 == 1 and cfg.do_mixer & 2:
                odd_mixer(l, S, si == 2)
            if cfg.do_mixer and l % 2 == 0 and cfg.do_mixer & 1:
                even_mixer(l, S, si == 2)
            with ExitStack() as st:
                fb = ffn_alloc(st)
                ffn_sublayer(fb, l, 2, S)
                K.barrier()
        with ExitStack() as st:
            store_segment(tok0, S, st)
            K.barrier()
        tok0 += S
    K.barrier()
    es.close()
    return nc


def _fm(v):
    v = np.asarray(v)
    lead = v.shape[:-1]
    n = v.shape[-1] // 128
    v = v.reshape(lead + (n, 128))
    return np.ascontiguousarray(np.moveaxis(v, -1, 0))


def prep_shared(inp, cfg):
    L = cfg.depth
    sh = {}
    ident = np.eye(128, dtype=np.float32)
    sh["consts"] = np.ascontiguousarray(np.concatenate([ident, np.ones((128, 128), np.float32)], axis=1))
    aw = np.asarray(inp["ada_w"])[:L]
    aw = aw.reshape(L, KC, 128, 72, 128).transpose(0, 3, 2, 1, 4)
    sh["ada_w"] = np.ascontiguousarray(aw).reshape(L * 72, 128, KC, 128)
    sh["ada_b"] = _fm(np.asarray(inp["ada_b"])[:L]).reshape(128, L * 72)
    sh["npre"] = _fm(np.asarray(inp["norm_pre"])[:L]).reshape(128, L * 3 * KC)
    sh["npost"] = _fm(np.asarray(inp["norm_post"])[:L]).reshape(128, L * 3 * KC)
    w13 = np.asarray(inp["ffn_w13"])[:L].reshape(L * 2, KC, 128, 2, NFC, 128)
    sh["w13"] = np.ascontiguousarray(w13.transpose(0, 4, 2, 1, 3, 5)).reshape(L * 2, NFC, 128, KC * 256)
    w2 = np.asarray(inp["ffn_w2"])[:L].reshape(L * 2, NFC, 128, KC, 128)
    sh["w2"] = np.ascontiguousarray(w2.transpose(0, 3, 2, 1, 4)).reshape(L * 2, KC, 128, NFC * 128)
    NOD = L // 2
    if NOD:
        ow = np.asarray(inp["od_w_in"])[:NOD]
        sh["od_win"] = np.ascontiguousarray(ow.reshape(NOD, KC, 128, 1536).transpose(0, 2, 1, 3)).reshape(NOD, 128, KC * 1536)
        oo = np.asarray(inp["od_w_out"])[:NOD]
        sh["od_wout"] = np.ascontiguousarray(oo.reshape(NOD, KC, 128, 1024).transpose(0, 2, 1, 3)).reshape(NOD, 128, KC * 1024)
        ws = np.asarray(inp["sgu_w_s"])[:NOD]
        sh["sgu_wsT"] = np.ascontiguousarray(ws.transpose(0, 3, 1, 2)).reshape(NOD, 128, 512)
        sh["sgu_b"] = np.ascontiguousarray(np.asarray(inp["sgu_b"])[:NOD]).reshape(NOD, 1, 512)
        sh["sgu_nrm"] = np.ascontiguousarray(np.broadcast_to(np.asarray(inp["sgu_norm"])[:NOD, None, :], (NOD, 128, 512)))
    NEV = (L + 1) // 2
    if NEV:
        def kmaj(w, nk):
            n, _, cols = w.shape
            return np.ascontiguousarray(w.reshape(n, nk, 128, cols).transpose(0, 2, 1, 3)).reshape(n, 128, nk * cols)
        wi = np.asarray(inp["ev_w_in"])[:NEV]
        q, k, v, g, alr, cq, ckv, kr = (wi[:, :, a:b] for a, b in ((0, 256), (256, 512), (512, 1024), (1024, 1536),
                                                                  (1536, 1568), (1568, 1824), (1824, 1952), (1952, 1984)))
        sh["ev_win1"] = kmaj(np.concatenate([q, k, v, alr], axis=2), KC)
        fill = ckv[:, :, 0:64]
        krrot = np.concatenate([kr[:, :, 16:32], kr[:, :, 0:16]], axis=2)
        sh["ev_win2"] = kmaj(np.concatenate([g, cq, ckv, fill, kr, fill, krrot], axis=2), KC)
        wo = np.asarray(inp["ev_w_out"])[:NEV]
        sh["ev_woutg"] = kmaj(wo[:, 0:512], 4)
        sh["ev_woutm"] = np.ascontiguousarray(wo[:, 512:1024].reshape(NEV, 8, 64, 1024).transpose(0, 2, 1, 3)).reshape(NEV, 64, 8 * 1024)
        wa = np.asarray(inp["gla_w_alpha"])[:NEV]
        ba = np.asarray(inp["gla_b_alpha"])[:NEV]
        wal = np.zeros((NEV, 33, 512), np.float32)
        wal[:, 0:16, 0:256] = wa[:, 0]
        wal[:, 16:32, 256:512] = wa[:, 1]
        wal[:, 32, 0:256] = ba[:, 0]
        wal[:, 32, 256:512] = ba[:, 1]
        sh["gla_wal"] = wal
        sh["gla_nrm"] = np.ascontiguousarray(np.asarray(inp["gla_norm"])[:NEV].reshape(NEV, 128, 1))
        sh["mla_qn"] = np.ascontiguousarray(np.asarray(inp["mla_q_norm"])[:NEV].reshape(NEV, 2, 128).transpose(0, 2, 1))
        sh["mla_kvn"] = np.ascontiguousarray(np.asarray(inp["mla_kv_norm"])[:NEV].reshape(NEV, 128, 1))
        wq = np.asarray(inp["mla_w_q_b"])[:NEV].reshape(NEV, 256, 8, 96)
        wqr = np.concatenate([wq[..., 0:64], wq[..., 80:96], wq[..., 64:80]], axis=-1)
        sh["mla_wqb"] = kmaj(np.concatenate([wq.reshape(NEV, 256, 768), wqr.reshape(NEV, 256, 768)], axis=2), 2)
        wk = np.asarray(inp["mla_w_kv_b"])[:NEV].reshape(NEV, 128, 8, 128)
        sh["mla_wkvb"] = np.ascontiguousarray(np.concatenate([wk[..., 0:64].reshape(NEV, 128, 512),
                                                             wk[..., 64:128].reshape(NEV, 128, 512)], axis=2))
    t = np.arange(128)
    same = (t[:, None] // 64) == (t[None, :] // 64)
    Tfi = (same & (t[:, None] <= t[None, :])).astype(np.float32)
    Tbe = (same & (t[:, None] > t[None, :])).astype(np.float32)
    Tbi = (same & (t[:, None] >= t[None, :])).astype(np.float32)
    Tpe = (same & (t[:, None] < t[None, :])).astype(np.float32)
    Ind = np.stack([(t < 64), (t >= 64)], axis=1).astype(np.float32)
    sh["tconst"] = np.ascontiguousarray(np.concatenate([Tfi, Ind, Tbe, Tbi, Ind, Tpe], axis=1))
    return sh


def prep_core(inp, cfg, core, n_cores=8):
    xp = np.asarray(inp["x_prompt"])
    xs = np.asarray(inp["x_sample"])
    cp = np.asarray(inp["c_prompt"])
    cs = np.asarray(inp["c_sample"])
    SP = cfg.seg_tokens[0]
    SQ = cfg.seg_tokens[2]
    per_grp = n_cores // xs.shape[0]
    sb_, r = core // per_grp, core % per_grp
    xin = np.concatenate([xp[2 * core, :SP], xp[2 * core + 1, :SP], xs[sb_, r * SQ:(r + 1) * SQ]], axis=0)
    c = np.stack([cp[2 * core], cp[2 * core + 1], cs[sb_], np.zeros(D, np.float32)], axis=0)
    c3 = np.ascontiguousarray(c.T.reshape(KC, 128, 4).transpose(1, 0, 2))
    ic = np.zeros((128, 8), np.int32)
    stot = per_grp * SQ
    ic[:, 0] = SP - 1
    ic[:, 1] = stot - 1
    ic[:, 2] = 127
    ic[:, 3] = 65535
    ic[:, 4] = (SQ * r * np.arange(128)) % stot
    ic[:, 5] = r * SQ
    ic[:, 6] = 15
    fc = np.zeros((128, 64), np.float32)
    fc[:, 0] = 49152.0
    fc[80:96, 1] = 32768.0
    for r1 in range(min(per_grp, 4)):
        fc[:, 8 + r1] = 1.0 if r1 < r else 0.0
        fc[:, 12 + r1] = 1.0 if r1 > r else 0.0
        for r2 in range(min(per_grp, 4)):
            fc[:, 16 + r1 * 4 + r2] = 1.0 if r1 < r2 < r else 0.0
            fc[:, 32 + r1 * 4 + r2] = 1.0 if r < r2 < r1 else 0.0
    return {"xin": np.ascontiguousarray(xin), "c3": c3, "iconst": ic, "fconst": fc}


_CACHE = {}


def run(inp, cfg, n_cores=8, trace=False):
    key = (cfg.seg_tokens, cfg.depth, cfg.do_mixer, cfg.n_cores, cfg.group)
    if key not in _CACHE:
        _CACHE[key] = build(cfg)
    nc = _CACHE[key]
    sh = prep_shared(inp, cfg)
    in_maps = []
    for c in range(n_cores):
        m = dict(sh)
        m.update(prep_core(inp, cfg, c, n_cores))
        in_maps.append(m)
    res = run_bass_kernel_spmd(nc, in_maps, core_ids=list(range(n_cores)), trace=trace)
    return res


def kernel(**inputs):
    cfg = Cfg()
    res = run(inputs, cfg)
    SP, SQ = cfg.seg_tokens[0], cfg.seg_tokens[2]
    B, S = inputs["x_prompt"].shape[:2]
    DB, DS = inputs["x_sample"].shape[:2]
    yp = np.empty((B, S, D), np.float32)
    ys = np.empty((DB, DS, D), np.float32)
    per_grp = 8 // DB
    for c in range(8):
        y = res.results[c]["yout"]
        yp[2 * c] = y[0:SP]
        yp[2 * c + 1] = y[SP:2 * SP]
        ys[c // per_grp, (c % per_grp) * SQ:(c % per_grp + 1) * SQ] = y[2 * SP:2 * SP + SQ]
    return (yp, ys)
```

```python
import numpy as np
import concourse.bass as bass
import concourse.mybir as mybir
from concourse.bass_utils import run_bass_kernel_spmd
from contextlib import ExitStack

F32 = mybir.dt.float32
BF16 = mybir.dt.bfloat16
I32 = mybir.dt.int32
ACT = mybir.ActivationFunctionType
ALU = mybir.AluOpType

D = 1024
KC = 8
DFF = 2816
NFC = 22
EPS = 1e-6


class Buf:
    __slots__ = ("name", "w", "r")

    def __init__(self, name=""):
        self.name = name
        self.w = None
        self.r = {}


class EngW:
    def __init__(self, name, eng, sid, sem, inorder=False):
        self.name = name
        self.eng = eng
        self.sid = sid
        self.sem = sem
        self.cnt = 0
        self.known = {}
        self.inorder = inorder
        self.ring = []
        self.ring_pos = 0


class KB:
    def __init__(self, nc, nring=20):
        self.nc = nc
        self.es = ExitStack()
        self.sems = []
        self.semcnt = []
        self.engs = {}
        for name, eng, inorder in (("pe", nc.tensor, True), ("act", nc.scalar, False), ("dve", nc.vector, False),
                                   ("pool", nc.gpsimd, False), ("sp", nc.sync, False)):
            sid = self._newsem("c_" + name)
            self.engs[name] = EngW(name, eng, sid, self.sems[sid], inorder)
        for q in ("sp", "pool", "act"):
            E = self.engs[q]
            for i in range(nring):
                E.ring.append(self._newsem("d_%s%d" % (q, i)))
        self.pe, self.act, self.dve, self.pool, self.sp = (self.engs[n] for n in ("pe", "act", "dve", "pool", "sp"))

    def _newsem(self, name):
        s = self.es.enter_context(self.nc.semaphore(name))
        self.sems.append(s)
        self.semcnt.append(0)
        return len(self.sems) - 1

    def _waits(self, E, reads, writes, extra=()):
        need = {}
        for b in reads:
            if b.w is not None and need.get(b.w[0], 0) < b.w[1]:
                need[b.w[0]] = b.w[1]
        for b in writes:
            if b.w is not None and need.get(b.w[0], 0) < b.w[1]:
                need[b.w[0]] = b.w[1]
            for sid, val in b.r.items():
                if need.get(sid, 0) < val:
                    need[sid] = val
        for sid, val in extra:
            if need.get(sid, 0) < val:
                need[sid] = val
        for sid, val in need.items():
            if sid == E.sid and E.inorder:
                continue
            if E.known.get(sid, 0) >= val:
                continue
            E.eng.wait_ge(self.sems[sid], val)
            E.known[sid] = val

    def op(self, E, emit, reads=(), writes=()):
        self._waits(E, reads, writes)
        ins = emit(E.eng)
        E.cnt += 1
        ins.then_inc(E.sem, 1)
        self.semcnt[E.sid] = E.cnt
        for b in reads:
            if b.r.get(E.sid, 0) < E.cnt:
                b.r[E.sid] = E.cnt
        for b in writes:
            b.w = (E.sid, E.cnt)
            b.r = {}

    def dma(self, Q, out, in_, reads=(), writes=(), **kw):
        sid = Q.ring[Q.ring_pos]
        Q.ring_pos = (Q.ring_pos + 1) % len(Q.ring)
        prev = self.semcnt[sid]
        self._waits(Q, reads, writes, extra=((sid, prev),) if prev else ())
        ins = Q.eng.dma_start(out=out, in_=in_, **kw)
        self.semcnt[sid] = prev + 16
        ins.then_inc(self.sems[sid], 16)
        val = prev + 16
        for b in reads:
            if b.r.get(sid, 0) < val:
                b.r[sid] = val
        for b in writes:
            b.w = (sid, val)
            b.r = {}

    def barrier(self):
        for E in self.engs.values():
            for sid in range(len(self.sems)):
                val = self.semcnt[sid]
                if val and sid != E.sid and E.known.get(sid, 0) < val:
                    E.eng.wait_ge(self.sems[sid], val)
                    E.known[sid] = val
            if E.cnt and not E.inorder and E.known.get(E.sid, 0) < E.cnt:
                E.eng.wait_ge(E.sem, E.cnt)
                E.known[E.sid] = E.cnt

    def mm(self, out, lhsT, rhs, start, stop, reads, writes, **kw):
        self.op(self.pe, lambda e: e.matmul(out, lhsT, rhs, start=start, stop=stop, **kw), reads, writes)

    def actf(self, out, in_, func, reads, writes, **kw):
        self.op(self.act, lambda e: e.activation(out, in_, func, **kw), reads, writes)

    def tt(self, E, out, in0, in1, op, reads, writes):
        self.op(E, lambda e: e.tensor_tensor(out, in0, in1, op), reads, writes)

    def ts(self, E, out, in0, s1, s2, op0, op1, reads, writes):
        if op1 is None:
            self.op(E, lambda e: e.tensor_scalar(out, in0, s1, None, op0), reads, writes)
        else:
            self.op(E, lambda e: e.tensor_scalar(out, in0, s1, s2, op0, op1), reads, writes)

    def stt(self, out, in0, scalar, in1, op0, op1, reads, writes):
        self.op(self.dve, lambda e: e.scalar_tensor_tensor(out, in0, scalar, in1, op0, op1), reads, writes)

    def copy(self, E, out, in_, reads, writes):
        if E is self.act:
            self.op(E, lambda e: e.copy(out, in_), reads, writes)
        else:
            self.op(E, lambda e: e.tensor_copy(out, in_), reads, writes)


class Cfg:
    def __init__(self, seg_tokens=(4096, 4096, 4096), depth=4, do_mixer=True, n_cores=8, group=4):
        self.seg_tokens = tuple(seg_tokens)
        self.ntok = sum(seg_tokens)
        self.depth = depth
        self.do_mixer = 3 if do_mixer is True else int(do_mixer)
        self.nffn = depth * 2
        self.n_cores = n_cores
        self.ev_stop = 0
        self.a2_stop = 0
        self.no_xg = 0
        self.cc_max = 4 * 1024 * 1024
        self.group = group
        self.replica_groups = [list(range(g * group, (g + 1) * group)) for g in range(n_cores // group)]


def build(cfg):
    nc = bass.Bass("TRN2", target_bir_lowering=False)
    L = cfg.depth
    NF = cfg.nffn
    NT = cfg.ntok

    def din(name, shape, dt=F32):
        return nc.dram_tensor(name, list(shape), dt, kind="ExternalInput").ap()

    def dscr(name, shape, dt):
        return nc.dram_tensor(name, list(shape), dt, kind="Internal").ap()

    xin = din("xin", [NT, D])
    c3 = din("c3", [128, KC, 4])
    consts = din("consts", [128, 256])
    ada_w = din("ada_w", [L * 72, 128, KC, 128])
    ada_b = din("ada_b", [128, L * 72])
    npre = din("npre", [128, L * 3 * KC])
    npost = din("npost", [128, L * 3 * KC])
    w13 = din("w13", [NF, NFC, 128, KC * 256])
    w2 = din("w2", [NF, KC, 128, NFC * 128])
    yout = nc.dram_tensor("yout", [NT, D], F32, kind="ExternalOutput").ap()
    NOD = L // 2
    NEV = (L + 1) // 2
    SQ = cfg.seg_tokens[2]
    GRP = cfg.group
    iconst = din("iconst", [128, 8], I32)
    if NOD:
        od_win = din("od_win", [NOD, 128, KC * 1536])
        od_wout = din("od_wout", [NOD, 128, KC * 1024])
        sgu_wsT = din("sgu_wsT", [NOD, 128, 512])
        sgu_b = din("sgu_b", [NOD, 1, 512])
        sgu_nrm = din("sgu_nrm", [NOD, 128, 512])
        od_win_b = dscr("od_win_b", [NOD, 128, KC * 1536], BF16)
        od_wout_b = dscr("od_wout_b", [NOD, 128, KC * 1024], BF16)
        sgu_wsT_b = dscr("sgu_wsT_b", [NOD, 128, 512], BF16)
        sgu_b_b = dscr("sgu_b_b", [NOD, 1, 512], BF16)
    SMAXL = max(cfg.seg_tokens)
    tconst = din("tconst", [128, 516])
    fconst = din("fconst", [128, 64])
    if NEV:
        ev_win1 = din("ev_win1", [NEV, 128, KC * 1056])
        ev_win2 = din("ev_win2", [NEV, 128, KC * 1088])
        ev_woutg = din("ev_woutg", [NEV, 128, 4 * 1024])
        ev_woutm = din("ev_woutm", [NEV, 64, 8 * 1024])
        gla_wal = din("gla_wal", [NEV, 33, 512])
        gla_nrm = din("gla_nrm", [NEV, 128, 1])
        mla_qn = din("mla_qn", [NEV, 128, 2])
        mla_kvn = din("mla_kvn", [NEV, 128, 1])
        mla_wqb = din("mla_wqb", [NEV, 128, 2 * 1536])
        mla_wkvb = din("mla_wkvb", [NEV, 128, 1024])
        ev_win1_b = dscr("ev_win1_b", [NEV, 128, KC * 1056], BF16)
        ev_win2_b = dscr("ev_win2_b", [NEV, 128, KC * 1088], BF16)
        ev_woutg_b = dscr("ev_woutg_b", [NEV, 128, 4 * 1024], BF16)
        ev_woutm_b = dscr("ev_woutm_b", [NEV, 64, 8 * 1024], BF16)
        gla_wal_b = dscr("gla_wal_b", [NEV, 33, 512], BF16)
        mla_wqb_b = dscr("mla_wqb_b", [NEV, 128, 2 * 1536], BF16)
        mla_wkvb_b = dscr("mla_wkvb_b", [NEV, 128, 1024], BF16)
    NCH = SMAXL // 64
    gq = dscr("gq", [4, 2, 128, SMAXL], BF16)
    gvt = dscr("gvt", [SMAXL // 128, 128, 512], BF16)
    gkv = dscr("gkv", [2, NCH, 2, 128, 128], F32)
    gs = dscr("gs", [2, NCH, 2, 128, 128], BF16)
    gg = dscr("gg", [512, SMAXL], BF16)
    gsum = dscr("gsum", [4 * 128, 129], F32)
    gsum_all = dscr("gsum_all", [GRP * 4 * 128, 129], F32)
    Qd = dscr("Qd", [8, 96, SMAXL], BF16)
    Kd = dscr("Kd", [8 * 96, SMAXL], BF16)
    Kall = dscr("Kall", [8, GRP * 96, SQ], BF16)
    Vd = dscr("Vd", [8 * 128, (SMAXL // 128) * 65], BF16)
    Vall = dscr("Vall", [8, GRP * 128, (SQ // 128) * 65], BF16)
    mixm = dscr("mixm", [8, 64, SMAXL], BF16)
    Ud = dscr("Ud", [SMAXL, 1024], BF16)
    CC_MAX = cfg.cc_max
    RCU = min(SQ, max(128, (CC_MAX // (GRP * 2048)) // 128 * 128))
    NUC = SQ // RCU
    Uall = dscr("Uall", [NUC, GRP * RCU, 1024], BF16)
    mixo = dscr("mixo", [1024, SMAXL], BF16)

    w13b = dscr("w13b", [NF, NFC, 128, KC * 256], BF16)
    w2b = dscr("w2b", [NF, KC, 128, NFC * 128], BF16)
    modv = dscr("modv", [3, 128, L * 3 * 3 * KC], F32)

    K = KB(nc)
    es = K.es
    pe, act, dve, pool, sp = K.pe, K.act, K.dve, K.pool, K.sp

    uid = [0]

    def sb(name, shape, dt, stack=es):
        uid[0] += 1
        return stack.enter_context(nc.sbuf_tensor("%s_u%d" % (name, uid[0]), list(shape), dt))

    psb = [es.enter_context(nc.psum_tensor("ps%d" % i, [128, 512], F32)) for i in range(8)]
    PB = [Buf("ps%d" % i) for i in range(8)]

    SMAX = max(cfg.seg_tokens)
    xT = sb("xT", [128, KC, SMAX], F32)
    XB = [Buf("x%d" % i) for i in range(SMAX // 512)]
    cst = sb("cst", [128, 256], F32)
    onesb = sb("onesb", [128, 128], BF16)
    mv = sb("mv", [128, L * 3 * 3 * KC], F32)
    Bcst, Bones, Bmv = Buf("cst"), Buf("ones"), Buf("mv")
    ident = cst[:, 0:128]

    K.dma(sp, cst[:], consts, writes=[Bcst])
    K.copy(dve, onesb[:], cst[:, 128:256], [Bcst], [Bones])

    WB13 = [Buf("w13b%d" % f) for f in range(NF)]
    WB2 = [Buf("w2b%d" % f) for f in range(NF)]
    for f in range(NF):
        K.dma(pool, w13b[f], w13[f], writes=[WB13[f]], max_dma_last_dim=4096)
        K.dma(pool, w2b[f], w2[f], writes=[WB2[f]], max_dma_last_dim=4096)

    Bodw = Buf("odw")
    if NOD:
        for src, dst in ((od_win, od_win_b), (od_wout, od_wout_b), (sgu_wsT, sgu_wsT_b), (sgu_b, sgu_b_b)):
            K.dma(pool, dst, src, writes=[Bodw], max_dma_last_dim=4096)
    Bevw = Buf("evw")
    if NEV:
        for src, dst in ((ev_win1, ev_win1_b), (ev_win2, ev_win2_b), (ev_woutg, ev_woutg_b), (ev_woutm, ev_woutm_b),
                         (gla_wal, gla_wal_b), (mla_wqb, mla_wqb_b), (mla_wkvb, mla_wkvb_b)):
            K.dma(pool, dst, src, writes=[Bevw], max_dma_last_dim=4096)
    icst = sb("icst", [128, 8], I32)
    Bic = Buf("icst")
    K.dma(sp, icst[:], iconst, writes=[Bic])

    Bmodv = Buf("modv")
    with ExitStack() as ps:
        ccT = sb("ccT", [128, KC, 4], F32, ps)
        adab = sb("adab", [128, L * 72], F32, ps)
        gpre = sb("gpre", [128, L * 3 * KC], F32, ps)
        gpost = sb("gpost", [128, L * 3 * KC], F32, ps)
        mfm = sb("mfm", [128, L * 72, 4], F32, ps)
        mvall = sb("mvall", [128, 3, L * 3 * 3 * KC], F32, ps)
        NAW = 4
        awt = [sb("awt%d" % i, [128, KC, 128], F32, ps) for i in range(NAW)]
        Bcc, Badab, Bgpre, Bgpost, Bmfm, Bmvall = (Buf(n) for n in ("cc", "adab", "gpre", "gpost", "mfm", "mvall"))
        Bawt = [Buf("awt%d" % i) for i in range(NAW)]
        K.dma(sp, ccT[:], c3, writes=[Bcc])
        K.dma(sp, adab[:], ada_b, writes=[Badab])
        K.dma(sp, gpre[:], npre, writes=[Bgpre])
        K.dma(sp, gpost[:], npost, writes=[Bgpost])
        K.actf(ccT[:], ccT[:], ACT.Silu, [Bcc], [Bcc])
        for t in range(L * 72):
            wt, Bw = awt[t % NAW], Bawt[t % NAW]
            K.dma(sp, wt[:], ada_w[t], writes=[Bw])
            bank = 7 - (t % 2)
            for kc in range(KC):
                K.mm(psb[bank][:, 0:4], wt[:, kc, :], ccT[:, kc, :], kc == 0, kc == KC - 1, [Bw, Bcc], [PB[bank]])
            K.ts(dve, mfm[:, t, :], psb[bank][:, 0:4], adab[:, t:t + 1], None, ALU.add, None,
                 [PB[bank], Badab], [Bmfm])
        gp4 = gpost[:].rearrange("p (l j c) -> p l j c", l=L, j=3)
        for j in (0, 2):
            K.ts(dve, gp4[:, :, j, :], gp4[:, :, j, :], 0.5, None, ALU.mult, None, [Bgpost], [Bgpost])
        mf5 = mfm[:].rearrange("p (l j t c) b -> p l j t c b", l=L, j=3, t=3)
        mv5 = mvall[:].rearrange("p b (l j v c) -> p b l j v c", l=L, j=3, v=3)
        gpr4 = gpre[:].rearrange("p (l j c) -> p l j c", l=L, j=3)
        for b in range(3):
            for l in range(L):
                for j in range(3):
                    K.stt(mv5[:, b, l, j, 0, :], mf5[:, l, j, 1, :, b], 1.0, gpr4[:, l, j, :], ALU.add, ALU.mult,
                          [Bmfm, Bgpre], [Bmvall])
                    K.copy(dve, mv5[:, b, l, j, 1, :], mf5[:, l, j, 0, :, b], [Bmfm], [Bmvall])
                    K.stt(mv5[:, b, l, j, 2, :], mf5[:, l, j, 2, :, b], 1.0, gp4[:, l, j, :], ALU.add, ALU.mult,
                          [Bmfm, Bgpost], [Bmvall])
        K.dma(sp, modv.rearrange("b p n -> p b n"), mvall[:], reads=[Bmvall], writes=[Bmodv])
        K.barrier()

    def vec(l, j, v):
        o = ((l * 3 + j) * 3 + v) * KC
        return mv[:, o:o + KC]

    def load_segment(tok0, S, stack):
        xtok = [sb("xtok%d" % i, [128, D], F32, stack) for i in range(2)]
        Bxt = [Buf("xtok%d" % i) for i in range(2)]
        for i in range(S // 128):
            xt_, Bx = xtok[i % 2], Bxt[i % 2]
            K.dma(sp, xt_[:], xin[tok0 + i * 128: tok0 + (i + 1) * 128, :], writes=[Bx])
            for hh in range(2):
                bank = (2 * i + hh) % 4
                for q in range(4):
                    kc = hh * 4 + q
                    K.op(pe, lambda e, kc=kc, q=q, bank=bank: e.transpose(psb[bank][:, q * 128:(q + 1) * 128],
                                                                           xt_[:, kc * 128:(kc + 1) * 128], ident),
                         [Bx, Bcst], [PB[bank]])
                dst = xT[:, hh * 4:(hh + 1) * 4, i * 128:(i + 1) * 128]
                src = psb[bank][:, :].rearrange("p (q t) -> p q t", q=4)
                K.copy(act if hh == 0 else dve, dst, src, [PB[bank]], [XB[i // 4]])

    def store_segment(tok0, S, stack):
        yt = [sb("ytok%d" % i, [128, D], F32, stack) for i in range(2)]
        Byt = [Buf("ytok%d" % i) for i in range(2)]
        for i in range(S // 128):
            y_, By = yt[i % 2], Byt[i % 2]
            for hh in range(2):
                bank = (2 * i + hh) % 4
                for q in range(4):
                    kc = hh * 4 + q
                    K.op(pe, lambda e, kc=kc, q=q, bank=bank: e.transpose(psb[bank][:, q * 128:(q + 1) * 128],
                                                                           xT[:, kc, i * 128:(i + 1) * 128], ident),
                         [XB[i // 4], Bcst], [PB[bank]])
                K.copy(act if hh == 0 else dve, y_[:, hh * 512:(hh + 1) * 512], psb[bank][:, :], [PB[bank]], [By])
            K.dma(sp, yout[tok0 + i * 128: tok0 + (i + 1) * 128, :], y_[:], reads=[By])

    class FfnBufs:
        pass

    def ffn_alloc(stack):
        fb = FfnBufs()
        fb.h = sb("f_h", [128, KC, 512], BF16, stack)
        fb.g = sb("f_g", [128, NFC, 512], BF16, stack)
        fb.y = sb("f_y", [128, KC, 512], F32, stack)
        fb.s = sb("f_s", [128, 512], F32, stack)
        fb.w13 = [sb("f_w13_%d" % i, [128, KC, 256], BF16, stack) for i in range(3)]
        fb.w2 = [sb("f_w2_%d" % i, [128, 11, 128], BF16, stack) for i in range(3)]
        fb.rstd = [sb("f_rstd%d" % i, [128, 512], F32, stack) for i in range(2)]
        fb.sq = [sb("f_sq%d" % i, [128, 512], BF16, stack) for i in range(1)]
        fb.tmp = [sb("f_tmp%d" % i, [128, 512], F32, stack) for i in range(1)]
        fb.Bh, fb.By, fb.Bs = Buf("h"), Buf("y"), Buf("s")
        fb.Bg = [Buf("g%d" % i) for i in range(NFC)]
        fb.Bw13 = [Buf() for _ in range(3)]
        fb.Bw2 = [Buf() for _ in range(3)]
        fb.Brstd = [Buf(), Buf()]
        fb.Bsq = [Buf(), Buf()]
        fb.Btmp = [Buf(), Buf()]
        fb.n13 = 0
        fb.n2 = 0
        fb.nsq = 0
        fb.ntmp = 0
        return fb

    def rstd_from_ss(fb, ri, bank):
        K.actf(fb.rstd[ri][:], psb[bank][:, :], ACT.Sqrt, [PB[bank]], [fb.Brstd[ri]], scale=1.0 / D, bias=EPS)
        K.op(dve, lambda e: e.reciprocal(fb.rstd[ri][:], fb.rstd[ri][:]), [fb.Brstd[ri]], [fb.Brstd[ri]])

    def prenorm_steps(fb, l, j, tt, hdst, Bh):
        tsl = slice(tt * 512, (tt + 1) * 512)
        A, Bv = vec(l, j, 0), vec(l, j, 1)
        steps = []

        def p0():
            for kc in range(KC):
                i = fb.nsq % len(fb.sq)
                fb.nsq += 1
                K.actf(fb.sq[i][:], xT[:, kc, tsl], ACT.Square, [XB[tt]], [fb.Bsq[i]])
                K.mm(psb[6][:, :], onesb[:], fb.sq[i][:], kc == 0, kc == KC - 1, [Bones, fb.Bsq[i]], [PB[6]])
        steps.append(p0)
        steps.append(lambda: rstd_from_ss(fb, 0, 6))
        for kc in range(KC):
            def pk(kc=kc):
                i = fb.ntmp % len(fb.tmp)
                fb.ntmp += 1
                K.tt(dve, fb.tmp[i][:], xT[:, kc, tsl], fb.rstd[0][:], ALU.mult, [XB[tt], fb.Brstd[0]], [fb.Btmp[i]])
                K.actf(hdst[:, kc, :], fb.tmp[i][:], ACT.Identity, [fb.Btmp[i], Bmv], [Bh],
                       scale=A[:, kc:kc + 1], bias=Bv[:, kc:kc + 1])
            steps.append(pk)
        return steps

    def yphase(fb, tt, Cg, mm_oc, nxt, pending_tail):
        tsl = slice(tt * 512, (tt + 1) * 512)
        prev_sq = None
        for oc in range(KC):
            bank = 4 + oc % 2
            mm_oc(oc, bank)
            if prev_sq is not None:
                po, pi = prev_sq
                K.mm(psb[7][:, :], onesb[:], fb.sq[pi][:], po == 0, False, [Bones, fb.Bsq[pi]], [PB[7]])
            K.copy(act, fb.y[:, oc, :], psb[bank][:, :], [PB[bank]], [fb.By])
            i = fb.nsq % len(fb.sq)
            fb.nsq += 1
            K.actf(fb.sq[i][:], psb[bank][:, :], ACT.Square, [PB[bank]], [fb.Bsq[i]])
            prev_sq = (oc, i)
            if nxt:
                nxt.pop(0)()
        po, pi = prev_sq
        K.mm(psb[7][:, :], onesb[:], fb.sq[pi][:], False, True, [Bones, fb.Bsq[pi]], [PB[7]])
        while nxt:
            nxt.pop(0)()
        pending_tail.append(lambda: rstd_from_ss(fb, 1, 7))
        for oc in range(KC):
            def tl(oc=oc):
                i = fb.ntmp % len(fb.tmp)
                fb.ntmp += 1
                K.tt(dve, fb.tmp[i][:], fb.y[:, oc, :], fb.rstd[1][:], ALU.mult, [fb.By, fb.Brstd[1]],
                     [fb.Btmp[i]])
                K.stt(xT[:, oc, tsl], fb.tmp[i][:], Cg[:, oc:oc + 1], xT[:, oc, tsl], ALU.mult, ALU.add,
                      [fb.Btmp[i], Bmv, XB[tt]], [XB[tt]])
            pending_tail.append(tl)

    def ffn_sublayer(fb, l, j, S):
        f = l * 2 + (0 if j == 0 else 1)
        ntile = S // 512
        Cg = vec(l, j, 2)
        pending_tail = []
        for st in prenorm_steps(fb, l, j, 0, fb.h, fb.Bh):
            st()
        for tt in range(ntile):
            tsl = slice(tt * 512, (tt + 1) * 512)
            for fc in range(NFC):
                r = fb.n13 % 3
                fb.n13 += 1
                K.dma(sp, fb.w13[r][:], w13b[f, fc].rearrange("p (k n) -> p k n", k=KC), reads=[WB13[f]],
                      writes=[fb.Bw13[r]])
                ba, bb = fc % 2, 2 + fc % 2
                for half, bank in ((0, ba), (1, bb)):
                    for kc in range(KC):
                        K.mm(psb[bank][:, :], fb.w13[r][:, kc, half * 128:(half + 1) * 128], fb.h[:, kc, :],
                             kc == 0, kc == KC - 1, [fb.Bw13[r], fb.Bh], [PB[bank]])
                K.actf(fb.s[:], psb[ba][:, :], ACT.Silu, [PB[ba]], [fb.Bs])
                K.tt(dve, fb.g[:, fc, :], fb.s[:], psb[bb][:, :], ALU.mult, [fb.Bs, PB[bb]], [fb.Bg[fc]])
                if pending_tail:
                    pending_tail.pop(0)()
            while pending_tail:
                pending_tail.pop(0)()
            nxt = prenorm_steps(fb, l, j, tt + 1, fb.h, fb.Bh) if tt + 1 < ntile else []
            if nxt:
                nxt.pop(0)()
            def mm_oc(oc, bank, f=f):
                for hf in range(2):
                    r = fb.n2 % 3
                    fb.n2 += 1
                    K.dma(sp, fb.w2[r][:],
                          w2b[f, oc].rearrange("p (k n) -> p k n", k=NFC)[:, hf * 11:(hf + 1) * 11, :],
                          reads=[WB2[f]], writes=[fb.Bw2[r]])
                    for q in range(11):
                        fc = hf * 11 + q
                        K.mm(psb[bank][:, :], fb.w2[r][:, q, :], fb.g[:, fc, :], fc == 0, fc == NFC - 1,
                             [fb.Bw2[r], fb.Bg[fc]], [PB[bank]])
            yphase(fb, tt, Cg, mm_oc, nxt, pending_tail)
        while pending_tail:
            pending_tail.pop(0)()


    def mx_alloc(stack, with_h=True, with_y=False, nrstd=2):
        fb = FfnBufs()
        if with_h:
            fb.h = sb("m_h", [128, KC, 512], BF16, stack)
        if with_y:
            fb.y = sb("m_y", [128, KC, 512], F32, stack)
        fb.rstd = [sb("m_rstd%d" % i, [128, 512], F32, stack) for i in range(nrstd)]
        fb.sq = [sb("m_sq%d" % i, [128, 512], BF16, stack) for i in range(2)]
        fb.tmp = [sb("m_tmp%d" % i, [128, 512], F32, stack) for i in range(2)]
        fb.Bh, fb.By = Buf("h"), Buf("y")
        fb.Brstd = [Buf(), Buf()]
        fb.Bsq = [Buf(), Buf()]
        fb.Btmp = [Buf(), Buf()]
        fb.nsq = 0
        fb.ntmp = 0
        return fb

    class Gen:
        pass

    def gen_alloc(stack, mask_col, use_pjx, nA=2, blocks=True):
        G = Gen()
        G.PJ = sb("g_pj", [128, 512], I32, stack)
        G.A = [sb("g_a%d" % i, [128, 512], I32, stack) for i in range(nA)]
        G.BPJ = Buf("pj")
        G.BA = [Buf() for _ in range(nA)]
        G.n = 0
        G.mask = icst[:, mask_col:mask_col + 1]
        K.op(pool, lambda e: e.iota(G.PJ[:], [[1, 512]], base=0, channel_multiplier=0), [], [G.BPJ])
        K.op(pool, lambda e: e.iota(G.A[0][:], [[0, 512]], base=0, channel_multiplier=1), [], [G.BA[0]])
        if blocks:
            G.Jf = sb("g_jf", [128, 512], I32, stack)
            G.BJf = Buf()
            K.copy(dve, G.Jf[:], G.PJ[:], [G.BPJ], [G.BJf])
        K.op(pool, lambda e: e.tensor_tensor(G.PJ[:], G.PJ[:], G.A[0][:], ALU.mult), [G.BPJ, G.BA[0]], [G.BPJ])
        if blocks:
            G.Pf = sb("g_pf", [128, 512], I32, stack)
            G.PJ0 = sb("g_pj0", [128, 512], I32, stack)
            G.BPf, G.BPJ0 = Buf(), Buf()
            K.copy(dve, G.Pf[:], G.A[0][:], [G.BA[0]], [G.BPf])
        if use_pjx:
            K.op(pool, lambda e: e.iota(G.A[0][:], [[0, 512]], base=0, channel_multiplier=0), [], [G.BA[0]])
            K.op(pool, lambda e: e.tensor_scalar(G.A[0][:], G.A[0][:], icst[:, 4:5], None, ALU.add),
                 [G.BA[0], Bic], [G.BA[0]])
            K.op(pool, lambda e: e.tensor_tensor(G.PJ[:], G.PJ[:], G.A[0][:], ALU.add), [G.BPJ, G.BA[0]], [G.BPJ])
        if blocks:
            K.copy(dve, G.PJ0[:], G.PJ[:], [G.BPJ], [G.BPJ0])
        return G

    def gen_block(G, S_tot, sp0):
        K.op(pool, lambda e: e.tensor_scalar(G.PJ[:], G.Pf[:], int(sp0 % S_tot), None, ALU.mult), [G.BPf], [G.BPJ])
        K.op(pool, lambda e: e.tensor_tensor(G.PJ[:], G.PJ[:], G.PJ0[:], ALU.add), [G.BPJ, G.BPJ0], [G.BPJ])

    def gen_tile(G, dst, Bdst, S_tot, s0, sp0, off):
        base = (s0 * sp0 + off) % S_tot
        step = s0 % S_tot
        i = G.n % len(G.A)
        G.n += 1
        A = G.A[i]
        if step == 0:
            K.op(pool, lambda e: e.iota(A[:], [[0, 512]], base=base, channel_multiplier=0), [], [G.BA[i]])
        else:
            K.op(pool, lambda e: e.tensor_scalar(A[:], G.Jf[:], int(step), int(base), ALU.mult, ALU.add), [G.BJf],
                 [G.BA[i]])
        K.op(pool, lambda e: e.tensor_tensor(A[:], A[:], G.PJ[:], ALU.add), [G.BA[i], G.BPJ], [G.BA[i]])
        K.op(dve, lambda e: e.tensor_scalar(A[:], A[:], G.mask, None, ALU.bitwise_and), [G.BA[i], Bic], [G.BA[i]])
        K.actf(dst, A[:], ACT.Sin, [G.BA[i]], [Bdst], scale=2.0 * np.pi / S_tot, bias=-np.pi)

    BUd, BUall, Bmixo = Buf("Ud"), Buf("Uall"), Buf("mixo")

    def odd_phase_a(l, S, stack):
        i_od = l // 2
        fb = mx_alloc(stack, nrstd=1)
        win = sb("o_win", [128, KC, 1536], BF16, stack)
        wsT = sb("o_wsT", [128, 512], BF16, stack)
        sgb = sb("o_sgb", [1, 512], BF16, stack)
        nrm = sb("o_nrm", [128, 512], F32, stack)
        ccsc = sb("o_ccsc", [128, 256], BF16, stack)
        gtmp = sb("o_gtmp", [128, 512], BF16, stack)
        zcT = sb("o_zcT", [128, 4, 512], BF16, stack)
        usb = [sb("o_usb%d" % i, [128, 1024], BF16, stack) for i in range(2)]
        uT = sb("o_uT", [128, 4, 512], F32, stack)
        gv = sb("o_gv", [128, 512], F32, stack)
        vtok = [sb("o_vtok%d" % i, [128, 512], BF16, stack) for i in range(2)]
        odT = sb("o_odT", [128, 4, 512], BF16, stack)
        ssq = sb("o_ssq", [128, 2], F32, stack)
        Bwin, BwsT, Bsgb, Bnrm, Bccsc, Bgtmp, BzcT, BuT, Bgv, BodT, Bssq = (Buf() for _ in range(11))
        Busb = [Buf(), Buf()]
        Bvtok = [Buf(), Buf()]
        K.dma(sp, win[:], od_win_b[i_od].rearrange("p (k n) -> p k n", k=KC), reads=[Bodw], writes=[Bwin])
        K.dma(sp, wsT[:], sgu_wsT_b[i_od], reads=[Bodw], writes=[BwsT])
        K.dma(sp, sgb[:], sgu_b_b[i_od], reads=[Bodw], writes=[Bsgb])
        K.dma(sp, nrm[:], sgu_nrm[i_od], writes=[Bnrm])
        G = gen_alloc(stack, 2, False, nA=1, blocks=False)
        gen_tile(G, gtmp[:], Bgtmp, 128, 0, 0, 96)
        K.copy(dve, ccsc[:, 0:128], gtmp[:, 0:128], [Bgtmp], [Bccsc])
        gen_tile(G, gtmp[:], Bgtmp, 128, 0, 0, 0)
        K.copy(dve, ccsc[:, 128:256], gtmp[:, 0:128], [Bgtmp], [Bccsc])
        mixo_v = mixo.rearrange("(c p) s -> p c s", p=128)
        nb = [0]

        def bank2():
            nb[0] += 1
            return nb[0] % 2

        for tt in range(S // 512):
            for st in prenorm_steps(fb, l, 1, tt, fb.h, fb.Bh):
                st()
            for g in range(4):
                bank = bank2()
                for kc in range(KC):
                    K.mm(psb[bank][:, :], win[:, kc, g * 128:(g + 1) * 128], fb.h[:, kc, :], kc == 0, kc == KC - 1,
                         [Bwin, fb.Bh], [PB[bank]])
                K.copy(act if g % 2 == 0 else dve, zcT[:, g, :], psb[bank][:, :], [PB[bank]], [BzcT])
            for g in range(4):
                bank = bank2()
                for kc in range(KC):
                    K.mm(psb[bank][:, :], win[:, kc, 512 + g * 128:512 + (g + 1) * 128], fb.h[:, kc, :], kc == 0,
                         kc == KC - 1, [Bwin, fb.Bh], [PB[bank]])
                K.actf(uT[:, g, :], psb[bank][:, :], ACT.Gelu, [PB[bank]], [BuT])
            for ts in range(4):
                tk = slice(ts * 128, (ts + 1) * 128)
                ub, Bub = usb[ts % 2], Busb[ts % 2]
                for gp in range(2):
                    bank = 2 + gp
                    for gg in range(2):
                        g = gp * 2 + gg
                        K.mm(psb[bank][:, gg * 256:(gg + 1) * 256], zcT[:, g, tk], ccsc[:], True, True,
                             [BzcT, Bccsc], [PB[bank]])
                    K.copy(act if gp == 0 else dve, ub[:, gp * 512:(gp + 1) * 512], psb[bank][:, :], [PB[bank]],
                           [Bub])
                K.dma(sp, Ud[tt * 512 + ts * 128: tt * 512 + (ts + 1) * 128, :], ub[:], reads=[Bub], writes=[BUd])
                bank = bank2()
                for kc in range(KC):
                    K.mm(psb[bank][:, :], fb.h[:, kc, tk], win[:, kc, 1024:1536], kc == 0, kc == KC - 1,
                         [Bwin, fb.Bh], [PB[bank]])
                K.actf(gv[:], psb[bank][:, :], ACT.Gelu, [PB[bank]], [Bgv])
                vt, Bvt = vtok[ts % 2], Bvtok[ts % 2]
                K.actf(vt[:], gv[:], ACT.Square, [Bgv], [Bvt, Bssq], accum_out=ssq[:, 0:1])
                K.actf(ssq[:, 1:2], ssq[:, 0:1], ACT.Sqrt, [Bssq], [Bssq], scale=1.0 / 512, bias=EPS)
                K.op(dve, lambda e: e.reciprocal(ssq[:, 1:2], ssq[:, 1:2]), [Bssq], [Bssq])
                K.stt(vt[:], gv[:], ssq[:, 1:2], nrm[:], ALU.mult, ALU.mult, [Bgv, Bssq, Bnrm], [Bvt])
                bank = 6
                for hd in range(4):
                    hs = slice(hd * 128, (hd + 1) * 128)
                    K.mm(psb[bank][:, hs], vt[:, hs], wsT[:, hs], True, False, [Bvt, BwsT], [PB[bank]])
                    K.mm(psb[bank][:, hs], onesb[0:1, :], sgb[0:1, hs], False, True, [Bones, Bsgb], [PB[bank]])
                K.tt(dve, odT[:, :, tk], uT[:, :, tk], psb[bank][:, :].rearrange("p (h i) -> p h i", h=4), ALU.mult,
                     [BuT, PB[bank]], [BodT])
            K.dma(sp, mixo_v[:, 4:8, tt * 512:(tt + 1) * 512], odT[:], reads=[BodT], writes=[Bmixo])

    def odd_phase_b(S, S_keys, Usrc, BUsrc, is_sample, stack):
        G = gen_alloc(stack, 1 if is_sample else 0, is_sample)
        ct = [sb("b_ct%d" % i, [128, 512], BF16, stack) for i in range(2)]
        stl = [sb("b_st%d" % i, [128, 512], BF16, stack) for i in range(2)]
        ut = [sb("b_ut%d" % i, [128, 1024], BF16, stack) for i in range(3)]
        fcs = sb("b_fcs", [128, 4, 512], BF16, stack)
        Rc = [sb("b_rc%d" % i, [128, 512], I32, stack) for i in range(2)]
        Rs = [sb("b_rs%d" % i, [128, 512], I32, stack) for i in range(2)]
        Di = sb("b_di", [128, 512], I32, stack)
        Df = sb("b_df", [128, 512], F32, stack)
        Bct, Bst = [Buf(), Buf()], [Buf(), Buf()]
        BRc, BRs = [Buf(), Buf()], [Buf(), Buf()]
        But = [Buf(), Buf(), Buf()]
        Bfcs, BDi, BDf = Buf(), Buf(), Buf()
        mixo_v = mixo.rearrange("(c p) s -> p c s", p=128)
        scale = 1.0 / float(np.sqrt(S_keys * 128.0))
        na = S_keys // 128
        sc_sin = 2.0 * np.pi / S_keys
        n = 0
        for bq in range(S // 512):
            sp0 = bq * 512
            gen_block(G, S_keys, sp0)
            K.op(pool, lambda e: e.tensor_scalar(Di[:], G.Jf[:], 128, int((128 * sp0) % S_keys), ALU.mult, ALU.add),
                 [G.BJf], [BDi])
            K.op(dve, lambda e: e.tensor_scalar(Di[:], Di[:], G.mask, None, ALU.bitwise_and), [BDi, Bic], [BDi])
            K.copy(dve, Df[:], Di[:], [BDi], [BDf])
            K.op(pool, lambda e: e.tensor_scalar(Rc[0][:], G.PJ[:], int((3 * S_keys) // 4), None, ALU.add), [G.BPJ],
                 [BRc[0]])
            K.op(dve, lambda e: e.tensor_scalar(Rc[0][:], Rc[0][:], G.mask, None, ALU.bitwise_and), [BRc[0], Bic],
                 [BRc[0]])
            K.op(pool, lambda e: e.tensor_scalar(Rs[0][:], G.PJ[:], int(S_keys // 2), None, ALU.add), [G.BPJ],
                 [BRs[0]])
            K.op(dve, lambda e: e.tensor_scalar(Rs[0][:], Rs[0][:], G.mask, None, ALU.bitwise_and), [BRs[0], Bic],
                 [BRs[0]])
            for a in range(na):
                s0 = a * 128
                i2, i3 = n % 2, n % 3
                n += 1
                cur, nxt = a % 2, (a + 1) % 2
                K.actf(ct[i2][:], Rc[cur][:], ACT.Sin, [BRc[cur]], [Bct[i2]], scale=sc_sin, bias=-np.pi)
                K.actf(stl[i2][:], Rs[cur][:], ACT.Sin, [BRs[cur]], [Bst[i2]], scale=sc_sin, bias=-np.pi)
                if a + 1 < na:
                    K.tt(dve, Rc[nxt][:], Rc[cur][:], Df[:], ALU.add, [BRc[cur], BDf], [BRc[nxt]])
                    K.op(dve, lambda e, nxt=nxt: e.tensor_scalar(Rc[nxt][:], Rc[nxt][:], G.mask, None,
                                                                 ALU.bitwise_and), [BRc[nxt], Bic], [BRc[nxt]])
                    K.tt(pool, Rs[nxt][:], Rs[cur][:], Di[:], ALU.add, [BRs[cur], BDi], [BRs[nxt]])
                    K.op(dve, lambda e, nxt=nxt: e.tensor_scalar(Rs[nxt][:], Rs[nxt][:], G.mask, None,
                                                                 ALU.bitwise_and), [BRs[nxt], Bic], [BRs[nxt]])
                K.dma(sp, ut[i3][:], Usrc(s0), reads=[BUsrc], writes=[But[i3]])
                for g in range(4):
                    K.mm(psb[g][:, :], ut[i3][:, g * 256:g * 256 + 128], ct[i2][:], a == 0, False,
                         [But[i3], Bct[i2]], [PB[g]])
                    K.mm(psb[g][:, :], ut[i3][:, g * 256 + 128:g * 256 + 256], stl[i2][:], False, a == na - 1,
                         [But[i3], Bst[i2]], [PB[g]])
            for g in range(4):
                if g % 2 == 0:
                    K.actf(fcs[:, g, :], psb[g][:, :], ACT.Copy, [PB[g]], [Bfcs], scale=scale)
                else:
                    K.ts(dve, fcs[:, g, :], psb[g][:, :], scale, None, ALU.mult, None, [PB[g]], [Bfcs])
            K.dma(sp, mixo_v[:, 0:4, bq * 512:(bq + 1) * 512], fcs[:], reads=[Bfcs], writes=[Bmixo])

    def mixer_phase_c(l, S, wout_dram, Bw_dram, stack):
        fb = mx_alloc(stack, with_h=False, with_y=True)
        wo = sb("c_wo", [128, KC, 1024], BF16, stack)
        ot = [sb("c_ot%d" % i, [128, KC, 512], BF16, stack) for i in range(2)]
        Bwo = Buf()
        Bot = [Buf(), Buf()]
        K.dma(sp, wo[:], wout_dram.rearrange("p (k n) -> p k n", k=KC), reads=[Bw_dram], writes=[Bwo])
        mixo_v = mixo.rearrange("(c p) s -> p c s", p=128)
        Cg = vec(l, 1, 2)
        pending = []
        for tt in range(S // 512):
            o_, Bo = ot[tt % 2], Bot[tt % 2]
            K.dma(sp, o_[:], mixo_v[:, :, tt * 512:(tt + 1) * 512], reads=[Bmixo], writes=[Bo])

            def mm_oc(oc, bank, o_=o_, Bo=Bo):
                for ic in range(KC):
                    K.mm(psb[bank][:, :], wo[:, ic, oc * 128:(oc + 1) * 128], o_[:, ic, :], ic == 0, ic == KC - 1,
                         [Bwo, Bo], [PB[bank]])
            yphase(fb, tt, Cg, mm_oc, [], pending)
            while pending:
                pending.pop(0)()

    def odd_mixer(l, S, is_sample):
        i_od = l // 2
        with ExitStack() as st:
            odd_phase_a(l, S, st)
            K.barrier()
        if is_sample and GRP > 1 and not cfg.no_xg:
            for ci in range(NUC):
                K.op(pool, lambda e, ci=ci: e.collective_compute(
                    "AllGather", ALU.bypass, replica_groups=cfg.replica_groups,
                    ins=[Ud[ci * RCU:(ci + 1) * RCU, :]], outs=[Uall[ci]]), [BUd], [BUall])
            K.barrier()

            def usrc(s0):
                g, i = s0 // S, s0 % S
                ci, w = i // RCU, i % RCU
                return Uall[ci, g * RCU + w:g * RCU + w + 128, :]
            Usrc, BUsrc, S_keys = usrc, BUall, GRP * S
        else:
            Usrc, BUsrc, S_keys = (lambda s0: Ud[s0:s0 + 128, :]), BUd, S
        with ExitStack() as st:
            odd_phase_b(S, S_keys, Usrc, BUsrc, is_sample and GRP > 1 and not cfg.no_xg, st)
            K.barrier()
        with ExitStack() as st:
            mixer_phase_c(l, S, od_wout_b[i_od], Bodw, st)
            K.barrier()

    fcst = sb("fcst", [128, 64], F32)
    Bfc = Buf("fcst")
    K.dma(sp, fcst[:], fconst, writes=[Bfc])
    NCHL = SMAXL // 64
    decs = sb("decs", [128, 2, 2, NCHL], F32)
    Bdecs = Buf("decs")
    Bgq, Bgvt, Bgkv, Bgs, Bgg, Bgsum, Bgsall = (Buf() for _ in range(7))
    BQd, BKd, BKall, BVd, BVall, Bmixm = (Buf() for _ in range(6))
    rr = [0]

    def rbank(lo=0, n=2):
        rr[0] += 1
        return lo + rr[0] % n

    def even_a1(l, S, stack):
        i_ev = l // 2
        fb = mx_alloc(stack)
        win = sb("a_win", [128, KC, 1056], BF16, stack)
        wal = sb("a_wal", [33, 512], BF16, stack)
        tc = sb("a_tc", [128, 516], F32, stack)
        alr = sb("a_alr", [33, 512], BF16, stack)
        qk = sb("a_qk", [128, 4, 512], F32, stack)
        spt = sb("a_spt", [128, 512], F32, stack)
        E = sb("a_E", [128, 4, 2, 128], F32, stack)
        ekd = sb("a_ekd", [128, 512], F32, stack)
        kd = sb("a_kd", [128, 512], BF16, stack)
        vtok = [sb("a_vtok%d" % i, [128, 512], BF16, stack) for i in range(2)]
        kvst = sb("a_kvst", [128, 2, 4, 128], F32, stack)
        qst = sb("a_qst", [128, 4, 2, 512], BF16, stack)
        Bwin, Bwal, Btc, Balr, Bqk, Bspt, BE, Bekd, Bkd, Bkvst, Bqst = (Buf() for _ in range(11))
        Bvtok = [Buf(), Buf()]
        K.dma(sp, win[:], ev_win1_b[i_ev].rearrange("p (k n) -> p k n", k=KC), reads=[Bevw], writes=[Bwin])
        K.dma(sp, wal[:], gla_wal_b[i_ev], reads=[Bevw], writes=[Bwal])
        K.dma(sp, tc[:], tconst, writes=[Btc])
        K.op(dve, lambda e: e.memset(alr[32:33, :], 1.0), [], [Balr])
        gq_v = gq.rearrange("k r p s -> p k r s")
        for tt in range(S // 512):
            for st in prenorm_steps(fb, l, 1, tt, fb.h, fb.Bh):
                st()
            for c4 in range(4):
                bank = rbank()
                for kc in range(KC):
                    K.mm(psb[bank][:, :], win[:, kc, c4 * 128:(c4 + 1) * 128], fb.h[:, kc, :], kc == 0, kc == KC - 1,
                         [Bwin, fb.Bh], [PB[bank]])
                K.copy(act if c4 % 2 == 0 else dve, qk[:, c4, :], psb[bank][:, :], [PB[bank]], [Bqk])
            bank = rbank()
            for kc in range(KC):
                K.mm(psb[bank][0:32, :], win[:, kc, 1024:1056], fb.h[:, kc, :], kc == 0, kc == KC - 1,
                     [Bwin, fb.Bh], [PB[bank]])
            K.copy(act, alr[0:32, :], psb[bank][0:32, :], [PB[bank]], [Balr])
            for ts in range(4):
                tk = slice(ts * 128, (ts + 1) * 128)
                n = tt * 4 + ts
                vt, Bvt = vtok[n % 2], Bvtok[n % 2]
                for kc in range(KC):
                    K.mm(psb[2][:, 0:256], fb.h[:, kc, tk], win[:, kc, 256:512], kc == 0, kc == KC - 1,
                         [Bwin, fb.Bh], [PB[2]])
                for kc in range(KC):
                    K.mm(psb[3][:, :], fb.h[:, kc, tk], win[:, kc, 512:1024], kc == 0, kc == KC - 1,
                         [Bwin, fb.Bh], [PB[3]])
                K.copy(act, vt[:], psb[3][:, :], [PB[3]], [Bvt])
                K.dma(sp, gvt[n], vt[:], reads=[Bvt], writes=[Bgvt])
                K.mm(psb[4][:, :], alr[0:33, tk], wal[0:33, :], True, True, [Balr, Bwal], [PB[4]])
                K.actf(spt[:], psb[4][:, :], ACT.Exp, [PB[4]], [Bspt], scale=-1.0)
                K.actf(spt[:], spt[:], ACT.Ln, [Bspt], [Bspt], bias=1.0)
                for pr in range(2):
                    K.mm(psb[5][:, pr * 130:pr * 130 + 130], spt[:, pr * 128:(pr + 1) * 128], tc[:, 0:130], True, True,
                         [Bspt, Btc], [PB[5]])
                for pr in range(2):
                    K.mm(psb[6 + pr][:, 0:258], spt[:, 256 + pr * 128:256 + (pr + 1) * 128], tc[:, 130:388], True,
                         True, [Bspt, Btc], [PB[6 + pr]])
                K.mm(psb[4][:, 0:256], tc[:, 130:258], spt[:, 0:256], True, True, [Bspt, Btc], [PB[4]])
                K.mm(psb[4][:, 256:512], tc[:, 388:516], spt[:, 256:512], True, True, [Bspt, Btc], [PB[4]])
                sc = 1.0 / 16.0
                for pr in range(2):
                    K.actf(E[:, 0, pr, :], psb[5][:, pr * 130:pr * 130 + 128], ACT.Exp, [PB[5]], [BE], scale=-sc)
                    K.actf(E[:, 1, pr, :], psb[5][:, pr * 130:pr * 130 + 128], ACT.Exp, [PB[5]], [BE], scale=sc)
                    K.actf(decs[:, 0, pr, 2 * n:2 * n + 2], psb[5][:, pr * 130 + 128:pr * 130 + 130], ACT.Exp,
                           [PB[5]], [Bdecs], scale=-sc)
                    K.actf(E[:, 2, pr, :], psb[6 + pr][:, 0:128], ACT.Exp, [PB[6 + pr]], [BE], scale=-sc)
                    K.actf(E[:, 3, pr, :], psb[6 + pr][:, 128:256], ACT.Exp, [PB[6 + pr]], [BE], scale=sc)
                    K.actf(decs[:, 1, pr, 2 * n:2 * n + 2], psb[6 + pr][:, 256:258], ACT.Exp, [PB[6 + pr]], [Bdecs],
                           scale=-sc)
                K.actf(ekd[:], psb[4][:, :], ACT.Exp, [PB[4]], [Bekd], scale=-sc)
                K.tt(dve, kd[:, 0:256], psb[2][:, 0:256], ekd[:, 0:256], ALU.mult, [PB[2], Bekd], [Bkd])
                K.tt(dve, kd[:, 256:512], psb[2][:, 0:256], ekd[:, 256:512], ALU.mult, [PB[2], Bekd], [Bkd])
                for pr in range(2):
                    K.stt(qst[:, 0, pr, tk], qk[:, pr, tk], 0.125, E[:, 0, pr, :], ALU.mult, ALU.mult, [Bqk, BE], [Bqst])
                    K.stt(qst[:, 1, pr, tk], qk[:, pr, tk], 0.125, E[:, 2, pr, :], ALU.mult, ALU.mult, [Bqk, BE], [Bqst])
                    K.tt(pool, qst[:, 2, pr, tk], qk[:, 2 + pr, tk], E[:, 1, pr, :], ALU.mult, [Bqk, BE], [Bqst])
                    K.tt(pool, qst[:, 3, pr, tk], qk[:, 2 + pr, tk], E[:, 3, pr, :], ALU.mult, [Bqk, BE], [Bqst])
                for c in range(2):
                    for dr in range(2):
                        for h in range(4):
                            hb = (h % 2) * 64
                            col = (dr * 2 + h // 2) * 128
                            K.mm(psb[c][hb:hb + 64, col:col + 128],
                                 kd[c * 64:(c + 1) * 64, dr * 256 + h * 64:dr * 256 + (h + 1) * 64],
                                 vt[c * 64:(c + 1) * 64, h * 128:(h + 1) * 128], True, True, [Bkd, Bvt], [PB[c]])
                    K.copy(act if c == 0 else dve, kvst[:, :, c * 2:c * 2 + 2, :],
                           psb[c][:, :].rearrange("p (d r v) -> p d r v", d=2, r=2), [PB[c]], [Bkvst])
                for dr in range(2):
                    K.dma(sp, gkv[dr, 2 * n:2 * n + 2].rearrange("c r p v -> p c r v"),
                          kvst[:, dr, :, :].rearrange("p (c r) v -> p c r v", c=2), reads=[Bkvst], writes=[Bgkv])
            K.dma(sp, gq_v[:, :, :, tt * 512:(tt + 1) * 512], qst[:], reads=[Bqst], writes=[Bgq])

    def even_r(S, stack, store, Sin=None):
        nch = S // 64
        CB = min(8, nch)
        St = [[sb("r_st%d%d" % (d_, p_), [128, 128], F32, stack) for p_ in range(2)] for d_ in range(2)]
        BSt = [[Buf(), Buf()], [Buf(), Buf()]]
        kvb = [sb("r_kvb%d" % i, [128, CB, 2, 128], F32, stack) for i in range(2)]
        stb = [sb("r_stb%d" % i, [128, CB, 2, 128], BF16, stack) for i in range(2)]
        Bkvb, Bstb = [Buf(), Buf()], [Buf(), Buf()]
        nb = 0
        for dr in range(2):
            for pr in range(2):
                if Sin is None:
                    K.op(dve, lambda e, dr=dr, pr=pr: e.memset(St[dr][pr][:], 0.0), [], [BSt[dr][pr]])
                else:
                    K.copy(dve, St[dr][pr][:], Sin[dr][pr][0][:], [Sin[dr][pr][1]], [BSt[dr][pr]])
            batches = list(range(0, nch, CB))
            if dr == 1:
                batches = batches[::-1]
            for c0 in batches:
                kb, Bk = kvb[nb % 2], Bkvb[nb % 2]
                sbf, Bs_ = stb[nb % 2], Bstb[nb % 2]
                nb += 1
                K.dma(sp, kb[:], gkv[dr, c0:c0 + CB].rearrange("c r p v -> p c r v"), reads=[Bgkv], writes=[Bk])
                cis = list(range(CB))
                if dr == 1:
                    cis = cis[::-1]
                for ci in cis:
                    c = c0 + ci
                    for pr in range(2):
                        if store:
                            K.copy(act, sbf[:, ci, pr, :], St[dr][pr][:], [BSt[dr][pr]], [Bs_])
                        K.stt(St[dr][pr][:], St[dr][pr][:], decs[:, dr, pr, c:c + 1], kb[:, ci, pr, :], ALU.mult,
                              ALU.add, [BSt[dr][pr], Bdecs, Bk], [BSt[dr][pr]])
                if store:
                    K.dma(sp, gs[dr, c0:c0 + CB].rearrange("c r p v -> p c r v"), sbf[:], reads=[Bs_], writes=[Bgs])
        return St, BSt

    def even_exchange(S, stack):
        nch = S // 64
        St, BSt = even_r(S, stack, False)
        pk = sb("x_pk", [128, 4, 129], F32, stack)
        Bpk = Buf()
        for dr in range(2):
            for pr in range(2):
                k4 = dr * 2 + pr
                K.copy(dve, pk[:, k4, 0:128], St[dr][pr][:], [BSt[dr][pr]], [Bpk])
                K.copy(dve, pk[:, k4, 128:129], decs[:, dr, pr, 0:1], [Bdecs], [Bpk])
                for c in range(1, nch):
                    K.tt(dve, pk[:, k4, 128:129], pk[:, k4, 128:129], decs[:, dr, pr, c:c + 1], ALU.mult,
                         [Bpk, Bdecs], [Bpk])
        K.dma(sp, gsum.rearrange("(k p) v -> p k v", p=128), pk[:], reads=[Bpk], writes=[Bgsum])
        K.barrier()
        K.op(pool, lambda e: e.collective_compute("AllGather", ALU.bypass, replica_groups=cfg.replica_groups,
                                                  ins=[gsum], outs=[gsum_all]), [Bgsum], [Bgsall])
        K.barrier()
        pa = sb("x_pa", [128, GRP, 4, 129], F32, stack)
        Bpa = Buf()
        K.dma(sp, pa[:], gsum_all.rearrange("(g k p) v -> p g k v", g=GRP, p=128), reads=[Bgsall], writes=[Bpa])
        Sin = [[None, None], [None, None]]
        cf = sb("x_cf", [128, 2], F32, stack)
        Bcf = Buf()
        for dr in range(2):
            for pr in range(2):
                k4 = dr * 2 + pr
                t_ = sb("x_sin%d" % k4, [128, 128], F32, stack)
                Bt = Buf()
                K.op(dve, lambda e, t_=t_: e.memset(t_[:], 0.0), [], [Bt])
                for r1 in range(GRP):
                    K.copy(dve, cf[:, 0:1], fcst[:, 8 + dr * 4 + r1:9 + dr * 4 + r1], [Bfc], [Bcf])
                    for r2 in range(GRP):
                        ic_ = 16 + dr * 16 + r1 * 4 + r2
                        K.ts(dve, cf[:, 1:2], pa[:, r2, k4, 128:129], -1.0, fcst[:, ic_:ic_ + 1], ALU.add, ALU.mult,
                             [Bpa, Bfc], [Bcf])
                        K.stt(cf[:, 0:1], cf[:, 1:2], 1.0, cf[:, 0:1], ALU.add, ALU.mult, [Bcf], [Bcf])
                    K.stt(t_[:], pa[:, r1, k4, 0:128], cf[:, 0:1], t_[:], ALU.mult, ALU.add, [Bpa, Bcf, Bt], [Bt])
                Sin[dr][pr] = (t_, Bt)
        return Sin

    def even_o(l, S, stack):
        i_ev = l // 2
        tcm = sb("o_tcm", [128, 256], F32, stack)
        gn = sb("o_gn", [128, 1], F32, stack)
        qt = [sb("o_qt%d" % i, [128, 4, 2, 512], BF16, stack) for i in range(2)]
        vtl = [sb("o_vt%d" % i, [128, 4, 512], BF16, stack) for i in range(2)]
        gt = [sb("o_gt%d" % i, [128, 4, 512], BF16, stack) for i in range(2)]
        sf = [sb("o_sf%d" % i, [128, 8, 2, 128], BF16, stack) for i in range(2)]
        sbw = [sb("o_sb%d" % i, [128, 8, 2, 128], BF16, stack) for i in range(2)]
        am = [sb("o_am%d" % i, [128, 256], BF16, stack) for i in range(2)]
        sq = sb("o_sq", [128, 512], BF16, stack)
        rs = sb("o_rs", [128, 512], F32, stack)
        on = sb("o_on", [128, 512], F32, stack)
        ost = sb("o_ost", [128, 4, 512], BF16, stack)
        Btcm, Bgn, Bsq, Brs, Bon, Bost = (Buf() for _ in range(6))
        Bqt, Bvtl, Bgt, Bsf, Bsbw, Bam = ([Buf(), Buf()] for _ in range(6))
        K.dma(sp, tcm[:, 0:128], tconst[:, 0:128], writes=[Btcm])
        K.dma(sp, tcm[:, 128:256], tconst[:, 130:258], writes=[Btcm])
        K.dma(sp, gn[:], gla_nrm[i_ev], writes=[Bgn])
        gq_v = gq.rearrange("k r p s -> p k r s")
        gg_v = gg.rearrange("(c p) s -> p c s", p=128)
        mixo_v = mixo.rearrange("(c p) s -> p c s", p=128)
        na = 0
        for tt in range(S // 512):
            i2 = tt % 2
            K.dma(sp, qt[i2][:], gq_v[:, :, :, tt * 512:(tt + 1) * 512], reads=[Bgq], writes=[Bqt[i2]])
            K.dma(sp, vtl[i2][:], gvt[tt * 4:(tt + 1) * 4].rearrange("n p v -> p n v"), reads=[Bgvt], writes=[Bvtl[i2]])
            K.dma(sp, gt[i2][:], gg_v[:, :, tt * 512:(tt + 1) * 512], reads=[Bgg], writes=[Bgt[i2]])
            K.dma(sp, sf[i2][:], gs[0, tt * 8:(tt + 1) * 8].rearrange("c r p v -> p c r v"), reads=[Bgs],
                  writes=[Bsf[i2]])
            K.dma(sp, sbw[i2][:], gs[1, tt * 8:(tt + 1) * 8].rearrange("c r p v -> p c r v"), reads=[Bgs],
                  writes=[Bsbw[i2]])
            q_ = qt[i2]
            for ts in range(4):
                tk = slice(ts * 128, (ts + 1) * 128)
                for h in range(4):
                    pr, hb = h // 2, (h % 2) * 64
                    rows = slice(hb, hb + 64)
                    ab = h % 2
                    ob = 2 + h % 2
                    K.mm(psb[ab][:, 0:128], q_[rows, 2, pr, tk], q_[rows, 0, pr, tk], True, True, [Bqt[i2]], [PB[ab]])
                    K.mm(psb[ab][:, 128:256], q_[rows, 3, pr, tk], q_[rows, 1, pr, tk], True, True, [Bqt[i2]],
                         [PB[ab]])
                    a_, Ba = am[na % 2], Bam[na % 2]
                    na += 1
                    K.tt(dve, a_[:], psb[ab][:, 0:256], tcm[:], ALU.mult, [PB[ab], Btcm], [Ba])
                    o0 = pr * 128
                    oc_ = slice(o0, o0 + 128)
                    K.mm(psb[ob][:, oc_], vtl[i2][:, ts, h * 128:(h + 1) * 128], a_[:, 0:128], True, False,
                         [Bvtl[i2], Ba], [PB[ob]])
                    K.mm(psb[ob][:, oc_], vtl[i2][:, ts, h * 128:(h + 1) * 128], a_[:, 128:256], False, False,
                         [Bvtl[i2], Ba], [PB[ob]])
                    for c in range(2):
                        ci = ts * 2 + c
                        cs = slice(o0 + c * 64, o0 + (c + 1) * 64)
                        tks = slice(ts * 128 + c * 64, ts * 128 + (c + 1) * 64)
                        K.mm(psb[ob][:, cs], sf[i2][rows, ci, pr, :], q_[rows, 0, pr, tks], False, False,
                             [Bsf[i2], Bqt[i2]], [PB[ob]])
                        K.mm(psb[ob][:, cs], sbw[i2][rows, ci, pr, :], q_[rows, 1, pr, tks], False, c == 1,
                             [Bsbw[i2], Bqt[i2]], [PB[ob]])
                for par in range(2):
                    ob = 2 + par
                    hsel = slice(par, 4, 2)
                    K.actf(sq[:, 0:256], psb[ob][:, 0:256], ACT.Square, [PB[ob]], [Bsq])
                    K.mm(psb[4][:, 0:256], onesb[:], sq[:, 0:256], True, True, [Bones, Bsq], [PB[4]])
                    K.actf(rs[:, 0:256], psb[4][:, 0:256], ACT.Sqrt, [PB[4]], [Brs], scale=1.0 / 128, bias=EPS)
                    K.op(dve, lambda e: e.reciprocal(rs[:, 0:256], rs[:, 0:256]), [Brs], [Brs])
                    K.tt(dve, on[:, 0:256], psb[ob][:, 0:256], rs[:, 0:256], ALU.mult, [PB[ob], Brs], [Bon])
                    K.stt(ost[:, hsel, tk], on[:, 0:256].rearrange("p (h i) -> p h i", h=2), gn[:, 0:1],
                          gt[i2][:, hsel, tk], ALU.mult, ALU.mult, [Bon, Bgn, Bgt[i2]], [Bost])
            K.dma(sp, mixo_v[:, 0:4, tt * 512:(tt + 1) * 512], ost[:], reads=[Bost], writes=[Bmixo])

    def even_a2(l, S, is_sample, stack):
        i_ev = l // 2
        fb = mx_alloc(stack)
        win = sb("m_win", [128, KC, 1088], BF16, stack)
        wqb = sb("m_wqb", [128, 2, 1536], BF16, stack)
        wkv = sb("m_wkv", [128, 1024], BF16, stack)
        qn = sb("m_qn", [128, 2], F32, stack)
        kvn = sb("m_kvn", [128, 1], F32, stack)
        cq = sb("m_cq", [128, 2, 512], F32, stack)
        cqn = sb("m_cqn", [128, 2, 512], BF16, stack)
        ckvn = sb("m_ckvn", [128, 512], BF16, stack)
        cf_ = sb("m_cf", [128, 4], F32, stack)
        zb = sb("m_zb", [128, 1], F32, stack)
        Bzb = Buf()
        K.op(dve, lambda e: e.memset(zb[:], 0.0), [], [Bzb])
        ai = sb("m_ai", [128, 512], I32, stack)
        pos = sb("m_pos", [128, 512], F32, stack)
        tab = sb("m_tab", [128, 2, 512], F32, stack)
        gst = [sb("m_gst%d" % i, [128, 512], BF16, stack) for i in range(2)]
        qh = [sb("m_qh%d" % i, [96, 512], BF16, stack) for i in range(2)]
        kst = sb("m_kst", [96, 8, 512], BF16, stack)
        kro = sb("m_kro", [96, 512], BF16, stack)
        vaug = [sb("m_vaug%d" % i, [128, 8, 65], BF16, stack) for i in range(2)]
        (Bwin, Bwqb, Bwkv, Bqn, Bkvn, Bcq, Bcqn, Bckvn, Bcf, Bai, Bpos, Btab, Bkst, Bkro) = (Buf() for _ in range(14))
        Bgst, Bqh, Bvaug = ([Buf(), Buf()] for _ in range(3))
        K.dma(sp, win[:], ev_win2_b[i_ev].rearrange("p (k n) -> p k n", k=KC), reads=[Bevw], writes=[Bwin])
        K.dma(sp, wqb[:], mla_wqb_b[i_ev].rearrange("p (k n) -> p k n", k=2), reads=[Bevw], writes=[Bwqb])
        K.dma(sp, wkv[:], mla_wkvb_b[i_ev], reads=[Bevw], writes=[Bwkv])
        K.dma(sp, qn[:], mla_qn[i_ev], writes=[Bqn])
        K.dma(sp, kvn[:], mla_kvn[i_ev], writes=[Bkvn])
        for i in range(2):
            K.op(dve, lambda e, i=i: e.memset(vaug[i][:], 1.0), [], [Bvaug[i]])
        K.op(pool, lambda e: e.iota(ai[:], [[0, 512]], base=0, channel_multiplier=1), [], [Bai])
        K.op(dve, lambda e: e.tensor_scalar(ai[:, 0:1], ai[:, 0:1], icst[:, 6:7], None, ALU.bitwise_and), [Bai, Bic],
             [Bai])
        K.copy(dve, cf_[:, 1:2], ai[:, 0:1], [Bai], [Bcf])
        K.actf(cf_[:, 0:1], cf_[:, 1:2], ACT.Exp, [Bcf], [Bcf], scale=-float(np.log(10000.0)) / 16.0)
        K.ts(dve, cf_[:, 0:1], cf_[:, 0:1], 65536.0 / (2.0 * np.pi), None, ALU.mult, None, [Bcf], [Bcf])
        jpos = sb("m_jpos", [128, 512], I32, stack)
        Bjpos = Buf()
        K.op(pool, lambda e: e.iota(jpos[:], [[1, 512]], base=0, channel_multiplier=0), [], [Bjpos])
        if is_sample:
            K.op(pool, lambda e: e.tensor_scalar(jpos[:], jpos[:], icst[:, 5:6], None, ALU.add), [Bjpos, Bic], [Bjpos])
        gg_v = gg.rearrange("(c p) s -> p c s", p=128)
        Kd_v = Kd.rearrange("(h r) s -> r h s", h=8)
        Vd_v = Vd.rearrange("(h p) (k e) -> p h k e", h=8, e=65)
        qs = float(96.0 ** -0.5)
        R = slice(64, 96)
        ng = 0
        a2s = cfg.a2_stop
        if a2s == 1:
            return
        for tt in range(S // 512):
            for st in prenorm_steps(fb, l, 1, tt, fb.h, fb.Bh):
                st()
            for c4 in range(4):
                bank = rbank()
                for kc in range(KC):
                    K.mm(psb[bank][:, :], win[:, kc, c4 * 128:(c4 + 1) * 128], fb.h[:, kc, :], kc == 0, kc == KC - 1,
                         [Bwin, fb.Bh], [PB[bank]])
                g_, Bg_ = gst[ng % 2], Bgst[ng % 2]
                ng += 1
                K.actf(g_[:], psb[bank][:, :], ACT.Silu, [PB[bank]], [Bg_])
                K.dma(sp, gg_v[:, c4, tt * 512:(tt + 1) * 512], g_[:], reads=[Bg_], writes=[Bgg])
            if a2s == 2:
                continue
            K.ts(dve, pos[:], jpos[:], float(tt * 512), None, ALU.add, None, [Bjpos], [Bpos])
            for k2 in range(2):
                K.ts(dve, ai[:], pos[:], cf_[:, 0:1], fcst[:, k2:k2 + 1], ALU.mult, ALU.add, [Bpos, Bcf, Bfc], [Bai])
                K.op(dve, lambda e: e.tensor_scalar(ai[:], ai[:], icst[:, 3:4], None, ALU.bitwise_and), [Bai, Bic],
                     [Bai])
                K.actf(tab[:, k2, :], ai[:], ACT.Sin, [Bai], [Btab], scale=2.0 * np.pi / 65536.0, bias=-np.pi)
            if a2s == 3:
                continue
            for c2 in range(2):
                bank = rbank()
                for kc in range(KC):
                    K.mm(psb[bank][:, :], win[:, kc, 512 + c2 * 128:512 + (c2 + 1) * 128], fb.h[:, kc, :], kc == 0,
                         kc == KC - 1, [Bwin, fb.Bh], [PB[bank]])
                K.copy(act, cq[:, c2, :], psb[bank][:, :], [PB[bank]], [Bcq])
                i = fb.nsq % len(fb.sq)
                fb.nsq += 1
                K.actf(fb.sq[i][:], psb[bank][:, :], ACT.Square, [PB[bank]], [fb.Bsq[i]])
                K.mm(psb[6][:, :], onesb[:], fb.sq[i][:], c2 == 0, c2 == 1, [Bones, fb.Bsq[i]], [PB[6]])
            K.actf(fb.rstd[1][:], psb[6][:, :], ACT.Sqrt, [PB[6]], [fb.Brstd[1]], scale=1.0 / 256, bias=EPS)
            K.op(dve, lambda e: e.reciprocal(fb.rstd[1][:], fb.rstd[1][:]), [fb.Brstd[1]], [fb.Brstd[1]])
            for c2 in range(2):
                i = fb.ntmp % len(fb.tmp)
                fb.ntmp += 1
                K.stt(fb.tmp[i][:], cq[:, c2, :], qs, fb.rstd[1][:], ALU.mult, ALU.mult, [Bcq, fb.Brstd[1]],
                      [fb.Btmp[i]])
                K.actf(cqn[:, c2, :], fb.tmp[i][:], ACT.Identity, [fb.Btmp[i], Bqn, Bzb], [Bcqn],
                       scale=qn[:, c2:c2 + 1], bias=zb[:, 0:1])
            bank = rbank()
            for kc in range(KC):
                K.mm(psb[bank][:, :], win[:, kc, 768:896], fb.h[:, kc, :], kc == 0, kc == KC - 1, [Bwin, fb.Bh],
                     [PB[bank]])
            i = fb.nsq % len(fb.sq)
            fb.nsq += 1
            K.actf(fb.sq[i][:], psb[bank][:, :], ACT.Square, [PB[bank]], [fb.Bsq[i]])
            K.mm(psb[6][:, :], onesb[:], fb.sq[i][:], True, True, [Bones, fb.Bsq[i]], [PB[6]])
            K.actf(fb.rstd[1][:], psb[6][:, :], ACT.Sqrt, [PB[6]], [fb.Brstd[1]], scale=1.0 / 128, bias=EPS)
            K.op(dve, lambda e: e.reciprocal(fb.rstd[1][:], fb.rstd[1][:]), [fb.Brstd[1]], [fb.Brstd[1]])
            i = fb.ntmp % len(fb.tmp)
            fb.ntmp += 1
            K.tt(dve, fb.tmp[i][:], psb[bank][:, :], fb.rstd[1][:], ALU.mult, [PB[bank], fb.Brstd[1]], [fb.Btmp[i]])
            K.actf(ckvn[:], fb.tmp[i][:], ACT.Identity, [fb.Btmp[i], Bkvn, Bzb], [Bckvn], scale=kvn[:, 0:1],
                   bias=zb[:, 0:1])
            if a2s == 4:
                continue
            for kc in range(KC):
                K.mm(psb[2][0:96, :], win[:, kc, 896:992], fb.h[:, kc, :], kc == 0, kc == KC - 1, [Bwin, fb.Bh], [PB[2]])
            for kc in range(KC):
                K.mm(psb[3][0:96, :], win[:, kc, 992:1088], fb.h[:, kc, :], kc == 0, kc == KC - 1, [Bwin, fb.Bh],
                     [PB[3]])
            i = fb.ntmp % len(fb.tmp)
            fb.ntmp += 1
            K.tt(dve, fb.tmp[i][R, :], psb[2][R, :], tab[R, 0, :], ALU.mult, [PB[2], Btab], [fb.Btmp[i]])
            i2 = fb.ntmp % len(fb.tmp)
            fb.ntmp += 1
            K.tt(dve, fb.tmp[i2][R, :], psb[3][R, :], tab[R, 1, :], ALU.mult, [PB[3], Btab], [fb.Btmp[i2]])
            K.tt(dve, kro[R, :], fb.tmp[i][R, :], fb.tmp[i2][R, :], ALU.add, [fb.Btmp[i], fb.Btmp[i2]], [Bkro])
            if a2s == 5:
                continue
            for h in range(8):
                bank = rbank()
                K.mm(psb[bank][0:64, :], wkv[:, h * 64:(h + 1) * 64], ckvn[:], True, True, [Bwkv, Bckvn], [PB[bank]])
                K.copy(act, kst[0:64, h, :], psb[bank][0:64, :], [PB[bank]], [Bkst])
                K.copy(dve, kst[R, h, :], kro[R, :], [Bkro], [Bkst])
                for kc in range(2):
                    K.mm(psb[2][0:96, :], wqb[:, kc, h * 96:(h + 1) * 96], cqn[:, kc, :], kc == 0, kc == 1,
                         [Bwqb, Bcqn], [PB[2]])
                for kc in range(2):
                    K.mm(psb[3][0:96, :], wqb[:, kc, 768 + h * 96:768 + (h + 1) * 96], cqn[:, kc, :], kc == 0, kc == 1,
                         [Bwqb, Bcqn], [PB[3]])
                q_, Bq_ = qh[h % 2], Bqh[h % 2]
                K.copy(dve, q_[0:64, :], psb[2][0:64, :], [PB[2]], [Bq_])
                i = fb.ntmp % len(fb.tmp)
                fb.ntmp += 1
                K.tt(dve, fb.tmp[i][R, :], psb[2][R, :], tab[R, 0, :], ALU.mult, [PB[2], Btab], [fb.Btmp[i]])
                i2 = fb.ntmp % len(fb.tmp)
                fb.ntmp += 1
                K.tt(dve, fb.tmp[i2][R, :], psb[3][R, :], tab[R, 1, :], ALU.mult, [PB[3], Btab], [fb.Btmp[i2]])
                K.tt(dve, q_[R, :], fb.tmp[i][R, :], fb.tmp[i2][R, :], ALU.add, [fb.Btmp[i], fb.Btmp[i2]], [Bq_])
                K.dma(sp, Qd[h, :, tt * 512:(tt + 1) * 512], q_[:], reads=[Bq_], writes=[BQd])
            K.dma(sp, Kd_v[:, :, tt * 512:(tt + 1) * 512], kst[:], reads=[Bkst], writes=[BKd])
            if a2s == 6:
                continue
            for ts in range(4):
                tk = slice(ts * 128, (ts + 1) * 128)
                kt = tt * 4 + ts
                va, Bva = vaug[kt % 2], Bvaug[kt % 2]
                bank = rbank()
                K.mm(psb[bank][:, :], ckvn[:, tk], wkv[:, 512:1024], True, True, [Bckvn, Bwkv], [PB[bank]])
                K.copy(act if ts % 2 == 0 else dve, va[:, :, 0:64], psb[bank][:, :].rearrange("p (h e) -> p h e", h=8),
                       [PB[bank]], [Bva])
                K.dma(sp, Vd_v[:, :, kt, :], va[:], reads=[Bva], writes=[BVd])

    def even_b(S, nrank, Ksrc, BKs, Vsrc, BVs, stack):
        SK = S
        KB_ = min(1024, SK)
        nkt = KB_ // 128
        LOOK = 2
        NP = 4
        qt = [sb("b_qt%d" % i, [96, 512], BF16, stack) for i in range(2)]
        ktl = [sb("b_kt%d" % i, [96, KB_], BF16, stack) for i in range(3)]
        vtl = [sb("b_vt%d" % i, [128, nkt, 65], BF16, stack) for i in range(3)]
        pt = [sb("b_pt%d" % i, [128, 512], BF16, stack) for i in range(NP)]
        rc = sb("b_rc", [128, 512], F32, stack)
        osb = sb("b_osb", [64, 512], F32, stack)
        onb = [sb("b_on%d" % i, [64, 512], BF16, stack) for i in range(2)]
        Brc, Bosb = Buf(), Buf()
        Bqt, Bonb = ([Buf(), Buf()] for _ in range(2))
        Bktl, Bvtl = ([Buf(), Buf(), Buf()] for _ in range(2))
        Bpt = [Buf() for _ in range(NP)]
        nq = nk = ns = 0
        pend = []
        tails = []

        def flush_one():
            pend.pop(0)()

        for qb in range(S // 512):
            for h in range(8):
                q_, Bq_ = qt[nq % 2], Bqt[nq % 2]
                ob = 4 + nq % 2
                nq += 1
                K.dma(sp, q_[:], Qd[h, :, qb * 512:(qb + 1) * 512], reads=[BQd], writes=[Bq_])
                nblk = nrank * (SK // KB_)
                nstep = nblk * nkt
                si = 0
                for g in range(nrank):
                    for k0 in range(0, SK, KB_):
                        k_, Bk_ = ktl[nk % 3], Bktl[nk % 3]
                        v_, Bv_ = vtl[nk % 3], Bvtl[nk % 3]
                        nk += 1
                        K.dma(sp, k_[:], Ksrc(g, h, k0, KB_), reads=[BKs], writes=[Bk_])
                        K.dma(sp, v_[:], Vsrc(g, h, k0 // 128, nkt), reads=[BVs], writes=[Bv_])
                        for kt in range(nkt):
                            sbk = ns % 4
                            p_, Bp_ = pt[ns % NP], Bpt[ns % NP]
                            ns += 1
                            K.mm(psb[sbk][:, :], k_[:, kt * 128:(kt + 1) * 128], q_[:], True, True, [Bk_, Bq_],
                                 [PB[sbk]])
                            K.actf(p_[:], psb[sbk][:, :], ACT.Exp, [PB[sbk]], [Bp_])

                            def pv(v_=v_, Bv_=Bv_, p_=p_, Bp_=Bp_, kt=kt, first=(si == 0), last=(si == nstep - 1),
                                   ob=ob):
                                K.mm(psb[ob][0:65, :], v_[:, kt, :], p_[:], first, last, [Bv_, Bp_], [PB[ob]])
                            pend.append(pv)
                            si += 1
                            if len(pend) > LOOK:
                                flush_one()
                            if si == 4 and tails:
                                tails.pop(0)()
                while pend:
                    flush_one()
                while tails:
                    tails.pop(0)()
                K.op(dve, lambda e, ob=ob: e.reciprocal(rc[64:65, :], psb[ob][64:65, :]), [PB[ob]], [Brc])
                K.copy(dve, osb[:], psb[ob][0:64, :], [PB[ob]], [Bosb])

                def tail(h=h, qb=qb):
                    K.mm(psb[6][0:64, :], cst[64:65, 128:192], rc[64:65, :], True, True, [Bcst, Brc], [PB[6]])
                    o_, Bo_ = onb[h % 2], Bonb[h % 2]
                    K.tt(dve, o_[:], osb[:], psb[6][0:64, :], ALU.mult, [Bosb, PB[6]], [Bo_])
                    K.dma(sp, mixm[h, :, qb * 512:(qb + 1) * 512], o_[:], reads=[Bo_], writes=[Bmixm])
                tails.append(tail)
        while tails:
            tails.pop(0)()

    def even_phase_c(l, S, stack):
        i_ev = l // 2
        fb = mx_alloc(stack, with_h=False, with_y=True)
        wg = sb("c_wg", [128, 4, 1024], BF16, stack)
        wm = sb("c_wm", [64, 8, 1024], BF16, stack)
        og = [sb("c_og%d" % i, [128, 4, 512], BF16, stack) for i in range(2)]
        om = [sb("c_om%d" % i, [64, 8, 512], BF16, stack) for i in range(2)]
        Bwg, Bwm = Buf(), Buf()
        Bog, Bom = [Buf(), Buf()], [Buf(), Buf()]
        K.dma(sp, wg[:], ev_woutg_b[i_ev].rearrange("p (k n) -> p k n", k=4), reads=[Bevw], writes=[Bwg])
        K.dma(sp, wm[:], ev_woutm_b[i_ev].rearrange("p (k n) -> p k n", k=8), reads=[Bevw], writes=[Bwm])
        mixo_v = mixo.rearrange("(c p) s -> p c s", p=128)
        mixm_v = mixm.rearrange("h e s -> e h s")
        Cg = vec(l, 1, 2)
        pending = []
        for tt in range(S // 512):
            g_, Bg_ = og[tt % 2], Bog[tt % 2]
            m_, Bm_ = om[tt % 2], Bom[tt % 2]
            K.dma(sp, g_[:], mixo_v[:, 0:4, tt * 512:(tt + 1) * 512], reads=[Bmixo], writes=[Bg_])
            K.dma(sp, m_[:], mixm_v[:, :, tt * 512:(tt + 1) * 512], reads=[Bmixm], writes=[Bm_])

            def mm_oc(oc, bank, g_=g_, m_=m_, Bg_=Bg_, Bm_=Bm_):
                for ic in range(4):
                    K.mm(psb[bank][:, :], wg[:, ic, oc * 128:(oc + 1) * 128], g_[:, ic, :], ic == 0, False,
                         [Bwg, Bg_], [PB[bank]])
                for hh in range(8):
                    K.mm(psb[bank][:, :], wm[:, hh, oc * 128:(oc + 1) * 128], m_[:, hh, :], False, hh == 7,
                         [Bwm, Bm_], [PB[bank]])
            yphase(fb, tt, Cg, mm_oc, [], pending)
            while pending:
                pending.pop(0)()

    def even_mixer(l, S, is_sample):
        xg = is_sample and GRP > 1
        stop = cfg.ev_stop
        with ExitStack() as st:
            even_a1(l, S, st)
            K.barrier()
        if stop == 1:
            return
        with ExitStack() as st:
            Sin = even_exchange(S, st) if xg else None
            even_r(S, st, True, Sin)
            K.barrier()
        if stop == 2:
            return
        with ExitStack() as st:
            even_a2(l, S, is_sample, st)
            K.barrier()
        if stop == 3:
            return
        if xg:
            for h in range(8):
                K.op(pool, lambda e, h=h: e.collective_compute(
                    "AllGather", ALU.bypass, replica_groups=cfg.replica_groups,
                    ins=[Kd[h * 96:(h + 1) * 96, 0:S]], outs=[Kall[h]]), [BKd], [BKall])
                K.op(pool, lambda e, h=h: e.collective_compute(
                    "AllGather", ALU.bypass, replica_groups=cfg.replica_groups,
                    ins=[Vd[h * 128:(h + 1) * 128, 0:(S // 128) * 65]], outs=[Vall[h]]), [BVd], [BVall])
        with ExitStack() as st:
            even_o(l, S, st)
            K.barrier()
        if stop == 4:
            return
        if xg:
            K.barrier()
            kget = lambda g, h, k0, n: Kall[h, g * 96:(g + 1) * 96, k0:k0 + n]
            vget = lambda g, h, kt0, n: Vall[h].rearrange("(g p) (k e) -> g p k e", p=128, e=65)[g, :, kt0:kt0 + n, :]
            srcs = (GRP, kget, BKall, vget, BVall)
        else:
            kget = lambda g, h, k0, n: Kd[h * 96:(h + 1) * 96, k0:k0 + n]
            vget = lambda g, h, kt0, n: Vd[h * 128:(h + 1) * 128, :].rearrange("p (k e) -> p k e", e=65)[:, kt0:kt0 + n, :]
            srcs = (1, kget, BKd, vget, BVd)
        with ExitStack() as st:
            even_b(S, *srcs, st)
            K.barrier()
        if stop == 5:
            return
        with ExitStack() as st:
            even_phase_c(l, S, st)
            K.barrier()

    tok0 = 0
    for si, S in enumerate(cfg.seg_tokens):
        K.dma(sp, mv[:], modv[si], reads=[Bmodv], writes=[Bmv])
        with ExitStack() as st:
            load_segment(tok0, S, st)
            K.barrier()
        for l in range(L):
            with ExitStack() as st:
                fb = ffn_alloc(st)
                ffn_sublayer(fb, l, 0, S)
                K.barrier()
            if cfg.do_mixer and l % 2 == 1 and cfg.do_mixer & 2:
                odd_mixer(l, S, si == 2)
            if cfg.do_mixer and l % 2 == 0 and cfg.do_mixer & 1:
                even_mixer(l, S, si == 2)
            with ExitStack() as st:
                fb = ffn_alloc(st)
                ffn_sublayer(fb, l, 2, S)
                K.barrier()
        with ExitStack() as st:
            store_segment(tok0, S, st)
            K.barrier()
        tok0 += S
    K.barrier()
    es.close()
    return nc


def _fm(v):
    v = np.asarray(v)
    lead = v.shape[:-1]
    n = v.shape[-1] // 128
    v = v.reshape(lead + (n, 128))
    return np.ascontiguousarray(np.moveaxis(v, -1, 0))


def prep_shared(inp, cfg):
    L = cfg.depth
    sh = {}
    ident = np.eye(128, dtype=np.float32)
    sh["consts"] = np.ascontiguousarray(np.concatenate([ident, np.ones((128, 128), np.float32)], axis=1))
    aw = np.asarray(inp["ada_w"])[:L]
    aw = aw.reshape(L, KC, 128, 72, 128).transpose(0, 3, 2, 1, 4)
    sh["ada_w"] = np.ascontiguousarray(aw).reshape(L * 72, 128, KC, 128)
    sh["ada_b"] = _fm(np.asarray(inp["ada_b"])[:L]).reshape(128, L * 72)
    sh["npre"] = _fm(np.asarray(inp["norm_pre"])[:L]).reshape(128, L * 3 * KC)
    sh["npost"] = _fm(np.asarray(inp["norm_post"])[:L]).reshape(128, L * 3 * KC)
    w13 = np.asarray(inp["ffn_w13"])[:L].reshape(L * 2, KC, 128, 2, NFC, 128)
    sh["w13"] = np.ascontiguousarray(w13.transpose(0, 4, 2, 1, 3, 5)).reshape(L * 2, NFC, 128, KC * 256)
    w2 = np.asarray(inp["ffn_w2"])[:L].reshape(L * 2, NFC, 128, KC, 128)
    sh["w2"] = np.ascontiguousarray(w2.transpose(0, 3, 2, 1, 4)).reshape(L * 2, KC, 128, NFC * 128)
    NOD = L // 2
    if NOD:
        ow = np.asarray(inp["od_w_in"])[:NOD]
        sh["od_win"] = np.ascontiguousarray(ow.reshape(NOD, KC, 128, 1536).transpose(0, 2, 1, 3)).reshape(NOD, 128, KC * 1536)
        oo = np.asarray(inp["od_w_out"])[:NOD]
        sh["od_wout"] = np.ascontiguousarray(oo.reshape(NOD, KC, 128, 1024).transpose(0, 2, 1, 3)).reshape(NOD, 128, KC * 1024)
        ws = np.asarray(inp["sgu_w_s"])[:NOD]
        sh["sgu_wsT"] = np.ascontiguousarray(ws.transpose(0, 3, 1, 2)).reshape(NOD, 128, 512)
        sh["sgu_b"] = np.ascontiguousarray(np.asarray(inp["sgu_b"])[:NOD]).reshape(NOD, 1, 512)
        sh["sgu_nrm"] = np.ascontiguousarray(np.broadcast_to(np.asarray(inp["sgu_norm"])[:NOD, None, :], (NOD, 128, 512)))
    NEV = (L + 1) // 2
    if NEV:
        def kmaj(w, nk):
            n, _, cols = w.shape
            return np.ascontiguousarray(w.reshape(n, nk, 128, cols).transpose(0, 2, 1, 3)).reshape(n, 128, nk * cols)
        wi = np.asarray(inp["ev_w_in"])[:NEV]
        q, k, v, g, alr, cq, ckv, kr = (wi[:, :, a:b] for a, b in ((0, 256), (256, 512), (512, 1024), (1024, 1536),
                                                                  (1536, 1568), (1568, 1824), (1824, 1952), (1952, 1984)))
        sh["ev_win1"] = kmaj(np.concatenate([q, k, v, alr], axis=2), KC)
        fill = ckv[:, :, 0:64]
        krrot = np.concatenate([kr[:, :, 16:32], kr[:, :, 0:16]], axis=2)
        sh["ev_win2"] = kmaj(np.concatenate([g, cq, ckv, fill, kr, fill, krrot], axis=2), KC)
        wo = np.asarray(inp["ev_w_out"])[:NEV]
        sh["ev_woutg"] = kmaj(wo[:, 0:512], 4)
        sh["ev_woutm"] = np.ascontiguousarray(wo[:, 512:1024].reshape(NEV, 8, 64, 1024).transpose(0, 2, 1, 3)).reshape(NEV, 64, 8 * 1024)
        wa = np.asarray(inp["gla_w_alpha"])[:NEV]
        ba = np.asarray(inp["gla_b_alpha"])[:NEV]
        wal = np.zeros((NEV, 33, 512), np.float32)
        wal[:, 0:16, 0:256] = wa[:, 0]
        wal[:, 16:32, 256:512] = wa[:, 1]
        wal[:, 32, 0:256] = ba[:, 0]
        wal[:, 32, 256:512] = ba[:, 1]
        sh["gla_wal"] = wal
        sh["gla_nrm"] = np.ascontiguousarray(np.asarray(inp["gla_norm"])[:NEV].reshape(NEV, 128, 1))
        sh["mla_qn"] = np.ascontiguousarray(np.asarray(inp["mla_q_norm"])[:NEV].reshape(NEV, 2, 128).transpose(0, 2, 1))
        sh["mla_kvn"] = np.ascontiguousarray(np.asarray(inp["mla_kv_norm"])[:NEV].reshape(NEV, 128, 1))
        wq = np.asarray(inp["mla_w_q_b"])[:NEV].reshape(NEV, 256, 8, 96)
        wqr = np.concatenate([wq[..., 0:64], wq[..., 80:96], wq[..., 64:80]], axis=-1)
        sh["mla_wqb"] = kmaj(np.concatenate([wq.reshape(NEV, 256, 768), wqr.reshape(NEV, 256, 768)], axis=2), 2)
        wk = np.asarray(inp["mla_w_kv_b"])[:NEV].reshape(NEV, 128, 8, 128)
        sh["mla_wkvb"] = np.ascontiguousarray(np.concatenate([wk[..., 0:64].reshape(NEV, 128, 512),
                                                             wk[..., 64:128].reshape(NEV, 128, 512)], axis=2))
    t = np.arange(128)
    same = (t[:, None] // 64) == (t[None, :] // 64)
    Tfi = (same & (t[:, None] <= t[None, :])).astype(np.float32)
    Tbe = (same & (t[:, None] > t[None, :])).astype(np.float32)
    Tbi = (same & (t[:, None] >= t[None, :])).astype(np.float32)
    Tpe = (same & (t[:, None] < t[None, :])).astype(np.float32)
    Ind = np.stack([(t < 64), (t >= 64)], axis=1).astype(np.float32)
    sh["tconst"] = np.ascontiguousarray(np.concatenate([Tfi, Ind, Tbe, Tbi, Ind, Tpe], axis=1))
    return sh


def prep_core(inp, cfg, core, n_cores=8):
    xp = np.asarray(inp["x_prompt"])
    xs = np.asarray(inp["x_sample"])
    cp = np.asarray(inp["c_prompt"])
    cs = np.asarray(inp["c_sample"])
    SP = cfg.seg_tokens[0]
    SQ = cfg.seg_tokens[2]
    per_grp = n_cores // xs.shape[0]
    sb_, r = core // per_grp, core % per_grp
    xin = np.concatenate([xp[2 * core, :SP], xp[2 * core + 1, :SP], xs[sb_, r * SQ:(r + 1) * SQ]], axis=0)
    c = np.stack([cp[2 * core], cp[2 * core + 1], cs[sb_], np.zeros(D, np.float32)], axis=0)
    c3 = np.ascontiguousarray(c.T.reshape(KC, 128, 4).transpose(1, 0, 2))
    ic = np.zeros((128, 8), np.int32)
    stot = per_grp * SQ
    ic[:, 0] = SP - 1
    ic[:, 1] = stot - 1
    ic[:, 2] = 127
    ic[:, 3] = 65535
    ic[:, 4] = (SQ * r * np.arange(128)) % stot
    ic[:, 5] = r * SQ
    ic[:, 6] = 15
    fc = np.zeros((128, 64), np.float32)
    fc[:, 0] = 49152.0
    fc[80:96, 1] = 32768.0
    for r1 in range(min(per_grp, 4)):
        fc[:, 8 + r1] = 1.0 if r1 < r else 0.0
        fc[:, 12 + r1] = 1.0 if r1 > r else 0.0
        for r2 in range(min(per_grp, 4)):
            fc[:, 16 + r1 * 4 + r2] = 1.0 if r1 < r2 < r else 0.0
            fc[:, 32 + r1 * 4 + r2] = 1.0 if r < r2 < r1 else 0.0
    return {"xin": np.ascontiguousarray(xin), "c3": c3, "iconst": ic, "fconst": fc}


_CACHE = {}


def run(inp, cfg, n_cores=8, trace=False):
    key = (cfg.seg_tokens, cfg.depth, cfg.do_mixer, cfg.n_cores, cfg.group)
    if key not in _CACHE:
        _CACHE[key] = build(cfg)
    nc = _CACHE[key]
    sh = prep_shared(inp, cfg)
    in_maps = []
    for c in range(n_cores):
        m = dict(sh)
        m.update(prep_core(inp, cfg, c, n_cores))
        in_maps.append(m)
    res = run_bass_kernel_spmd(nc, in_maps, core_ids=list(range(n_cores)), trace=trace)
    return res


def kernel(**inputs):
    cfg = Cfg()
    res = run(inputs, cfg)
    SP, SQ = cfg.seg_tokens[0], cfg.seg_tokens[2]
    B, S = inputs["x_prompt"].shape[:2]
    DB, DS = inputs["x_sample"].shape[:2]
    yp = np.empty((B, S, D), np.float32)
    ys = np.empty((DB, DS, D), np.float32)
    per_grp = 8 // DB
    for c in range(8):
        y = res.results[c]["yout"]
        yp[2 * c] = y[0:SP]
        yp[2 * c + 1] = y[SP:2 * SP]
        ys[c // per_grp, (c % per_grp) * SQ:(c % per_grp + 1) * SQ] = y[2 * SP:2 * SP + SQ]
    return (yp, ys)
```

```python
import numpy as np
import concourse.bass as bass
import concourse.mybir as mybir
from concourse.bass_utils import run_bass_kernel_spmd
from contextlib import ExitStack

F32 = mybir.dt.float32
BF16 = mybir.dt.bfloat16
I32 = mybir.dt.int32
I16 = mybir.dt.int16
ACT = mybir.ActivationFunctionType
ALU = mybir.AluOpType

D = 1024
KC = 8
DFF = 2816
NFC = 22
EPS = 1e-6


class Buf:
    __slots__ = ("name", "w", "r")

    def __init__(self, name=""):
        self.name = name
        self.w = None
        self.r = {}


class EngW:
    def __init__(self, name, eng, sid, sem, inorder=False):
        self.name = name
        self.eng = eng
        self.sid = sid
        self.sem = sem
        self.cnt = 0
        self.known = {}
        self.inorder = inorder
        self.ring = []
        self.ring_pos = 0


class KB:
    def __init__(self, nc, nring=20):
        self.nc = nc
        self.es = ExitStack()
        self.sems = []
        self.semcnt = []
        self.engs = {}
        for name, eng, inorder in (("pe", nc.tensor, True), ("act", nc.scalar, False), ("dve", nc.vector, False),
                                   ("pool", nc.gpsimd, False), ("sp", nc.sync, False)):
            sid = self._newsem("c_" + name)
            self.engs[name] = EngW(name, eng, sid, self.sems[sid], inorder)
        for q in ("sp", "pool", "act"):
            E = self.engs[q]
            for i in range(nring):
                E.ring.append(self._newsem("d_%s%d" % (q, i)))
        self.pe, self.act, self.dve, self.pool, self.sp = (self.engs[n] for n in ("pe", "act", "dve", "pool", "sp"))

    def _newsem(self, name):
        s = self.es.enter_context(self.nc.semaphore(name))
        self.sems.append(s)
        self.semcnt.append(0)
        return len(self.sems) - 1

    def _waits(self, E, reads, writes, extra=()):
        need = {}
        for b in reads:
            if b.w is not None and need.get(b.w[0], 0) < b.w[1]:
                need[b.w[0]] = b.w[1]
        for b in writes:
            if b.w is not None and need.get(b.w[0], 0) < b.w[1]:
                need[b.w[0]] = b.w[1]
            for sid, val in b.r.items():
                if need.get(sid, 0) < val:
                    need[sid] = val
        for sid, val in extra:
            if need.get(sid, 0) < val:
                need[sid] = val
        for sid, val in need.items():
            if sid == E.sid and E.inorder:
                continue
            if E.known.get(sid, 0) >= val:
                continue
            E.eng.wait_ge(self.sems[sid], val)
            E.known[sid] = val

    def op(self, E, emit, reads=(), writes=()):
        self._waits(E, reads, writes)
        ins = emit(E.eng)
        E.cnt += 1
        ins.then_inc(E.sem, 1)
        self.semcnt[E.sid] = E.cnt
        for b in reads:
            if b.r.get(E.sid, 0) < E.cnt:
                b.r[E.sid] = E.cnt
        for b in writes:
            b.w = (E.sid, E.cnt)
            b.r = {}

    def dma(self, Q, out, in_, reads=(), writes=(), **kw):
        sid = Q.ring[Q.ring_pos]
        Q.ring_pos = (Q.ring_pos + 1) % len(Q.ring)
        prev = self.semcnt[sid]
        self._waits(Q, reads, writes, extra=((sid, prev),) if prev else ())
        ins = Q.eng.dma_start(out=out, in_=in_, **kw)
        self.semcnt[sid] = prev + 16
        ins.then_inc(self.sems[sid], 16)
        val = prev + 16
        for b in reads:
            if b.r.get(sid, 0) < val:
                b.r[sid] = val
        for b in writes:
            b.w = (sid, val)
            b.r = {}

    def barrier(self):
        for E in self.engs.values():
            for sid in range(len(self.sems)):
                val = self.semcnt[sid]
                if val and sid != E.sid and E.known.get(sid, 0) < val:
                    E.eng.wait_ge(self.sems[sid], val)
                    E.known[sid] = val
            if E.cnt and not E.inorder and E.known.get(E.sid, 0) < E.cnt:
                E.eng.wait_ge(E.sem, E.cnt)
                E.known[E.sid] = E.cnt

    def mm(self, out, lhsT, rhs, start, stop, reads, writes, **kw):
        self.op(self.pe, lambda e: e.matmul(out, lhsT, rhs, start=start, stop=stop, **kw), reads, writes)

    def actf(self, out, in_, func, reads, writes, **kw):
        self.op(self.act, lambda e: e.activation(out, in_, func, **kw), reads, writes)

    def tt(self, E, out, in0, in1, op, reads, writes):
        self.op(E, lambda e: e.tensor_tensor(out, in0, in1, op), reads, writes)

    def ts(self, E, out, in0, s1, s2, op0, op1, reads, writes):
        if op1 is None:
            self.op(E, lambda e: e.tensor_scalar(out, in0, s1, None, op0), reads, writes)
        else:
            self.op(E, lambda e: e.tensor_scalar(out, in0, s1, s2, op0, op1), reads, writes)

    def stt(self, out, in0, scalar, in1, op0, op1, reads, writes):
        self.op(self.dve, lambda e: e.scalar_tensor_tensor(out, in0, scalar, in1, op0, op1), reads, writes)

    def copy(self, E, out, in_, reads, writes):
        if E is self.act:
            self.op(E, lambda e: e.copy(out, in_), reads, writes)
        else:
            self.op(E, lambda e: e.tensor_copy(out, in_), reads, writes)


class Cfg:
    def __init__(self, seg_tokens=(4096, 4096, 4096), depth=4, do_mixer=True, n_cores=8, group=4):
        self.seg_tokens = tuple(seg_tokens)
        self.ntok = sum(seg_tokens)
        self.depth = depth
        self.do_mixer = 3 if do_mixer is True else int(do_mixer)
        self.nffn = depth * 2
        self.n_cores = n_cores
        self.ev_stop = 0
        self.a2_stop = 0
        self.no_xg = 0
        self.cc_max = 4 * 1024 * 1024
        self.group = group
        self.replica_groups = [list(range(g * group, (g + 1) * group)) for g in range(n_cores // group)]


def build(cfg):
    nc = bass.Bass("TRN2", target_bir_lowering=False)
    L = cfg.depth
    NF = cfg.nffn
    NT = cfg.ntok

    def din(name, shape, dt=F32):
        return nc.dram_tensor(name, list(shape), dt, kind="ExternalInput").ap()

    def dscr(name, shape, dt):
        return nc.dram_tensor(name, list(shape), dt, kind="Internal").ap()

    xin = din("xin", [NT, D])
    c3 = din("c3", [128, KC, 4])
    consts = din("consts", [128, 256])
    ada_w = din("ada_w", [L * 72, 128, KC, 128])
    ada_b = din("ada_b", [128, L * 72])
    npre = din("npre", [128, L * 3 * KC])
    npost = din("npost", [128, L * 3 * KC])
    w13 = din("w13", [NF, NFC, 128, KC * 256])
    w2 = din("w2", [NF, KC, 128, NFC * 128])
    yout = nc.dram_tensor("yout", [NT, D], F32, kind="ExternalOutput").ap()
    NOD = L // 2
    NEV = (L + 1) // 2
    SQ = cfg.seg_tokens[2]
    GRP = cfg.group
    iconst = din("iconst", [128, 8], I32)
    if NOD:
        od_win = din("od_win", [NOD, 128, KC * 1536])
        od_wout = din("od_wout", [NOD, 128, KC * 1024])
        sgu_wsT = din("sgu_wsT", [NOD, 128, 512])
        sgu_b = din("sgu_b", [NOD, 1, 512])
        sgu_nrm = din("sgu_nrm", [NOD, 128, 512])
        od_win_b = dscr("od_win_b", [NOD, 128, KC * 1536], BF16)
        od_wout_b = dscr("od_wout_b", [NOD, 128, KC * 1024], BF16)
        sgu_wsT_b = dscr("sgu_wsT_b", [NOD, 128, 512], BF16)
        sgu_b_b = dscr("sgu_b_b", [NOD, 1, 512], BF16)
    SMAXL = max(cfg.seg_tokens)
    tconst = din("tconst", [128, 516])
    fconst = din("fconst", [128, 64])
    if NEV:
        ev_win1 = din("ev_win1", [NEV, 128, KC * 1056])
        ev_win2 = din("ev_win2", [NEV, 128, KC * 1088])
        ev_woutg = din("ev_woutg", [NEV, 128, 4 * 1024])
        ev_woutm = din("ev_woutm", [NEV, 64, 8 * 1024])
        gla_wal = din("gla_wal", [NEV, 33, 512])
        gla_nrm = din("gla_nrm", [NEV, 128, 1])
        mla_qn = din("mla_qn", [NEV, 128, 2])
        mla_kvn = din("mla_kvn", [NEV, 128, 1])
        mla_wqb = din("mla_wqb", [NEV, 128, 2 * 1536])
        mla_wkvb = din("mla_wkvb", [NEV, 128, 1024])
        ev_win1_b = dscr("ev_win1_b", [NEV, 128, KC * 1056], BF16)
        ev_win2_b = dscr("ev_win2_b", [NEV, 128, KC * 1088], BF16)
        ev_woutg_b = dscr("ev_woutg_b", [NEV, 128, 4 * 1024], BF16)
        ev_woutm_b = dscr("ev_woutm_b", [NEV, 64, 8 * 1024], BF16)
        gla_wal_b = dscr("gla_wal_b", [NEV, 33, 512], BF16)
        mla_wqb_b = dscr("mla_wqb_b", [NEV, 128, 2 * 1536], BF16)
        mla_wkvb_b = dscr("mla_wkvb_b", [NEV, 128, 1024], BF16)
    NCH = SMAXL // 64
    gq = dscr("gq", [4, 2, 128, SMAXL], BF16)
    gvt = dscr("gvt", [SMAXL // 128, 128, 512], BF16)
    gkv = dscr("gkv", [2, NCH, 2, 128, 128], F32)
    gs = dscr("gs", [2, NCH, 2, 128, 128], BF16)
    gg = dscr("gg", [512, SMAXL], BF16)
    gsum = dscr("gsum", [4 * 128, 129], F32)
    gsum_all = dscr("gsum_all", [GRP * 4 * 128, 129], F32)
    Qd = dscr("Qd", [8, 96, SMAXL], BF16)
    Kd = dscr("Kd", [8 * 96, SMAXL], BF16)
    Kall = dscr("Kall", [8, GRP * 96, SQ], BF16)
    Vd = dscr("Vd", [8 * 128, (SMAXL // 128) * 65], BF16)
    Vall = dscr("Vall", [8, GRP * 128, (SQ // 128) * 65], BF16)
    mixm = dscr("mixm", [8, 64, SMAXL], BF16)
    Ud = dscr("Ud", [SMAXL, 1024], BF16)
    CC_MAX = cfg.cc_max
    RCU = min(SQ, max(128, (CC_MAX // (GRP * 2048)) // 128 * 128))
    NUC = SQ // RCU
    Uall = dscr("Uall", [NUC, GRP * RCU, 1024], BF16)
    mixo = dscr("mixo", [1024, SMAXL], BF16)

    w13b = dscr("w13b", [NF, NFC, 128, KC * 256], BF16)
    w2b = dscr("w2b", [NF, KC, 128, NFC * 128], BF16)
    modv = dscr("modv", [3, 128, L * 3 * 3 * KC], F32)

    K = KB(nc)
    es = K.es
    pe, act, dve, pool, sp = K.pe, K.act, K.dve, K.pool, K.sp

    uid = [0]

    def sb(name, shape, dt, stack=es):
        uid[0] += 1
        return stack.enter_context(nc.sbuf_tensor("%s_u%d" % (name, uid[0]), list(shape), dt))

    psb = [es.enter_context(nc.psum_tensor("ps%d" % i, [128, 512], F32)) for i in range(8)]
    PB = [Buf("ps%d" % i) for i in range(8)]

    SMAX = max(cfg.seg_tokens)
    xT = sb("xT", [128, KC, SMAX], F32)
    XB = [Buf("x%d" % i) for i in range(SMAX // 512)]
    cst = sb("cst", [128, 256], F32)
    onesb = sb("onesb", [128, 128], BF16)
    mv = sb("mv", [128, L * 3 * 3 * KC], F32)
    Bcst, Bones, Bmv = Buf("cst"), Buf("ones"), Buf("mv")
    ident = cst[:, 0:128]

    K.dma(sp, cst[:], consts, writes=[Bcst])
    K.copy(dve, onesb[:], cst[:, 128:256], [Bcst], [Bones])

    WB13 = [Buf("w13b%d" % f) for f in range(NF)]
    WB2 = [Buf("w2b%d" % f) for f in range(NF)]
    late_conv = []
    for f in range(NF):
        def cv(f=f):
            K.dma(pool, w13b[f], w13[f], writes=[WB13[f]], max_dma_last_dim=4096)
            K.dma(pool, w2b[f], w2[f], writes=[WB2[f]], max_dma_last_dim=4096)
        if f == 0:
            cv()
        else:
            late_conv.append(cv)

    Bodw = Buf("odw")
    if NOD:
        def cvo():
            for src, dst in ((od_win, od_win_b), (od_wout, od_wout_b), (sgu_wsT, sgu_wsT_b), (sgu_b, sgu_b_b)):
                K.dma(pool, dst, src, writes=[Bodw], max_dma_last_dim=4096)
        late_conv.append(cvo)
    Bevw = Buf("evw")
    if NEV:
        for src, dst in ((ev_win1, ev_win1_b), (ev_win2, ev_win2_b), (ev_woutg, ev_woutg_b), (ev_woutm, ev_woutm_b),
                         (gla_wal, gla_wal_b), (mla_wqb, mla_wqb_b), (mla_wkvb, mla_wkvb_b)):
            K.dma(pool, dst, src, writes=[Bevw], max_dma_last_dim=4096)
    icst = sb("icst", [128, 8], I32)
    Bic = Buf("icst")
    K.dma(sp, icst[:], iconst, writes=[Bic])

    Bmodv = Buf("modv")
    with ExitStack() as ps:
        ccT = sb("ccT", [128, KC, 4], F32, ps)
        adab = sb("adab", [128, L * 72], F32, ps)
        gpre = sb("gpre", [128, L * 3 * KC], F32, ps)
        gpost = sb("gpost", [128, L * 3 * KC], F32, ps)
        mfm = sb("mfm", [128, L * 72, 4], F32, ps)
        mvall = sb("mvall", [128, 3, L * 3 * 3 * KC], F32, ps)
        NAW = 4
        awt = [sb("awt%d" % i, [128, KC, 128], F32, ps) for i in range(NAW)]
        Bcc, Badab, Bgpre, Bgpost, Bmfm, Bmvall = (Buf(n) for n in ("cc", "adab", "gpre", "gpost", "mfm", "mvall"))
        Bawt = [Buf("awt%d" % i) for i in range(NAW)]
        K.dma(sp, ccT[:], c3, writes=[Bcc])
        K.dma(sp, adab[:], ada_b, writes=[Badab])
        K.dma(sp, gpre[:], npre, writes=[Bgpre])
        K.dma(sp, gpost[:], npost, writes=[Bgpost])
        K.actf(ccT[:], ccT[:], ACT.Silu, [Bcc], [Bcc])
        for t in range(L * 72):
            wt, Bw = awt[t % NAW], Bawt[t % NAW]
            K.dma(sp, wt[:], ada_w[t], writes=[Bw])
            bank = 7 - (t % 2)
            for kc in range(KC):
                K.mm(psb[bank][:, 0:4], wt[:, kc, :], ccT[:, kc, :], kc == 0, kc == KC - 1, [Bw, Bcc], [PB[bank]])
            K.ts(dve, mfm[:, t, :], psb[bank][:, 0:4], adab[:, t:t + 1], None, ALU.add, None,
                 [PB[bank], Badab], [Bmfm])
        gp4 = gpost[:].rearrange("p (l j c) -> p l j c", l=L, j=3)
        for j in (0, 2):
            K.ts(dve, gp4[:, :, j, :], gp4[:, :, j, :], 0.5, None, ALU.mult, None, [Bgpost], [Bgpost])
        mf5 = mfm[:].rearrange("p (l j t c) b -> p l j t c b", l=L, j=3, t=3)
        mv5 = mvall[:].rearrange("p b (l j v c) -> p b l j v c", l=L, j=3, v=3)
        gpr4 = gpre[:].rearrange("p (l j c) -> p l j c", l=L, j=3)
        for b in range(3):
            for l in range(L):
                for j in range(3):
                    K.stt(mv5[:, b, l, j, 0, :], mf5[:, l, j, 1, :, b], 1.0, gpr4[:, l, j, :], ALU.add, ALU.mult,
                          [Bmfm, Bgpre], [Bmvall])
                    K.copy(dve, mv5[:, b, l, j, 1, :], mf5[:, l, j, 0, :, b], [Bmfm], [Bmvall])
                    K.stt(mv5[:, b, l, j, 2, :], mf5[:, l, j, 2, :, b], 1.0, gp4[:, l, j, :], ALU.add, ALU.mult,
                          [Bmfm, Bgpost], [Bmvall])
        K.dma(sp, modv.rearrange("b p n -> p b n"), mvall[:], reads=[Bmvall], writes=[Bmodv])
        K.barrier()
    for cv in late_conv:
        cv()

    def vec(l, j, v):
        o = ((l * 3 + j) * 3 + v) * KC
        return mv[:, o:o + KC]

    def load_segment(tok0, S, stack):
        xtok = [sb("xtok%d" % i, [128, D], F32, stack) for i in range(2)]
        Bxt = [Buf("xtok%d" % i) for i in range(2)]
        for i in range(S // 128):
            xt_, Bx = xtok[i % 2], Bxt[i % 2]
            K.dma(sp, xt_[:], xin[tok0 + i * 128: tok0 + (i + 1) * 128, :], writes=[Bx])
            for hh in range(2):
                bank = (2 * i + hh) % 4
                for q in range(4):
                    kc = hh * 4 + q
                    K.op(pe, lambda e, kc=kc, q=q, bank=bank: e.transpose(psb[bank][:, q * 128:(q + 1) * 128],
                                                                           xt_[:, kc * 128:(kc + 1) * 128], ident),
                         [Bx, Bcst], [PB[bank]])
                dst = xT[:, hh * 4:(hh + 1) * 4, i * 128:(i + 1) * 128]
                src = psb[bank][:, :].rearrange("p (q t) -> p q t", q=4)
                K.copy(act if hh == 0 else dve, dst, src, [PB[bank]], [XB[i // 4]])

    def store_segment(tok0, S, stack):
        yt = [sb("ytok%d" % i, [128, D], F32, stack) for i in range(2)]
        Byt = [Buf("ytok%d" % i) for i in range(2)]
        for i in range(S // 128):
            y_, By = yt[i % 2], Byt[i % 2]
            for hh in range(2):
                bank = (2 * i + hh) % 4
                for q in range(4):
                    kc = hh * 4 + q
                    K.op(pe, lambda e, kc=kc, q=q, bank=bank: e.transpose(psb[bank][:, q * 128:(q + 1) * 128],
                                                                           xT[:, kc, i * 128:(i + 1) * 128], ident),
                         [XB[i // 4], Bcst], [PB[bank]])
                K.copy(act if hh == 0 else dve, y_[:, hh * 512:(hh + 1) * 512], psb[bank][:, :], [PB[bank]], [By])
            K.dma(sp, yout[tok0 + i * 128: tok0 + (i + 1) * 128, :], y_[:], reads=[By])

    class FfnBufs:
        pass

    def ffn_alloc(stack):
        fb = FfnBufs()
        fb.h = sb("f_h", [128, KC, 512], BF16, stack)
        fb.g = sb("f_g", [128, NFC, 512], BF16, stack)
        fb.y = sb("f_y", [128, KC, 512], F32, stack)
        fb.s = sb("f_s", [128, 512], F32, stack)
        fb.w13 = [sb("f_w13_%d" % i, [128, KC, 256], BF16, stack) for i in range(3)]
        fb.w2 = [sb("f_w2_%d" % i, [128, 11, 128], BF16, stack) for i in range(3)]
        fb.rstd = [sb("f_rstd%d" % i, [128, 512], F32, stack) for i in range(2)]
        fb.sq = [sb("f_sq%d" % i, [128, 512], BF16, stack) for i in range(1)]
        fb.tmp = [sb("f_tmp%d" % i, [128, 512], F32, stack) for i in range(1)]
        fb.Bh, fb.By, fb.Bs = Buf("h"), Buf("y"), Buf("s")
        fb.Bg = [Buf("g%d" % i) for i in range(NFC)]
        fb.Bw13 = [Buf() for _ in range(3)]
        fb.Bw2 = [Buf() for _ in range(3)]
        fb.Brstd = [Buf(), Buf()]
        fb.Bsq = [Buf(), Buf()]
        fb.Btmp = [Buf(), Buf()]
        fb.n13 = 0
        fb.n2 = 0
        fb.nsq = 0
        fb.ntmp = 0
        return fb

    def rstd_from_ss(fb, ri, bank):
        K.actf(fb.rstd[ri][:], psb[bank][:, :], ACT.Sqrt, [PB[bank]], [fb.Brstd[ri]], scale=1.0 / D, bias=EPS)
        K.op(dve, lambda e: e.reciprocal(fb.rstd[ri][:], fb.rstd[ri][:]), [fb.Brstd[ri]], [fb.Brstd[ri]])

    def prenorm_steps(fb, l, j, tt, hdst, Bh):
        tsl = slice(tt * 512, (tt + 1) * 512)
        A, Bv = vec(l, j, 0), vec(l, j, 1)
        steps = []

        def p0():
            for kc in range(KC):
                i = fb.nsq % len(fb.sq)
                fb.nsq += 1
                K.actf(fb.sq[i][:], xT[:, kc, tsl], ACT.Square, [XB[tt]], [fb.Bsq[i]])
                K.mm(psb[6][:, :], onesb[:], fb.sq[i][:], kc == 0, kc == KC - 1, [Bones, fb.Bsq[i]], [PB[6]])
        steps.append(p0)
        steps.append(lambda: rstd_from_ss(fb, 0, 6))
        for kc in range(KC):
            def pk(kc=kc):
                i = fb.ntmp % len(fb.tmp)
                fb.ntmp += 1
                K.tt(dve, fb.tmp[i][:], xT[:, kc, tsl], fb.rstd[0][:], ALU.mult, [XB[tt], fb.Brstd[0]], [fb.Btmp[i]])
                K.actf(hdst[:, kc, :], fb.tmp[i][:], ACT.Identity, [fb.Btmp[i], Bmv], [Bh],
                       scale=A[:, kc:kc + 1], bias=Bv[:, kc:kc + 1])
            steps.append(pk)
        return steps

    def yphase(fb, tt, Cg, mm_oc, nxt, pending_tail):
        tsl = slice(tt * 512, (tt + 1) * 512)
        prev_sq = None
        for oc in range(KC):
            bank = 4 + oc % 2
            mm_oc(oc, bank)
            if prev_sq is not None:
                po, pi = prev_sq
                K.mm(psb[7][:, :], onesb[:], fb.sq[pi][:], po == 0, False, [Bones, fb.Bsq[pi]], [PB[7]])
            K.copy(act, fb.y[:, oc, :], psb[bank][:, :], [PB[bank]], [fb.By])
            i = fb.nsq % len(fb.sq)
            fb.nsq += 1
            K.actf(fb.sq[i][:], psb[bank][:, :], ACT.Square, [PB[bank]], [fb.Bsq[i]])
            prev_sq = (oc, i)
            if nxt:
                nxt.pop(0)()
        po, pi = prev_sq
        K.mm(psb[7][:, :], onesb[:], fb.sq[pi][:], False, True, [Bones, fb.Bsq[pi]], [PB[7]])
        while nxt:
            nxt.pop(0)()
        pending_tail.append(lambda: rstd_from_ss(fb, 1, 7))
        for oc in range(KC):
            def tl(oc=oc):
                i = fb.ntmp % len(fb.tmp)
                fb.ntmp += 1
                K.tt(dve, fb.tmp[i][:], fb.y[:, oc, :], fb.rstd[1][:], ALU.mult, [fb.By, fb.Brstd[1]],
                     [fb.Btmp[i]])
                K.stt(xT[:, oc, tsl], fb.tmp[i][:], Cg[:, oc:oc + 1], xT[:, oc, tsl], ALU.mult, ALU.add,
                      [fb.Btmp[i], Bmv, XB[tt]], [XB[tt]])
            pending_tail.append(tl)

    def ffn_sublayer(fb, l, j, S):
        f = l * 2 + (0 if j == 0 else 1)
        ntile = S // 512
        Cg = vec(l, j, 2)
        pending_tail = []
        for st in prenorm_steps(fb, l, j, 0, fb.h, fb.Bh):
            st()
        for tt in range(ntile):
            tsl = slice(tt * 512, (tt + 1) * 512)
            for fc in range(NFC):
                r = fb.n13 % 3
                fb.n13 += 1
                K.dma(sp, fb.w13[r][:], w13b[f, fc].rearrange("p (k n) -> p k n", k=KC), reads=[WB13[f]],
                      writes=[fb.Bw13[r]])
                ba, bb = fc % 2, 2 + fc % 2
                for half, bank in ((0, ba), (1, bb)):
                    for kc in range(KC):
                        K.mm(psb[bank][:, :], fb.w13[r][:, kc, half * 128:(half + 1) * 128], fb.h[:, kc, :],
                             kc == 0, kc == KC - 1, [fb.Bw13[r], fb.Bh], [PB[bank]])
                K.actf(fb.s[:], psb[ba][:, :], ACT.Silu, [PB[ba]], [fb.Bs])
                K.tt(dve, fb.g[:, fc, :], fb.s[:], psb[bb][:, :], ALU.mult, [fb.Bs, PB[bb]], [fb.Bg[fc]])
                if pending_tail:
                    pending_tail.pop(0)()
            while pending_tail:
                pending_tail.pop(0)()
            nxt = prenorm_steps(fb, l, j, tt + 1, fb.h, fb.Bh) if tt + 1 < ntile else []
            if nxt:
                nxt.pop(0)()
            def mm_oc(oc, bank, f=f):
                for hf in range(2):
                    r = fb.n2 % 3
                    fb.n2 += 1
                    K.dma(sp, fb.w2[r][:],
                          w2b[f, oc].rearrange("p (k n) -> p k n", k=NFC)[:, hf * 11:(hf + 1) * 11, :],
                          reads=[WB2[f]], writes=[fb.Bw2[r]])
                    for q in range(11):
                        fc = hf * 11 + q
                        K.mm(psb[bank][:, :], fb.w2[r][:, q, :], fb.g[:, fc, :], fc == 0, fc == NFC - 1,
                             [fb.Bw2[r], fb.Bg[fc]], [PB[bank]])
            yphase(fb, tt, Cg, mm_oc, nxt, pending_tail)
        while pending_tail:
            pending_tail.pop(0)()


    def mx_alloc(stack, with_h=True, with_y=False, nrstd=2):
        fb = FfnBufs()
        if with_h:
            fb.h = sb("m_h", [128, KC, 512], BF16, stack)
        if with_y:
            fb.y = sb("m_y", [128, KC, 512], F32, stack)
        fb.rstd = [sb("m_rstd%d" % i, [128, 512], F32, stack) for i in range(nrstd)]
        fb.sq = [sb("m_sq%d" % i, [128, 512], BF16, stack) for i in range(2)]
        fb.tmp = [sb("m_tmp%d" % i, [128, 512], F32, stack) for i in range(2)]
        fb.Bh, fb.By = Buf("h"), Buf("y")
        fb.Brstd = [Buf(), Buf()]
        fb.Bsq = [Buf(), Buf()]
        fb.Btmp = [Buf(), Buf()]
        fb.nsq = 0
        fb.ntmp = 0
        return fb

    class Gen:
        pass

    def gen_alloc(stack, mask_col, use_pjx, nA=2, blocks=True):
        G = Gen()
        G.PJ = sb("g_pj", [128, 512], I32, stack)
        G.A = [sb("g_a%d" % i, [128, 512], I32, stack) for i in range(nA)]
        G.BPJ = Buf("pj")
        G.BA = [Buf() for _ in range(nA)]
        G.n = 0
        G.mask = icst[:, mask_col:mask_col + 1]
        K.op(pool, lambda e: e.iota(G.PJ[:], [[1, 512]], base=0, channel_multiplier=0), [], [G.BPJ])
        K.op(pool, lambda e: e.iota(G.A[0][:], [[0, 512]], base=0, channel_multiplier=1), [], [G.BA[0]])
        if blocks:
            G.Jf = sb("g_jf", [128, 512], I32, stack)
            G.BJf = Buf()
            K.copy(dve, G.Jf[:], G.PJ[:], [G.BPJ], [G.BJf])
        K.op(pool, lambda e: e.tensor_tensor(G.PJ[:], G.PJ[:], G.A[0][:], ALU.mult), [G.BPJ, G.BA[0]], [G.BPJ])
        if blocks:
            G.Pf = sb("g_pf", [128, 512], I32, stack)
            G.PJ0 = sb("g_pj0", [128, 512], I32, stack)
            G.BPf, G.BPJ0 = Buf(), Buf()
            K.copy(dve, G.Pf[:], G.A[0][:], [G.BA[0]], [G.BPf])
        if use_pjx:
            K.op(pool, lambda e: e.iota(G.A[0][:], [[0, 512]], base=0, channel_multiplier=0), [], [G.BA[0]])
            K.op(pool, lambda e: e.tensor_scalar(G.A[0][:], G.A[0][:], icst[:, 4:5], None, ALU.add),
                 [G.BA[0], Bic], [G.BA[0]])
            K.op(pool, lambda e: e.tensor_tensor(G.PJ[:], G.PJ[:], G.A[0][:], ALU.add), [G.BPJ, G.BA[0]], [G.BPJ])
        if blocks:
            K.copy(dve, G.PJ0[:], G.PJ[:], [G.BPJ], [G.BPJ0])
        return G

    def gen_block(G, S_tot, sp0):
        K.op(pool, lambda e: e.tensor_scalar(G.PJ[:], G.Pf[:], int(sp0 % S_tot), None, ALU.mult), [G.BPf], [G.BPJ])
        K.op(pool, lambda e: e.tensor_tensor(G.PJ[:], G.PJ[:], G.PJ0[:], ALU.add), [G.BPJ, G.BPJ0], [G.BPJ])

    def gen_tile(G, dst, Bdst, S_tot, s0, sp0, off):
        base = (s0 * sp0 + off) % S_tot
        step = s0 % S_tot
        i = G.n % len(G.A)
        G.n += 1
        A = G.A[i]
        if step == 0:
            K.op(pool, lambda e: e.iota(A[:], [[0, 512]], base=base, channel_multiplier=0), [], [G.BA[i]])
        else:
            K.op(pool, lambda e: e.tensor_scalar(A[:], G.Jf[:], int(step), int(base), ALU.mult, ALU.add), [G.BJf],
                 [G.BA[i]])
        K.op(pool, lambda e: e.tensor_tensor(A[:], A[:], G.PJ[:], ALU.add), [G.BA[i], G.BPJ], [G.BA[i]])
        K.op(dve, lambda e: e.tensor_scalar(A[:], A[:], G.mask, None, ALU.bitwise_and), [G.BA[i], Bic], [G.BA[i]])
        K.actf(dst, A[:], ACT.Sin, [G.BA[i]], [Bdst], scale=2.0 * np.pi / S_tot, bias=-np.pi)

    BUd, BUall, Bmixo = Buf("Ud"), Buf("Uall"), Buf("mixo")

    def odd_phase_a(l, S, stack):
        i_od = l // 2
        fb = mx_alloc(stack, nrstd=1)
        win = sb("o_win", [128, KC, 1536], BF16, stack)
        wsT = sb("o_wsT", [128, 512], BF16, stack)
        sgb = sb("o_sgb", [1, 512], BF16, stack)
        nrm = sb("o_nrm", [128, 512], F32, stack)
        ccsc = sb("o_ccsc", [128, 256], BF16, stack)
        gtmp = sb("o_gtmp", [128, 512], BF16, stack)
        zcT = sb("o_zcT", [128, 4, 512], BF16, stack)
        usb = [sb("o_usb%d" % i, [128, 1024], BF16, stack) for i in range(2)]
        uT = sb("o_uT", [128, 4, 512], F32, stack)
        gv = sb("o_gv", [128, 512], F32, stack)
        vtok = [sb("o_vtok%d" % i, [128, 512], BF16, stack) for i in range(2)]
        odT = sb("o_odT", [128, 4, 512], BF16, stack)
        ssq = sb("o_ssq", [128, 2], F32, stack)
        Bwin, BwsT, Bsgb, Bnrm, Bccsc, Bgtmp, BzcT, BuT, Bgv, BodT, Bssq = (Buf() for _ in range(11))
        Busb = [Buf(), Buf()]
        Bvtok = [Buf(), Buf()]
        K.dma(sp, win[:], od_win_b[i_od].rearrange("p (k n) -> p k n", k=KC), reads=[Bodw], writes=[Bwin])
        K.dma(sp, wsT[:], sgu_wsT_b[i_od], reads=[Bodw], writes=[BwsT])
        K.dma(sp, sgb[:], sgu_b_b[i_od], reads=[Bodw], writes=[Bsgb])
        K.dma(sp, nrm[:], sgu_nrm[i_od], writes=[Bnrm])
        G = gen_alloc(stack, 2, False, nA=1, blocks=False)
        gen_tile(G, gtmp[:], Bgtmp, 128, 0, 0, 96)
        K.copy(dve, ccsc[:, 0:128], gtmp[:, 0:128], [Bgtmp], [Bccsc])
        gen_tile(G, gtmp[:], Bgtmp, 128, 0, 0, 0)
        K.copy(dve, ccsc[:, 128:256], gtmp[:, 0:128], [Bgtmp], [Bccsc])
        mixo_v = mixo.rearrange("(c p) s -> p c s", p=128)
        nb = [0]

        def bank2():
            nb[0] += 1
            return nb[0] % 2

        for tt in range(S // 512):
            for st in prenorm_steps(fb, l, 1, tt, fb.h, fb.Bh):
                st()
            for g in range(4):
                bank = bank2()
                for kc in range(KC):
                    K.mm(psb[bank][:, :], win[:, kc, g * 128:(g + 1) * 128], fb.h[:, kc, :], kc == 0, kc == KC - 1,
                         [Bwin, fb.Bh], [PB[bank]])
                K.copy(act if g % 2 == 0 else dve, zcT[:, g, :], psb[bank][:, :], [PB[bank]], [BzcT])
            for g in range(4):
                bank = bank2()
                for kc in range(KC):
                    K.mm(psb[bank][:, :], win[:, kc, 512 + g * 128:512 + (g + 1) * 128], fb.h[:, kc, :], kc == 0,
                         kc == KC - 1, [Bwin, fb.Bh], [PB[bank]])
                K.actf(uT[:, g, :], psb[bank][:, :], ACT.Gelu, [PB[bank]], [BuT])
            for ts in range(4):
                tk = slice(ts * 128, (ts + 1) * 128)
                ub, Bub = usb[ts % 2], Busb[ts % 2]
                for gp in range(2):
                    bank = 2 + gp
                    for gg in range(2):
                        g = gp * 2 + gg
                        K.mm(psb[bank][:, gg * 256:(gg + 1) * 256], zcT[:, g, tk], ccsc[:], True, True,
                             [BzcT, Bccsc], [PB[bank]])
                    K.copy(act if gp == 0 else dve, ub[:, gp * 512:(gp + 1) * 512], psb[bank][:, :], [PB[bank]],
                           [Bub])
                K.dma(sp, Ud[tt * 512 + ts * 128: tt * 512 + (ts + 1) * 128, :], ub[:], reads=[Bub], writes=[BUd])
                bank = bank2()
                for kc in range(KC):
                    K.mm(psb[bank][:, :], fb.h[:, kc, tk], win[:, kc, 1024:1536], kc == 0, kc == KC - 1,
                         [Bwin, fb.Bh], [PB[bank]])
                K.actf(gv[:], psb[bank][:, :], ACT.Gelu, [PB[bank]], [Bgv])
                vt, Bvt = vtok[ts % 2], Bvtok[ts % 2]
                K.actf(vt[:], gv[:], ACT.Square, [Bgv], [Bvt, Bssq], accum_out=ssq[:, 0:1])
                K.actf(ssq[:, 1:2], ssq[:, 0:1], ACT.Sqrt, [Bssq], [Bssq], scale=1.0 / 512, bias=EPS)
                K.op(dve, lambda e: e.reciprocal(ssq[:, 1:2], ssq[:, 1:2]), [Bssq], [Bssq])
                K.stt(vt[:], gv[:], ssq[:, 1:2], nrm[:], ALU.mult, ALU.mult, [Bgv, Bssq, Bnrm], [Bvt])
                bank = 6
                for hd in range(4):
                    hs = slice(hd * 128, (hd + 1) * 128)
                    K.mm(psb[bank][:, hs], vt[:, hs], wsT[:, hs], True, False, [Bvt, BwsT], [PB[bank]])
                    K.mm(psb[bank][:, hs], onesb[0:1, :], sgb[0:1, hs], False, True, [Bones, Bsgb], [PB[bank]])
                K.tt(dve, odT[:, :, tk], uT[:, :, tk], psb[bank][:, :].rearrange("p (h i) -> p h i", h=4), ALU.mult,
                     [BuT, PB[bank]], [BodT])
            K.dma(sp, mixo_v[:, 4:8, tt * 512:(tt + 1) * 512], odT[:], reads=[BodT], writes=[Bmixo])

    def odd_phase_b(S, S_keys, Usrc, BUsrc, is_sample, stack):
        G = gen_alloc(stack, 1 if is_sample else 0, is_sample)
        ct = [sb("b_ct%d" % i, [128, 512], BF16, stack) for i in range(2)]
        stl = [sb("b_st%d" % i, [128, 512], BF16, stack) for i in range(2)]
        ut = [sb("b_ut%d" % i, [128, 1024], BF16, stack) for i in range(3)]
        fcs = sb("b_fcs", [128, 4, 512], BF16, stack)
        Rc = [sb("b_rc%d" % i, [128, 512], I16, stack) for i in range(2)]
        Rs = [sb("b_rs%d" % i, [128, 512], I16, stack) for i in range(2)]
        R32 = sb("b_r32", [128, 512], I32, stack)
        Di = sb("b_di", [128, 512], I32, stack)
        D16 = sb("b_d16", [128, 512], I16, stack)
        m16 = sb("b_m16", [128, 1], I16, stack)
        Bct, Bst = [Buf(), Buf()], [Buf(), Buf()]
        BRc, BRs = [Buf(), Buf()], [Buf(), Buf()]
        But = [Buf(), Buf(), Buf()]
        Bfcs, BDi, BD16, BR32, Bm16 = Buf(), Buf(), Buf(), Buf(), Buf()
        mixo_v = mixo.rearrange("(c p) s -> p c s", p=128)
        scale = 1.0 / float(np.sqrt(S_keys * 128.0))
        na = S_keys // 128
        sc_sin = 2.0 * np.pi / S_keys
        n = 0
        for bq in range(S // 512):
            sp0 = bq * 512
            gen_block(G, S_keys, sp0)
            K.op(pool, lambda e: e.tensor_scalar(Di[:], G.Jf[:], 128, int((128 * sp0) % S_keys), ALU.mult, ALU.add),
                 [G.BJf], [BDi])
            K.op(dve, lambda e: e.tensor_scalar(Di[:], Di[:], G.mask, None, ALU.bitwise_and), [BDi, Bic], [BDi])
            K.copy(dve, D16[:], Di[:], [BDi], [BD16])
            for R0, BR0, off in ((Rc[0], BRc[0], (3 * S_keys) // 4), (Rs[0], BRs[0], S_keys // 2)):
                K.op(pool, lambda e, off=off: e.tensor_scalar(R32[:], G.PJ[:], int(off), None, ALU.add), [G.BPJ],
                     [BR32])
                K.op(dve, lambda e: e.tensor_scalar(R32[:], R32[:], G.mask, None, ALU.bitwise_and), [BR32, Bic],
                     [BR32])
                K.copy(dve, R0[:], R32[:], [BR32], [BR0])
            for a in range(na):
                s0 = a * 128
                i2, i3 = n % 2, n % 3
                n += 1
                cur, nxt = a % 2, (a + 1) % 2
                K.actf(ct[i2][:], Rc[cur][:], ACT.Sin, [BRc[cur]], [Bct[i2]], scale=sc_sin, bias=-np.pi)
                K.actf(stl[i2][:], Rs[cur][:], ACT.Sin, [BRs[cur]], [Bst[i2]], scale=sc_sin, bias=-np.pi)
                if a + 1 < na:
                    for R_, BR_ in ((Rc, BRc), (Rs, BRs)):
                        K.tt(dve, R_[nxt][:], R_[cur][:], D16[:], ALU.add, [BR_[cur], BD16], [BR_[nxt]])
                        K.op(dve, lambda e, R_=R_, nxt=nxt: e.tensor_scalar(R_[nxt][:], R_[nxt][:], G.mask, None,
                                                                            ALU.bitwise_and), [BR_[nxt], Bic],
                             [BR_[nxt]])
                K.dma(sp, ut[i3][:], Usrc(s0), reads=[BUsrc], writes=[But[i3]])
                for g in range(4):
                    K.mm(psb[g][:, :], ut[i3][:, g * 256:g * 256 + 128], ct[i2][:], a == 0, False,
                         [But[i3], Bct[i2]], [PB[g]])
                    K.mm(psb[g][:, :], ut[i3][:, g * 256 + 128:g * 256 + 256], stl[i2][:], False, a == na - 1,
                         [But[i3], Bst[i2]], [PB[g]])
            for g in range(4):
                if g % 2 == 0:
                    K.actf(fcs[:, g, :], psb[g][:, :], ACT.Copy, [PB[g]], [Bfcs], scale=scale)
                else:
                    K.ts(dve, fcs[:, g, :], psb[g][:, :], scale, None, ALU.mult, None, [PB[g]], [Bfcs])
            K.dma(sp, mixo_v[:, 0:4, bq * 512:(bq + 1) * 512], fcs[:], reads=[Bfcs], writes=[Bmixo])

    def mixer_phase_c(l, S, wout_dram, Bw_dram, stack):
        fb = mx_alloc(stack, with_h=False, with_y=True)
        wo = sb("c_wo", [128, KC, 1024], BF16, stack)
        ot = [sb("c_ot%d" % i, [128, KC, 512], BF16, stack) for i in range(2)]
        Bwo = Buf()
        Bot = [Buf(), Buf()]
        K.dma(sp, wo[:], wout_dram.rearrange("p (k n) -> p k n", k=KC), reads=[Bw_dram], writes=[Bwo])
        mixo_v = mixo.rearrange("(c p) s -> p c s", p=128)
        Cg = vec(l, 1, 2)
        pending = []
        for tt in range(S // 512):
            o_, Bo = ot[tt % 2], Bot[tt % 2]
            K.dma(sp, o_[:], mixo_v[:, :, tt * 512:(tt + 1) * 512], reads=[Bmixo], writes=[Bo])

            def mm_oc(oc, bank, o_=o_, Bo=Bo):
                for ic in range(KC):
                    K.mm(psb[bank][:, :], wo[:, ic, oc * 128:(oc + 1) * 128], o_[:, ic, :], ic == 0, ic == KC - 1,
                         [Bwo, Bo], [PB[bank]])
            yphase(fb, tt, Cg, mm_oc, [], pending)
            while pending:
                pending.pop(0)()

    def odd_mixer(l, S, is_sample):
        i_od = l // 2
        with ExitStack() as st:
            odd_phase_a(l, S, st)
            K.barrier()
        if is_sample and GRP > 1 and not cfg.no_xg:
            for ci in range(NUC):
                K.op(pool, lambda e, ci=ci: e.collective_compute(
                    "AllGather", ALU.bypass, replica_groups=cfg.replica_groups,
                    ins=[Ud[ci * RCU:(ci + 1) * RCU, :]], outs=[Uall[ci]]), [BUd], [BUall])
            K.barrier()

            def usrc(s0):
                g, i = s0 // S, s0 % S
                ci, w = i // RCU, i % RCU
                return Uall[ci, g * RCU + w:g * RCU + w + 128, :]
            Usrc, BUsrc, S_keys = usrc, BUall, GRP * S
        else:
            Usrc, BUsrc, S_keys = (lambda s0: Ud[s0:s0 + 128, :]), BUd, S
        with ExitStack() as st:
            odd_phase_b(S, S_keys, Usrc, BUsrc, is_sample and GRP > 1 and not cfg.no_xg, st)
            K.barrier()
        with ExitStack() as st:
            mixer_phase_c(l, S, od_wout_b[i_od], Bodw, st)
            K.barrier()

    fcst = sb("fcst", [128, 64], F32)
    Bfc = Buf("fcst")
    K.dma(sp, fcst[:], fconst, writes=[Bfc])
    NCHL = SMAXL // 64
    decs = sb("decs", [128, 2, 2, NCHL], F32)
    Bdecs = Buf("decs")
    Bgq, Bgvt, Bgkv, Bgs, Bgg, Bgsum, Bgsall = (Buf() for _ in range(7))
    BQd, BKd, BKall, BVd, BVall, Bmixm = (Buf() for _ in range(6))
    rr = [0]

    def rbank(lo=0, n=2):
        rr[0] += 1
        return lo + rr[0] % n

    def even_a1(l, S, stack):
        i_ev = l // 2
        fb = mx_alloc(stack)
        win = sb("a_win", [128, KC, 1056], BF16, stack)
        wal = sb("a_wal", [33, 512], BF16, stack)
        tc = sb("a_tc", [128, 516], F32, stack)
        alr = sb("a_alr", [33, 512], BF16, stack)
        qk = sb("a_qk", [128, 4, 512], F32, stack)
        spt = sb("a_spt", [128, 512], F32, stack)
        E = sb("a_E", [128, 4, 2, 128], F32, stack)
        ekd = sb("a_ekd", [128, 512], F32, stack)
        kd = sb("a_kd", [128, 512], BF16, stack)
        vtok = [sb("a_vtok%d" % i, [128, 512], BF16, stack) for i in range(2)]
        kvst = sb("a_kvst", [128, 2, 4, 128], F32, stack)
        qst = sb("a_qst", [128, 4, 2, 512], BF16, stack)
        Bwin, Bwal, Btc, Balr, Bqk, Bspt, BE, Bekd, Bkd, Bkvst, Bqst = (Buf() for _ in range(11))
        Bvtok = [Buf(), Buf()]
        K.dma(sp, win[:], ev_win1_b[i_ev].rearrange("p (k n) -> p k n", k=KC), reads=[Bevw], writes=[Bwin])
        K.dma(sp, wal[:], gla_wal_b[i_ev], reads=[Bevw], writes=[Bwal])
        K.dma(sp, tc[:], tconst, writes=[Btc])
        K.op(dve, lambda e: e.memset(alr[32:33, :], 1.0), [], [Balr])
        gq_v = gq.rearrange("k r p s -> p k r s")
        for tt in range(S // 512):
            for st in prenorm_steps(fb, l, 1, tt, fb.h, fb.Bh):
                st()
            for c4 in range(4):
                bank = rbank()
                for kc in range(KC):
                    K.mm(psb[bank][:, :], win[:, kc, c4 * 128:(c4 + 1) * 128], fb.h[:, kc, :], kc == 0, kc == KC - 1,
                         [Bwin, fb.Bh], [PB[bank]])
                K.copy(act if c4 % 2 == 0 else dve, qk[:, c4, :], psb[bank][:, :], [PB[bank]], [Bqk])
            bank = rbank()
            for kc in range(KC):
                K.mm(psb[bank][0:32, :], win[:, kc, 1024:1056], fb.h[:, kc, :], kc == 0, kc == KC - 1,
                     [Bwin, fb.Bh], [PB[bank]])
            K.copy(act, alr[0:32, :], psb[bank][0:32, :], [PB[bank]], [Balr])
            for ts in range(4):
                tk = slice(ts * 128, (ts + 1) * 128)
                n = tt * 4 + ts
                vt, Bvt = vtok[n % 2], Bvtok[n % 2]
                for kc in range(KC):
                    K.mm(psb[2][:, 0:256], fb.h[:, kc, tk], win[:, kc, 256:512], kc == 0, kc == KC - 1,
                         [Bwin, fb.Bh], [PB[2]])
                for kc in range(KC):
                    K.mm(psb[3][:, :], fb.h[:, kc, tk], win[:, kc, 512:1024], kc == 0, kc == KC - 1,
                         [Bwin, fb.Bh], [PB[3]])
                K.copy(act, vt[:], psb[3][:, :], [PB[3]], [Bvt])
                K.dma(sp, gvt[n], vt[:], reads=[Bvt], writes=[Bgvt])
                K.mm(psb[4][:, :], alr[0:33, tk], wal[0:33, :], True, True, [Balr, Bwal], [PB[4]])
                K.actf(spt[:], psb[4][:, :], ACT.Exp, [PB[4]], [Bspt], scale=-1.0)
                K.actf(spt[:], spt[:], ACT.Ln, [Bspt], [Bspt], bias=1.0)
                for pr in range(2):
                    K.mm(psb[5][:, pr * 130:pr * 130 + 130], spt[:, pr * 128:(pr + 1) * 128], tc[:, 0:130], True, True,
                         [Bspt, Btc], [PB[5]])
                for pr in range(2):
                    K.mm(psb[6 + pr][:, 0:258], spt[:, 256 + pr * 128:256 + (pr + 1) * 128], tc[:, 130:388], True,
                         True, [Bspt, Btc], [PB[6 + pr]])
                K.mm(psb[4][:, 0:256], tc[:, 130:258], spt[:, 0:256], True, True, [Bspt, Btc], [PB[4]])
                K.mm(psb[4][:, 256:512], tc[:, 388:516], spt[:, 256:512], True, True, [Bspt, Btc], [PB[4]])
                sc = 1.0 / 16.0
                for pr in range(2):
                    K.actf(E[:, 0, pr, :], psb[5][:, pr * 130:pr * 130 + 128], ACT.Exp, [PB[5]], [BE], scale=-sc)
                    K.actf(E[:, 1, pr, :], psb[5][:, pr * 130:pr * 130 + 128], ACT.Exp, [PB[5]], [BE], scale=sc)
                    K.actf(decs[:, 0, pr, 2 * n:2 * n + 2], psb[5][:, pr * 130 + 128:pr * 130 + 130], ACT.Exp,
                           [PB[5]], [Bdecs], scale=-sc)
                    K.actf(E[:, 2, pr, :], psb[6 + pr][:, 0:128], ACT.Exp, [PB[6 + pr]], [BE], scale=-sc)
                    K.actf(E[:, 3, pr, :], psb[6 + pr][:, 128:256], ACT.Exp, [PB[6 + pr]], [BE], scale=sc)
                    K.actf(decs[:, 1, pr, 2 * n:2 * n + 2], psb[6 + pr][:, 256:258], ACT.Exp, [PB[6 + pr]], [Bdecs],
                           scale=-sc)
                K.actf(ekd[:], psb[4][:, :], ACT.Exp, [PB[4]], [Bekd], scale=-sc)
                K.tt(dve, kd[:, 0:256], psb[2][:, 0:256], ekd[:, 0:256], ALU.mult, [PB[2], Bekd], [Bkd])
                K.tt(dve, kd[:, 256:512], psb[2][:, 0:256], ekd[:, 256:512], ALU.mult, [PB[2], Bekd], [Bkd])
                for pr in range(2):
                    K.stt(qst[:, 0, pr, tk], qk[:, pr, tk], 0.125, E[:, 0, pr, :], ALU.mult, ALU.mult, [Bqk, BE], [Bqst])
                    K.stt(qst[:, 1, pr, tk], qk[:, pr, tk], 0.125, E[:, 2, pr, :], ALU.mult, ALU.mult, [Bqk, BE], [Bqst])
                    K.tt(pool, qst[:, 2, pr, tk], qk[:, 2 + pr, tk], E[:, 1, pr, :], ALU.mult, [Bqk, BE], [Bqst])
                    K.tt(pool, qst[:, 3, pr, tk], qk[:, 2 + pr, tk], E[:, 3, pr, :], ALU.mult, [Bqk, BE], [Bqst])
                for c in range(2):
                    for dr in range(2):
                        for h in range(4):
                            hb = (h % 2) * 64
                            col = (dr * 2 + h // 2) * 128
                            K.mm(psb[c][hb:hb + 64, col:col + 128],
                                 kd[c * 64:(c + 1) * 64, dr * 256 + h * 64:dr * 256 + (h + 1) * 64],
                                 vt[c * 64:(c + 1) * 64, h * 128:(h + 1) * 128], True, True, [Bkd, Bvt], [PB[c]])
                    K.copy(act if c == 0 else dve, kvst[:, :, c * 2:c * 2 + 2, :],
                           psb[c][:, :].rearrange("p (d r v) -> p d r v", d=2, r=2), [PB[c]], [Bkvst])
                for dr in range(2):
                    K.dma(sp, gkv[dr, 2 * n:2 * n + 2].rearrange("c r p v -> p c r v"),
                          kvst[:, dr, :, :].rearrange("p (c r) v -> p c r v", c=2), reads=[Bkvst], writes=[Bgkv])
            K.dma(sp, gq_v[:, :, :, tt * 512:(tt + 1) * 512], qst[:], reads=[Bqst], writes=[Bgq])

    def even_r(S, stack, store, Sin=None):
        nch = S // 64
        CB = min(8, nch)
        St = [[sb("r_st%d%d" % (d_, p_), [128, 128], F32, stack) for p_ in range(2)] for d_ in range(2)]
        BSt = [[Buf(), Buf()], [Buf(), Buf()]]
        kvb = [sb("r_kvb%d" % i, [128, CB, 2, 128], F32, stack) for i in range(2)]
        stb = [sb("r_stb%d" % i, [128, CB, 2, 128], BF16, stack) for i in range(2)]
        Bkvb, Bstb = [Buf(), Buf()], [Buf(), Buf()]
        nb = 0
        for dr in range(2):
            for pr in range(2):
                if Sin is None:
                    K.op(dve, lambda e, dr=dr, pr=pr: e.memset(St[dr][pr][:], 0.0), [], [BSt[dr][pr]])
                else:
                    K.copy(dve, St[dr][pr][:], Sin[dr][pr][0][:], [Sin[dr][pr][1]], [BSt[dr][pr]])
            batches = list(range(0, nch, CB))
            if dr == 1:
                batches = batches[::-1]
            for c0 in batches:
                kb, Bk = kvb[nb % 2], Bkvb[nb % 2]
                sbf, Bs_ = stb[nb % 2], Bstb[nb % 2]
                nb += 1
                K.dma(sp, kb[:], gkv[dr, c0:c0 + CB].rearrange("c r p v -> p c r v"), reads=[Bgkv], writes=[Bk])
                cis = list(range(CB))
                if dr == 1:
                    cis = cis[::-1]
                for ci in cis:
                    c = c0 + ci
                    for pr in range(2):
                        if store:
                            K.copy(act, sbf[:, ci, pr, :], St[dr][pr][:], [BSt[dr][pr]], [Bs_])
                        K.stt(St[dr][pr][:], St[dr][pr][:], decs[:, dr, pr, c:c + 1], kb[:, ci, pr, :], ALU.mult,
                              ALU.add, [BSt[dr][pr], Bdecs, Bk], [BSt[dr][pr]])
                if store:
                    K.dma(sp, gs[dr, c0:c0 + CB].rearrange("c r p v -> p c r v"), sbf[:], reads=[Bs_], writes=[Bgs])
        return St, BSt

    def even_exchange(S, stack):
        nch = S // 64
        St, BSt = even_r(S, stack, False)
        pk = sb("x_pk", [128, 4, 129], F32, stack)
        Bpk = Buf()
        for dr in range(2):
            for pr in range(2):
                k4 = dr * 2 + pr
                K.copy(dve, pk[:, k4, 0:128], St[dr][pr][:], [BSt[dr][pr]], [Bpk])
                K.copy(dve, pk[:, k4, 128:129], decs[:, dr, pr, 0:1], [Bdecs], [Bpk])
                for c in range(1, nch):
                    K.tt(dve, pk[:, k4, 128:129], pk[:, k4, 128:129], decs[:, dr, pr, c:c + 1], ALU.mult,
                         [Bpk, Bdecs], [Bpk])
        K.dma(sp, gsum.rearrange("(k p) v -> p k v", p=128), pk[:], reads=[Bpk], writes=[Bgsum])
        K.barrier()
        K.op(pool, lambda e: e.collective_compute("AllGather", ALU.bypass, replica_groups=cfg.replica_groups,
                                                  ins=[gsum], outs=[gsum_all]), [Bgsum], [Bgsall])
        K.barrier()
        pa = sb("x_pa", [128, GRP, 4, 129], F32, stack)
        Bpa = Buf()
        K.dma(sp, pa[:], gsum_all.rearrange("(g k p) v -> p g k v", g=GRP, p=128), reads=[Bgsall], writes=[Bpa])
        Sin = [[None, None], [None, None]]
        cf = sb("x_cf", [128, 2], F32, stack)
        Bcf = Buf()
        for dr in range(2):
            for pr in range(2):
                k4 = dr * 2 + pr
                t_ = sb("x_sin%d" % k4, [128, 128], F32, stack)
                Bt = Buf()
                K.op(dve, lambda e, t_=t_: e.memset(t_[:], 0.0), [], [Bt])
                for r1 in range(GRP):
                    K.copy(dve, cf[:, 0:1], fcst[:, 8 + dr * 4 + r1:9 + dr * 4 + r1], [Bfc], [Bcf])
                    for r2 in range(GRP):
                        ic_ = 16 + dr * 16 + r1 * 4 + r2
                        K.ts(dve, cf[:, 1:2], pa[:, r2, k4, 128:129], -1.0, fcst[:, ic_:ic_ + 1], ALU.add, ALU.mult,
                             [Bpa, Bfc], [Bcf])
                        K.stt(cf[:, 0:1], cf[:, 1:2], 1.0, cf[:, 0:1], ALU.add, ALU.mult, [Bcf], [Bcf])
                    K.stt(t_[:], pa[:, r1, k4, 0:128], cf[:, 0:1], t_[:], ALU.mult, ALU.add, [Bpa, Bcf, Bt], [Bt])
                Sin[dr][pr] = (t_, Bt)
        return Sin

    def even_o(l, S, stack):
        i_ev = l // 2
        tcm = sb("o_tcm", [128, 256], F32, stack)
        gn = sb("o_gn", [128, 1], F32, stack)
        qt = [sb("o_qt%d" % i, [128, 4, 2, 512], BF16, stack) for i in range(2)]
        vtl = [sb("o_vt%d" % i, [128, 4, 512], BF16, stack) for i in range(2)]
        gt = [sb("o_gt%d" % i, [128, 4, 512], BF16, stack) for i in range(2)]
        sf = [sb("o_sf%d" % i, [128, 8, 2, 128], BF16, stack) for i in range(2)]
        sbw = [sb("o_sb%d" % i, [128, 8, 2, 128], BF16, stack) for i in range(2)]
        am = [sb("o_am%d" % i, [128, 256], BF16, stack) for i in range(2)]
        sq = sb("o_sq", [128, 512], BF16, stack)
        rs = sb("o_rs", [128, 512], F32, stack)
        on = sb("o_on", [128, 512], F32, stack)
        ost = sb("o_ost", [128, 4, 512], BF16, stack)
        Btcm, Bgn, Bsq, Brs, Bon, Bost = (Buf() for _ in range(6))
        Bqt, Bvtl, Bgt, Bsf, Bsbw, Bam = ([Buf(), Buf()] for _ in range(6))
        K.dma(sp, tcm[:, 0:128], tconst[:, 0:128], writes=[Btcm])
        K.dma(sp, tcm[:, 128:256], tconst[:, 130:258], writes=[Btcm])
        K.dma(sp, gn[:], gla_nrm[i_ev], writes=[Bgn])
        gq_v = gq.rearrange("k r p s -> p k r s")
        gg_v = gg.rearrange("(c p) s -> p c s", p=128)
        mixo_v = mixo.rearrange("(c p) s -> p c s", p=128)
        na = 0
        for tt in range(S // 512):
            i2 = tt % 2
            K.dma(sp, qt[i2][:], gq_v[:, :, :, tt * 512:(tt + 1) * 512], reads=[Bgq], writes=[Bqt[i2]])
            K.dma(sp, vtl[i2][:], gvt[tt * 4:(tt + 1) * 4].rearrange("n p v -> p n v"), reads=[Bgvt], writes=[Bvtl[i2]])
            K.dma(sp, gt[i2][:], gg_v[:, :, tt * 512:(tt + 1) * 512], reads=[Bgg], writes=[Bgt[i2]])
            K.dma(sp, sf[i2][:], gs[0, tt * 8:(tt + 1) * 8].rearrange("c r p v -> p c r v"), reads=[Bgs],
                  writes=[Bsf[i2]])
            K.dma(sp, sbw[i2][:], gs[1, tt * 8:(tt + 1) * 8].rearrange("c r p v -> p c r v"), reads=[Bgs],
                  writes=[Bsbw[i2]])
            q_ = qt[i2]
            for ts in range(4):
                tk = slice(ts * 128, (ts + 1) * 128)
                for h in range(4):
                    pr, hb = h // 2, (h % 2) * 64
                    rows = slice(hb, hb + 64)
                    ab = h % 2
                    ob = 2 + h % 2
                    K.mm(psb[ab][:, 0:128], q_[rows, 2, pr, tk], q_[rows, 0, pr, tk], True, True, [Bqt[i2]], [PB[ab]])
                    K.mm(psb[ab][:, 128:256], q_[rows, 3, pr, tk], q_[rows, 1, pr, tk], True, True, [Bqt[i2]],
                         [PB[ab]])
                    a_, Ba = am[na % 2], Bam[na % 2]
                    na += 1
                    K.tt(dve, a_[:], psb[ab][:, 0:256], tcm[:], ALU.mult, [PB[ab], Btcm], [Ba])
                    o0 = pr * 128
                    oc_ = slice(o0, o0 + 128)
                    K.mm(psb[ob][:, oc_], vtl[i2][:, ts, h * 128:(h + 1) * 128], a_[:, 0:128], True, False,
                         [Bvtl[i2], Ba], [PB[ob]])
                    K.mm(psb[ob][:, oc_], vtl[i2][:, ts, h * 128:(h + 1) * 128], a_[:, 128:256], False, False,
                         [Bvtl[i2], Ba], [PB[ob]])
                    for c in range(2):
                        ci = ts * 2 + c
                        cs = slice(o0 + c * 64, o0 + (c + 1) * 64)
                        tks = slice(ts * 128 + c * 64, ts * 128 + (c + 1) * 64)
                        K.mm(psb[ob][:, cs], sf[i2][rows, ci, pr, :], q_[rows, 0, pr, tks], False, False,
                             [Bsf[i2], Bqt[i2]], [PB[ob]])
                        K.mm(psb[ob][:, cs], sbw[i2][rows, ci, pr, :], q_[rows, 1, pr, tks], False, c == 1,
                             [Bsbw[i2], Bqt[i2]], [PB[ob]])
                for par in range(2):
                    ob = 2 + par
                    hsel = slice(par, 4, 2)
                    K.actf(sq[:, 0:256], psb[ob][:, 0:256], ACT.Square, [PB[ob]], [Bsq])
                    K.mm(psb[4][:, 0:256], onesb[:], sq[:, 0:256], True, True, [Bones, Bsq], [PB[4]])
                    K.actf(rs[:, 0:256], psb[4][:, 0:256], ACT.Sqrt, [PB[4]], [Brs], scale=1.0 / 128, bias=EPS)
                    K.op(dve, lambda e: e.reciprocal(rs[:, 0:256], rs[:, 0:256]), [Brs], [Brs])
                    K.tt(dve, on[:, 0:256], psb[ob][:, 0:256], rs[:, 0:256], ALU.mult, [PB[ob], Brs], [Bon])
                    K.stt(ost[:, hsel, tk], on[:, 0:256].rearrange("p (h i) -> p h i", h=2), gn[:, 0:1],
                          gt[i2][:, hsel, tk], ALU.mult, ALU.mult, [Bon, Bgn, Bgt[i2]], [Bost])
            K.dma(sp, mixo_v[:, 0:4, tt * 512:(tt + 1) * 512], ost[:], reads=[Bost], writes=[Bmixo])

    def even_a2(l, S, is_sample, stack):
        i_ev = l // 2
        fb = mx_alloc(stack)
        win = sb("m_win", [128, KC, 1088], BF16, stack)
        wqb = sb("m_wqb", [128, 2, 1536], BF16, stack)
        wkv = sb("m_wkv", [128, 1024], BF16, stack)
        qn = sb("m_qn", [128, 2], F32, stack)
        kvn = sb("m_kvn", [128, 1], F32, stack)
        cq = sb("m_cq", [128, 2, 512], F32, stack)
        cqn = sb("m_cqn", [128, 2, 512], BF16, stack)
        ckvn = sb("m_ckvn", [128, 512], BF16, stack)
        cf_ = sb("m_cf", [128, 4], F32, stack)
        zb = sb("m_zb", [128, 1], F32, stack)
        Bzb = Buf()
        K.op(dve, lambda e: e.memset(zb[:], 0.0), [], [Bzb])
        ai = sb("m_ai", [128, 512], I32, stack)
        pos = sb("m_pos", [128, 512], F32, stack)
        tab = sb("m_tab", [128, 2, 512], F32, stack)
        gst = [sb("m_gst%d" % i, [128, 512], BF16, stack) for i in range(2)]
        qh = [sb("m_qh%d" % i, [96, 512], BF16, stack) for i in range(2)]
        kst = sb("m_kst", [96, 8, 512], BF16, stack)
        kro = sb("m_kro", [96, 512], BF16, stack)
        vaug = [sb("m_vaug%d" % i, [128, 8, 65], BF16, stack) for i in range(2)]
        (Bwin, Bwqb, Bwkv, Bqn, Bkvn, Bcq, Bcqn, Bckvn, Bcf, Bai, Bpos, Btab, Bkst, Bkro) = (Buf() for _ in range(14))
        Bgst, Bqh, Bvaug = ([Buf(), Buf()] for _ in range(3))
        K.dma(sp, win[:], ev_win2_b[i_ev].rearrange("p (k n) -> p k n", k=KC), reads=[Bevw], writes=[Bwin])
        K.dma(sp, wqb[:], mla_wqb_b[i_ev].rearrange("p (k n) -> p k n", k=2), reads=[Bevw], writes=[Bwqb])
        K.dma(sp, wkv[:], mla_wkvb_b[i_ev], reads=[Bevw], writes=[Bwkv])
        K.dma(sp, qn[:], mla_qn[i_ev], writes=[Bqn])
        K.dma(sp, kvn[:], mla_kvn[i_ev], writes=[Bkvn])
        for i in range(2):
            K.op(dve, lambda e, i=i: e.memset(vaug[i][:], 1.0), [], [Bvaug[i]])
        K.op(pool, lambda e: e.iota(ai[:], [[0, 512]], base=0, channel_multiplier=1), [], [Bai])
        K.op(dve, lambda e: e.tensor_scalar(ai[:, 0:1], ai[:, 0:1], icst[:, 6:7], None, ALU.bitwise_and), [Bai, Bic],
             [Bai])
        K.copy(dve, cf_[:, 1:2], ai[:, 0:1], [Bai], [Bcf])
        K.actf(cf_[:, 0:1], cf_[:, 1:2], ACT.Exp, [Bcf], [Bcf], scale=-float(np.log(10000.0)) / 16.0)
        K.ts(dve, cf_[:, 0:1], cf_[:, 0:1], 65536.0 / (2.0 * np.pi), None, ALU.mult, None, [Bcf], [Bcf])
        jpos = sb("m_jpos", [128, 512], I32, stack)
        Bjpos = Buf()
        K.op(pool, lambda e: e.iota(jpos[:], [[1, 512]], base=0, channel_multiplier=0), [], [Bjpos])
        if is_sample:
            K.op(pool, lambda e: e.tensor_scalar(jpos[:], jpos[:], icst[:, 5:6], None, ALU.add), [Bjpos, Bic], [Bjpos])
        gg_v = gg.rearrange("(c p) s -> p c s", p=128)
        Kd_v = Kd.rearrange("(h r) s -> r h s", h=8)
        Vd_v = Vd.rearrange("(h p) (k e) -> p h k e", h=8, e=65)
        qs = float(96.0 ** -0.5)
        R = slice(64, 96)
        ng = 0
        a2s = cfg.a2_stop
        if a2s == 1:
            return
        for tt in range(S // 512):
            for st in prenorm_steps(fb, l, 1, tt, fb.h, fb.Bh):
                st()
            for c4 in range(4):
                bank = rbank()
                for kc in range(KC):
                    K.mm(psb[bank][:, :], win[:, kc, c4 * 128:(c4 + 1) * 128], fb.h[:, kc, :], kc == 0, kc == KC - 1,
                         [Bwin, fb.Bh], [PB[bank]])
                g_, Bg_ = gst[ng % 2], Bgst[ng % 2]
                ng += 1
                K.actf(g_[:], psb[bank][:, :], ACT.Silu, [PB[bank]], [Bg_])
                K.dma(sp, gg_v[:, c4, tt * 512:(tt + 1) * 512], g_[:], reads=[Bg_], writes=[Bgg])
            if a2s == 2:
                continue
            K.ts(dve, pos[:], jpos[:], float(tt * 512), None, ALU.add, None, [Bjpos], [Bpos])
            for k2 in range(2):
                K.ts(dve, ai[:], pos[:], cf_[:, 0:1], fcst[:, k2:k2 + 1], ALU.mult, ALU.add, [Bpos, Bcf, Bfc], [Bai])
                K.op(dve, lambda e: e.tensor_scalar(ai[:], ai[:], icst[:, 3:4], None, ALU.bitwise_and), [Bai, Bic],
                     [Bai])
                K.actf(tab[:, k2, :], ai[:], ACT.Sin, [Bai], [Btab], scale=2.0 * np.pi / 65536.0, bias=-np.pi)
            if a2s == 3:
                continue
            for c2 in range(2):
                bank = rbank()
                for kc in range(KC):
                    K.mm(psb[bank][:, :], win[:, kc, 512 + c2 * 128:512 + (c2 + 1) * 128], fb.h[:, kc, :], kc == 0,
                         kc == KC - 1, [Bwin, fb.Bh], [PB[bank]])
                K.copy(act, cq[:, c2, :], psb[bank][:, :], [PB[bank]], [Bcq])
                i = fb.nsq % len(fb.sq)
                fb.nsq += 1
                K.actf(fb.sq[i][:], psb[bank][:, :], ACT.Square, [PB[bank]], [fb.Bsq[i]])
                K.mm(psb[6][:, :], onesb[:], fb.sq[i][:], c2 == 0, c2 == 1, [Bones, fb.Bsq[i]], [PB[6]])
            K.actf(fb.rstd[1][:], psb[6][:, :], ACT.Sqrt, [PB[6]], [fb.Brstd[1]], scale=1.0 / 256, bias=EPS)
            K.op(dve, lambda e: e.reciprocal(fb.rstd[1][:], fb.rstd[1][:]), [fb.Brstd[1]], [fb.Brstd[1]])
            for c2 in range(2):
                i = fb.ntmp % len(fb.tmp)
                fb.ntmp += 1
                K.stt(fb.tmp[i][:], cq[:, c2, :], qs, fb.rstd[1][:], ALU.mult, ALU.mult, [Bcq, fb.Brstd[1]],
                      [fb.Btmp[i]])
                K.actf(cqn[:, c2, :], fb.tmp[i][:], ACT.Identity, [fb.Btmp[i], Bqn, Bzb], [Bcqn],
                       scale=qn[:, c2:c2 + 1], bias=zb[:, 0:1])
            bank = rbank()
            for kc in range(KC):
                K.mm(psb[bank][:, :], win[:, kc, 768:896], fb.h[:, kc, :], kc == 0, kc == KC - 1, [Bwin, fb.Bh],
                     [PB[bank]])
            i = fb.nsq % len(fb.sq)
            fb.nsq += 1
            K.actf(fb.sq[i][:], psb[bank][:, :], ACT.Square, [PB[bank]], [fb.Bsq[i]])
            K.mm(psb[6][:, :], onesb[:], fb.sq[i][:], True, True, [Bones, fb.Bsq[i]], [PB[6]])
            K.actf(fb.rstd[1][:], psb[6][:, :], ACT.Sqrt, [PB[6]], [fb.Brstd[1]], scale=1.0 / 128, bias=EPS)
            K.op(dve, lambda e: e.reciprocal(fb.rstd[1][:], fb.rstd[1][:]), [fb.Brstd[1]], [fb.Brstd[1]])
            i = fb.ntmp % len(fb.tmp)
            fb.ntmp += 1
            K.tt(dve, fb.tmp[i][:], psb[bank][:, :], fb.rstd[1][:], ALU.mult, [PB[bank], fb.Brstd[1]], [fb.Btmp[i]])
            K.actf(ckvn[:], fb.tmp[i][:], ACT.Identity, [fb.Btmp[i], Bkvn, Bzb], [Bckvn], scale=kvn[:, 0:1],
                   bias=zb[:, 0:1])
            if a2s == 4:
                continue
            for kc in range(KC):
                K.mm(psb[2][0:96, :], win[:, kc, 896:992], fb.h[:, kc, :], kc == 0, kc == KC - 1, [Bwin, fb.Bh], [PB[2]])
            for kc in range(KC):
                K.mm(psb[3][0:96, :], win[:, kc, 992:1088], fb.h[:, kc, :], kc == 0, kc == KC - 1, [Bwin, fb.Bh],
                     [PB[3]])
            i = fb.ntmp % len(fb.tmp)
            fb.ntmp += 1
            K.tt(dve, fb.tmp[i][R, :], psb[2][R, :], tab[R, 0, :], ALU.mult, [PB[2], Btab], [fb.Btmp[i]])
            i2 = fb.ntmp % len(fb.tmp)
            fb.ntmp += 1
            K.tt(dve, fb.tmp[i2][R, :], psb[3][R, :], tab[R, 1, :], ALU.mult, [PB[3], Btab], [fb.Btmp[i2]])
            K.tt(dve, kro[R, :], fb.tmp[i][R, :], fb.tmp[i2][R, :], ALU.add, [fb.Btmp[i], fb.Btmp[i2]], [Bkro])
            if a2s == 5:
                continue
            for h in range(8):
                bank = rbank()
                K.mm(psb[bank][0:64, :], wkv[:, h * 64:(h + 1) * 64], ckvn[:], True, True, [Bwkv, Bckvn], [PB[bank]])
                K.copy(act, kst[0:64, h, :], psb[bank][0:64, :], [PB[bank]], [Bkst])
                K.copy(dve, kst[R, h, :], kro[R, :], [Bkro], [Bkst])
                for kc in range(2):
                    K.mm(psb[2][0:96, :], wqb[:, kc, h * 96:(h + 1) * 96], cqn[:, kc, :], kc == 0, kc == 1,
                         [Bwqb, Bcqn], [PB[2]])
                for kc in range(2):
                    K.mm(psb[3][0:96, :], wqb[:, kc, 768 + h * 96:768 + (h + 1) * 96], cqn[:, kc, :], kc == 0, kc == 1,
                         [Bwqb, Bcqn], [PB[3]])
                q_, Bq_ = qh[h % 2], Bqh[h % 2]
                K.copy(dve, q_[0:64, :], psb[2][0:64, :], [PB[2]], [Bq_])
                i = fb.ntmp % len(fb.tmp)
                fb.ntmp += 1
                K.tt(dve, fb.tmp[i][R, :], psb[2][R, :], tab[R, 0, :], ALU.mult, [PB[2], Btab], [fb.Btmp[i]])
                i2 = fb.ntmp % len(fb.tmp)
                fb.ntmp += 1
                K.tt(dve, fb.tmp[i2][R, :], psb[3][R, :], tab[R, 1, :], ALU.mult, [PB[3], Btab], [fb.Btmp[i2]])
                K.tt(dve, q_[R, :], fb.tmp[i][R, :], fb.tmp[i2][R, :], ALU.add, [fb.Btmp[i], fb.Btmp[i2]], [Bq_])
                K.dma(sp, Qd[h, :, tt * 512:(tt + 1) * 512], q_[:], reads=[Bq_], writes=[BQd])
            K.dma(sp, Kd_v[:, :, tt * 512:(tt + 1) * 512], kst[:], reads=[Bkst], writes=[BKd])
            if a2s == 6:
                continue
            for ts in range(4):
                tk = slice(ts * 128, (ts + 1) * 128)
                kt = tt * 4 + ts
                va, Bva = vaug[kt % 2], Bvaug[kt % 2]
                bank = rbank()
                K.mm(psb[bank][:, :], ckvn[:, tk], wkv[:, 512:1024], True, True, [Bckvn, Bwkv], [PB[bank]])
                K.copy(act if ts % 2 == 0 else dve, va[:, :, 0:64], psb[bank][:, :].rearrange("p (h e) -> p h e", h=8),
                       [PB[bank]], [Bva])
                K.dma(sp, Vd_v[:, :, kt, :], va[:], reads=[Bva], writes=[BVd])

    def even_b(S, nrank, Ksrc, BKs, Vsrc, BVs, stack):
        SK = S
        KB_ = min(1024, SK)
        nkt = KB_ // 128
        LOOK = 2
        NP = 4
        qt = [sb("b_qt%d" % i, [96, 512], BF16, stack) for i in range(2)]
        ktl = [sb("b_kt%d" % i, [96, KB_], BF16, stack) for i in range(3)]
        vtl = [sb("b_vt%d" % i, [128, nkt, 65], BF16, stack) for i in range(3)]
        pt = [sb("b_pt%d" % i, [128, 512], BF16, stack) for i in range(NP)]
        rc = sb("b_rc", [128, 512], F32, stack)
        osb = sb("b_osb", [64, 512], F32, stack)
        onb = [sb("b_on%d" % i, [64, 512], BF16, stack) for i in range(2)]
        Brc, Bosb = Buf(), Buf()
        Bqt, Bonb = ([Buf(), Buf()] for _ in range(2))
        Bktl, Bvtl = ([Buf(), Buf(), Buf()] for _ in range(2))
        Bpt = [Buf() for _ in range(NP)]
        nq = nk = ns = 0
        pend = []
        tails = []

        def flush_one():
            pend.pop(0)()

        for qb in range(S // 512):
            for h in range(8):
                q_, Bq_ = qt[nq % 2], Bqt[nq % 2]
                ob = 4 + nq % 2
                nq += 1
                K.dma(sp, q_[:], Qd[h, :, qb * 512:(qb + 1) * 512], reads=[BQd], writes=[Bq_])
                nblk = nrank * (SK // KB_)
                nstep = nblk * nkt
                si = 0
                for g in range(nrank):
                    for k0 in range(0, SK, KB_):
                        k_, Bk_ = ktl[nk % 3], Bktl[nk % 3]
                        v_, Bv_ = vtl[nk % 3], Bvtl[nk % 3]
                        nk += 1
                        K.dma(sp, k_[:], Ksrc(g, h, k0, KB_), reads=[BKs], writes=[Bk_])
                        K.dma(sp, v_[:], Vsrc(g, h, k0 // 128, nkt), reads=[BVs], writes=[Bv_])
                        for kt in range(nkt):
                            sbk = ns % 4
                            p_, Bp_ = pt[ns % NP], Bpt[ns % NP]
                            ns += 1
                            K.mm(psb[sbk][:, :], k_[:, kt * 128:(kt + 1) * 128], q_[:], True, True, [Bk_, Bq_],
                                 [PB[sbk]])
                            K.actf(p_[:], psb[sbk][:, :], ACT.Exp, [PB[sbk]], [Bp_])

                            def pv(v_=v_, Bv_=Bv_, p_=p_, Bp_=Bp_, kt=kt, first=(si == 0), last=(si == nstep - 1),
                                   ob=ob):
                                K.mm(psb[ob][0:65, :], v_[:, kt, :], p_[:], first, last, [Bv_, Bp_], [PB[ob]])
                            pend.append(pv)
                            si += 1
                            if len(pend) > LOOK:
                                flush_one()
                            if si == 4 and tails:
                                tails.pop(0)()
                while pend:
                    flush_one()
                while tails:
                    tails.pop(0)()
                K.op(dve, lambda e, ob=ob: e.reciprocal(rc[64:65, :], psb[ob][64:65, :]), [PB[ob]], [Brc])
                K.copy(dve, osb[:], psb[ob][0:64, :], [PB[ob]], [Bosb])

                def tail(h=h, qb=qb):
                    K.mm(psb[6][0:64, :], cst[64:65, 128:192], rc[64:65, :], True, True, [Bcst, Brc], [PB[6]])
                    o_, Bo_ = onb[h % 2], Bonb[h % 2]
                    K.tt(dve, o_[:], osb[:], psb[6][0:64, :], ALU.mult, [Bosb, PB[6]], [Bo_])
                    K.dma(sp, mixm[h, :, qb * 512:(qb + 1) * 512], o_[:], reads=[Bo_], writes=[Bmixm])
                tails.append(tail)
        while tails:
            tails.pop(0)()

    def even_phase_c(l, S, stack):
        i_ev = l // 2
        fb = mx_alloc(stack, with_h=False, with_y=True)
        wg = sb("c_wg", [128, 4, 1024], BF16, stack)
        wm = sb("c_wm", [64, 8, 1024], BF16, stack)
        og = [sb("c_og%d" % i, [128, 4, 512], BF16, stack) for i in range(2)]
        om = [sb("c_om%d" % i, [64, 8, 512], BF16, stack) for i in range(2)]
        Bwg, Bwm = Buf(), Buf()
        Bog, Bom = [Buf(), Buf()], [Buf(), Buf()]
        K.dma(sp, wg[:], ev_woutg_b[i_ev].rearrange("p (k n) -> p k n", k=4), reads=[Bevw], writes=[Bwg])
        K.dma(sp, wm[:], ev_woutm_b[i_ev].rearrange("p (k n) -> p k n", k=8), reads=[Bevw], writes=[Bwm])
        mixo_v = mixo.rearrange("(c p) s -> p c s", p=128)
        mixm_v = mixm.rearrange("h e s -> e h s")
        Cg = vec(l, 1, 2)
        pending = []
        for tt in range(S // 512):
            g_, Bg_ = og[tt % 2], Bog[tt % 2]
            m_, Bm_ = om[tt % 2], Bom[tt % 2]
            K.dma(sp, g_[:], mixo_v[:, 0:4, tt * 512:(tt + 1) * 512], reads=[Bmixo], writes=[Bg_])
            K.dma(sp, m_[:], mixm_v[:, :, tt * 512:(tt + 1) * 512], reads=[Bmixm], writes=[Bm_])

            def mm_oc(oc, bank, g_=g_, m_=m_, Bg_=Bg_, Bm_=Bm_):
                for ic in range(4):
                    K.mm(psb[bank][:, :], wg[:, ic, oc * 128:(oc + 1) * 128], g_[:, ic, :], ic == 0, False,
                         [Bwg, Bg_], [PB[bank]])
                for hh in range(8):
                    K.mm(psb[bank][:, :], wm[:, hh, oc * 128:(oc + 1) * 128], m_[:, hh, :], False, hh == 7,
                         [Bwm, Bm_], [PB[bank]])
            yphase(fb, tt, Cg, mm_oc, [], pending)
            while pending:
                pending.pop(0)()

    def even_mixer(l, S, is_sample):
        xg = is_sample and GRP > 1
        stop = cfg.ev_stop
        with ExitStack() as st:
            even_a1(l, S, st)
            K.barrier()
        if stop == 1:
            return
        with ExitStack() as st:
            Sin = even_exchange(S, st) if xg else None
            even_r(S, st, True, Sin)
            K.barrier()
        if stop == 2:
            return
        with ExitStack() as st:
            even_a2(l, S, is_sample, st)
            K.barrier()
        if stop == 3:
            return
        if xg:
            for h in range(8):
                K.op(pool, lambda e, h=h: e.collective_compute(
                    "AllGather", ALU.bypass, replica_groups=cfg.replica_groups,
                    ins=[Kd[h * 96:(h + 1) * 96, 0:S]], outs=[Kall[h]]), [BKd], [BKall])
                K.op(pool, lambda e, h=h: e.collective_compute(
                    "AllGather", ALU.bypass, replica_groups=cfg.replica_groups,
                    ins=[Vd[h * 128:(h + 1) * 128, 0:(S // 128) * 65]], outs=[Vall[h]]), [BVd], [BVall])
        with ExitStack() as st:
            even_o(l, S, st)
            K.barrier()
        if stop == 4:
            return
        if xg:
            K.barrier()
            kget = lambda g, h, k0, n: Kall[h, g * 96:(g + 1) * 96, k0:k0 + n]
            vget = lambda g, h, kt0, n: Vall[h].rearrange("(g p) (k e) -> g p k e", p=128, e=65)[g, :, kt0:kt0 + n, :]
            srcs = (GRP, kget, BKall, vget, BVall)
        else:
            kget = lambda g, h, k0, n: Kd[h * 96:(h + 1) * 96, k0:k0 + n]
            vget = lambda g, h, kt0, n: Vd[h * 128:(h + 1) * 128, :].rearrange("p (k e) -> p k e", e=65)[:, kt0:kt0 + n, :]
            srcs = (1, kget, BKd, vget, BVd)
        with ExitStack() as st:
            even_b(S, *srcs, st)
            K.barrier()
        if stop == 5:
            return
        with ExitStack() as st:
            even_phase_c(l, S, st)
            K.barrier()

    tok0 = 0
    for si, S in enumerate(cfg.seg_tokens):
        K.dma(sp, mv[:], modv[si], reads=[Bmodv], writes=[Bmv])
        with ExitStack() as st:
            load_segment(tok0, S, st)
            K.barrier()
        for l in range(L):
            with ExitStack() as st:
                fb = ffn_alloc(st)
                ffn_sublayer(fb, l, 0, S)
                K.barrier()
            if cfg.do_mixer and l % 2 == 1 and cfg.do_mixer & 2:
                odd_mixer(l, S, si == 2)
            if cfg.do_mixer and l % 2 == 0 and cfg.do_mixer & 1:
                even_mixer(l, S, si == 2)
            with ExitStack() as st:
                fb = ffn_alloc(st)
                ffn_sublayer(fb, l, 2, S)
                K.barrier()
        with ExitStack() as st:
            store_segment(tok0, S, st)
            K.barrier()
        tok0 += S
    K.barrier()
    es.close()
    return nc


def _fm(v):
    v = np.asarray(v)
    lead = v.shape[:-1]
    n = v.shape[-1] // 128
    v = v.reshape(lead + (n, 128))
    return np.ascontiguousarray(np.moveaxis(v, -1, 0))


def prep_shared(inp, cfg):
    L = cfg.depth
    sh = {}
    ident = np.eye(128, dtype=np.float32)
    sh["consts"] = np.ascontiguousarray(np.concatenate([ident, np.ones((128, 128), np.float32)], axis=1))
    aw = np.asarray(inp["ada_w"])[:L]
    aw = aw.reshape(L, KC, 128, 72, 128).transpose(0, 3, 2, 1, 4)
    sh["ada_w"] = np.ascontiguousarray(aw).reshape(L * 72, 128, KC, 128)
    sh["ada_b"] = _fm(np.asarray(inp["ada_b"])[:L]).reshape(128, L * 72)
    sh["npre"] = _fm(np.asarray(inp["norm_pre"])[:L]).reshape(128, L * 3 * KC)
    sh["npost"] = _fm(np.asarray(inp["norm_post"])[:L]).reshape(128, L * 3 * KC)
    w13 = np.asarray(inp["ffn_w13"])[:L].reshape(L * 2, KC, 128, 2, NFC, 128)
    sh["w13"] = np.ascontiguousarray(w13.transpose(0, 4, 2, 1, 3, 5)).reshape(L * 2, NFC, 128, KC * 256)
    w2 = np.asarray(inp["ffn_w2"])[:L].reshape(L * 2, NFC, 128, KC, 128)
    sh["w2"] = np.ascontiguousarray(w2.transpose(0, 3, 2, 1, 4)).reshape(L * 2, KC, 128, NFC * 128)
    NOD = L // 2
    if NOD:
        ow = np.asarray(inp["od_w_in"])[:NOD]
        sh["od_win"] = np.ascontiguousarray(ow.reshape(NOD, KC, 128, 1536).transpose(0, 2, 1, 3)).reshape(NOD, 128, KC * 1536)
        oo = np.asarray(inp["od_w_out"])[:NOD]
        sh["od_wout"] = np.ascontiguousarray(oo.reshape(NOD, KC, 128, 1024).transpose(0, 2, 1, 3)).reshape(NOD, 128, KC * 1024)
        ws = np.asarray(inp["sgu_w_s"])[:NOD]
        sh["sgu_wsT"] = np.ascontiguousarray(ws.transpose(0, 3, 1, 2)).reshape(NOD, 128, 512)
        sh["sgu_b"] = np.ascontiguousarray(np.asarray(inp["sgu_b"])[:NOD]).reshape(NOD, 1, 512)
        sh["sgu_nrm"] = np.ascontiguousarray(np.broadcast_to(np.asarray(inp["sgu_norm"])[:NOD, None, :], (NOD, 128, 512)))
    NEV = (L + 1) // 2
    if NEV:
        def kmaj(w, nk):
            n, _, cols = w.shape
            return np.ascontiguousarray(w.reshape(n, nk, 128, cols).transpose(0, 2, 1, 3)).reshape(n, 128, nk * cols)
        wi = np.asarray(inp["ev_w_in"])[:NEV]
        q, k, v, g, alr, cq, ckv, kr = (wi[:, :, a:b] for a, b in ((0, 256), (256, 512), (512, 1024), (1024, 1536),
                                                                  (1536, 1568), (1568, 1824), (1824, 1952), (1952, 1984)))
        sh["ev_win1"] = kmaj(np.concatenate([q, k, v, alr], axis=2), KC)
        fill = ckv[:, :, 0:64]
        krrot = np.concatenate([kr[:, :, 16:32], kr[:, :, 0:16]], axis=2)
        sh["ev_win2"] = kmaj(np.concatenate([g, cq, ckv, fill, kr, fill, krrot], axis=2), KC)
        wo = np.asarray(inp["ev_w_out"])[:NEV]
        sh["ev_woutg"] = kmaj(wo[:, 0:512], 4)
        sh["ev_woutm"] = np.ascontiguousarray(wo[:, 512:1024].reshape(NEV, 8, 64, 1024).transpose(0, 2, 1, 3)).reshape(NEV, 64, 8 * 1024)
        wa = np.asarray(inp["gla_w_alpha"])[:NEV]
        ba = np.asarray(inp["gla_b_alpha"])[:NEV]
        wal = np.zeros((NEV, 33, 512), np.float32)
        wal[:, 0:16, 0:256] = wa[:, 0]
        wal[:, 16:32, 256:512] = wa[:, 1]
        wal[:, 32, 0:256] = ba[:, 0]
        wal[:, 32, 256:512] = ba[:, 1]
        sh["gla_wal"] = wal
        sh["gla_nrm"] = np.ascontiguousarray(np.asarray(inp["gla_norm"])[:NEV].reshape(NEV, 128, 1))
        sh["mla_qn"] = np.ascontiguousarray(np.asarray(inp["mla_q_norm"])[:NEV].reshape(NEV, 2, 128).transpose(0, 2, 1))
        sh["mla_kvn"] = np.ascontiguousarray(np.asarray(inp["mla_kv_norm"])[:NEV].reshape(NEV, 128, 1))
        wq = np.asarray(inp["mla_w_q_b"])[:NEV].reshape(NEV, 256, 8, 96)
        wqr = np.concatenate([wq[..., 0:64], wq[..., 80:96], wq[..., 64:80]], axis=-1)
        sh["mla_wqb"] = kmaj(np.concatenate([wq.reshape(NEV, 256, 768), wqr.reshape(NEV, 256, 768)], axis=2), 2)
        wk = np.asarray(inp["mla_w_kv_b"])[:NEV].reshape(NEV, 128, 8, 128)
        sh["mla_wkvb"] = np.ascontiguousarray(np.concatenate([wk[..., 0:64].reshape(NEV, 128, 512),
                                                             wk[..., 64:128].reshape(NEV, 128, 512)], axis=2))
    t = np.arange(128)
    same = (t[:, None] // 64) == (t[None, :] // 64)
    Tfi = (same & (t[:, None] <= t[None, :])).astype(np.float32)
    Tbe = (same & (t[:, None] > t[None, :])).astype(np.float32)
    Tbi = (same & (t[:, None] >= t[None, :])).astype(np.float32)
    Tpe = (same & (t[:, None] < t[None, :])).astype(np.float32)
    Ind = np.stack([(t < 64), (t >= 64)], axis=1).astype(np.float32)
    sh["tconst"] = np.ascontiguousarray(np.concatenate([Tfi, Ind, Tbe, Tbi, Ind, Tpe], axis=1))
    return sh


def prep_core(inp, cfg, core, n_cores=8):
    xp = np.asarray(inp["x_prompt"])
    xs = np.asarray(inp["x_sample"])
    cp = np.asarray(inp["c_prompt"])
    cs = np.asarray(inp["c_sample"])
    SP = cfg.seg_tokens[0]
    SQ = cfg.seg_tokens[2]
    per_grp = n_cores // xs.shape[0]
    sb_, r = core // per_grp, core % per_grp
    xin = np.concatenate([xp[2 * core, :SP], xp[2 * core + 1, :SP], xs[sb_, r * SQ:(r + 1) * SQ]], axis=0)
    c = np.stack([cp[2 * core], cp[2 * core + 1], cs[sb_], np.zeros(D, np.float32)], axis=0)
    c3 = np.ascontiguousarray(c.T.reshape(KC, 128, 4).transpose(1, 0, 2))
    ic = np.zeros((128, 8), np.int32)
    stot = per_grp * SQ
    ic[:, 0] = SP - 1
    ic[:, 1] = stot - 1
    ic[:, 2] = 127
    ic[:, 3] = 65535
    ic[:, 4] = (SQ * r * np.arange(128)) % stot
    ic[:, 5] = r * SQ
    ic[:, 6] = 15
    fc = np.zeros((128, 64), np.float32)
    fc[:, 0] = 49152.0
    fc[80:96, 1] = 32768.0
    for r1 in range(min(per_grp, 4)):
        fc[:, 8 + r1] = 1.0 if r1 < r else 0.0
        fc[:, 12 + r1] = 1.0 if r1 > r else 0.0
        for r2 in range(min(per_grp, 4)):
            fc[:, 16 + r1 * 4 + r2] = 1.0 if r1 < r2 < r else 0.0
            fc[:, 32 + r1 * 4 + r2] = 1.0 if r < r2 < r1 else 0.0
    return {"xin": np.ascontiguousarray(xin), "c3": c3, "iconst": ic, "fconst": fc}


_CACHE = {}


def run(inp, cfg, n_cores=8, trace=False):
    key = (cfg.seg_tokens, cfg.depth, cfg.do_mixer, cfg.n_cores, cfg.group)
    if key not in _CACHE:
        _CACHE[key] = build(cfg)
    nc = _CACHE[key]
    sh = prep_shared(inp, cfg)
    in_maps = []
    for c in range(n_cores):
        m = dict(sh)
        m.update(prep_core(inp, cfg, c, n_cores))
        in_maps.append(m)
    res = run_bass_kernel_spmd(nc, in_maps, core_ids=list(range(n_cores)), trace=trace)
    return res


def kernel(**inputs):
    cfg = Cfg()
    res = run(inputs, cfg)
    SP, SQ = cfg.seg_tokens[0], cfg.seg_tokens[2]
    B, S = inputs["x_prompt"].shape[:2]
    DB, DS = inputs["x_sample"].shape[:2]
    yp = np.empty((B, S, D), np.float32)
    ys = np.empty((DB, DS, D), np.float32)
    per_grp = 8 // DB
    for c in range(8):
        y = res.results[c]["yout"]
        yp[2 * c] = y[0:SP]
        yp[2 * c + 1] = y[SP:2 * SP]
        ys[c // per_grp, (c % per_grp) * SQ:(c % per_grp + 1) * SQ] = y[2 * SP:2 * SP + SQ]
    return (yp, ys)
```

```python
import numpy as np
import concourse.bass as bass
import concourse.mybir as mybir
from concourse.bass_utils import run_bass_kernel_spmd
from contextlib import ExitStack

F32 = mybir.dt.float32
BF16 = mybir.dt.bfloat16
I32 = mybir.dt.int32
I16 = mybir.dt.int16
ACT = mybir.ActivationFunctionType
ALU = mybir.AluOpType

D = 1024
KC = 8
DFF = 2816
NFC = 22
EPS = 1e-6


class Buf:
    __slots__ = ("name", "w", "r")

    def __init__(self, name=""):
        self.name = name
        self.w = None
        self.r = {}


class EngW:
    def __init__(self, name, eng, sid, sem, inorder=False):
        self.name = name
        self.eng = eng
        self.sid = sid
        self.sem = sem
        self.cnt = 0
        self.known = {}
        self.inorder = inorder
        self.ring = []
        self.ring_pos = 0


class KB:
    def __init__(self, nc, nring=20):
        self.nc = nc
        self.es = ExitStack()
        self.sems = []
        self.semcnt = []
        self.engs = {}
        for name, eng, inorder in (("pe", nc.tensor, True), ("act", nc.scalar, False), ("dve", nc.vector, False),
                                   ("pool", nc.gpsimd, False), ("sp", nc.sync, False)):
            sid = self._newsem("c_" + name)
            self.engs[name] = EngW(name, eng, sid, self.sems[sid], inorder)
        for q in ("sp", "pool", "act"):
            E = self.engs[q]
            for i in range(nring):
                E.ring.append(self._newsem("d_%s%d" % (q, i)))
        self.pe, self.act, self.dve, self.pool, self.sp = (self.engs[n] for n in ("pe", "act", "dve", "pool", "sp"))
        self.bg_ring = [self._newsem("d_bg%d" % i) for i in range(24)]
        self.bg_pos = 0
        self.bg_sids = set(self.bg_ring)

    def _newsem(self, name):
        s = self.es.enter_context(self.nc.semaphore(name))
        self.sems.append(s)
        self.semcnt.append(0)
        return len(self.sems) - 1

    def _waits(self, E, reads, writes, extra=()):
        need = {}
        for b in reads:
            if b.w is not None and need.get(b.w[0], 0) < b.w[1]:
                need[b.w[0]] = b.w[1]
        for b in writes:
            if b.w is not None and need.get(b.w[0], 0) < b.w[1]:
                need[b.w[0]] = b.w[1]
            for sid, val in b.r.items():
                if need.get(sid, 0) < val:
                    need[sid] = val
        for sid, val in extra:
            if need.get(sid, 0) < val:
                need[sid] = val
        for sid, val in need.items():
            if sid == E.sid and E.inorder:
                continue
            if E.known.get(sid, 0) >= val:
                continue
            E.eng.wait_ge(self.sems[sid], val)
            E.known[sid] = val

    def op(self, E, emit, reads=(), writes=()):
        self._waits(E, reads, writes)
        ins = emit(E.eng)
        E.cnt += 1
        ins.then_inc(E.sem, 1)
        self.semcnt[E.sid] = E.cnt
        for b in reads:
            if b.r.get(E.sid, 0) < E.cnt:
                b.r[E.sid] = E.cnt
        for b in writes:
            b.w = (E.sid, E.cnt)
            b.r = {}

    def dma(self, Q, out, in_, reads=(), writes=(), bg=False, **kw):
        if bg:
            sid = self.bg_ring[self.bg_pos]
            self.bg_pos = (self.bg_pos + 1) % len(self.bg_ring)
        else:
            sid = Q.ring[Q.ring_pos]
            Q.ring_pos = (Q.ring_pos + 1) % len(Q.ring)
        prev = self.semcnt[sid]
        self._waits(Q, reads, writes, extra=((sid, prev),) if prev else ())
        ins = Q.eng.dma_start(out=out, in_=in_, **kw)
        self.semcnt[sid] = prev + 16
        ins.then_inc(self.sems[sid], 16)
        val = prev + 16
        for b in reads:
            if b.r.get(sid, 0) < val:
                b.r[sid] = val
        for b in writes:
            b.w = (sid, val)
            b.r = {}

    def barrier(self, full=False):
        for E in self.engs.values():
            for sid in range(len(self.sems)):
                val = self.semcnt[sid]
                if sid in self.bg_sids and not full:
                    continue
                if val and sid != E.sid and E.known.get(sid, 0) < val:
                    E.eng.wait_ge(self.sems[sid], val)
                    E.known[sid] = val
            if E.cnt and not E.inorder and E.known.get(E.sid, 0) < E.cnt:
                E.eng.wait_ge(E.sem, E.cnt)
                E.known[E.sid] = E.cnt

    def mm(self, out, lhsT, rhs, start, stop, reads, writes, **kw):
        self.op(self.pe, lambda e: e.matmul(out, lhsT, rhs, start=start, stop=stop, **kw), reads, writes)

    def actf(self, out, in_, func, reads, writes, **kw):
        self.op(self.act, lambda e: e.activation(out, in_, func, **kw), reads, writes)

    def tt(self, E, out, in0, in1, op, reads, writes):
        self.op(E, lambda e: e.tensor_tensor(out, in0, in1, op), reads, writes)

    def ts(self, E, out, in0, s1, s2, op0, op1, reads, writes):
        if op1 is None:
            self.op(E, lambda e: e.tensor_scalar(out, in0, s1, None, op0), reads, writes)
        else:
            self.op(E, lambda e: e.tensor_scalar(out, in0, s1, s2, op0, op1), reads, writes)

    def stt(self, out, in0, scalar, in1, op0, op1, reads, writes):
        self.op(self.dve, lambda e: e.scalar_tensor_tensor(out, in0, scalar, in1, op0, op1), reads, writes)

    def copy(self, E, out, in_, reads, writes):
        if E is self.act:
            self.op(E, lambda e: e.copy(out, in_), reads, writes)
        else:
            self.op(E, lambda e: e.tensor_copy(out, in_), reads, writes)


class Cfg:
    def __init__(self, seg_tokens=(4096, 4096, 4096), depth=4, do_mixer=True, n_cores=8, group=4):
        self.seg_tokens = tuple(seg_tokens)
        self.ntok = sum(seg_tokens)
        self.depth = depth
        self.do_mixer = 3 if do_mixer is True else int(do_mixer)
        self.nffn = depth * 2
        self.n_cores = n_cores
        self.ev_stop = 0
        self.a2_stop = 0
        self.no_xg = 0
        self.cc_max = 4 * 1024 * 1024
        self.group = group
        self.replica_groups = [list(range(g * group, (g + 1) * group)) for g in range(n_cores // group)]


def build(cfg):
    nc = bass.Bass("TRN2", target_bir_lowering=False)
    L = cfg.depth
    NF = cfg.nffn
    NT = cfg.ntok

    def din(name, shape, dt=F32):
        return nc.dram_tensor(name, list(shape), dt, kind="ExternalInput").ap()

    def dscr(name, shape, dt):
        return nc.dram_tensor(name, list(shape), dt, kind="Internal").ap()

    xin = din("xin", [NT, D])
    c3 = din("c3", [128, KC, 4])
    consts = din("consts", [128, 256])
    ada_w = din("ada_w", [L * 72, 128, KC, 128])
    ada_b = din("ada_b", [128, L * 72])
    npre = din("npre", [128, L * 3 * KC])
    npost = din("npost", [128, L * 3 * KC])
    w13 = din("w13", [NF, NFC, 128, KC * 256])
    w2 = din("w2", [NF, KC, 128, NFC * 128])
    yout = nc.dram_tensor("yout", [NT, D], F32, kind="ExternalOutput").ap()
    NOD = L // 2
    NEV = (L + 1) // 2
    SQ = cfg.seg_tokens[2]
    GRP = cfg.group
    iconst = din("iconst", [128, 8], I32)
    if NOD:
        od_win = din("od_win", [NOD, 128, KC * 1536])
        od_wout = din("od_wout", [NOD, 128, KC * 1024])
        sgu_wsT = din("sgu_wsT", [NOD, 128, 512])
        sgu_b = din("sgu_b", [NOD, 1, 512])
        sgu_nrm = din("sgu_nrm", [NOD, 128, 512])
        od_win_b = dscr("od_win_b", [NOD, 128, KC * 1536], BF16)
        od_wout_b = dscr("od_wout_b", [NOD, 128, KC * 1024], BF16)
        sgu_wsT_b = dscr("sgu_wsT_b", [NOD, 128, 512], BF16)
        sgu_b_b = dscr("sgu_b_b", [NOD, 1, 512], BF16)
    SMAXL = max(cfg.seg_tokens)
    tconst = din("tconst", [128, 516])
    fconst = din("fconst", [128, 64])
    if NEV:
        ev_win1 = din("ev_win1", [NEV, 128, KC * 1056])
        ev_win2 = din("ev_win2", [NEV, 128, KC * 1088])
        ev_woutg = din("ev_woutg", [NEV, 128, 4 * 1024])
        ev_woutm = din("ev_woutm", [NEV, 64, 8 * 1024])
        gla_wal = din("gla_wal", [NEV, 33, 512])
        gla_nrm = din("gla_nrm", [NEV, 128, 1])
        mla_qn = din("mla_qn", [NEV, 128, 2])
        mla_kvn = din("mla_kvn", [NEV, 128, 1])
        mla_wqb = din("mla_wqb", [NEV, 128, 2 * 1536])
        mla_wkvb = din("mla_wkvb", [NEV, 128, 1024])
        ev_win1_b = dscr("ev_win1_b", [NEV, 128, KC * 1056], BF16)
        ev_win2_b = dscr("ev_win2_b", [NEV, 128, KC * 1088], BF16)
        ev_woutg_b = dscr("ev_woutg_b", [NEV, 128, 4 * 1024], BF16)
        ev_woutm_b = dscr("ev_woutm_b", [NEV, 64, 8 * 1024], BF16)
        gla_wal_b = dscr("gla_wal_b", [NEV, 33, 512], BF16)
        mla_wqb_b = dscr("mla_wqb_b", [NEV, 128, 2 * 1536], BF16)
        mla_wkvb_b = dscr("mla_wkvb_b", [NEV, 128, 1024], BF16)
    NCH = SMAXL // 64
    gq = dscr("gq", [4, 2, 128, SMAXL], BF16)
    gvt = dscr("gvt", [SMAXL // 128, 128, 512], BF16)
    gkv = dscr("gkv", [2, NCH, 2, 128, 128], F32)
    gs = dscr("gs", [2, NCH, 2, 128, 128], BF16)
    gg = dscr("gg", [512, SMAXL], BF16)
    gsum = dscr("gsum", [4 * 128, 129], F32)
    gsum_all = dscr("gsum_all", [GRP * 4 * 128, 129], F32)
    Qd = dscr("Qd", [8, 96, SMAXL], BF16)
    Kd = dscr("Kd", [8 * 96, SMAXL], BF16)
    Kall = dscr("Kall", [8, GRP * 96, SQ], BF16)
    Vd = dscr("Vd", [8 * 128, (SMAXL // 128) * 65], BF16)
    Vall = dscr("Vall", [8, GRP * 128, (SQ // 128) * 65], BF16)
    mixm = dscr("mixm", [8, 64, SMAXL], BF16)
    Ud = dscr("Ud", [SMAXL, 1024], BF16)
    CC_MAX = cfg.cc_max
    RCU = min(SQ, max(128, (CC_MAX // (GRP * 2048)) // 128 * 128))
    NUC = SQ // RCU
    Uall = dscr("Uall", [NUC, GRP * RCU, 1024], BF16)
    mixo = dscr("mixo", [1024, SMAXL], BF16)

    w13b = dscr("w13b", [NF, NFC, 128, KC * 256], BF16)
    w2b = dscr("w2b", [NF, KC, 128, NFC * 128], BF16)
    modv = dscr("modv", [3, 128, L * 3 * 3 * KC], F32)

    K = KB(nc)
    es = K.es
    pe, act, dve, pool, sp = K.pe, K.act, K.dve, K.pool, K.sp

    uid = [0]

    def sb(name, shape, dt, stack=es):
        uid[0] += 1
        return stack.enter_context(nc.sbuf_tensor("%s_u%d" % (name, uid[0]), list(shape), dt))

    psb = [es.enter_context(nc.psum_tensor("ps%d" % i, [128, 512], F32)) for i in range(8)]
    PB = [Buf("ps%d" % i) for i in range(8)]

    SMAX = max(cfg.seg_tokens)
    xT = sb("xT", [128, KC, SMAX], F32)
    XB = [Buf("x%d" % i) for i in range(SMAX // 512)]
    cst = sb("cst", [128, 256], F32)
    onesb = sb("onesb", [128, 128], BF16)
    mv = sb("mv", [128, L * 3 * 3 * KC], F32)
    Bcst, Bones, Bmv = Buf("cst"), Buf("ones"), Buf("mv")
    ident = cst[:, 0:128]

    K.dma(sp, cst[:], consts, writes=[Bcst])
    K.copy(dve, onesb[:], cst[:, 128:256], [Bcst], [Bones])

    WB13 = [Buf("w13b%d" % f) for f in range(NF)]
    WB2 = [Buf("w2b%d" % f) for f in range(NF)]
    late_conv = []
    for f in range(NF):
        def cv(f=f):
            K.dma(pool, w13b[f], w13[f], writes=[WB13[f]], bg=True, max_dma_last_dim=4096)
            K.dma(pool, w2b[f], w2[f], writes=[WB2[f]], bg=True, max_dma_last_dim=4096)
        if f == 0:
            cv()
        else:
            late_conv.append(cv)

    Bodw = Buf("odw")
    if NOD:
        def cvo():
            for src, dst in ((od_win, od_win_b), (od_wout, od_wout_b), (sgu_wsT, sgu_wsT_b), (sgu_b, sgu_b_b)):
                K.dma(pool, dst, src, writes=[Bodw], bg=True, max_dma_last_dim=4096)
        late_conv.append(cvo)
    Bevw = Buf("evw")
    if NEV:
        for src, dst in ((ev_win1, ev_win1_b), (ev_win2, ev_win2_b), (ev_woutg, ev_woutg_b), (ev_woutm, ev_woutm_b),
                         (gla_wal, gla_wal_b), (mla_wqb, mla_wqb_b), (mla_wkvb, mla_wkvb_b)):
            K.dma(pool, dst, src, writes=[Bevw], bg=True, max_dma_last_dim=4096)
    icst = sb("icst", [128, 8], I32)
    Bic = Buf("icst")
    K.dma(sp, icst[:], iconst, writes=[Bic])

    Bmodv = Buf("modv")
    with ExitStack() as ps:
        ccT = sb("ccT", [128, KC, 4], F32, ps)
        adab = sb("adab", [128, L * 72], F32, ps)
        gpre = sb("gpre", [128, L * 3 * KC], F32, ps)
        gpost = sb("gpost", [128, L * 3 * KC], F32, ps)
        mfm = sb("mfm", [128, L * 72, 4], F32, ps)
        mvall = sb("mvall", [128, 3, L * 3 * 3 * KC], F32, ps)
        NAW = 4
        awt = [sb("awt%d" % i, [128, KC, 128], F32, ps) for i in range(NAW)]
        Bcc, Badab, Bgpre, Bgpost, Bmfm, Bmvall = (Buf(n) for n in ("cc", "adab", "gpre", "gpost", "mfm", "mvall"))
        Bawt = [Buf("awt%d" % i) for i in range(NAW)]
        K.dma(sp, ccT[:], c3, writes=[Bcc])
        K.dma(sp, adab[:], ada_b, writes=[Badab])
        K.dma(sp, gpre[:], npre, writes=[Bgpre])
        K.dma(sp, gpost[:], npost, writes=[Bgpost])
        K.actf(ccT[:], ccT[:], ACT.Silu, [Bcc], [Bcc])
        for t in range(L * 72):
            wt, Bw = awt[t % NAW], Bawt[t % NAW]
            K.dma(sp, wt[:], ada_w[t], writes=[Bw])
            bank = 7 - (t % 2)
            for kc in range(KC):
                K.mm(psb[bank][:, 0:4], wt[:, kc, :], ccT[:, kc, :], kc == 0, kc == KC - 1, [Bw, Bcc], [PB[bank]])
            K.ts(dve, mfm[:, t, :], psb[bank][:, 0:4], adab[:, t:t + 1], None, ALU.add, None,
                 [PB[bank], Badab], [Bmfm])
        gp4 = gpost[:].rearrange("p (l j c) -> p l j c", l=L, j=3)
        for j in (0, 2):
            K.ts(dve, gp4[:, :, j, :], gp4[:, :, j, :], 0.5, None, ALU.mult, None, [Bgpost], [Bgpost])
        mf5 = mfm[:].rearrange("p (l j t c) b -> p l j t c b", l=L, j=3, t=3)
        mv5 = mvall[:].rearrange("p b (l j v c) -> p b l j v c", l=L, j=3, v=3)
        gpr4 = gpre[:].rearrange("p (l j c) -> p l j c", l=L, j=3)
        for b in range(3):
            for l in range(L):
                for j in range(3):
                    K.stt(mv5[:, b, l, j, 0, :], mf5[:, l, j, 1, :, b], 1.0, gpr4[:, l, j, :], ALU.add, ALU.mult,
                          [Bmfm, Bgpre], [Bmvall])
                    K.copy(dve, mv5[:, b, l, j, 1, :], mf5[:, l, j, 0, :, b], [Bmfm], [Bmvall])
                    K.stt(mv5[:, b, l, j, 2, :], mf5[:, l, j, 2, :, b], 1.0, gp4[:, l, j, :], ALU.add, ALU.mult,
                          [Bmfm, Bgpost], [Bmvall])
        K.dma(sp, modv.rearrange("b p n -> p b n"), mvall[:], reads=[Bmvall], writes=[Bmodv])
        K.barrier()
    for cv in late_conv:
        cv()

    def vec(l, j, v):
        o = ((l * 3 + j) * 3 + v) * KC
        return mv[:, o:o + KC]

    def load_segment(tok0, S, stack):
        xtok = [sb("xtok%d" % i, [128, D], F32, stack) for i in range(2)]
        Bxt = [Buf("xtok%d" % i) for i in range(2)]
        for i in range(S // 128):
            xt_, Bx = xtok[i % 2], Bxt[i % 2]
            K.dma(sp, xt_[:], xin[tok0 + i * 128: tok0 + (i + 1) * 128, :], writes=[Bx])
            for hh in range(2):
                bank = (2 * i + hh) % 4
                for q in range(4):
                    kc = hh * 4 + q
                    K.op(pe, lambda e, kc=kc, q=q, bank=bank: e.transpose(psb[bank][:, q * 128:(q + 1) * 128],
                                                                           xt_[:, kc * 128:(kc + 1) * 128], ident),
                         [Bx, Bcst], [PB[bank]])
                dst = xT[:, hh * 4:(hh + 1) * 4, i * 128:(i + 1) * 128]
                src = psb[bank][:, :].rearrange("p (q t) -> p q t", q=4)
                K.copy(act if hh == 0 else dve, dst, src, [PB[bank]], [XB[i // 4]])

    def store_segment(tok0, S, stack):
        yt = [sb("ytok%d" % i, [128, D], F32, stack) for i in range(2)]
        Byt = [Buf("ytok%d" % i) for i in range(2)]
        for i in range(S // 128):
            y_, By = yt[i % 2], Byt[i % 2]
            for hh in range(2):
                bank = (2 * i + hh) % 4
                for q in range(4):
                    kc = hh * 4 + q
                    K.op(pe, lambda e, kc=kc, q=q, bank=bank: e.transpose(psb[bank][:, q * 128:(q + 1) * 128],
                                                                           xT[:, kc, i * 128:(i + 1) * 128], ident),
                         [XB[i // 4], Bcst], [PB[bank]])
                K.copy(act if hh == 0 else dve, y_[:, hh * 512:(hh + 1) * 512], psb[bank][:, :], [PB[bank]], [By])
            K.dma(sp, yout[tok0 + i * 128: tok0 + (i + 1) * 128, :], y_[:], reads=[By])

    class FfnBufs:
        pass

    def ffn_alloc(stack):
        fb = FfnBufs()
        fb.h = sb("f_h", [128, KC, 512], BF16, stack)
        fb.g = sb("f_g", [128, NFC, 512], BF16, stack)
        fb.y = sb("f_y", [128, KC, 512], F32, stack)
        fb.s = sb("f_s", [128, 512], F32, stack)
        fb.w13 = [sb("f_w13_%d" % i, [128, KC, 256], BF16, stack) for i in range(3)]
        fb.w2 = [sb("f_w2_%d" % i, [128, 11, 128], BF16, stack) for i in range(3)]
        fb.rstd = [sb("f_rstd%d" % i, [128, 512], F32, stack) for i in range(2)]
        fb.sq = [sb("f_sq%d" % i, [128, 512], BF16, stack) for i in range(1)]
        fb.tmp = [sb("f_tmp%d" % i, [128, 512], F32, stack) for i in range(1)]
        fb.Bh, fb.By, fb.Bs = Buf("h"), Buf("y"), Buf("s")
        fb.Bg = [Buf("g%d" % i) for i in range(NFC)]
        fb.Bw13 = [Buf() for _ in range(3)]
        fb.Bw2 = [Buf() for _ in range(3)]
        fb.Brstd = [Buf(), Buf()]
        fb.Bsq = [Buf(), Buf()]
        fb.Btmp = [Buf(), Buf()]
        fb.n13 = 0
        fb.n2 = 0
        fb.nsq = 0
        fb.ntmp = 0
        return fb

    def rstd_from_ss(fb, ri, bank):
        K.actf(fb.rstd[ri][:], psb[bank][:, :], ACT.Sqrt, [PB[bank]], [fb.Brstd[ri]], scale=1.0 / D, bias=EPS)
        K.op(dve, lambda e: e.reciprocal(fb.rstd[ri][:], fb.rstd[ri][:]), [fb.Brstd[ri]], [fb.Brstd[ri]])

    def prenorm_steps(fb, l, j, tt, hdst, Bh):
        tsl = slice(tt * 512, (tt + 1) * 512)
        A, Bv = vec(l, j, 0), vec(l, j, 1)
        steps = []

        def p0():
            for kc in range(KC):
                i = fb.nsq % len(fb.sq)
                fb.nsq += 1
                K.actf(fb.sq[i][:], xT[:, kc, tsl], ACT.Square, [XB[tt]], [fb.Bsq[i]])
                K.mm(psb[6][:, :], onesb[:], fb.sq[i][:], kc == 0, kc == KC - 1, [Bones, fb.Bsq[i]], [PB[6]])
        steps.append(p0)
        steps.append(lambda: rstd_from_ss(fb, 0, 6))
        for kc in range(KC):
            def pk(kc=kc):
                i = fb.ntmp % len(fb.tmp)
                fb.ntmp += 1
                K.tt(dve, fb.tmp[i][:], xT[:, kc, tsl], fb.rstd[0][:], ALU.mult, [XB[tt], fb.Brstd[0]], [fb.Btmp[i]])
                K.actf(hdst[:, kc, :], fb.tmp[i][:], ACT.Identity, [fb.Btmp[i], Bmv], [Bh],
                       scale=A[:, kc:kc + 1], bias=Bv[:, kc:kc + 1])
            steps.append(pk)
        return steps

    def yphase(fb, tt, Cg, mm_oc, nxt, pending_tail):
        tsl = slice(tt * 512, (tt + 1) * 512)
        prev_sq = None
        for oc in range(KC):
            bank = 4 + oc % 2
            mm_oc(oc, bank)
            if prev_sq is not None:
                po, pi = prev_sq
                K.mm(psb[7][:, :], onesb[:], fb.sq[pi][:], po == 0, False, [Bones, fb.Bsq[pi]], [PB[7]])
            K.copy(act, fb.y[:, oc, :], psb[bank][:, :], [PB[bank]], [fb.By])
            i = fb.nsq % len(fb.sq)
            fb.nsq += 1
            K.actf(fb.sq[i][:], psb[bank][:, :], ACT.Square, [PB[bank]], [fb.Bsq[i]])
            prev_sq = (oc, i)
            if nxt:
                nxt.pop(0)()
        po, pi = prev_sq
        K.mm(psb[7][:, :], onesb[:], fb.sq[pi][:], False, True, [Bones, fb.Bsq[pi]], [PB[7]])
        while nxt:
            nxt.pop(0)()
        pending_tail.append(lambda: rstd_from_ss(fb, 1, 7))
        for oc in range(KC):
            def tl(oc=oc):
                i = fb.ntmp % len(fb.tmp)
                fb.ntmp += 1
                K.tt(dve, fb.tmp[i][:], fb.y[:, oc, :], fb.rstd[1][:], ALU.mult, [fb.By, fb.Brstd[1]],
                     [fb.Btmp[i]])
                K.stt(xT[:, oc, tsl], fb.tmp[i][:], Cg[:, oc:oc + 1], xT[:, oc, tsl], ALU.mult, ALU.add,
                      [fb.Btmp[i], Bmv, XB[tt]], [XB[tt]])
            pending_tail.append(tl)

    def ffn_sublayer(fb, l, j, S):
        f = l * 2 + (0 if j == 0 else 1)
        ntile = S // 512
        Cg = vec(l, j, 2)
        pending_tail = []
        for st in prenorm_steps(fb, l, j, 0, fb.h, fb.Bh):
            st()
        for tt in range(ntile):
            tsl = slice(tt * 512, (tt + 1) * 512)
            for fc in range(NFC):
                r = fb.n13 % 3
                fb.n13 += 1
                K.dma(sp, fb.w13[r][:], w13b[f, fc].rearrange("p (k n) -> p k n", k=KC), reads=[WB13[f]],
                      writes=[fb.Bw13[r]])
                ba, bb = fc % 2, 2 + fc % 2
                for half, bank in ((0, ba), (1, bb)):
                    for kc in range(KC):
                        K.mm(psb[bank][:, :], fb.w13[r][:, kc, half * 128:(half + 1) * 128], fb.h[:, kc, :],
                             kc == 0, kc == KC - 1, [fb.Bw13[r], fb.Bh], [PB[bank]])
                K.actf(fb.s[:], psb[ba][:, :], ACT.Silu, [PB[ba]], [fb.Bs])
                K.tt(dve, fb.g[:, fc, :], fb.s[:], psb[bb][:, :], ALU.mult, [fb.Bs, PB[bb]], [fb.Bg[fc]])
                if pending_tail:
                    pending_tail.pop(0)()
            while pending_tail:
                pending_tail.pop(0)()
            nxt = prenorm_steps(fb, l, j, tt + 1, fb.h, fb.Bh) if tt + 1 < ntile else []
            if nxt:
                nxt.pop(0)()
            def mm_oc(oc, bank, f=f):
                for hf in range(2):
                    r = fb.n2 % 3
                    fb.n2 += 1
                    K.dma(sp, fb.w2[r][:],
                          w2b[f, oc].rearrange("p (k n) -> p k n", k=NFC)[:, hf * 11:(hf + 1) * 11, :],
                          reads=[WB2[f]], writes=[fb.Bw2[r]])
                    for q in range(11):
                        fc = hf * 11 + q
                        K.mm(psb[bank][:, :], fb.w2[r][:, q, :], fb.g[:, fc, :], fc == 0, fc == NFC - 1,
                             [fb.Bw2[r], fb.Bg[fc]], [PB[bank]])
            yphase(fb, tt, Cg, mm_oc, nxt, pending_tail)
        while pending_tail:
            pending_tail.pop(0)()


    def mx_alloc(stack, with_h=True, with_y=False, nrstd=2):
        fb = FfnBufs()
        if with_h:
            fb.h = sb("m_h", [128, KC, 512], BF16, stack)
        if with_y:
            fb.y = sb("m_y", [128, KC, 512], F32, stack)
        fb.rstd = [sb("m_rstd%d" % i, [128, 512], F32, stack) for i in range(nrstd)]
        fb.sq = [sb("m_sq%d" % i, [128, 512], BF16, stack) for i in range(2)]
        fb.tmp = [sb("m_tmp%d" % i, [128, 512], F32, stack) for i in range(2)]
        fb.Bh, fb.By = Buf("h"), Buf("y")
        fb.Brstd = [Buf(), Buf()]
        fb.Bsq = [Buf(), Buf()]
        fb.Btmp = [Buf(), Buf()]
        fb.nsq = 0
        fb.ntmp = 0
        return fb

    class Gen:
        pass

    def gen_alloc(stack, mask_col, use_pjx, nA=2, blocks=True):
        G = Gen()
        G.PJ = sb("g_pj", [128, 512], I32, stack)
        G.A = [sb("g_a%d" % i, [128, 512], I32, stack) for i in range(nA)]
        G.BPJ = Buf("pj")
        G.BA = [Buf() for _ in range(nA)]
        G.n = 0
        G.mask = icst[:, mask_col:mask_col + 1]
        K.op(pool, lambda e: e.iota(G.PJ[:], [[1, 512]], base=0, channel_multiplier=0), [], [G.BPJ])
        K.op(pool, lambda e: e.iota(G.A[0][:], [[0, 512]], base=0, channel_multiplier=1), [], [G.BA[0]])
        if blocks:
            G.Jf = sb("g_jf", [128, 512], I32, stack)
            G.BJf = Buf()
            K.copy(dve, G.Jf[:], G.PJ[:], [G.BPJ], [G.BJf])
        K.op(pool, lambda e: e.tensor_tensor(G.PJ[:], G.PJ[:], G.A[0][:], ALU.mult), [G.BPJ, G.BA[0]], [G.BPJ])
        if blocks:
            G.Pf = sb("g_pf", [128, 512], I32, stack)
            G.PJ0 = sb("g_pj0", [128, 512], I32, stack)
            G.BPf, G.BPJ0 = Buf(), Buf()
            K.copy(dve, G.Pf[:], G.A[0][:], [G.BA[0]], [G.BPf])
        if use_pjx:
            K.op(pool, lambda e: e.iota(G.A[0][:], [[0, 512]], base=0, channel_multiplier=0), [], [G.BA[0]])
            K.op(pool, lambda e: e.tensor_scalar(G.A[0][:], G.A[0][:], icst[:, 4:5], None, ALU.add),
                 [G.BA[0], Bic], [G.BA[0]])
            K.op(pool, lambda e: e.tensor_tensor(G.PJ[:], G.PJ[:], G.A[0][:], ALU.add), [G.BPJ, G.BA[0]], [G.BPJ])
        if blocks:
            K.copy(dve, G.PJ0[:], G.PJ[:], [G.BPJ], [G.BPJ0])
        return G

    def gen_block(G, S_tot, sp0):
        K.op(pool, lambda e: e.tensor_scalar(G.PJ[:], G.Pf[:], int(sp0 % S_tot), None, ALU.mult), [G.BPf], [G.BPJ])
        K.op(pool, lambda e: e.tensor_tensor(G.PJ[:], G.PJ[:], G.PJ0[:], ALU.add), [G.BPJ, G.BPJ0], [G.BPJ])

    def gen_tile(G, dst, Bdst, S_tot, s0, sp0, off):
        base = (s0 * sp0 + off) % S_tot
        step = s0 % S_tot
        i = G.n % len(G.A)
        G.n += 1
        A = G.A[i]
        if step == 0:
            K.op(pool, lambda e: e.iota(A[:], [[0, 512]], base=base, channel_multiplier=0), [], [G.BA[i]])
        else:
            K.op(pool, lambda e: e.tensor_scalar(A[:], G.Jf[:], int(step), int(base), ALU.mult, ALU.add), [G.BJf],
                 [G.BA[i]])
        K.op(pool, lambda e: e.tensor_tensor(A[:], A[:], G.PJ[:], ALU.add), [G.BA[i], G.BPJ], [G.BA[i]])
        K.op(dve, lambda e: e.tensor_scalar(A[:], A[:], G.mask, None, ALU.bitwise_and), [G.BA[i], Bic], [G.BA[i]])
        K.actf(dst, A[:], ACT.Sin, [G.BA[i]], [Bdst], scale=2.0 * np.pi / S_tot, bias=-np.pi)

    BUd, BUall, Bmixo = Buf("Ud"), Buf("Uall"), Buf("mixo")

    def odd_phase_a(l, S, stack):
        i_od = l // 2
        fb = mx_alloc(stack, nrstd=1)
        win = sb("o_win", [128, KC, 1536], BF16, stack)
        wsT = sb("o_wsT", [128, 512], BF16, stack)
        sgb = sb("o_sgb", [1, 512], BF16, stack)
        nrm = sb("o_nrm", [128, 512], F32, stack)
        ccsc = sb("o_ccsc", [128, 256], BF16, stack)
        gtmp = sb("o_gtmp", [128, 512], BF16, stack)
        zcT = sb("o_zcT", [128, 4, 512], BF16, stack)
        usb = [sb("o_usb%d" % i, [128, 1024], BF16, stack) for i in range(2)]
        uT = sb("o_uT", [128, 4, 512], F32, stack)
        gv = sb("o_gv", [128, 512], F32, stack)
        vtok = [sb("o_vtok%d" % i, [128, 512], BF16, stack) for i in range(2)]
        odT = sb("o_odT", [128, 4, 512], BF16, stack)
        ssq = sb("o_ssq", [128, 2], F32, stack)
        Bwin, BwsT, Bsgb, Bnrm, Bccsc, Bgtmp, BzcT, BuT, Bgv, BodT, Bssq = (Buf() for _ in range(11))
        Busb = [Buf(), Buf()]
        Bvtok = [Buf(), Buf()]
        K.dma(sp, win[:], od_win_b[i_od].rearrange("p (k n) -> p k n", k=KC), reads=[Bodw], writes=[Bwin])
        K.dma(sp, wsT[:], sgu_wsT_b[i_od], reads=[Bodw], writes=[BwsT])
        K.dma(sp, sgb[:], sgu_b_b[i_od], reads=[Bodw], writes=[Bsgb])
        K.dma(sp, nrm[:], sgu_nrm[i_od], writes=[Bnrm])
        G = gen_alloc(stack, 2, False, nA=1, blocks=False)
        gen_tile(G, gtmp[:], Bgtmp, 128, 0, 0, 96)
        K.copy(dve, ccsc[:, 0:128], gtmp[:, 0:128], [Bgtmp], [Bccsc])
        gen_tile(G, gtmp[:], Bgtmp, 128, 0, 0, 0)
        K.copy(dve, ccsc[:, 128:256], gtmp[:, 0:128], [Bgtmp], [Bccsc])
        mixo_v = mixo.rearrange("(c p) s -> p c s", p=128)
        nb = [0]

        def bank2():
            nb[0] += 1
            return nb[0] % 2

        for tt in range(S // 512):
            for st in prenorm_steps(fb, l, 1, tt, fb.h, fb.Bh):
                st()
            for g in range(4):
                bank = bank2()
                for kc in range(KC):
                    K.mm(psb[bank][:, :], win[:, kc, g * 128:(g + 1) * 128], fb.h[:, kc, :], kc == 0, kc == KC - 1,
                         [Bwin, fb.Bh], [PB[bank]])
                K.copy(act if g % 2 == 0 else dve, zcT[:, g, :], psb[bank][:, :], [PB[bank]], [BzcT])
            for g in range(4):
                bank = bank2()
                for kc in range(KC):
                    K.mm(psb[bank][:, :], win[:, kc, 512 + g * 128:512 + (g + 1) * 128], fb.h[:, kc, :], kc == 0,
                         kc == KC - 1, [Bwin, fb.Bh], [PB[bank]])
                K.actf(uT[:, g, :], psb[bank][:, :], ACT.Gelu, [PB[bank]], [BuT])
            for ts in range(4):
                tk = slice(ts * 128, (ts + 1) * 128)
                ub, Bub = usb[ts % 2], Busb[ts % 2]
                for gp in range(2):
                    bank = 2 + gp
                    for gg in range(2):
                        g = gp * 2 + gg
                        K.mm(psb[bank][:, gg * 256:(gg + 1) * 256], zcT[:, g, tk], ccsc[:], True, True,
                             [BzcT, Bccsc], [PB[bank]])
                    K.copy(act if gp == 0 else dve, ub[:, gp * 512:(gp + 1) * 512], psb[bank][:, :], [PB[bank]],
                           [Bub])
                K.dma(sp, Ud[tt * 512 + ts * 128: tt * 512 + (ts + 1) * 128, :], ub[:], reads=[Bub], writes=[BUd])
                bank = bank2()
                for kc in range(KC):
                    K.mm(psb[bank][:, :], fb.h[:, kc, tk], win[:, kc, 1024:1536], kc == 0, kc == KC - 1,
                         [Bwin, fb.Bh], [PB[bank]])
                K.actf(gv[:], psb[bank][:, :], ACT.Gelu, [PB[bank]], [Bgv])
                vt, Bvt = vtok[ts % 2], Bvtok[ts % 2]
                K.actf(vt[:], gv[:], ACT.Square, [Bgv], [Bvt, Bssq], accum_out=ssq[:, 0:1])
                K.actf(ssq[:, 1:2], ssq[:, 0:1], ACT.Sqrt, [Bssq], [Bssq], scale=1.0 / 512, bias=EPS)
                K.op(dve, lambda e: e.reciprocal(ssq[:, 1:2], ssq[:, 1:2]), [Bssq], [Bssq])
                K.stt(vt[:], gv[:], ssq[:, 1:2], nrm[:], ALU.mult, ALU.mult, [Bgv, Bssq, Bnrm], [Bvt])
                bank = 6
                for hd in range(4):
                    hs = slice(hd * 128, (hd + 1) * 128)
                    K.mm(psb[bank][:, hs], vt[:, hs], wsT[:, hs], True, False, [Bvt, BwsT], [PB[bank]])
                    K.mm(psb[bank][:, hs], onesb[0:1, :], sgb[0:1, hs], False, True, [Bones, Bsgb], [PB[bank]])
                K.tt(dve, odT[:, :, tk], uT[:, :, tk], psb[bank][:, :].rearrange("p (h i) -> p h i", h=4), ALU.mult,
                     [BuT, PB[bank]], [BodT])
            K.dma(sp, mixo_v[:, 4:8, tt * 512:(tt + 1) * 512], odT[:], reads=[BodT], writes=[Bmixo])

    def odd_phase_b(S, S_keys, Usrc, BUsrc, is_sample, stack):
        G = gen_alloc(stack, 1 if is_sample else 0, is_sample)
        ct = [sb("b_ct%d" % i, [128, 512], BF16, stack) for i in range(2)]
        stl = [sb("b_st%d" % i, [128, 512], BF16, stack) for i in range(2)]
        ut = [sb("b_ut%d" % i, [128, 1024], BF16, stack) for i in range(3)]
        fcs = sb("b_fcs", [128, 4, 512], BF16, stack)
        Rc = [sb("b_rc%d" % i, [128, 512], I16, stack) for i in range(2)]
        Rs = [sb("b_rs%d" % i, [128, 512], I16, stack) for i in range(2)]
        R32 = sb("b_r32", [128, 512], I32, stack)
        Di = sb("b_di", [128, 512], I32, stack)
        D16 = sb("b_d16", [128, 512], I16, stack)
        m16 = sb("b_m16", [128, 1], I16, stack)
        Bct, Bst = [Buf(), Buf()], [Buf(), Buf()]
        BRc, BRs = [Buf(), Buf()], [Buf(), Buf()]
        But = [Buf(), Buf(), Buf()]
        Bfcs, BDi, BD16, BR32, Bm16 = Buf(), Buf(), Buf(), Buf(), Buf()
        mixo_v = mixo.rearrange("(c p) s -> p c s", p=128)
        scale = 1.0 / float(np.sqrt(S_keys * 128.0))
        na = S_keys // 128
        sc_sin = 2.0 * np.pi / S_keys
        n = 0
        for bq in range(S // 512):
            sp0 = bq * 512
            gen_block(G, S_keys, sp0)
            K.op(pool, lambda e: e.tensor_scalar(Di[:], G.Jf[:], 128, int((128 * sp0) % S_keys), ALU.mult, ALU.add),
                 [G.BJf], [BDi])
            K.op(dve, lambda e: e.tensor_scalar(Di[:], Di[:], G.mask, None, ALU.bitwise_and), [BDi, Bic], [BDi])
            K.copy(dve, D16[:], Di[:], [BDi], [BD16])
            for R0, BR0, off in ((Rc[0], BRc[0], (3 * S_keys) // 4), (Rs[0], BRs[0], S_keys // 2)):
                K.op(pool, lambda e, off=off: e.tensor_scalar(R32[:], G.PJ[:], int(off), None, ALU.add), [G.BPJ],
                     [BR32])
                K.op(dve, lambda e: e.tensor_scalar(R32[:], R32[:], G.mask, None, ALU.bitwise_and), [BR32, Bic],
                     [BR32])
                K.copy(dve, R0[:], R32[:], [BR32], [BR0])
            for a in range(na):
                s0 = a * 128
                i2, i3 = n % 2, n % 3
                n += 1
                cur, nxt = a % 2, (a + 1) % 2
                K.actf(ct[i2][:], Rc[cur][:], ACT.Sin, [BRc[cur]], [Bct[i2]], scale=sc_sin, bias=-np.pi)
                K.actf(stl[i2][:], Rs[cur][:], ACT.Sin, [BRs[cur]], [Bst[i2]], scale=sc_sin, bias=-np.pi)
                if a + 1 < na:
                    for R_, BR_ in ((Rc, BRc), (Rs, BRs)):
                        K.tt(dve, R_[nxt][:], R_[cur][:], D16[:], ALU.add, [BR_[cur], BD16], [BR_[nxt]])
                        K.op(dve, lambda e, R_=R_, nxt=nxt: e.tensor_scalar(R_[nxt][:], R_[nxt][:], G.mask, None,
                                                                            ALU.bitwise_and), [BR_[nxt], Bic],
                             [BR_[nxt]])
                K.dma(sp, ut[i3][:], Usrc(s0), reads=[BUsrc], writes=[But[i3]])
                for g in range(4):
                    K.mm(psb[g][:, :], ut[i3][:, g * 256:g * 256 + 128], ct[i2][:], a == 0, False,
                         [But[i3], Bct[i2]], [PB[g]])
                    K.mm(psb[g][:, :], ut[i3][:, g * 256 + 128:g * 256 + 256], stl[i2][:], False, a == na - 1,
                         [But[i3], Bst[i2]], [PB[g]])
            for g in range(4):
                if g % 2 == 0:
                    K.actf(fcs[:, g, :], psb[g][:, :], ACT.Copy, [PB[g]], [Bfcs], scale=scale)
                else:
                    K.ts(dve, fcs[:, g, :], psb[g][:, :], scale, None, ALU.mult, None, [PB[g]], [Bfcs])
            K.dma(sp, mixo_v[:, 0:4, bq * 512:(bq + 1) * 512], fcs[:], reads=[Bfcs], writes=[Bmixo])

    def mixer_phase_c(l, S, wout_dram, Bw_dram, stack):
        fb = mx_alloc(stack, with_h=False, with_y=True)
        wo = sb("c_wo", [128, KC, 1024], BF16, stack)
        ot = [sb("c_ot%d" % i, [128, KC, 512], BF16, stack) for i in range(2)]
        Bwo = Buf()
        Bot = [Buf(), Buf()]
        K.dma(sp, wo[:], wout_dram.rearrange("p (k n) -> p k n", k=KC), reads=[Bw_dram], writes=[Bwo])
        mixo_v = mixo.rearrange("(c p) s -> p c s", p=128)
        Cg = vec(l, 1, 2)
        pending = []
        for tt in range(S // 512):
            o_, Bo = ot[tt % 2], Bot[tt % 2]
            K.dma(sp, o_[:], mixo_v[:, :, tt * 512:(tt + 1) * 512], reads=[Bmixo], writes=[Bo])

            def mm_oc(oc, bank, o_=o_, Bo=Bo):
                for ic in range(KC):
                    K.mm(psb[bank][:, :], wo[:, ic, oc * 128:(oc + 1) * 128], o_[:, ic, :], ic == 0, ic == KC - 1,
                         [Bwo, Bo], [PB[bank]])
            yphase(fb, tt, Cg, mm_oc, [], pending)
            while pending:
                pending.pop(0)()

    def odd_mixer(l, S, is_sample):
        i_od = l // 2
        with ExitStack() as st:
            odd_phase_a(l, S, st)
            K.barrier()
        if is_sample and GRP > 1 and not cfg.no_xg:
            for ci in range(NUC):
                K.op(pool, lambda e, ci=ci: e.collective_compute(
                    "AllGather", ALU.bypass, replica_groups=cfg.replica_groups,
                    ins=[Ud[ci * RCU:(ci + 1) * RCU, :]], outs=[Uall[ci]]), [BUd], [BUall])
            K.barrier()

            def usrc(s0):
                g, i = s0 // S, s0 % S
                ci, w = i // RCU, i % RCU
                return Uall[ci, g * RCU + w:g * RCU + w + 128, :]
            Usrc, BUsrc, S_keys = usrc, BUall, GRP * S
        else:
            Usrc, BUsrc, S_keys = (lambda s0: Ud[s0:s0 + 128, :]), BUd, S
        with ExitStack() as st:
            odd_phase_b(S, S_keys, Usrc, BUsrc, is_sample and GRP > 1 and not cfg.no_xg, st)
            K.barrier()
        with ExitStack() as st:
            mixer_phase_c(l, S, od_wout_b[i_od], Bodw, st)
            K.barrier()

    fcst = sb("fcst", [128, 64], F32)
    Bfc = Buf("fcst")
    K.dma(sp, fcst[:], fconst, writes=[Bfc])
    NCHL = SMAXL // 64
    decs = sb("decs", [128, 2, 2, NCHL], F32)
    Bdecs = Buf("decs")
    Bgq, Bgvt, Bgkv, Bgs, Bgg, Bgsum, Bgsall = (Buf() for _ in range(7))
    BQd, BKd, BKall, BVd, BVall, Bmixm = (Buf() for _ in range(6))
    rr = [0]

    def rbank(lo=0, n=2):
        rr[0] += 1
        return lo + rr[0] % n

    def even_a1(l, S, stack):
        i_ev = l // 2
        fb = mx_alloc(stack)
        win = sb("a_win", [128, KC, 1056], BF16, stack)
        wal = sb("a_wal", [33, 512], BF16, stack)
        tc = sb("a_tc", [128, 516], F32, stack)
        alr = sb("a_alr", [33, 512], BF16, stack)
        qk = sb("a_qk", [128, 4, 512], F32, stack)
        spt = sb("a_spt", [128, 512], F32, stack)
        E = sb("a_E", [128, 4, 2, 128], F32, stack)
        ekd = sb("a_ekd", [128, 512], F32, stack)
        kd = sb("a_kd", [128, 512], BF16, stack)
        vtok = [sb("a_vtok%d" % i, [128, 512], BF16, stack) for i in range(2)]
        kvst = sb("a_kvst", [128, 2, 4, 128], F32, stack)
        qst = sb("a_qst", [128, 4, 2, 512], BF16, stack)
        Bwin, Bwal, Btc, Balr, Bqk, Bspt, BE, Bekd, Bkd, Bkvst, Bqst = (Buf() for _ in range(11))
        Bvtok = [Buf(), Buf()]
        K.dma(sp, win[:], ev_win1_b[i_ev].rearrange("p (k n) -> p k n", k=KC), reads=[Bevw], writes=[Bwin])
        K.dma(sp, wal[:], gla_wal_b[i_ev], reads=[Bevw], writes=[Bwal])
        K.dma(sp, tc[:], tconst, writes=[Btc])
        K.op(dve, lambda e: e.memset(alr[32:33, :], 1.0), [], [Balr])
        gq_v = gq.rearrange("k r p s -> p k r s")
        for tt in range(S // 512):
            for st in prenorm_steps(fb, l, 1, tt, fb.h, fb.Bh):
                st()
            for c4 in range(4):
                bank = rbank()
                for kc in range(KC):
                    K.mm(psb[bank][:, :], win[:, kc, c4 * 128:(c4 + 1) * 128], fb.h[:, kc, :], kc == 0, kc == KC - 1,
                         [Bwin, fb.Bh], [PB[bank]])
                K.copy(act if c4 % 2 == 0 else dve, qk[:, c4, :], psb[bank][:, :], [PB[bank]], [Bqk])
            bank = rbank()
            for kc in range(KC):
                K.mm(psb[bank][0:32, :], win[:, kc, 1024:1056], fb.h[:, kc, :], kc == 0, kc == KC - 1,
                     [Bwin, fb.Bh], [PB[bank]])
            K.copy(act, alr[0:32, :], psb[bank][0:32, :], [PB[bank]], [Balr])
            for ts in range(4):
                tk = slice(ts * 128, (ts + 1) * 128)
                n = tt * 4 + ts
                vt, Bvt = vtok[n % 2], Bvtok[n % 2]
                for kc in range(KC):
                    K.mm(psb[2][:, 0:256], fb.h[:, kc, tk], win[:, kc, 256:512], kc == 0, kc == KC - 1,
                         [Bwin, fb.Bh], [PB[2]])
                for kc in range(KC):
                    K.mm(psb[3][:, :], fb.h[:, kc, tk], win[:, kc, 512:1024], kc == 0, kc == KC - 1,
                         [Bwin, fb.Bh], [PB[3]])
                K.copy(act, vt[:], psb[3][:, :], [PB[3]], [Bvt])
                K.dma(sp, gvt[n], vt[:], reads=[Bvt], writes=[Bgvt])
                K.mm(psb[4][:, :], alr[0:33, tk], wal[0:33, :], True, True, [Balr, Bwal], [PB[4]])
                K.actf(spt[:], psb[4][:, :], ACT.Exp, [PB[4]], [Bspt], scale=-1.0)
                K.actf(spt[:], spt[:], ACT.Ln, [Bspt], [Bspt], bias=1.0)
                for pr in range(2):
                    K.mm(psb[5][:, pr * 130:pr * 130 + 130], spt[:, pr * 128:(pr + 1) * 128], tc[:, 0:130], True, True,
                         [Bspt, Btc], [PB[5]])
                for pr in range(2):
                    K.mm(psb[6 + pr][:, 0:258], spt[:, 256 + pr * 128:256 + (pr + 1) * 128], tc[:, 130:388], True,
                         True, [Bspt, Btc], [PB[6 + pr]])
                K.mm(psb[4][:, 0:256], tc[:, 130:258], spt[:, 0:256], True, True, [Bspt, Btc], [PB[4]])
                K.mm(psb[4][:, 256:512], tc[:, 388:516], spt[:, 256:512], True, True, [Bspt, Btc], [PB[4]])
                sc = 1.0 / 16.0
                for pr in range(2):
                    K.actf(E[:, 0, pr, :], psb[5][:, pr * 130:pr * 130 + 128], ACT.Exp, [PB[5]], [BE], scale=-sc)
                    K.actf(E[:, 1, pr, :], psb[5][:, pr * 130:pr * 130 + 128], ACT.Exp, [PB[5]], [BE], scale=sc)
                    K.actf(decs[:, 0, pr, 2 * n:2 * n + 2], psb[5][:, pr * 130 + 128:pr * 130 + 130], ACT.Exp,
                           [PB[5]], [Bdecs], scale=-sc)
                    K.actf(E[:, 2, pr, :], psb[6 + pr][:, 0:128], ACT.Exp, [PB[6 + pr]], [BE], scale=-sc)
                    K.actf(E[:, 3, pr, :], psb[6 + pr][:, 128:256], ACT.Exp, [PB[6 + pr]], [BE], scale=sc)
                    K.actf(decs[:, 1, pr, 2 * n:2 * n + 2], psb[6 + pr][:, 256:258], ACT.Exp, [PB[6 + pr]], [Bdecs],
                           scale=-sc)
                K.actf(ekd[:], psb[4][:, :], ACT.Exp, [PB[4]], [Bekd], scale=-sc)
                K.tt(dve, kd[:, 0:256], psb[2][:, 0:256], ekd[:, 0:256], ALU.mult, [PB[2], Bekd], [Bkd])
                K.tt(dve, kd[:, 256:512], psb[2][:, 0:256], ekd[:, 256:512], ALU.mult, [PB[2], Bekd], [Bkd])
                for pr in range(2):
                    K.stt(qst[:, 0, pr, tk], qk[:, pr, tk], 0.125, E[:, 0, pr, :], ALU.mult, ALU.mult, [Bqk, BE], [Bqst])
                    K.stt(qst[:, 1, pr, tk], qk[:, pr, tk], 0.125, E[:, 2, pr, :], ALU.mult, ALU.mult, [Bqk, BE], [Bqst])
                    K.tt(pool, qst[:, 2, pr, tk], qk[:, 2 + pr, tk], E[:, 1, pr, :], ALU.mult, [Bqk, BE], [Bqst])
                    K.tt(pool, qst[:, 3, pr, tk], qk[:, 2 + pr, tk], E[:, 3, pr, :], ALU.mult, [Bqk, BE], [Bqst])
                for c in range(2):
                    for dr in range(2):
                        for h in range(4):
                            hb = (h % 2) * 64
                            col = (dr * 2 + h // 2) * 128
                            K.mm(psb[c][hb:hb + 64, col:col + 128],
                                 kd[c * 64:(c + 1) * 64, dr * 256 + h * 64:dr * 256 + (h + 1) * 64],
                                 vt[c * 64:(c + 1) * 64, h * 128:(h + 1) * 128], True, True, [Bkd, Bvt], [PB[c]])
                    K.copy(act if c == 0 else dve, kvst[:, :, c * 2:c * 2 + 2, :],
                           psb[c][:, :].rearrange("p (d r v) -> p d r v", d=2, r=2), [PB[c]], [Bkvst])
                for dr in range(2):
                    K.dma(sp, gkv[dr, 2 * n:2 * n + 2].rearrange("c r p v -> p c r v"),
                          kvst[:, dr, :, :].rearrange("p (c r) v -> p c r v", c=2), reads=[Bkvst], writes=[Bgkv])
            K.dma(sp, gq_v[:, :, :, tt * 512:(tt + 1) * 512], qst[:], reads=[Bqst], writes=[Bgq])

    def even_r(S, stack, store, Sin=None):
        nch = S // 64
        CB = min(8, nch)
        St = [[sb("r_st%d%d" % (d_, p_), [128, 128], F32, stack) for p_ in range(2)] for d_ in range(2)]
        BSt = [[Buf(), Buf()], [Buf(), Buf()]]
        kvb = [sb("r_kvb%d" % i, [128, CB, 2, 128], F32, stack) for i in range(2)]
        stb = [sb("r_stb%d" % i, [128, CB, 2, 128], BF16, stack) for i in range(2)]
        Bkvb, Bstb = [Buf(), Buf()], [Buf(), Buf()]
        nb = 0
        for dr in range(2):
            for pr in range(2):
                if Sin is None:
                    K.op(dve, lambda e, dr=dr, pr=pr: e.memset(St[dr][pr][:], 0.0), [], [BSt[dr][pr]])
                else:
                    K.copy(dve, St[dr][pr][:], Sin[dr][pr][0][:], [Sin[dr][pr][1]], [BSt[dr][pr]])
            batches = list(range(0, nch, CB))
            if dr == 1:
                batches = batches[::-1]
            for c0 in batches:
                kb, Bk = kvb[nb % 2], Bkvb[nb % 2]
                sbf, Bs_ = stb[nb % 2], Bstb[nb % 2]
                nb += 1
                K.dma(sp, kb[:], gkv[dr, c0:c0 + CB].rearrange("c r p v -> p c r v"), reads=[Bgkv], writes=[Bk])
                cis = list(range(CB))
                if dr == 1:
                    cis = cis[::-1]
                for ci in cis:
                    c = c0 + ci
                    for pr in range(2):
                        if store:
                            K.copy(act, sbf[:, ci, pr, :], St[dr][pr][:], [BSt[dr][pr]], [Bs_])
                        K.stt(St[dr][pr][:], St[dr][pr][:], decs[:, dr, pr, c:c + 1], kb[:, ci, pr, :], ALU.mult,
                              ALU.add, [BSt[dr][pr], Bdecs, Bk], [BSt[dr][pr]])
                if store:
                    K.dma(sp, gs[dr, c0:c0 + CB].rearrange("c r p v -> p c r v"), sbf[:], reads=[Bs_], writes=[Bgs])
        return St, BSt

    def even_exchange(S, stack):
        nch = S // 64
        St, BSt = even_r(S, stack, False)
        pk = sb("x_pk", [128, 4, 129], F32, stack)
        Bpk = Buf()
        for dr in range(2):
            for pr in range(2):
                k4 = dr * 2 + pr
                K.copy(dve, pk[:, k4, 0:128], St[dr][pr][:], [BSt[dr][pr]], [Bpk])
                K.copy(dve, pk[:, k4, 128:129], decs[:, dr, pr, 0:1], [Bdecs], [Bpk])
                for c in range(1, nch):
                    K.tt(dve, pk[:, k4, 128:129], pk[:, k4, 128:129], decs[:, dr, pr, c:c + 1], ALU.mult,
                         [Bpk, Bdecs], [Bpk])
        K.dma(sp, gsum.rearrange("(k p) v -> p k v", p=128), pk[:], reads=[Bpk], writes=[Bgsum])
        K.barrier()
        K.op(pool, lambda e: e.collective_compute("AllGather", ALU.bypass, replica_groups=cfg.replica_groups,
                                                  ins=[gsum], outs=[gsum_all]), [Bgsum], [Bgsall])
        K.barrier()
        pa = sb("x_pa", [128, GRP, 4, 129], F32, stack)
        Bpa = Buf()
        K.dma(sp, pa[:], gsum_all.rearrange("(g k p) v -> p g k v", g=GRP, p=128), reads=[Bgsall], writes=[Bpa])
        Sin = [[None, None], [None, None]]
        cf = sb("x_cf", [128, 2], F32, stack)
        Bcf = Buf()
        for dr in range(2):
            for pr in range(2):
                k4 = dr * 2 + pr
                t_ = sb("x_sin%d" % k4, [128, 128], F32, stack)
                Bt = Buf()
                K.op(dve, lambda e, t_=t_: e.memset(t_[:], 0.0), [], [Bt])
                for r1 in range(GRP):
                    K.copy(dve, cf[:, 0:1], fcst[:, 8 + dr * 4 + r1:9 + dr * 4 + r1], [Bfc], [Bcf])
                    for r2 in range(GRP):
                        ic_ = 16 + dr * 16 + r1 * 4 + r2
                        K.ts(dve, cf[:, 1:2], pa[:, r2, k4, 128:129], -1.0, fcst[:, ic_:ic_ + 1], ALU.add, ALU.mult,
                             [Bpa, Bfc], [Bcf])
                        K.stt(cf[:, 0:1], cf[:, 1:2], 1.0, cf[:, 0:1], ALU.add, ALU.mult, [Bcf], [Bcf])
                    K.stt(t_[:], pa[:, r1, k4, 0:128], cf[:, 0:1], t_[:], ALU.mult, ALU.add, [Bpa, Bcf, Bt], [Bt])
                Sin[dr][pr] = (t_, Bt)
        return Sin

    def even_o(l, S, stack):
        i_ev = l // 2
        tcm = sb("o_tcm", [128, 256], F32, stack)
        gn = sb("o_gn", [128, 1], F32, stack)
        qt = [sb("o_qt%d" % i, [128, 4, 2, 512], BF16, stack) for i in range(2)]
        vtl = [sb("o_vt%d" % i, [128, 4, 512], BF16, stack) for i in range(2)]
        gt = [sb("o_gt%d" % i, [128, 4, 512], BF16, stack) for i in range(2)]
        sf = [sb("o_sf%d" % i, [128, 8, 2, 128], BF16, stack) for i in range(2)]
        sbw = [sb("o_sb%d" % i, [128, 8, 2, 128], BF16, stack) for i in range(2)]
        am = [sb("o_am%d" % i, [128, 256], BF16, stack) for i in range(2)]
        sq = sb("o_sq", [128, 512], BF16, stack)
        rs = sb("o_rs", [128, 512], F32, stack)
        on = sb("o_on", [128, 512], F32, stack)
        ost = sb("o_ost", [128, 4, 512], BF16, stack)
        Btcm, Bgn, Bsq, Brs, Bon, Bost = (Buf() for _ in range(6))
        Bqt, Bvtl, Bgt, Bsf, Bsbw, Bam = ([Buf(), Buf()] for _ in range(6))
        K.dma(sp, tcm[:, 0:128], tconst[:, 0:128], writes=[Btcm])
        K.dma(sp, tcm[:, 128:256], tconst[:, 130:258], writes=[Btcm])
        K.dma(sp, gn[:], gla_nrm[i_ev], writes=[Bgn])
        gq_v = gq.rearrange("k r p s -> p k r s")
        gg_v = gg.rearrange("(c p) s -> p c s", p=128)
        mixo_v = mixo.rearrange("(c p) s -> p c s", p=128)
        na = 0
        for tt in range(S // 512):
            i2 = tt % 2
            K.dma(sp, qt[i2][:], gq_v[:, :, :, tt * 512:(tt + 1) * 512], reads=[Bgq], writes=[Bqt[i2]])
            K.dma(sp, vtl[i2][:], gvt[tt * 4:(tt + 1) * 4].rearrange("n p v -> p n v"), reads=[Bgvt], writes=[Bvtl[i2]])
            K.dma(sp, gt[i2][:], gg_v[:, :, tt * 512:(tt + 1) * 512], reads=[Bgg], writes=[Bgt[i2]])
            K.dma(sp, sf[i2][:], gs[0, tt * 8:(tt + 1) * 8].rearrange("c r p v -> p c r v"), reads=[Bgs],
                  writes=[Bsf[i2]])
            K.dma(sp, sbw[i2][:], gs[1, tt * 8:(tt + 1) * 8].rearrange("c r p v -> p c r v"), reads=[Bgs],
                  writes=[Bsbw[i2]])
            q_ = qt[i2]
            for ts in range(4):
                tk = slice(ts * 128, (ts + 1) * 128)
                for h in range(4):
                    pr, hb = h // 2, (h % 2) * 64
                    rows = slice(hb, hb + 64)
                    ab = h % 2
                    ob = 2 + h % 2
                    K.mm(psb[ab][:, 0:128], q_[rows, 2, pr, tk], q_[rows, 0, pr, tk], True, True, [Bqt[i2]], [PB[ab]])
                    K.mm(psb[ab][:, 128:256], q_[rows, 3, pr, tk], q_[rows, 1, pr, tk], True, True, [Bqt[i2]],
                         [PB[ab]])
                    a_, Ba = am[na % 2], Bam[na % 2]
                    na += 1
                    K.tt(dve, a_[:], psb[ab][:, 0:256], tcm[:], ALU.mult, [PB[ab], Btcm], [Ba])
                    o0 = pr * 128
                    oc_ = slice(o0, o0 + 128)
                    K.mm(psb[ob][:, oc_], vtl[i2][:, ts, h * 128:(h + 1) * 128], a_[:, 0:128], True, False,
                         [Bvtl[i2], Ba], [PB[ob]])
                    K.mm(psb[ob][:, oc_], vtl[i2][:, ts, h * 128:(h + 1) * 128], a_[:, 128:256], False, False,
                         [Bvtl[i2], Ba], [PB[ob]])
                    for c in range(2):
                        ci = ts * 2 + c
                        cs = slice(o0 + c * 64, o0 + (c + 1) * 64)
                        tks = slice(ts * 128 + c * 64, ts * 128 + (c + 1) * 64)
                        K.mm(psb[ob][:, cs], sf[i2][rows, ci, pr, :], q_[rows, 0, pr, tks], False, False,
                             [Bsf[i2], Bqt[i2]], [PB[ob]])
                        K.mm(psb[ob][:, cs], sbw[i2][rows, ci, pr, :], q_[rows, 1, pr, tks], False, c == 1,
                             [Bsbw[i2], Bqt[i2]], [PB[ob]])
                for par in range(2):
                    ob = 2 + par
                    hsel = slice(par, 4, 2)
                    K.actf(sq[:, 0:256], psb[ob][:, 0:256], ACT.Square, [PB[ob]], [Bsq])
                    K.mm(psb[4][:, 0:256], onesb[:], sq[:, 0:256], True, True, [Bones, Bsq], [PB[4]])
                    K.actf(rs[:, 0:256], psb[4][:, 0:256], ACT.Sqrt, [PB[4]], [Brs], scale=1.0 / 128, bias=EPS)
                    K.op(dve, lambda e: e.reciprocal(rs[:, 0:256], rs[:, 0:256]), [Brs], [Brs])
                    K.tt(dve, on[:, 0:256], psb[ob][:, 0:256], rs[:, 0:256], ALU.mult, [PB[ob], Brs], [Bon])
                    K.stt(ost[:, hsel, tk], on[:, 0:256].rearrange("p (h i) -> p h i", h=2), gn[:, 0:1],
                          gt[i2][:, hsel, tk], ALU.mult, ALU.mult, [Bon, Bgn, Bgt[i2]], [Bost])
            K.dma(sp, mixo_v[:, 0:4, tt * 512:(tt + 1) * 512], ost[:], reads=[Bost], writes=[Bmixo])

    def even_a2(l, S, is_sample, stack):
        i_ev = l // 2
        fb = mx_alloc(stack)
        win = sb("m_win", [128, KC, 1088], BF16, stack)
        wqb = sb("m_wqb", [128, 2, 1536], BF16, stack)
        wkv = sb("m_wkv", [128, 1024], BF16, stack)
        qn = sb("m_qn", [128, 2], F32, stack)
        kvn = sb("m_kvn", [128, 1], F32, stack)
        cq = sb("m_cq", [128, 2, 512], F32, stack)
        cqn = sb("m_cqn", [128, 2, 512], BF16, stack)
        ckvn = sb("m_ckvn", [128, 512], BF16, stack)
        cf_ = sb("m_cf", [128, 4], F32, stack)
        zb = sb("m_zb", [128, 1], F32, stack)
        Bzb = Buf()
        K.op(dve, lambda e: e.memset(zb[:], 0.0), [], [Bzb])
        ai = sb("m_ai", [128, 512], I32, stack)
        pos = sb("m_pos", [128, 512], F32, stack)
        tab = sb("m_tab", [128, 2, 512], F32, stack)
        gst = [sb("m_gst%d" % i, [128, 512], BF16, stack) for i in range(2)]
        qh = [sb("m_qh%d" % i, [96, 512], BF16, stack) for i in range(2)]
        kst = sb("m_kst", [96, 8, 512], BF16, stack)
        kro = sb("m_kro", [96, 512], BF16, stack)
        vaug = [sb("m_vaug%d" % i, [128, 8, 65], BF16, stack) for i in range(2)]
        (Bwin, Bwqb, Bwkv, Bqn, Bkvn, Bcq, Bcqn, Bckvn, Bcf, Bai, Bpos, Btab, Bkst, Bkro) = (Buf() for _ in range(14))
        Bgst, Bqh, Bvaug = ([Buf(), Buf()] for _ in range(3))
        K.dma(sp, win[:], ev_win2_b[i_ev].rearrange("p (k n) -> p k n", k=KC), reads=[Bevw], writes=[Bwin])
        K.dma(sp, wqb[:], mla_wqb_b[i_ev].rearrange("p (k n) -> p k n", k=2), reads=[Bevw], writes=[Bwqb])
        K.dma(sp, wkv[:], mla_wkvb_b[i_ev], reads=[Bevw], writes=[Bwkv])
        K.dma(sp, qn[:], mla_qn[i_ev], writes=[Bqn])
        K.dma(sp, kvn[:], mla_kvn[i_ev], writes=[Bkvn])
        for i in range(2):
            K.op(dve, lambda e, i=i: e.memset(vaug[i][:], 1.0), [], [Bvaug[i]])
        K.op(pool, lambda e: e.iota(ai[:], [[0, 512]], base=0, channel_multiplier=1), [], [Bai])
        K.op(dve, lambda e: e.tensor_scalar(ai[:, 0:1], ai[:, 0:1], icst[:, 6:7], None, ALU.bitwise_and), [Bai, Bic],
             [Bai])
        K.copy(dve, cf_[:, 1:2], ai[:, 0:1], [Bai], [Bcf])
        K.actf(cf_[:, 0:1], cf_[:, 1:2], ACT.Exp, [Bcf], [Bcf], scale=-float(np.log(10000.0)) / 16.0)
        K.ts(dve, cf_[:, 0:1], cf_[:, 0:1], 65536.0 / (2.0 * np.pi), None, ALU.mult, None, [Bcf], [Bcf])
        jpos = sb("m_jpos", [128, 512], I32, stack)
        Bjpos = Buf()
        K.op(pool, lambda e: e.iota(jpos[:], [[1, 512]], base=0, channel_multiplier=0), [], [Bjpos])
        if is_sample:
            K.op(pool, lambda e: e.tensor_scalar(jpos[:], jpos[:], icst[:, 5:6], None, ALU.add), [Bjpos, Bic], [Bjpos])
        gg_v = gg.rearrange("(c p) s -> p c s", p=128)
        Kd_v = Kd.rearrange("(h r) s -> r h s", h=8)
        Vd_v = Vd.rearrange("(h p) (k e) -> p h k e", h=8, e=65)
        qs = float(96.0 ** -0.5)
        R = slice(64, 96)
        ng = 0
        a2s = cfg.a2_stop
        if a2s == 1:
            return
        for tt in range(S // 512):
            for st in prenorm_steps(fb, l, 1, tt, fb.h, fb.Bh):
                st()
            for c4 in range(4):
                bank = rbank()
                for kc in range(KC):
                    K.mm(psb[bank][:, :], win[:, kc, c4 * 128:(c4 + 1) * 128], fb.h[:, kc, :], kc == 0, kc == KC - 1,
                         [Bwin, fb.Bh], [PB[bank]])
                g_, Bg_ = gst[ng % 2], Bgst[ng % 2]
                ng += 1
                K.actf(g_[:], psb[bank][:, :], ACT.Silu, [PB[bank]], [Bg_])
                K.dma(sp, gg_v[:, c4, tt * 512:(tt + 1) * 512], g_[:], reads=[Bg_], writes=[Bgg])
            if a2s == 2:
                continue
            K.ts(dve, pos[:], jpos[:], float(tt * 512), None, ALU.add, None, [Bjpos], [Bpos])
            for k2 in range(2):
                K.ts(dve, ai[:], pos[:], cf_[:, 0:1], fcst[:, k2:k2 + 1], ALU.mult, ALU.add, [Bpos, Bcf, Bfc], [Bai])
                K.op(dve, lambda e: e.tensor_scalar(ai[:], ai[:], icst[:, 3:4], None, ALU.bitwise_and), [Bai, Bic],
                     [Bai])
                K.actf(tab[:, k2, :], ai[:], ACT.Sin, [Bai], [Btab], scale=2.0 * np.pi / 65536.0, bias=-np.pi)
            if a2s == 3:
                continue
            for c2 in range(2):
                bank = rbank()
                for kc in range(KC):
                    K.mm(psb[bank][:, :], win[:, kc, 512 + c2 * 128:512 + (c2 + 1) * 128], fb.h[:, kc, :], kc == 0,
                         kc == KC - 1, [Bwin, fb.Bh], [PB[bank]])
                K.copy(act, cq[:, c2, :], psb[bank][:, :], [PB[bank]], [Bcq])
                i = fb.nsq % len(fb.sq)
                fb.nsq += 1
                K.actf(fb.sq[i][:], psb[bank][:, :], ACT.Square, [PB[bank]], [fb.Bsq[i]])
                K.mm(psb[6][:, :], onesb[:], fb.sq[i][:], c2 == 0, c2 == 1, [Bones, fb.Bsq[i]], [PB[6]])
            K.actf(fb.rstd[1][:], psb[6][:, :], ACT.Sqrt, [PB[6]], [fb.Brstd[1]], scale=1.0 / 256, bias=EPS)
            K.op(dve, lambda e: e.reciprocal(fb.rstd[1][:], fb.rstd[1][:]), [fb.Brstd[1]], [fb.Brstd[1]])
            for c2 in range(2):
                i = fb.ntmp % len(fb.tmp)
                fb.ntmp += 1
                K.stt(fb.tmp[i][:], cq[:, c2, :], qs, fb.rstd[1][:], ALU.mult, ALU.mult, [Bcq, fb.Brstd[1]],
                      [fb.Btmp[i]])
                K.actf(cqn[:, c2, :], fb.tmp[i][:], ACT.Identity, [fb.Btmp[i], Bqn, Bzb], [Bcqn],
                       scale=qn[:, c2:c2 + 1], bias=zb[:, 0:1])
            bank = rbank()
            for kc in range(KC):
                K.mm(psb[bank][:, :], win[:, kc, 768:896], fb.h[:, kc, :], kc == 0, kc == KC - 1, [Bwin, fb.Bh],
                     [PB[bank]])
            i = fb.nsq % len(fb.sq)
            fb.nsq += 1
            K.actf(fb.sq[i][:], psb[bank][:, :], ACT.Square, [PB[bank]], [fb.Bsq[i]])
            K.mm(psb[6][:, :], onesb[:], fb.sq[i][:], True, True, [Bones, fb.Bsq[i]], [PB[6]])
            K.actf(fb.rstd[1][:], psb[6][:, :], ACT.Sqrt, [PB[6]], [fb.Brstd[1]], scale=1.0 / 128, bias=EPS)
            K.op(dve, lambda e: e.reciprocal(fb.rstd[1][:], fb.rstd[1][:]), [fb.Brstd[1]], [fb.Brstd[1]])
            i = fb.ntmp % len(fb.tmp)
            fb.ntmp += 1
            K.tt(dve, fb.tmp[i][:], psb[bank][:, :], fb.rstd[1][:], ALU.mult, [PB[bank], fb.Brstd[1]], [fb.Btmp[i]])
            K.actf(ckvn[:], fb.tmp[i][:], ACT.Identity, [fb.Btmp[i], Bkvn, Bzb], [Bckvn], scale=kvn[:, 0:1],
                   bias=zb[:, 0:1])
            if a2s == 4:
                continue
            for kc in range(KC):
                K.mm(psb[2][0:96, :], win[:, kc, 896:992], fb.h[:, kc, :], kc == 0, kc == KC - 1, [Bwin, fb.Bh], [PB[2]])
            for kc in range(KC):
                K.mm(psb[3][0:96, :], win[:, kc, 992:1088], fb.h[:, kc, :], kc == 0, kc == KC - 1, [Bwin, fb.Bh],
                     [PB[3]])
            i = fb.ntmp % len(fb.tmp)
            fb.ntmp += 1
            K.tt(dve, fb.tmp[i][R, :], psb[2][R, :], tab[R, 0, :], ALU.mult, [PB[2], Btab], [fb.Btmp[i]])
            i2 = fb.ntmp % len(fb.tmp)
            fb.ntmp += 1
            K.tt(dve, fb.tmp[i2][R, :], psb[3][R, :], tab[R, 1, :], ALU.mult, [PB[3], Btab], [fb.Btmp[i2]])
            K.tt(dve, kro[R, :], fb.tmp[i][R, :], fb.tmp[i2][R, :], ALU.add, [fb.Btmp[i], fb.Btmp[i2]], [Bkro])
            if a2s == 5:
                continue
            for h in range(8):
                bank = rbank()
                K.mm(psb[bank][0:64, :], wkv[:, h * 64:(h + 1) * 64], ckvn[:], True, True, [Bwkv, Bckvn], [PB[bank]])
                K.copy(act, kst[0:64, h, :], psb[bank][0:64, :], [PB[bank]], [Bkst])
                K.copy(dve, kst[R, h, :], kro[R, :], [Bkro], [Bkst])
                for kc in range(2):
                    K.mm(psb[2][0:96, :], wqb[:, kc, h * 96:(h + 1) * 96], cqn[:, kc, :], kc == 0, kc == 1,
                         [Bwqb, Bcqn], [PB[2]])
                for kc in range(2):
                    K.mm(psb[3][0:96, :], wqb[:, kc, 768 + h * 96:768 + (h + 1) * 96], cqn[:, kc, :], kc == 0, kc == 1,
                         [Bwqb, Bcqn], [PB[3]])
                q_, Bq_ = qh[h % 2], Bqh[h % 2]
                K.copy(dve, q_[0:64, :], psb[2][0:64, :], [PB[2]], [Bq_])
                i = fb.ntmp % len(fb.tmp)
                fb.ntmp += 1
                K.tt(dve, fb.tmp[i][R, :], psb[2][R, :], tab[R, 0, :], ALU.mult, [PB[2], Btab], [fb.Btmp[i]])
                i2 = fb.ntmp % len(fb.tmp)
                fb.ntmp += 1
                K.tt(dve, fb.tmp[i2][R, :], psb[3][R, :], tab[R, 1, :], ALU.mult, [PB[3], Btab], [fb.Btmp[i2]])
                K.tt(dve, q_[R, :], fb.tmp[i][R, :], fb.tmp[i2][R, :], ALU.add, [fb.Btmp[i], fb.Btmp[i2]], [Bq_])
                K.dma(sp, Qd[h, :, tt * 512:(tt + 1) * 512], q_[:], reads=[Bq_], writes=[BQd])
            K.dma(sp, Kd_v[:, :, tt * 512:(tt + 1) * 512], kst[:], reads=[Bkst], writes=[BKd])
            if a2s == 6:
                continue
            for ts in range(4):
                tk = slice(ts * 128, (ts + 1) * 128)
                kt = tt * 4 + ts
                va, Bva = vaug[kt % 2], Bvaug[kt % 2]
                bank = rbank()
                K.mm(psb[bank][:, :], ckvn[:, tk], wkv[:, 512:1024], True, True, [Bckvn, Bwkv], [PB[bank]])
                K.copy(act if ts % 2 == 0 else dve, va[:, :, 0:64], psb[bank][:, :].rearrange("p (h e) -> p h e", h=8),
                       [PB[bank]], [Bva])
                K.dma(sp, Vd_v[:, :, kt, :], va[:], reads=[Bva], writes=[BVd])

    def even_b(S, nrank, Ksrc, BKs, Vsrc, BVs, stack):
        SK = S
        KB_ = min(1024, SK)
        nkt = KB_ // 128
        LOOK = 2
        NP = 4
        qt = [sb("b_qt%d" % i, [96, 512], BF16, stack) for i in range(2)]
        ktl = [sb("b_kt%d" % i, [96, KB_], BF16, stack) for i in range(3)]
        vtl = [sb("b_vt%d" % i, [128, nkt, 65], BF16, stack) for i in range(3)]
        pt = [sb("b_pt%d" % i, [128, 512], BF16, stack) for i in range(NP)]
        rc = sb("b_rc", [128, 512], F32, stack)
        osb = sb("b_osb", [64, 512], F32, stack)
        onb = [sb("b_on%d" % i, [64, 512], BF16, stack) for i in range(2)]
        Brc, Bosb = Buf(), Buf()
        Bqt, Bonb = ([Buf(), Buf()] for _ in range(2))
        Bktl, Bvtl = ([Buf(), Buf(), Buf()] for _ in range(2))
        Bpt = [Buf() for _ in range(NP)]
        nq = nk = ns = 0
        pend = []
        tails = []

        def flush_one():
            pend.pop(0)()

        for qb in range(S // 512):
            for h in range(8):
                q_, Bq_ = qt[nq % 2], Bqt[nq % 2]
                ob = 4 + nq % 2
                nq += 1
                K.dma(sp, q_[:], Qd[h, :, qb * 512:(qb + 1) * 512], reads=[BQd], writes=[Bq_])
                nblk = nrank * (SK // KB_)
                nstep = nblk * nkt
                si = 0
                for g in range(nrank):
                    for k0 in range(0, SK, KB_):
                        k_, Bk_ = ktl[nk % 3], Bktl[nk % 3]
                        v_, Bv_ = vtl[nk % 3], Bvtl[nk % 3]
                        nk += 1
                        K.dma(sp, k_[:], Ksrc(g, h, k0, KB_), reads=[BKs], writes=[Bk_])
                        K.dma(sp, v_[:], Vsrc(g, h, k0 // 128, nkt), reads=[BVs], writes=[Bv_])
                        for kt in range(nkt):
                            sbk = ns % 4
                            p_, Bp_ = pt[ns % NP], Bpt[ns % NP]
                            ns += 1
                            K.mm(psb[sbk][:, :], k_[:, kt * 128:(kt + 1) * 128], q_[:], True, True, [Bk_, Bq_],
                                 [PB[sbk]])
                            K.actf(p_[:], psb[sbk][:, :], ACT.Exp, [PB[sbk]], [Bp_])

                            def pv(v_=v_, Bv_=Bv_, p_=p_, Bp_=Bp_, kt=kt, first=(si == 0), last=(si == nstep - 1),
                                   ob=ob):
                                K.mm(psb[ob][0:65, :], v_[:, kt, :], p_[:], first, last, [Bv_, Bp_], [PB[ob]])
                            pend.append(pv)
                            si += 1
                            if len(pend) > LOOK:
                                flush_one()
                            if si == 4 and tails:
                                tails.pop(0)()
                while pend:
                    flush_one()
                while tails:
                    tails.pop(0)()
                K.op(dve, lambda e, ob=ob: e.reciprocal(rc[64:65, :], psb[ob][64:65, :]), [PB[ob]], [Brc])
                K.copy(dve, osb[:], psb[ob][0:64, :], [PB[ob]], [Bosb])

                def tail(h=h, qb=qb):
                    K.mm(psb[6][0:64, :], cst[64:65, 128:192], rc[64:65, :], True, True, [Bcst, Brc], [PB[6]])
                    o_, Bo_ = onb[h % 2], Bonb[h % 2]
                    K.tt(dve, o_[:], osb[:], psb[6][0:64, :], ALU.mult, [Bosb, PB[6]], [Bo_])
                    K.dma(sp, mixm[h, :, qb * 512:(qb + 1) * 512], o_[:], reads=[Bo_], writes=[Bmixm])
                tails.append(tail)
        while tails:
            tails.pop(0)()

    def even_phase_c(l, S, stack):
        i_ev = l // 2
        fb = mx_alloc(stack, with_h=False, with_y=True)
        wg = sb("c_wg", [128, 4, 1024], BF16, stack)
        wm = sb("c_wm", [64, 8, 1024], BF16, stack)
        og = [sb("c_og%d" % i, [128, 4, 512], BF16, stack) for i in range(2)]
        om = [sb("c_om%d" % i, [64, 8, 512], BF16, stack) for i in range(2)]
        Bwg, Bwm = Buf(), Buf()
        Bog, Bom = [Buf(), Buf()], [Buf(), Buf()]
        K.dma(sp, wg[:], ev_woutg_b[i_ev].rearrange("p (k n) -> p k n", k=4), reads=[Bevw], writes=[Bwg])
        K.dma(sp, wm[:], ev_woutm_b[i_ev].rearrange("p (k n) -> p k n", k=8), reads=[Bevw], writes=[Bwm])
        mixo_v = mixo.rearrange("(c p) s -> p c s", p=128)
        mixm_v = mixm.rearrange("h e s -> e h s")
        Cg = vec(l, 1, 2)
        pending = []
        for tt in range(S // 512):
            g_, Bg_ = og[tt % 2], Bog[tt % 2]
            m_, Bm_ = om[tt % 2], Bom[tt % 2]
            K.dma(sp, g_[:], mixo_v[:, 0:4, tt * 512:(tt + 1) * 512], reads=[Bmixo], writes=[Bg_])
            K.dma(sp, m_[:], mixm_v[:, :, tt * 512:(tt + 1) * 512], reads=[Bmixm], writes=[Bm_])

            def mm_oc(oc, bank, g_=g_, m_=m_, Bg_=Bg_, Bm_=Bm_):
                for ic in range(4):
                    K.mm(psb[bank][:, :], wg[:, ic, oc * 128:(oc + 1) * 128], g_[:, ic, :], ic == 0, False,
                         [Bwg, Bg_], [PB[bank]])
                for hh in range(8):
                    K.mm(psb[bank][:, :], wm[:, hh, oc * 128:(oc + 1) * 128], m_[:, hh, :], False, hh == 7,
                         [Bwm, Bm_], [PB[bank]])
            yphase(fb, tt, Cg, mm_oc, [], pending)
            while pending:
                pending.pop(0)()

    def even_mixer(l, S, is_sample):
        xg = is_sample and GRP > 1
        stop = cfg.ev_stop
        with ExitStack() as st:
            even_a1(l, S, st)
            K.barrier()
        if stop == 1:
            return
        with ExitStack() as st:
            Sin = even_exchange(S, st) if xg else None
            even_r(S, st, True, Sin)
            K.barrier()
        if stop == 2:
            return
        with ExitStack() as st:
            even_a2(l, S, is_sample, st)
            K.barrier()
        if stop == 3:
            return
        if xg:
            for h in range(8):
                K.op(pool, lambda e, h=h: e.collective_compute(
                    "AllGather", ALU.bypass, replica_groups=cfg.replica_groups,
                    ins=[Kd[h * 96:(h + 1) * 96, 0:S]], outs=[Kall[h]]), [BKd], [BKall])
                K.op(pool, lambda e, h=h: e.collective_compute(
                    "AllGather", ALU.bypass, replica_groups=cfg.replica_groups,
                    ins=[Vd[h * 128:(h + 1) * 128, 0:(S // 128) * 65]], outs=[Vall[h]]), [BVd], [BVall])
        with ExitStack() as st:
            even_o(l, S, st)
            K.barrier()
        if stop == 4:
            return
        if xg:
            K.barrier()
            kget = lambda g, h, k0, n: Kall[h, g * 96:(g + 1) * 96, k0:k0 + n]
            vget = lambda g, h, kt0, n: Vall[h].rearrange("(g p) (k e) -> g p k e", p=128, e=65)[g, :, kt0:kt0 + n, :]
            srcs = (GRP, kget, BKall, vget, BVall)
        else:
            kget = lambda g, h, k0, n: Kd[h * 96:(h + 1) * 96, k0:k0 + n]
            vget = lambda g, h, kt0, n: Vd[h * 128:(h + 1) * 128, :].rearrange("p (k e) -> p k e", e=65)[:, kt0:kt0 + n, :]
            srcs = (1, kget, BKd, vget, BVd)
        with ExitStack() as st:
            even_b(S, *srcs, st)
            K.barrier()
        if stop == 5:
            return
        with ExitStack() as st:
            even_phase_c(l, S, st)
            K.barrier()

    tok0 = 0
    for si, S in enumerate(cfg.seg_tokens):
        K.dma(sp, mv[:], modv[si], reads=[Bmodv], writes=[Bmv])
        with ExitStack() as st:
            load_segment(tok0, S, st)
            K.barrier()
        for l in range(L):
            with ExitStack() as st:
                fb = ffn_alloc(st)
                ffn_sublayer(fb, l, 0, S)
                K.barrier()
            if cfg.do_mixer and l % 2 == 1 and cfg.do_mixer & 2:
                odd_mixer(l, S, si == 2)
            if cfg.do_mixer and l % 2 == 0 and cfg.do_mixer & 1:
                even_mixer(l, S, si == 2)
            with ExitStack() as st:
                fb = ffn_alloc(st)
                ffn_sublayer(fb, l, 2, S)
                K.barrier()
        with ExitStack() as st:
            store_segment(tok0, S, st)
            K.barrier()
        tok0 += S
    K.barrier(full=True)
    es.close()
    return nc


def _fm(v):
    v = np.asarray(v)
    lead = v.shape[:-1]
    n = v.shape[-1] // 128
    v = v.reshape(lead + (n, 128))
    return np.ascontiguousarray(np.moveaxis(v, -1, 0))


def prep_shared(inp, cfg):
    L = cfg.depth
    sh = {}
    ident = np.eye(128, dtype=np.float32)
    sh["consts"] = np.ascontiguousarray(np.concatenate([ident, np.ones((128, 128), np.float32)], axis=1))
    aw = np.asarray(inp["ada_w"])[:L]
    aw = aw.reshape(L, KC, 128, 72, 128).transpose(0, 3, 2, 1, 4)
    sh["ada_w"] = np.ascontiguousarray(aw).reshape(L * 72, 128, KC, 128)
    sh["ada_b"] = _fm(np.asarray(inp["ada_b"])[:L]).reshape(128, L * 72)
    sh["npre"] = _fm(np.asarray(inp["norm_pre"])[:L]).reshape(128, L * 3 * KC)
    sh["npost"] = _fm(np.asarray(inp["norm_post"])[:L]).reshape(128, L * 3 * KC)
    w13 = np.asarray(inp["ffn_w13"])[:L].reshape(L * 2, KC, 128, 2, NFC, 128)
    sh["w13"] = np.ascontiguousarray(w13.transpose(0, 4, 2, 1, 3, 5)).reshape(L * 2, NFC, 128, KC * 256)
    w2 = np.asarray(inp["ffn_w2"])[:L].reshape(L * 2, NFC, 128, KC, 128)
    sh["w2"] = np.ascontiguousarray(w2.transpose(0, 3, 2, 1, 4)).reshape(L * 2, KC, 128, NFC * 128)
    NOD = L // 2
    if NOD:
        ow = np.asarray(inp["od_w_in"])[:NOD]
        sh["od_win"] = np.ascontiguousarray(ow.reshape(NOD, KC, 128, 1536).transpose(0, 2, 1, 3)).reshape(NOD, 128, KC * 1536)
        oo = np.asarray(inp["od_w_out"])[:NOD]
        sh["od_wout"] = np.ascontiguousarray(oo.reshape(NOD, KC, 128, 1024).transpose(0, 2, 1, 3)).reshape(NOD, 128, KC * 1024)
        ws = np.asarray(inp["sgu_w_s"])[:NOD]
        sh["sgu_wsT"] = np.ascontiguousarray(ws.transpose(0, 3, 1, 2)).reshape(NOD, 128, 512)
        sh["sgu_b"] = np.ascontiguousarray(np.asarray(inp["sgu_b"])[:NOD]).reshape(NOD, 1, 512)
        sh["sgu_nrm"] = np.ascontiguousarray(np.broadcast_to(np.asarray(inp["sgu_norm"])[:NOD, None, :], (NOD, 128, 512)))
    NEV = (L + 1) // 2
    if NEV:
        def kmaj(w, nk):
            n, _, cols = w.shape
            return np.ascontiguousarray(w.reshape(n, nk, 128, cols).transpose(0, 2, 1, 3)).reshape(n, 128, nk * cols)
        wi = np.asarray(inp["ev_w_in"])[:NEV]
        q, k, v, g, alr, cq, ckv, kr = (wi[:, :, a:b] for a, b in ((0, 256), (256, 512), (512, 1024), (1024, 1536),
                                                                  (1536, 1568), (1568, 1824), (1824, 1952), (1952, 1984)))
        sh["ev_win1"] = kmaj(np.concatenate([q, k, v, alr], axis=2), KC)
        fill = ckv[:, :, 0:64]
        krrot = np.concatenate([kr[:, :, 16:32], kr[:, :, 0:16]], axis=2)
        sh["ev_win2"] = kmaj(np.concatenate([g, cq, ckv, fill, kr, fill, krrot], axis=2), KC)
        wo = np.asarray(inp["ev_w_out"])[:NEV]
        sh["ev_woutg"] = kmaj(wo[:, 0:512], 4)
        sh["ev_woutm"] = np.ascontiguousarray(wo[:, 512:1024].reshape(NEV, 8, 64, 1024).transpose(0, 2, 1, 3)).reshape(NEV, 64, 8 * 1024)
        wa = np.asarray(inp["gla_w_alpha"])[:NEV]
        ba = np.asarray(inp["gla_b_alpha"])[:NEV]
        wal = np.zeros((NEV, 33, 512), np.float32)
        wal[:, 0:16, 0:256] = wa[:, 0]
        wal[:, 16:32, 256:512] = wa[:, 1]
        wal[:, 32, 0:256] = ba[:, 0]
        wal[:, 32, 256:512] = ba[:, 1]
        sh["gla_wal"] = wal
        sh["gla_nrm"] = np.ascontiguousarray(np.asarray(inp["gla_norm"])[:NEV].reshape(NEV, 128, 1))
        sh["mla_qn"] = np.ascontiguousarray(np.asarray(inp["mla_q_norm"])[:NEV].reshape(NEV, 2, 128).transpose(0, 2, 1))
        sh["mla_kvn"] = np.ascontiguousarray(np.asarray(inp["mla_kv_norm"])[:NEV].reshape(NEV, 128, 1))
        wq = np.asarray(inp["mla_w_q_b"])[:NEV].reshape(NEV, 256, 8, 96)
        wqr = np.concatenate([wq[..., 0:64], wq[..., 80:96], wq[..., 64:80]], axis=-1)
        sh["mla_wqb"] = kmaj(np.concatenate([wq.reshape(NEV, 256, 768), wqr.reshape(NEV, 256, 768)], axis=2), 2)
        wk = np.asarray(inp["mla_w_kv_b"])[:NEV].reshape(NEV, 128, 8, 128)
        sh["mla_wkvb"] = np.ascontiguousarray(np.concatenate([wk[..., 0:64].reshape(NEV, 128, 512),
                                                             wk[..., 64:128].reshape(NEV, 128, 512)], axis=2))
    t = np.arange(128)
    same = (t[:, None] // 64) == (t[None, :] // 64)
    Tfi = (same & (t[:, None] <= t[None, :])).astype(np.float32)
    Tbe = (same & (t[:, None] > t[None, :])).astype(np.float32)
    Tbi = (same & (t[:, None] >= t[None, :])).astype(np.float32)
    Tpe = (same & (t[:, None] < t[None, :])).astype(np.float32)
    Ind = np.stack([(t < 64), (t >= 64)], axis=1).astype(np.float32)
    sh["tconst"] = np.ascontiguousarray(np.concatenate([Tfi, Ind, Tbe, Tbi, Ind, Tpe], axis=1))
    return sh


def prep_core(inp, cfg, core, n_cores=8):
    xp = np.asarray(inp["x_prompt"])
    xs = np.asarray(inp["x_sample"])
    cp = np.asarray(inp["c_prompt"])
    cs = np.asarray(inp["c_sample"])
    SP = cfg.seg_tokens[0]
    SQ = cfg.seg_tokens[2]
    per_grp = n_cores // xs.shape[0]
    sb_, r = core // per_grp, core % per_grp
    xin = np.concatenate([xp[2 * core, :SP], xp[2 * core + 1, :SP], xs[sb_, r * SQ:(r + 1) * SQ]], axis=0)
    c = np.stack([cp[2 * core], cp[2 * core + 1], cs[sb_], np.zeros(D, np.float32)], axis=0)
    c3 = np.ascontiguousarray(c.T.reshape(KC, 128, 4).transpose(1, 0, 2))
    ic = np.zeros((128, 8), np.int32)
    stot = per_grp * SQ
    ic[:, 0] = SP - 1
    ic[:, 1] = stot - 1
    ic[:, 2] = 127
    ic[:, 3] = 65535
    ic[:, 4] = (SQ * r * np.arange(128)) % stot
    ic[:, 5] = r * SQ
    ic[:, 6] = 15
    fc = np.zeros((128, 64), np.float32)
    fc[:, 0] = 49152.0
    fc[80:96, 1] = 32768.0
    for r1 in range(min(per_grp, 4)):
        fc[:, 8 + r1] = 1.0 if r1 < r else 0.0
        fc[:, 12 + r1] = 1.0 if r1 > r else 0.0
        for r2 in range(min(per_grp, 4)):
            fc[:, 16 + r1 * 4 + r2] = 1.0 if r1 < r2 < r else 0.0
            fc[:, 32 + r1 * 4 + r2] = 1.0 if r < r2 < r1 else 0.0
    return {"xin": np.ascontiguousarray(xin), "c3": c3, "iconst": ic, "fconst": fc}


_CACHE = {}


def run(inp, cfg, n_cores=8, trace=False):
    key = (cfg.seg_tokens, cfg.depth, cfg.do_mixer, cfg.n_cores, cfg.group)
    if key not in _CACHE:
        _CACHE[key] = build(cfg)
    nc = _CACHE[key]
    sh = prep_shared(inp, cfg)
    in_maps = []
    for c in range(n_cores):
        m = dict(sh)
        m.update(prep_core(inp, cfg, c, n_cores))
        in_maps.append(m)
    res = run_bass_kernel_spmd(nc, in_maps, core_ids=list(range(n_cores)), trace=trace)
    return res


def kernel(**inputs):
    cfg = Cfg()
    res = run(inputs, cfg)
    SP, SQ = cfg.seg_tokens[0], cfg.seg_tokens[2]
    B, S = inputs["x_prompt"].shape[:2]
    DB, DS = inputs["x_sample"].shape[:2]
    yp = np.empty((B, S, D), np.float32)
    ys = np.empty((DB, DS, D), np.float32)
    per_grp = 8 // DB
    for c in range(8):
        y = res.results[c]["yout"]
        yp[2 * c] = y[0:SP]
        yp[2 * c + 1] = y[SP:2 * SP]
        ys[c // per_grp, (c % per_grp) * SQ:(c % per_grp + 1) * SQ] = y[2 * SP:2 * SP + SQ]
    return (yp, ys)
```

```python
import numpy as np
import concourse.bass as bass
import concourse.mybir as mybir
from concourse.bass_utils import run_bass_kernel_spmd
from contextlib import ExitStack

F32 = mybir.dt.float32
BF16 = mybir.dt.bfloat16
I32 = mybir.dt.int32
I16 = mybir.dt.int16
ACT = mybir.ActivationFunctionType
ALU = mybir.AluOpType

D = 1024
KC = 8
DFF = 2816
NFC = 22
EPS = 1e-6


class Buf:
    __slots__ = ("name", "w", "r", "wl")

    def __init__(self, name=""):
        self.name = name
        self.w = None
        self.r = {}
        self.wl = []


class EngW:
    def __init__(self, name, eng, sid, sem, inorder=False):
        self.name = name
        self.eng = eng
        self.sid = sid
        self.sem = sem
        self.cnt = 0
        self.known = {}
        self.inorder = inorder
        self.ring = []
        self.ring_pos = 0


class KB:
    def __init__(self, nc, nring=20):
        self.nc = nc
        self.es = ExitStack()
        self.sems = []
        self.semcnt = []
        self.engs = {}
        for name, eng, inorder in (("pe", nc.tensor, True), ("act", nc.scalar, False), ("dve", nc.vector, False),
                                   ("pool", nc.gpsimd, False), ("sp", nc.sync, False)):
            sid = self._newsem("c_" + name)
            self.engs[name] = EngW(name, eng, sid, self.sems[sid], inorder)
        for q in ("sp", "pool", "act"):
            E = self.engs[q]
            for i in range(nring):
                E.ring.append(self._newsem("d_%s%d" % (q, i)))
        self.pe, self.act, self.dve, self.pool, self.sp = (self.engs[n] for n in ("pe", "act", "dve", "pool", "sp"))
        self.bg_ring = [self._newsem("d_bg%d" % i) for i in range(24)]
        self.bg_pos = 0
        self.bg_sids = set(self.bg_ring)

    def _newsem(self, name):
        s = self.es.enter_context(self.nc.semaphore(name))
        self.sems.append(s)
        self.semcnt.append(0)
        return len(self.sems) - 1

    def _waits(self, E, reads, writes, extra=()):
        need = {}
        for b in reads:
            if b.w is not None and need.get(b.w[0], 0) < b.w[1]:
                need[b.w[0]] = b.w[1]
            for sid, val in b.wl:
                if need.get(sid, 0) < val:
                    need[sid] = val
        for b in writes:
            if b.w is not None and need.get(b.w[0], 0) < b.w[1]:
                need[b.w[0]] = b.w[1]
            for sid, val in b.wl:
                if need.get(sid, 0) < val:
                    need[sid] = val
            for sid, val in b.r.items():
                if need.get(sid, 0) < val:
                    need[sid] = val
        for sid, val in extra:
            if need.get(sid, 0) < val:
                need[sid] = val
        for sid, val in need.items():
            if sid == E.sid and E.inorder:
                continue
            if E.known.get(sid, 0) >= val:
                continue
            E.eng.wait_ge(self.sems[sid], val)
            E.known[sid] = val

    def op(self, E, emit, reads=(), writes=()):
        self._waits(E, reads, writes)
        ins = emit(E.eng)
        E.cnt += 1
        ins.then_inc(E.sem, 1)
        self.semcnt[E.sid] = E.cnt
        for b in reads:
            if b.r.get(E.sid, 0) < E.cnt:
                b.r[E.sid] = E.cnt
        for b in writes:
            b.w = (E.sid, E.cnt)
            b.r = {}

    def dma(self, Q, out, in_, reads=(), writes=(), bg=False, **kw):
        if bg:
            sid = self.bg_ring[self.bg_pos]
            self.bg_pos = (self.bg_pos + 1) % len(self.bg_ring)
        else:
            sid = Q.ring[Q.ring_pos]
            Q.ring_pos = (Q.ring_pos + 1) % len(Q.ring)
        prev = self.semcnt[sid]
        self._waits(Q, reads, writes, extra=((sid, prev),) if prev else ())
        ins = Q.eng.dma_start(out=out, in_=in_, **kw)
        self.semcnt[sid] = prev + 16
        ins.then_inc(self.sems[sid], 16)
        val = prev + 16
        for b in reads:
            if b.r.get(sid, 0) < val:
                b.r[sid] = val
        for b in writes:
            if bg:
                b.wl.append((sid, val))
            else:
                b.w = (sid, val)
                b.r = {}
                b.wl = []

    def barrier(self, full=False):
        for E in self.engs.values():
            for sid in range(len(self.sems)):
                val = self.semcnt[sid]
                if sid in self.bg_sids and not full:
                    continue
                if val and sid != E.sid and E.known.get(sid, 0) < val:
                    E.eng.wait_ge(self.sems[sid], val)
                    E.known[sid] = val
            if E.cnt and not E.inorder and E.known.get(E.sid, 0) < E.cnt:
                E.eng.wait_ge(E.sem, E.cnt)
                E.known[E.sid] = E.cnt

    def mm(self, out, lhsT, rhs, start, stop, reads, writes, **kw):
        self.op(self.pe, lambda e: e.matmul(out, lhsT, rhs, start=start, stop=stop, **kw), reads, writes)

    def actf(self, out, in_, func, reads, writes, **kw):
        self.op(self.act, lambda e: e.activation(out, in_, func, **kw), reads, writes)

    def tt(self, E, out, in0, in1, op, reads, writes):
        self.op(E, lambda e: e.tensor_tensor(out, in0, in1, op), reads, writes)

    def ts(self, E, out, in0, s1, s2, op0, op1, reads, writes):
        if op1 is None:
            self.op(E, lambda e: e.tensor_scalar(out, in0, s1, None, op0), reads, writes)
        else:
            self.op(E, lambda e: e.tensor_scalar(out, in0, s1, s2, op0, op1), reads, writes)

    def stt(self, out, in0, scalar, in1, op0, op1, reads, writes):
        self.op(self.dve, lambda e: e.scalar_tensor_tensor(out, in0, scalar, in1, op0, op1), reads, writes)

    def copy(self, E, out, in_, reads, writes):
        if E is self.act:
            self.op(E, lambda e: e.copy(out, in_), reads, writes)
        else:
            self.op(E, lambda e: e.tensor_copy(out, in_), reads, writes)


class Cfg:
    def __init__(self, seg_tokens=(4096, 4096, 4096), depth=4, do_mixer=True, n_cores=8, group=4):
        self.seg_tokens = tuple(seg_tokens)
        self.ntok = sum(seg_tokens)
        self.depth = depth
        self.do_mixer = 3 if do_mixer is True else int(do_mixer)
        self.nffn = depth * 2
        self.n_cores = n_cores
        self.ev_stop = 0
        self.a2_stop = 0
        self.no_xg = 0
        self.cc_max = 4 * 1024 * 1024
        self.group = group
        self.replica_groups = [list(range(g * group, (g + 1) * group)) for g in range(n_cores // group)]


def build(cfg):
    nc = bass.Bass("TRN2", target_bir_lowering=False)
    L = cfg.depth
    NF = cfg.nffn
    NT = cfg.ntok

    def din(name, shape, dt=F32):
        return nc.dram_tensor(name, list(shape), dt, kind="ExternalInput").ap()

    def dscr(name, shape, dt):
        return nc.dram_tensor(name, list(shape), dt, kind="Internal").ap()

    xin = din("xin", [NT, D])
    c3 = din("c3", [128, KC, 4])
    consts = din("consts", [128, 256])
    ada_w = din("ada_w", [L * 72, 128, KC, 128])
    ada_b = din("ada_b", [128, L * 72])
    npre = din("npre", [128, L * 3 * KC])
    npost = din("npost", [128, L * 3 * KC])
    w13 = din("w13", [NF, NFC, 128, KC * 256])
    w2 = din("w2", [NF, KC, 128, NFC * 128])
    yout = nc.dram_tensor("yout", [NT, D], F32, kind="ExternalOutput").ap()
    NOD = L // 2
    NEV = (L + 1) // 2
    SQ = cfg.seg_tokens[2]
    GRP = cfg.group
    iconst = din("iconst", [128, 8], I32)
    if NOD:
        od_win = din("od_win", [NOD, 128, KC * 1536])
        od_wout = din("od_wout", [NOD, 128, KC * 1024])
        sgu_wsT = din("sgu_wsT", [NOD, 128, 512])
        sgu_b = din("sgu_b", [NOD, 1, 512])
        sgu_nrm = din("sgu_nrm", [NOD, 128, 512])
        od_win_b = dscr("od_win_b", [NOD, 128, KC * 1536], BF16)
        od_wout_b = dscr("od_wout_b", [NOD, 128, KC * 1024], BF16)
        sgu_wsT_b = dscr("sgu_wsT_b", [NOD, 128, 512], BF16)
        sgu_b_b = dscr("sgu_b_b", [NOD, 1, 512], BF16)
    SMAXL = max(cfg.seg_tokens)
    tconst = din("tconst", [128, 516])
    fconst = din("fconst", [128, 64])
    if NEV:
        ev_win1 = din("ev_win1", [NEV, 128, KC * 1056])
        ev_win2 = din("ev_win2", [NEV, 128, KC * 1088])
        ev_woutg = din("ev_woutg", [NEV, 128, 4 * 1024])
        ev_woutm = din("ev_woutm", [NEV, 64, 8 * 1024])
        gla_wal = din("gla_wal", [NEV, 33, 512])
        gla_nrm = din("gla_nrm", [NEV, 128, 1])
        mla_qn = din("mla_qn", [NEV, 128, 2])
        mla_kvn = din("mla_kvn", [NEV, 128, 1])
        mla_wqb = din("mla_wqb", [NEV, 128, 2 * 1536])
        mla_wkvb = din("mla_wkvb", [NEV, 128, 1024])
        ev_win1_b = dscr("ev_win1_b", [NEV, 128, KC * 1056], BF16)
        ev_win2_b = dscr("ev_win2_b", [NEV, 128, KC * 1088], BF16)
        ev_woutg_b = dscr("ev_woutg_b", [NEV, 128, 4 * 1024], BF16)
        ev_woutm_b = dscr("ev_woutm_b", [NEV, 64, 8 * 1024], BF16)
        gla_wal_b = dscr("gla_wal_b", [NEV, 33, 512], BF16)
        mla_wqb_b = dscr("mla_wqb_b", [NEV, 128, 2 * 1536], BF16)
        mla_wkvb_b = dscr("mla_wkvb_b", [NEV, 128, 1024], BF16)
    NCH = SMAXL // 64
    gq = dscr("gq", [4, 2, 128, SMAXL], BF16)
    gvt = dscr("gvt", [SMAXL // 128, 128, 512], BF16)
    gkv = dscr("gkv", [2, NCH, 2, 128, 128], F32)
    gs = dscr("gs", [2, NCH, 2, 128, 128], BF16)
    gg = dscr("gg", [512, SMAXL], BF16)
    gsum = dscr("gsum", [4 * 128, 129], F32)
    gsum_all = dscr("gsum_all", [GRP * 4 * 128, 129], F32)
    Qd = dscr("Qd", [8, 96, SMAXL], BF16)
    Kd = dscr("Kd", [8 * 96, SMAXL], BF16)
    Kall = dscr("Kall", [8, GRP * 96, SQ], BF16)
    Vd = dscr("Vd", [8 * 128, (SMAXL // 128) * 65], BF16)
    Vall = dscr("Vall", [8, GRP * 128, (SQ // 128) * 65], BF16)
    mixm = dscr("mixm", [8, 64, SMAXL], BF16)
    Ud = dscr("Ud", [SMAXL, 1024], BF16)
    CC_MAX = cfg.cc_max
    RCU = min(SQ, max(128, (CC_MAX // (GRP * 2048)) // 128 * 128))
    NUC = SQ // RCU
    Uall = dscr("Uall", [NUC, GRP * RCU, 1024], BF16)
    mixo = dscr("mixo", [1024, SMAXL], BF16)

    w13b = dscr("w13b", [NF, NFC, 128, KC * 256], BF16)
    w2b = dscr("w2b", [NF, KC, 128, NFC * 128], BF16)
    modv = dscr("modv", [3, 128, L * 3 * 3 * KC], F32)

    K = KB(nc)
    es = K.es
    pe, act, dve, pool, sp = K.pe, K.act, K.dve, K.pool, K.sp

    uid = [0]

    def sb(name, shape, dt, stack=es):
        uid[0] += 1
        return stack.enter_context(nc.sbuf_tensor("%s_u%d" % (name, uid[0]), list(shape), dt))

    psb = [es.enter_context(nc.psum_tensor("ps%d" % i, [128, 512], F32)) for i in range(8)]
    PB = [Buf("ps%d" % i) for i in range(8)]

    SMAX = max(cfg.seg_tokens)
    xT = sb("xT", [128, KC, SMAX], F32)
    XB = [Buf("x%d" % i) for i in range(SMAX // 512)]
    cst = sb("cst", [128, 256], F32)
    onesb = sb("onesb", [128, 128], BF16)
    mv = sb("mv", [128, L * 3 * 3 * KC], F32)
    Bcst, Bones, Bmv = Buf("cst"), Buf("ones"), Buf("mv")
    ident = cst[:, 0:128]

    K.dma(sp, cst[:], consts, writes=[Bcst])
    K.copy(dve, onesb[:], cst[:, 128:256], [Bcst], [Bones])

    WB13 = [Buf("w13b%d" % f) for f in range(NF)]
    WB2 = [Buf("w2b%d" % f) for f in range(NF)]
    late_conv = []
    for f in range(NF):
        def cv(f=f):
            K.dma(pool, w13b[f], w13[f], writes=[WB13[f]], bg=True, max_dma_last_dim=4096)
            K.dma(pool, w2b[f], w2[f], writes=[WB2[f]], bg=True, max_dma_last_dim=4096)
        if f == 0:
            cv()
        else:
            late_conv.append(cv)

    Bodw = Buf("odw")
    if NOD:
        def cvo():
            for src, dst in ((od_win, od_win_b), (od_wout, od_wout_b), (sgu_wsT, sgu_wsT_b), (sgu_b, sgu_b_b)):
                K.dma(pool, dst, src, writes=[Bodw], bg=True, max_dma_last_dim=4096)
        late_conv.append(cvo)
    Bevw = Buf("evw")
    if NEV:
        for src, dst in ((ev_win1, ev_win1_b), (ev_win2, ev_win2_b), (ev_woutg, ev_woutg_b), (ev_woutm, ev_woutm_b),
                         (gla_wal, gla_wal_b), (mla_wqb, mla_wqb_b), (mla_wkvb, mla_wkvb_b)):
            K.dma(pool, dst, src, writes=[Bevw], bg=True, max_dma_last_dim=4096)
    icst = sb("icst", [128, 8], I32)
    Bic = Buf("icst")
    K.dma(sp, icst[:], iconst, writes=[Bic])

    Bmodv = Buf("modv")
    with ExitStack() as ps:
        ccT = sb("ccT", [128, KC, 4], F32, ps)
        adab = sb("adab", [128, L * 72], F32, ps)
        gpre = sb("gpre", [128, L * 3 * KC], F32, ps)
        gpost = sb("gpost", [128, L * 3 * KC], F32, ps)
        mfm = sb("mfm", [128, L * 72, 4], F32, ps)
        mvall = sb("mvall", [128, 3, L * 3 * 3 * KC], F32, ps)
        NAW = 4
        awt = [sb("awt%d" % i, [128, KC, 128], F32, ps) for i in range(NAW)]
        Bcc, Badab, Bgpre, Bgpost, Bmfm, Bmvall = (Buf(n) for n in ("cc", "adab", "gpre", "gpost", "mfm", "mvall"))
        Bawt = [Buf("awt%d" % i) for i in range(NAW)]
        K.dma(sp, ccT[:], c3, writes=[Bcc])
        K.dma(sp, adab[:], ada_b, writes=[Badab])
        K.dma(sp, gpre[:], npre, writes=[Bgpre])
        K.dma(sp, gpost[:], npost, writes=[Bgpost])
        K.actf(ccT[:], ccT[:], ACT.Silu, [Bcc], [Bcc])
        for t in range(L * 72):
            wt, Bw = awt[t % NAW], Bawt[t % NAW]
            K.dma(sp, wt[:], ada_w[t], writes=[Bw])
            bank = 7 - (t % 2)
            for kc in range(KC):
                K.mm(psb[bank][:, 0:4], wt[:, kc, :], ccT[:, kc, :], kc == 0, kc == KC - 1, [Bw, Bcc], [PB[bank]])
            K.ts(dve, mfm[:, t, :], psb[bank][:, 0:4], adab[:, t:t + 1], None, ALU.add, None,
                 [PB[bank], Badab], [Bmfm])
        gp4 = gpost[:].rearrange("p (l j c) -> p l j c", l=L, j=3)
        for j in (0, 2):
            K.ts(dve, gp4[:, :, j, :], gp4[:, :, j, :], 0.5, None, ALU.mult, None, [Bgpost], [Bgpost])
        mf5 = mfm[:].rearrange("p (l j t c) b -> p l j t c b", l=L, j=3, t=3)
        mv5 = mvall[:].rearrange("p b (l j v c) -> p b l j v c", l=L, j=3, v=3)
        gpr4 = gpre[:].rearrange("p (l j c) -> p l j c", l=L, j=3)
        for b in range(3):
            for l in range(L):
                for j in range(3):
                    K.stt(mv5[:, b, l, j, 0, :], mf5[:, l, j, 1, :, b], 1.0, gpr4[:, l, j, :], ALU.add, ALU.mult,
                          [Bmfm, Bgpre], [Bmvall])
                    K.copy(dve, mv5[:, b, l, j, 1, :], mf5[:, l, j, 0, :, b], [Bmfm], [Bmvall])
                    K.stt(mv5[:, b, l, j, 2, :], mf5[:, l, j, 2, :, b], 1.0, gp4[:, l, j, :], ALU.add, ALU.mult,
                          [Bmfm, Bgpost], [Bmvall])
        K.dma(sp, modv.rearrange("b p n -> p b n"), mvall[:], reads=[Bmvall], writes=[Bmodv])
        K.barrier()
    for cv in late_conv:
        cv()

    def vec(l, j, v):
        o = ((l * 3 + j) * 3 + v) * KC
        return mv[:, o:o + KC]

    def load_segment(tok0, S, stack):
        xtok = [sb("xtok%d" % i, [128, D], F32, stack) for i in range(2)]
        Bxt = [Buf("xtok%d" % i) for i in range(2)]
        for i in range(S // 128):
            xt_, Bx = xtok[i % 2], Bxt[i % 2]
            K.dma(sp, xt_[:], xin[tok0 + i * 128: tok0 + (i + 1) * 128, :], writes=[Bx])
            for hh in range(2):
                bank = (2 * i + hh) % 4
                for q in range(4):
                    kc = hh * 4 + q
                    K.op(pe, lambda e, kc=kc, q=q, bank=bank: e.transpose(psb[bank][:, q * 128:(q + 1) * 128],
                                                                           xt_[:, kc * 128:(kc + 1) * 128], ident),
                         [Bx, Bcst], [PB[bank]])
                dst = xT[:, hh * 4:(hh + 1) * 4, i * 128:(i + 1) * 128]
                src = psb[bank][:, :].rearrange("p (q t) -> p q t", q=4)
                K.copy(act if hh == 0 else dve, dst, src, [PB[bank]], [XB[i // 4]])

    def store_segment(tok0, S, stack):
        yt = [sb("ytok%d" % i, [128, D], F32, stack) for i in range(2)]
        Byt = [Buf("ytok%d" % i) for i in range(2)]
        for i in range(S // 128):
            y_, By = yt[i % 2], Byt[i % 2]
            for hh in range(2):
                bank = (2 * i + hh) % 4
                for q in range(4):
                    kc = hh * 4 + q
                    K.op(pe, lambda e, kc=kc, q=q, bank=bank: e.transpose(psb[bank][:, q * 128:(q + 1) * 128],
                                                                           xT[:, kc, i * 128:(i + 1) * 128], ident),
                         [XB[i // 4], Bcst], [PB[bank]])
                K.copy(act if hh == 0 else dve, y_[:, hh * 512:(hh + 1) * 512], psb[bank][:, :], [PB[bank]], [By])
            K.dma(sp, yout[tok0 + i * 128: tok0 + (i + 1) * 128, :], y_[:], reads=[By])

    class FfnBufs:
        pass

    def ffn_alloc(stack):
        fb = FfnBufs()
        fb.h = sb("f_h", [128, KC, 512], BF16, stack)
        fb.g = sb("f_g", [128, NFC, 512], BF16, stack)
        fb.y = sb("f_y", [128, KC, 512], F32, stack)
        fb.s = sb("f_s", [128, 512], F32, stack)
        fb.w13 = [sb("f_w13_%d" % i, [128, KC, 256], BF16, stack) for i in range(3)]
        fb.w2 = [sb("f_w2_%d" % i, [128, 11, 128], BF16, stack) for i in range(3)]
        fb.rstd = [sb("f_rstd%d" % i, [128, 512], F32, stack) for i in range(2)]
        fb.sq = [sb("f_sq%d" % i, [128, 512], BF16, stack) for i in range(1)]
        fb.tmp = [sb("f_tmp%d" % i, [128, 512], F32, stack) for i in range(1)]
        fb.Bh, fb.By, fb.Bs = Buf("h"), Buf("y"), Buf("s")
        fb.Bg = [Buf("g%d" % i) for i in range(NFC)]
        fb.Bw13 = [Buf() for _ in range(3)]
        fb.Bw2 = [Buf() for _ in range(3)]
        fb.Brstd = [Buf(), Buf()]
        fb.Bsq = [Buf(), Buf()]
        fb.Btmp = [Buf(), Buf()]
        fb.n13 = 0
        fb.n2 = 0
        fb.nsq = 0
        fb.ntmp = 0
        return fb

    def rstd_from_ss(fb, ri, bank):
        K.actf(fb.rstd[ri][:], psb[bank][:, :], ACT.Sqrt, [PB[bank]], [fb.Brstd[ri]], scale=1.0 / D, bias=EPS)
        K.op(dve, lambda e: e.reciprocal(fb.rstd[ri][:], fb.rstd[ri][:]), [fb.Brstd[ri]], [fb.Brstd[ri]])

    def prenorm_steps(fb, l, j, tt, hdst, Bh):
        tsl = slice(tt * 512, (tt + 1) * 512)
        A, Bv = vec(l, j, 0), vec(l, j, 1)
        steps = []

        def p0():
            for kc in range(KC):
                i = fb.nsq % len(fb.sq)
                fb.nsq += 1
                K.actf(fb.sq[i][:], xT[:, kc, tsl], ACT.Square, [XB[tt]], [fb.Bsq[i]])
                K.mm(psb[6][:, :], onesb[:], fb.sq[i][:], kc == 0, kc == KC - 1, [Bones, fb.Bsq[i]], [PB[6]])
        steps.append(p0)
        steps.append(lambda: rstd_from_ss(fb, 0, 6))
        for kc in range(KC):
            def pk(kc=kc):
                i = fb.ntmp % len(fb.tmp)
                fb.ntmp += 1
                K.tt(dve, fb.tmp[i][:], xT[:, kc, tsl], fb.rstd[0][:], ALU.mult, [XB[tt], fb.Brstd[0]], [fb.Btmp[i]])
                K.actf(hdst[:, kc, :], fb.tmp[i][:], ACT.Identity, [fb.Btmp[i], Bmv], [Bh],
                       scale=A[:, kc:kc + 1], bias=Bv[:, kc:kc + 1])
            steps.append(pk)
        return steps

    def yphase(fb, tt, Cg, mm_oc, nxt, pending_tail):
        tsl = slice(tt * 512, (tt + 1) * 512)
        prev_sq = None
        for oc in range(KC):
            bank = 4 + oc % 2
            mm_oc(oc, bank)
            if prev_sq is not None:
                po, pi = prev_sq
                K.mm(psb[7][:, :], onesb[:], fb.sq[pi][:], po == 0, False, [Bones, fb.Bsq[pi]], [PB[7]])
            K.copy(act, fb.y[:, oc, :], psb[bank][:, :], [PB[bank]], [fb.By])
            i = fb.nsq % len(fb.sq)
            fb.nsq += 1
            K.actf(fb.sq[i][:], psb[bank][:, :], ACT.Square, [PB[bank]], [fb.Bsq[i]])
            prev_sq = (oc, i)
            if nxt:
                nxt.pop(0)()
        po, pi = prev_sq
        K.mm(psb[7][:, :], onesb[:], fb.sq[pi][:], False, True, [Bones, fb.Bsq[pi]], [PB[7]])
        while nxt:
            nxt.pop(0)()
        pending_tail.append(lambda: rstd_from_ss(fb, 1, 7))
        for oc in range(KC):
            def tl(oc=oc):
                i = fb.ntmp % len(fb.tmp)
                fb.ntmp += 1
                K.tt(dve, fb.tmp[i][:], fb.y[:, oc, :], fb.rstd[1][:], ALU.mult, [fb.By, fb.Brstd[1]],
                     [fb.Btmp[i]])
                K.stt(xT[:, oc, tsl], fb.tmp[i][:], Cg[:, oc:oc + 1], xT[:, oc, tsl], ALU.mult, ALU.add,
                      [fb.Btmp[i], Bmv, XB[tt]], [XB[tt]])
            pending_tail.append(tl)

    def ffn_sublayer(fb, l, j, S):
        f = l * 2 + (0 if j == 0 else 1)
        ntile = S // 512
        Cg = vec(l, j, 2)
        pending_tail = []
        for st in prenorm_steps(fb, l, j, 0, fb.h, fb.Bh):
            st()
        for tt in range(ntile):
            tsl = slice(tt * 512, (tt + 1) * 512)
            for fc in range(NFC):
                r = fb.n13 % 3
                fb.n13 += 1
                K.dma(sp, fb.w13[r][:], w13b[f, fc].rearrange("p (k n) -> p k n", k=KC), reads=[WB13[f]],
                      writes=[fb.Bw13[r]])
                ba, bb = fc % 2, 2 + fc % 2
                for half, bank in ((0, ba), (1, bb)):
                    for kc in range(KC):
                        K.mm(psb[bank][:, :], fb.w13[r][:, kc, half * 128:(half + 1) * 128], fb.h[:, kc, :],
                             kc == 0, kc == KC - 1, [fb.Bw13[r], fb.Bh], [PB[bank]])
                K.actf(fb.s[:], psb[ba][:, :], ACT.Silu, [PB[ba]], [fb.Bs])
                K.tt(dve, fb.g[:, fc, :], fb.s[:], psb[bb][:, :], ALU.mult, [fb.Bs, PB[bb]], [fb.Bg[fc]])
                if pending_tail:
                    pending_tail.pop(0)()
            while pending_tail:
                pending_tail.pop(0)()
            nxt = prenorm_steps(fb, l, j, tt + 1, fb.h, fb.Bh) if tt + 1 < ntile else []
            if nxt:
                nxt.pop(0)()
            def mm_oc(oc, bank, f=f):
                for hf in range(2):
                    r = fb.n2 % 3
                    fb.n2 += 1
                    K.dma(sp, fb.w2[r][:],
                          w2b[f, oc].rearrange("p (k n) -> p k n", k=NFC)[:, hf * 11:(hf + 1) * 11, :],
                          reads=[WB2[f]], writes=[fb.Bw2[r]])
                    for q in range(11):
                        fc = hf * 11 + q
                        K.mm(psb[bank][:, :], fb.w2[r][:, q, :], fb.g[:, fc, :], fc == 0, fc == NFC - 1,
                             [fb.Bw2[r], fb.Bg[fc]], [PB[bank]])
            yphase(fb, tt, Cg, mm_oc, nxt, pending_tail)
        while pending_tail:
            pending_tail.pop(0)()


    def mx_alloc(stack, with_h=True, with_y=False, nrstd=2):
        fb = FfnBufs()
        if with_h:
            fb.h = sb("m_h", [128, KC, 512], BF16, stack)
        if with_y:
            fb.y = sb("m_y", [128, KC, 512], F32, stack)
        fb.rstd = [sb("m_rstd%d" % i, [128, 512], F32, stack) for i in range(nrstd)]
        fb.sq = [sb("m_sq%d" % i, [128, 512], BF16, stack) for i in range(2)]
        fb.tmp = [sb("m_tmp%d" % i, [128, 512], F32, stack) for i in range(2)]
        fb.Bh, fb.By = Buf("h"), Buf("y")
        fb.Brstd = [Buf(), Buf()]
        fb.Bsq = [Buf(), Buf()]
        fb.Btmp = [Buf(), Buf()]
        fb.nsq = 0
        fb.ntmp = 0
        return fb

    class Gen:
        pass

    def gen_alloc(stack, mask_col, use_pjx, nA=2, blocks=True):
        G = Gen()
        G.PJ = sb("g_pj", [128, 512], I32, stack)
        G.A = [sb("g_a%d" % i, [128, 512], I32, stack) for i in range(nA)]
        G.BPJ = Buf("pj")
        G.BA = [Buf() for _ in range(nA)]
        G.n = 0
        G.mask = icst[:, mask_col:mask_col + 1]
        K.op(pool, lambda e: e.iota(G.PJ[:], [[1, 512]], base=0, channel_multiplier=0), [], [G.BPJ])
        K.op(pool, lambda e: e.iota(G.A[0][:], [[0, 512]], base=0, channel_multiplier=1), [], [G.BA[0]])
        if blocks:
            G.Jf = sb("g_jf", [128, 512], I32, stack)
            G.BJf = Buf()
            K.copy(dve, G.Jf[:], G.PJ[:], [G.BPJ], [G.BJf])
        K.op(pool, lambda e: e.tensor_tensor(G.PJ[:], G.PJ[:], G.A[0][:], ALU.mult), [G.BPJ, G.BA[0]], [G.BPJ])
        if blocks:
            G.Pf = sb("g_pf", [128, 512], I32, stack)
            G.PJ0 = sb("g_pj0", [128, 512], I32, stack)
            G.BPf, G.BPJ0 = Buf(), Buf()
            K.copy(dve, G.Pf[:], G.A[0][:], [G.BA[0]], [G.BPf])
        if use_pjx:
            K.op(pool, lambda e: e.iota(G.A[0][:], [[0, 512]], base=0, channel_multiplier=0), [], [G.BA[0]])
            K.op(pool, lambda e: e.tensor_scalar(G.A[0][:], G.A[0][:], icst[:, 4:5], None, ALU.add),
                 [G.BA[0], Bic], [G.BA[0]])
            K.op(pool, lambda e: e.tensor_tensor(G.PJ[:], G.PJ[:], G.A[0][:], ALU.add), [G.BPJ, G.BA[0]], [G.BPJ])
        if blocks:
            K.copy(dve, G.PJ0[:], G.PJ[:], [G.BPJ], [G.BPJ0])
        return G

    def gen_block(G, S_tot, sp0):
        K.op(pool, lambda e: e.tensor_scalar(G.PJ[:], G.Pf[:], int(sp0 % S_tot), None, ALU.mult), [G.BPf], [G.BPJ])
        K.op(pool, lambda e: e.tensor_tensor(G.PJ[:], G.PJ[:], G.PJ0[:], ALU.add), [G.BPJ, G.BPJ0], [G.BPJ])

    def gen_tile(G, dst, Bdst, S_tot, s0, sp0, off):
        base = (s0 * sp0 + off) % S_tot
        step = s0 % S_tot
        i = G.n % len(G.A)
        G.n += 1
        A = G.A[i]
        if step == 0:
            K.op(pool, lambda e: e.iota(A[:], [[0, 512]], base=base, channel_multiplier=0), [], [G.BA[i]])
        else:
            K.op(pool, lambda e: e.tensor_scalar(A[:], G.Jf[:], int(step), int(base), ALU.mult, ALU.add), [G.BJf],
                 [G.BA[i]])
        K.op(pool, lambda e: e.tensor_tensor(A[:], A[:], G.PJ[:], ALU.add), [G.BA[i], G.BPJ], [G.BA[i]])
        K.op(dve, lambda e: e.tensor_scalar(A[:], A[:], G.mask, None, ALU.bitwise_and), [G.BA[i], Bic], [G.BA[i]])
        K.actf(dst, A[:], ACT.Sin, [G.BA[i]], [Bdst], scale=2.0 * np.pi / S_tot, bias=-np.pi)

    BUd, BUall, Bmixo = Buf("Ud"), Buf("Uall"), Buf("mixo")

    def odd_phase_a(l, S, stack):
        i_od = l // 2
        fb = mx_alloc(stack, nrstd=1)
        win = sb("o_win", [128, KC, 1536], BF16, stack)
        wsT = sb("o_wsT", [128, 512], BF16, stack)
        sgb = sb("o_sgb", [1, 512], BF16, stack)
        nrm = sb("o_nrm", [128, 512], F32, stack)
        ccsc = sb("o_ccsc", [128, 256], BF16, stack)
        gtmp = sb("o_gtmp", [128, 512], BF16, stack)
        zcT = sb("o_zcT", [128, 4, 512], BF16, stack)
        usb = [sb("o_usb%d" % i, [128, 1024], BF16, stack) for i in range(2)]
        uT = sb("o_uT", [128, 4, 512], F32, stack)
        gv = sb("o_gv", [128, 512], F32, stack)
        vtok = [sb("o_vtok%d" % i, [128, 512], BF16, stack) for i in range(2)]
        odT = sb("o_odT", [128, 4, 512], BF16, stack)
        ssq = sb("o_ssq", [128, 2], F32, stack)
        Bwin, BwsT, Bsgb, Bnrm, Bccsc, Bgtmp, BzcT, BuT, Bgv, BodT, Bssq = (Buf() for _ in range(11))
        Busb = [Buf(), Buf()]
        Bvtok = [Buf(), Buf()]
        K.dma(sp, win[:], od_win_b[i_od].rearrange("p (k n) -> p k n", k=KC), reads=[Bodw], writes=[Bwin])
        K.dma(sp, wsT[:], sgu_wsT_b[i_od], reads=[Bodw], writes=[BwsT])
        K.dma(sp, sgb[:], sgu_b_b[i_od], reads=[Bodw], writes=[Bsgb])
        K.dma(sp, nrm[:], sgu_nrm[i_od], writes=[Bnrm])
        G = gen_alloc(stack, 2, False, nA=1, blocks=False)
        gen_tile(G, gtmp[:], Bgtmp, 128, 0, 0, 96)
        K.copy(dve, ccsc[:, 0:128], gtmp[:, 0:128], [Bgtmp], [Bccsc])
        gen_tile(G, gtmp[:], Bgtmp, 128, 0, 0, 0)
        K.copy(dve, ccsc[:, 128:256], gtmp[:, 0:128], [Bgtmp], [Bccsc])
        mixo_v = mixo.rearrange("(c p) s -> p c s", p=128)
        nb = [0]

        def bank2():
            nb[0] += 1
            return nb[0] % 2

        for tt in range(S // 512):
            for st in prenorm_steps(fb, l, 1, tt, fb.h, fb.Bh):
                st()
            for g in range(4):
                bank = bank2()
                for kc in range(KC):
                    K.mm(psb[bank][:, :], win[:, kc, g * 128:(g + 1) * 128], fb.h[:, kc, :], kc == 0, kc == KC - 1,
                         [Bwin, fb.Bh], [PB[bank]])
                K.copy(act if g % 2 == 0 else dve, zcT[:, g, :], psb[bank][:, :], [PB[bank]], [BzcT])
            for g in range(4):
                bank = bank2()
                for kc in range(KC):
                    K.mm(psb[bank][:, :], win[:, kc, 512 + g * 128:512 + (g + 1) * 128], fb.h[:, kc, :], kc == 0,
                         kc == KC - 1, [Bwin, fb.Bh], [PB[bank]])
                K.actf(uT[:, g, :], psb[bank][:, :], ACT.Gelu, [PB[bank]], [BuT])
            for ts in range(4):
                tk = slice(ts * 128, (ts + 1) * 128)
                ub, Bub = usb[ts % 2], Busb[ts % 2]
                for gp in range(2):
                    bank = 2 + gp
                    for gg in range(2):
                        g = gp * 2 + gg
                        K.mm(psb[bank][:, gg * 256:(gg + 1) * 256], zcT[:, g, tk], ccsc[:], True, True,
                             [BzcT, Bccsc], [PB[bank]])
                    K.copy(act if gp == 0 else dve, ub[:, gp * 512:(gp + 1) * 512], psb[bank][:, :], [PB[bank]],
                           [Bub])
                K.dma(sp, Ud[tt * 512 + ts * 128: tt * 512 + (ts + 1) * 128, :], ub[:], reads=[Bub], writes=[BUd])
                bank = bank2()
                for kc in range(KC):
                    K.mm(psb[bank][:, :], fb.h[:, kc, tk], win[:, kc, 1024:1536], kc == 0, kc == KC - 1,
                         [Bwin, fb.Bh], [PB[bank]])
                K.actf(gv[:], psb[bank][:, :], ACT.Gelu, [PB[bank]], [Bgv])
                vt, Bvt = vtok[ts % 2], Bvtok[ts % 2]
                K.actf(vt[:], gv[:], ACT.Square, [Bgv], [Bvt, Bssq], accum_out=ssq[:, 0:1])
                K.actf(ssq[:, 1:2], ssq[:, 0:1], ACT.Sqrt, [Bssq], [Bssq], scale=1.0 / 512, bias=EPS)
                K.op(dve, lambda e: e.reciprocal(ssq[:, 1:2], ssq[:, 1:2]), [Bssq], [Bssq])
                K.stt(vt[:], gv[:], ssq[:, 1:2], nrm[:], ALU.mult, ALU.mult, [Bgv, Bssq, Bnrm], [Bvt])
                bank = 6
                for hd in range(4):
                    hs = slice(hd * 128, (hd + 1) * 128)
                    K.mm(psb[bank][:, hs], vt[:, hs], wsT[:, hs], True, False, [Bvt, BwsT], [PB[bank]])
                    K.mm(psb[bank][:, hs], onesb[0:1, :], sgb[0:1, hs], False, True, [Bones, Bsgb], [PB[bank]])
                K.tt(dve, odT[:, :, tk], uT[:, :, tk], psb[bank][:, :].rearrange("p (h i) -> p h i", h=4), ALU.mult,
                     [BuT, PB[bank]], [BodT])
            K.dma(sp, mixo_v[:, 4:8, tt * 512:(tt + 1) * 512], odT[:], reads=[BodT], writes=[Bmixo])

    def odd_phase_b(S, S_keys, Usrc, BUsrc, is_sample, stack):
        G = gen_alloc(stack, 1 if is_sample else 0, is_sample)
        ct = [sb("b_ct%d" % i, [128, 512], BF16, stack) for i in range(2)]
        stl = [sb("b_st%d" % i, [128, 512], BF16, stack) for i in range(2)]
        ut = [sb("b_ut%d" % i, [128, 1024], BF16, stack) for i in range(3)]
        fcs = sb("b_fcs", [128, 4, 512], BF16, stack)
        Rc = [sb("b_rc%d" % i, [128, 512], I16, stack) for i in range(2)]
        Rs = [sb("b_rs%d" % i, [128, 512], I16, stack) for i in range(2)]
        R32 = sb("b_r32", [128, 512], I32, stack)
        Di = sb("b_di", [128, 512], I32, stack)
        D16 = sb("b_d16", [128, 512], I16, stack)
        m16 = sb("b_m16", [128, 1], I16, stack)
        Bct, Bst = [Buf(), Buf()], [Buf(), Buf()]
        BRc, BRs = [Buf(), Buf()], [Buf(), Buf()]
        But = [Buf(), Buf(), Buf()]
        Bfcs, BDi, BD16, BR32, Bm16 = Buf(), Buf(), Buf(), Buf(), Buf()
        mixo_v = mixo.rearrange("(c p) s -> p c s", p=128)
        scale = 1.0 / float(np.sqrt(S_keys * 128.0))
        na = S_keys // 128
        sc_sin = 2.0 * np.pi / S_keys
        n = 0
        for bq in range(S // 512):
            sp0 = bq * 512
            gen_block(G, S_keys, sp0)
            K.op(pool, lambda e: e.tensor_scalar(Di[:], G.Jf[:], 128, int((128 * sp0) % S_keys), ALU.mult, ALU.add),
                 [G.BJf], [BDi])
            K.op(dve, lambda e: e.tensor_scalar(Di[:], Di[:], G.mask, None, ALU.bitwise_and), [BDi, Bic], [BDi])
            K.copy(dve, D16[:], Di[:], [BDi], [BD16])
            for R0, BR0, off in ((Rc[0], BRc[0], (3 * S_keys) // 4), (Rs[0], BRs[0], S_keys // 2)):
                K.op(pool, lambda e, off=off: e.tensor_scalar(R32[:], G.PJ[:], int(off), None, ALU.add), [G.BPJ],
                     [BR32])
                K.op(dve, lambda e: e.tensor_scalar(R32[:], R32[:], G.mask, None, ALU.bitwise_and), [BR32, Bic],
                     [BR32])
                K.copy(dve, R0[:], R32[:], [BR32], [BR0])
            for a in range(na):
                s0 = a * 128
                i2, i3 = n % 2, n % 3
                n += 1
                cur, nxt = a % 2, (a + 1) % 2
                K.actf(ct[i2][:], Rc[cur][:], ACT.Sin, [BRc[cur]], [Bct[i2]], scale=sc_sin, bias=-np.pi)
                K.actf(stl[i2][:], Rs[cur][:], ACT.Sin, [BRs[cur]], [Bst[i2]], scale=sc_sin, bias=-np.pi)
                if a + 1 < na:
                    for R_, BR_ in ((Rc, BRc), (Rs, BRs)):
                        K.tt(dve, R_[nxt][:], R_[cur][:], D16[:], ALU.add, [BR_[cur], BD16], [BR_[nxt]])
                        K.op(dve, lambda e, R_=R_, nxt=nxt: e.tensor_scalar(R_[nxt][:], R_[nxt][:], G.mask, None,
                                                                            ALU.bitwise_and), [BR_[nxt], Bic],
                             [BR_[nxt]])
                K.dma(sp, ut[i3][:], Usrc(s0), reads=[BUsrc], writes=[But[i3]])
                for g in range(4):
                    K.mm(psb[g][:, :], ut[i3][:, g * 256:g * 256 + 128], ct[i2][:], a == 0, False,
                         [But[i3], Bct[i2]], [PB[g]])
                    K.mm(psb[g][:, :], ut[i3][:, g * 256 + 128:g * 256 + 256], stl[i2][:], False, a == na - 1,
                         [But[i3], Bst[i2]], [PB[g]])
            for g in range(4):
                if g % 2 == 0:
                    K.actf(fcs[:, g, :], psb[g][:, :], ACT.Copy, [PB[g]], [Bfcs], scale=scale)
                else:
                    K.ts(dve, fcs[:, g, :], psb[g][:, :], scale, None, ALU.mult, None, [PB[g]], [Bfcs])
            K.dma(sp, mixo_v[:, 0:4, bq * 512:(bq + 1) * 512], fcs[:], reads=[Bfcs], writes=[Bmixo])

    def mixer_phase_c(l, S, wout_dram, Bw_dram, stack):
        fb = mx_alloc(stack, with_h=False, with_y=True)
        wo = sb("c_wo", [128, KC, 1024], BF16, stack)
        ot = [sb("c_ot%d" % i, [128, KC, 512], BF16, stack) for i in range(2)]
        Bwo = Buf()
        Bot = [Buf(), Buf()]
        K.dma(sp, wo[:], wout_dram.rearrange("p (k n) -> p k n", k=KC), reads=[Bw_dram], writes=[Bwo])
        mixo_v = mixo.rearrange("(c p) s -> p c s", p=128)
        Cg = vec(l, 1, 2)
        pending = []
        for tt in range(S // 512):
            o_, Bo = ot[tt % 2], Bot[tt % 2]
            K.dma(sp, o_[:], mixo_v[:, :, tt * 512:(tt + 1) * 512], reads=[Bmixo], writes=[Bo])

            def mm_oc(oc, bank, o_=o_, Bo=Bo):
                for ic in range(KC):
                    K.mm(psb[bank][:, :], wo[:, ic, oc * 128:(oc + 1) * 128], o_[:, ic, :], ic == 0, ic == KC - 1,
                         [Bwo, Bo], [PB[bank]])
            yphase(fb, tt, Cg, mm_oc, [], pending)
            while pending:
                pending.pop(0)()

    def odd_mixer(l, S, is_sample):
        i_od = l // 2
        with ExitStack() as st:
            odd_phase_a(l, S, st)
            K.barrier()
        if is_sample and GRP > 1 and not cfg.no_xg:
            for ci in range(NUC):
                K.op(pool, lambda e, ci=ci: e.collective_compute(
                    "AllGather", ALU.bypass, replica_groups=cfg.replica_groups,
                    ins=[Ud[ci * RCU:(ci + 1) * RCU, :]], outs=[Uall[ci]]), [BUd], [BUall])
            K.barrier()

            def usrc(s0):
                g, i = s0 // S, s0 % S
                ci, w = i // RCU, i % RCU
                return Uall[ci, g * RCU + w:g * RCU + w + 128, :]
            Usrc, BUsrc, S_keys = usrc, BUall, GRP * S
        else:
            Usrc, BUsrc, S_keys = (lambda s0: Ud[s0:s0 + 128, :]), BUd, S
        with ExitStack() as st:
            odd_phase_b(S, S_keys, Usrc, BUsrc, is_sample and GRP > 1 and not cfg.no_xg, st)
            K.barrier()
        with ExitStack() as st:
            mixer_phase_c(l, S, od_wout_b[i_od], Bodw, st)
            K.barrier()

    fcst = sb("fcst", [128, 64], F32)
    Bfc = Buf("fcst")
    K.dma(sp, fcst[:], fconst, writes=[Bfc])
    NCHL = SMAXL // 64
    decs = sb("decs", [128, 2, 2, NCHL], F32)
    Bdecs = Buf("decs")
    Bgq, Bgvt, Bgkv, Bgs, Bgg, Bgsum, Bgsall = (Buf() for _ in range(7))
    BQd, BKd, BKall, BVd, BVall, Bmixm = (Buf() for _ in range(6))
    rr = [0]

    def rbank(lo=0, n=2):
        rr[0] += 1
        return lo + rr[0] % n

    def even_a1(l, S, stack):
        i_ev = l // 2
        fb = mx_alloc(stack)
        win = sb("a_win", [128, KC, 1056], BF16, stack)
        wal = sb("a_wal", [33, 512], BF16, stack)
        tc = sb("a_tc", [128, 516], F32, stack)
        alr = sb("a_alr", [33, 512], BF16, stack)
        qk = sb("a_qk", [128, 4, 512], F32, stack)
        spt = sb("a_spt", [128, 512], F32, stack)
        E = sb("a_E", [128, 4, 2, 128], F32, stack)
        ekd = sb("a_ekd", [128, 512], F32, stack)
        kd = sb("a_kd", [128, 512], BF16, stack)
        vtok = [sb("a_vtok%d" % i, [128, 512], BF16, stack) for i in range(2)]
        kvst = sb("a_kvst", [128, 2, 4, 128], F32, stack)
        qst = sb("a_qst", [128, 4, 2, 512], BF16, stack)
        Bwin, Bwal, Btc, Balr, Bqk, Bspt, BE, Bekd, Bkd, Bkvst, Bqst = (Buf() for _ in range(11))
        Bvtok = [Buf(), Buf()]
        K.dma(sp, win[:], ev_win1_b[i_ev].rearrange("p (k n) -> p k n", k=KC), reads=[Bevw], writes=[Bwin])
        K.dma(sp, wal[:], gla_wal_b[i_ev], reads=[Bevw], writes=[Bwal])
        K.dma(sp, tc[:], tconst, writes=[Btc])
        K.op(dve, lambda e: e.memset(alr[32:33, :], 1.0), [], [Balr])
        gq_v = gq.rearrange("k r p s -> p k r s")
        for tt in range(S // 512):
            for st in prenorm_steps(fb, l, 1, tt, fb.h, fb.Bh):
                st()
            for c4 in range(4):
                bank = rbank()
                for kc in range(KC):
                    K.mm(psb[bank][:, :], win[:, kc, c4 * 128:(c4 + 1) * 128], fb.h[:, kc, :], kc == 0, kc == KC - 1,
                         [Bwin, fb.Bh], [PB[bank]])
                K.copy(act if c4 % 2 == 0 else dve, qk[:, c4, :], psb[bank][:, :], [PB[bank]], [Bqk])
            bank = rbank()
            for kc in range(KC):
                K.mm(psb[bank][0:32, :], win[:, kc, 1024:1056], fb.h[:, kc, :], kc == 0, kc == KC - 1,
                     [Bwin, fb.Bh], [PB[bank]])
            K.copy(act, alr[0:32, :], psb[bank][0:32, :], [PB[bank]], [Balr])
            for ts in range(4):
                tk = slice(ts * 128, (ts + 1) * 128)
                n = tt * 4 + ts
                vt, Bvt = vtok[n % 2], Bvtok[n % 2]
                for kc in range(KC):
                    K.mm(psb[2][:, 0:256], fb.h[:, kc, tk], win[:, kc, 256:512], kc == 0, kc == KC - 1,
                         [Bwin, fb.Bh], [PB[2]])
                for kc in range(KC):
                    K.mm(psb[3][:, :], fb.h[:, kc, tk], win[:, kc, 512:1024], kc == 0, kc == KC - 1,
                         [Bwin, fb.Bh], [PB[3]])
                K.copy(act, vt[:], psb[3][:, :], [PB[3]], [Bvt])
                K.dma(sp, gvt[n], vt[:], reads=[Bvt], writes=[Bgvt])
                K.mm(psb[4][:, :], alr[0:33, tk], wal[0:33, :], True, True, [Balr, Bwal], [PB[4]])
                K.actf(spt[:], psb[4][:, :], ACT.Exp, [PB[4]], [Bspt], scale=-1.0)
                K.actf(spt[:], spt[:], ACT.Ln, [Bspt], [Bspt], bias=1.0)
                for pr in range(2):
                    K.mm(psb[5][:, pr * 130:pr * 130 + 130], spt[:, pr * 128:(pr + 1) * 128], tc[:, 0:130], True, True,
                         [Bspt, Btc], [PB[5]])
                for pr in range(2):
                    K.mm(psb[6 + pr][:, 0:258], spt[:, 256 + pr * 128:256 + (pr + 1) * 128], tc[:, 130:388], True,
                         True, [Bspt, Btc], [PB[6 + pr]])
                K.mm(psb[4][:, 0:256], tc[:, 130:258], spt[:, 0:256], True, True, [Bspt, Btc], [PB[4]])
                K.mm(psb[4][:, 256:512], tc[:, 388:516], spt[:, 256:512], True, True, [Bspt, Btc], [PB[4]])
                sc = 1.0 / 16.0
                for pr in range(2):
                    K.actf(E[:, 0, pr, :], psb[5][:, pr * 130:pr * 130 + 128], ACT.Exp, [PB[5]], [BE], scale=-sc)
                    K.actf(E[:, 1, pr, :], psb[5][:, pr * 130:pr * 130 + 128], ACT.Exp, [PB[5]], [BE], scale=sc)
                    K.actf(decs[:, 0, pr, 2 * n:2 * n + 2], psb[5][:, pr * 130 + 128:pr * 130 + 130], ACT.Exp,
                           [PB[5]], [Bdecs], scale=-sc)
                    K.actf(E[:, 2, pr, :], psb[6 + pr][:, 0:128], ACT.Exp, [PB[6 + pr]], [BE], scale=-sc)
                    K.actf(E[:, 3, pr, :], psb[6 + pr][:, 128:256], ACT.Exp, [PB[6 + pr]], [BE], scale=sc)
                    K.actf(decs[:, 1, pr, 2 * n:2 * n + 2], psb[6 + pr][:, 256:258], ACT.Exp, [PB[6 + pr]], [Bdecs],
                           scale=-sc)
                K.actf(ekd[:], psb[4][:, :], ACT.Exp, [PB[4]], [Bekd], scale=-sc)
                K.tt(dve, kd[:, 0:256], psb[2][:, 0:256], ekd[:, 0:256], ALU.mult, [PB[2], Bekd], [Bkd])
                K.tt(dve, kd[:, 256:512], psb[2][:, 0:256], ekd[:, 256:512], ALU.mult, [PB[2], Bekd], [Bkd])
                for pr in range(2):
                    K.stt(qst[:, 0, pr, tk], qk[:, pr, tk], 0.125, E[:, 0, pr, :], ALU.mult, ALU.mult, [Bqk, BE], [Bqst])
                    K.stt(qst[:, 1, pr, tk], qk[:, pr, tk], 0.125, E[:, 2, pr, :], ALU.mult, ALU.mult, [Bqk, BE], [Bqst])
                    K.tt(pool, qst[:, 2, pr, tk], qk[:, 2 + pr, tk], E[:, 1, pr, :], ALU.mult, [Bqk, BE], [Bqst])
                    K.tt(pool, qst[:, 3, pr, tk], qk[:, 2 + pr, tk], E[:, 3, pr, :], ALU.mult, [Bqk, BE], [Bqst])
                for c in range(2):
                    for dr in range(2):
                        for h in range(4):
                            hb = (h % 2) * 64
                            col = (dr * 2 + h // 2) * 128
                            K.mm(psb[c][hb:hb + 64, col:col + 128],
                                 kd[c * 64:(c + 1) * 64, dr * 256 + h * 64:dr * 256 + (h + 1) * 64],
                                 vt[c * 64:(c + 1) * 64, h * 128:(h + 1) * 128], True, True, [Bkd, Bvt], [PB[c]])
                    K.copy(act if c == 0 else dve, kvst[:, :, c * 2:c * 2 + 2, :],
                           psb[c][:, :].rearrange("p (d r v) -> p d r v", d=2, r=2), [PB[c]], [Bkvst])
                for dr in range(2):
                    K.dma(sp, gkv[dr, 2 * n:2 * n + 2].rearrange("c r p v -> p c r v"),
                          kvst[:, dr, :, :].rearrange("p (c r) v -> p c r v", c=2), reads=[Bkvst], writes=[Bgkv])
            K.dma(sp, gq_v[:, :, :, tt * 512:(tt + 1) * 512], qst[:], reads=[Bqst], writes=[Bgq])

    def even_r(S, stack, store, Sin=None):
        nch = S // 64
        CB = min(8, nch)
        St = [[sb("r_st%d%d" % (d_, p_), [128, 128], F32, stack) for p_ in range(2)] for d_ in range(2)]
        BSt = [[Buf(), Buf()], [Buf(), Buf()]]
        kvb = [sb("r_kvb%d" % i, [128, CB, 2, 128], F32, stack) for i in range(2)]
        stb = [sb("r_stb%d" % i, [128, CB, 2, 128], BF16, stack) for i in range(2)]
        Bkvb, Bstb = [Buf(), Buf()], [Buf(), Buf()]
        nb = 0
        for dr in range(2):
            for pr in range(2):
                if Sin is None:
                    K.op(dve, lambda e, dr=dr, pr=pr: e.memset(St[dr][pr][:], 0.0), [], [BSt[dr][pr]])
                else:
                    K.copy(dve, St[dr][pr][:], Sin[dr][pr][0][:], [Sin[dr][pr][1]], [BSt[dr][pr]])
            batches = list(range(0, nch, CB))
            if dr == 1:
                batches = batches[::-1]
            for c0 in batches:
                kb, Bk = kvb[nb % 2], Bkvb[nb % 2]
                sbf, Bs_ = stb[nb % 2], Bstb[nb % 2]
                nb += 1
                K.dma(sp, kb[:], gkv[dr, c0:c0 + CB].rearrange("c r p v -> p c r v"), reads=[Bgkv], writes=[Bk])
                cis = list(range(CB))
                if dr == 1:
                    cis = cis[::-1]
                for ci in cis:
                    c = c0 + ci
                    for pr in range(2):
                        if store:
                            K.copy(act, sbf[:, ci, pr, :], St[dr][pr][:], [BSt[dr][pr]], [Bs_])
                        K.stt(St[dr][pr][:], St[dr][pr][:], decs[:, dr, pr, c:c + 1], kb[:, ci, pr, :], ALU.mult,
                              ALU.add, [BSt[dr][pr], Bdecs, Bk], [BSt[dr][pr]])
                if store:
                    K.dma(sp, gs[dr, c0:c0 + CB].rearrange("c r p v -> p c r v"), sbf[:], reads=[Bs_], writes=[Bgs])
        return St, BSt

    def even_exchange(S, stack):
        nch = S // 64
        St, BSt = even_r(S, stack, False)
        pk = sb("x_pk", [128, 4, 129], F32, stack)
        Bpk = Buf()
        for dr in range(2):
            for pr in range(2):
                k4 = dr * 2 + pr
                K.copy(dve, pk[:, k4, 0:128], St[dr][pr][:], [BSt[dr][pr]], [Bpk])
                K.copy(dve, pk[:, k4, 128:129], decs[:, dr, pr, 0:1], [Bdecs], [Bpk])
                for c in range(1, nch):
                    K.tt(dve, pk[:, k4, 128:129], pk[:, k4, 128:129], decs[:, dr, pr, c:c + 1], ALU.mult,
                         [Bpk, Bdecs], [Bpk])
        K.dma(sp, gsum.rearrange("(k p) v -> p k v", p=128), pk[:], reads=[Bpk], writes=[Bgsum])
        K.barrier()
        K.op(pool, lambda e: e.collective_compute("AllGather", ALU.bypass, replica_groups=cfg.replica_groups,
                                                  ins=[gsum], outs=[gsum_all]), [Bgsum], [Bgsall])
        K.barrier()
        pa = sb("x_pa", [128, GRP, 4, 129], F32, stack)
        Bpa = Buf()
        K.dma(sp, pa[:], gsum_all.rearrange("(g k p) v -> p g k v", g=GRP, p=128), reads=[Bgsall], writes=[Bpa])
        Sin = [[None, None], [None, None]]
        cf = sb("x_cf", [128, 2], F32, stack)
        Bcf = Buf()
        for dr in range(2):
            for pr in range(2):
                k4 = dr * 2 + pr
                t_ = sb("x_sin%d" % k4, [128, 128], F32, stack)
                Bt = Buf()
                K.op(dve, lambda e, t_=t_: e.memset(t_[:], 0.0), [], [Bt])
                for r1 in range(GRP):
                    K.copy(dve, cf[:, 0:1], fcst[:, 8 + dr * 4 + r1:9 + dr * 4 + r1], [Bfc], [Bcf])
                    for r2 in range(GRP):
                        ic_ = 16 + dr * 16 + r1 * 4 + r2
                        K.ts(dve, cf[:, 1:2], pa[:, r2, k4, 128:129], -1.0, fcst[:, ic_:ic_ + 1], ALU.add, ALU.mult,
                             [Bpa, Bfc], [Bcf])
                        K.stt(cf[:, 0:1], cf[:, 1:2], 1.0, cf[:, 0:1], ALU.add, ALU.mult, [Bcf], [Bcf])
                    K.stt(t_[:], pa[:, r1, k4, 0:128], cf[:, 0:1], t_[:], ALU.mult, ALU.add, [Bpa, Bcf, Bt], [Bt])
                Sin[dr][pr] = (t_, Bt)
        return Sin

    def even_o(l, S, stack):
        i_ev = l // 2
        tcm = sb("o_tcm", [128, 256], F32, stack)
        gn = sb("o_gn", [128, 1], F32, stack)
        qt = [sb("o_qt%d" % i, [128, 4, 2, 512], BF16, stack) for i in range(2)]
        vtl = [sb("o_vt%d" % i, [128, 4, 512], BF16, stack) for i in range(2)]
        gt = [sb("o_gt%d" % i, [128, 4, 512], BF16, stack) for i in range(2)]
        sf = [sb("o_sf%d" % i, [128, 8, 2, 128], BF16, stack) for i in range(2)]
        sbw = [sb("o_sb%d" % i, [128, 8, 2, 128], BF16, stack) for i in range(2)]
        am = [sb("o_am%d" % i, [128, 256], BF16, stack) for i in range(2)]
        sq = sb("o_sq", [128, 512], BF16, stack)
        rs = sb("o_rs", [128, 512], F32, stack)
        on = sb("o_on", [128, 512], F32, stack)
        ost = sb("o_ost", [128, 4, 512], BF16, stack)
        Btcm, Bgn, Bsq, Brs, Bon, Bost = (Buf() for _ in range(6))
        Bqt, Bvtl, Bgt, Bsf, Bsbw, Bam = ([Buf(), Buf()] for _ in range(6))
        K.dma(sp, tcm[:, 0:128], tconst[:, 0:128], writes=[Btcm])
        K.dma(sp, tcm[:, 128:256], tconst[:, 130:258], writes=[Btcm])
        K.dma(sp, gn[:], gla_nrm[i_ev], writes=[Bgn])
        gq_v = gq.rearrange("k r p s -> p k r s")
        gg_v = gg.rearrange("(c p) s -> p c s", p=128)
        mixo_v = mixo.rearrange("(c p) s -> p c s", p=128)
        na = 0
        for tt in range(S // 512):
            i2 = tt % 2
            K.dma(sp, qt[i2][:], gq_v[:, :, :, tt * 512:(tt + 1) * 512], reads=[Bgq], writes=[Bqt[i2]])
            K.dma(sp, vtl[i2][:], gvt[tt * 4:(tt + 1) * 4].rearrange("n p v -> p n v"), reads=[Bgvt], writes=[Bvtl[i2]])
            K.dma(sp, gt[i2][:], gg_v[:, :, tt * 512:(tt + 1) * 512], reads=[Bgg], writes=[Bgt[i2]])
            K.dma(sp, sf[i2][:], gs[0, tt * 8:(tt + 1) * 8].rearrange("c r p v -> p c r v"), reads=[Bgs],
                  writes=[Bsf[i2]])
            K.dma(sp, sbw[i2][:], gs[1, tt * 8:(tt + 1) * 8].rearrange("c r p v -> p c r v"), reads=[Bgs],
                  writes=[Bsbw[i2]])
            q_ = qt[i2]
            for ts in range(4):
                tk = slice(ts * 128, (ts + 1) * 128)
                for h in range(4):
                    pr, hb = h // 2, (h % 2) * 64
                    rows = slice(hb, hb + 64)
                    ab = h % 2
                    ob = 2 + h % 2
                    K.mm(psb[ab][:, 0:128], q_[rows, 2, pr, tk], q_[rows, 0, pr, tk], True, True, [Bqt[i2]], [PB[ab]])
                    K.mm(psb[ab][:, 128:256], q_[rows, 3, pr, tk], q_[rows, 1, pr, tk], True, True, [Bqt[i2]],
                         [PB[ab]])
                    a_, Ba = am[na % 2], Bam[na % 2]
                    na += 1
                    K.tt(dve, a_[:], psb[ab][:, 0:256], tcm[:], ALU.mult, [PB[ab], Btcm], [Ba])
                    o0 = pr * 128
                    oc_ = slice(o0, o0 + 128)
                    K.mm(psb[ob][:, oc_], vtl[i2][:, ts, h * 128:(h + 1) * 128], a_[:, 0:128], True, False,
                         [Bvtl[i2], Ba], [PB[ob]])
                    K.mm(psb[ob][:, oc_], vtl[i2][:, ts, h * 128:(h + 1) * 128], a_[:, 128:256], False, False,
                         [Bvtl[i2], Ba], [PB[ob]])
                    for c in range(2):
                        ci = ts * 2 + c
                        cs = slice(o0 + c * 64, o0 + (c + 1) * 64)
                        tks = slice(ts * 128 + c * 64, ts * 128 + (c + 1) * 64)
                        K.mm(psb[ob][:, cs], sf[i2][rows, ci, pr, :], q_[rows, 0, pr, tks], False, False,
                             [Bsf[i2], Bqt[i2]], [PB[ob]])
                        K.mm(psb[ob][:, cs], sbw[i2][rows, ci, pr, :], q_[rows, 1, pr, tks], False, c == 1,
                             [Bsbw[i2], Bqt[i2]], [PB[ob]])
                for par in range(2):
                    ob = 2 + par
                    hsel = slice(par, 4, 2)
                    K.actf(sq[:, 0:256], psb[ob][:, 0:256], ACT.Square, [PB[ob]], [Bsq])
                    K.mm(psb[4][:, 0:256], onesb[:], sq[:, 0:256], True, True, [Bones, Bsq], [PB[4]])
                    K.actf(rs[:, 0:256], psb[4][:, 0:256], ACT.Sqrt, [PB[4]], [Brs], scale=1.0 / 128, bias=EPS)
                    K.op(dve, lambda e: e.reciprocal(rs[:, 0:256], rs[:, 0:256]), [Brs], [Brs])
                    K.tt(dve, on[:, 0:256], psb[ob][:, 0:256], rs[:, 0:256], ALU.mult, [PB[ob], Brs], [Bon])
                    K.stt(ost[:, hsel, tk], on[:, 0:256].rearrange("p (h i) -> p h i", h=2), gn[:, 0:1],
                          gt[i2][:, hsel, tk], ALU.mult, ALU.mult, [Bon, Bgn, Bgt[i2]], [Bost])
            K.dma(sp, mixo_v[:, 0:4, tt * 512:(tt + 1) * 512], ost[:], reads=[Bost], writes=[Bmixo])

    def even_a2(l, S, is_sample, stack):
        i_ev = l // 2
        fb = mx_alloc(stack)
        win = sb("m_win", [128, KC, 1088], BF16, stack)
        wqb = sb("m_wqb", [128, 2, 1536], BF16, stack)
        wkv = sb("m_wkv", [128, 1024], BF16, stack)
        qn = sb("m_qn", [128, 2], F32, stack)
        kvn = sb("m_kvn", [128, 1], F32, stack)
        cq = sb("m_cq", [128, 2, 512], F32, stack)
        cqn = sb("m_cqn", [128, 2, 512], BF16, stack)
        ckvn = sb("m_ckvn", [128, 512], BF16, stack)
        cf_ = sb("m_cf", [128, 4], F32, stack)
        zb = sb("m_zb", [128, 1], F32, stack)
        Bzb = Buf()
        K.op(dve, lambda e: e.memset(zb[:], 0.0), [], [Bzb])
        ai = sb("m_ai", [128, 512], I32, stack)
        pos = sb("m_pos", [128, 512], F32, stack)
        tab = sb("m_tab", [128, 2, 512], F32, stack)
        gst = [sb("m_gst%d" % i, [128, 512], BF16, stack) for i in range(2)]
        qh = [sb("m_qh%d" % i, [96, 512], BF16, stack) for i in range(2)]
        kst = sb("m_kst", [96, 8, 512], BF16, stack)
        kro = sb("m_kro", [96, 512], BF16, stack)
        vaug = [sb("m_vaug%d" % i, [128, 8, 65], BF16, stack) for i in range(2)]
        (Bwin, Bwqb, Bwkv, Bqn, Bkvn, Bcq, Bcqn, Bckvn, Bcf, Bai, Bpos, Btab, Bkst, Bkro) = (Buf() for _ in range(14))
        Bgst, Bqh, Bvaug = ([Buf(), Buf()] for _ in range(3))
        K.dma(sp, win[:], ev_win2_b[i_ev].rearrange("p (k n) -> p k n", k=KC), reads=[Bevw], writes=[Bwin])
        K.dma(sp, wqb[:], mla_wqb_b[i_ev].rearrange("p (k n) -> p k n", k=2), reads=[Bevw], writes=[Bwqb])
        K.dma(sp, wkv[:], mla_wkvb_b[i_ev], reads=[Bevw], writes=[Bwkv])
        K.dma(sp, qn[:], mla_qn[i_ev], writes=[Bqn])
        K.dma(sp, kvn[:], mla_kvn[i_ev], writes=[Bkvn])
        for i in range(2):
            K.op(dve, lambda e, i=i: e.memset(vaug[i][:], 1.0), [], [Bvaug[i]])
        K.op(pool, lambda e: e.iota(ai[:], [[0, 512]], base=0, channel_multiplier=1), [], [Bai])
        K.op(dve, lambda e: e.tensor_scalar(ai[:, 0:1], ai[:, 0:1], icst[:, 6:7], None, ALU.bitwise_and), [Bai, Bic],
             [Bai])
        K.copy(dve, cf_[:, 1:2], ai[:, 0:1], [Bai], [Bcf])
        K.actf(cf_[:, 0:1], cf_[:, 1:2], ACT.Exp, [Bcf], [Bcf], scale=-float(np.log(10000.0)) / 16.0)
        K.ts(dve, cf_[:, 0:1], cf_[:, 0:1], 65536.0 / (2.0 * np.pi), None, ALU.mult, None, [Bcf], [Bcf])
        jpos = sb("m_jpos", [128, 512], I32, stack)
        Bjpos = Buf()
        K.op(pool, lambda e: e.iota(jpos[:], [[1, 512]], base=0, channel_multiplier=0), [], [Bjpos])
        if is_sample:
            K.op(pool, lambda e: e.tensor_scalar(jpos[:], jpos[:], icst[:, 5:6], None, ALU.add), [Bjpos, Bic], [Bjpos])
        gg_v = gg.rearrange("(c p) s -> p c s", p=128)
        Kd_v = Kd.rearrange("(h r) s -> r h s", h=8)
        Vd_v = Vd.rearrange("(h p) (k e) -> p h k e", h=8, e=65)
        qs = float(96.0 ** -0.5)
        R = slice(64, 96)
        ng = 0
        a2s = cfg.a2_stop
        if a2s == 1:
            return
        for tt in range(S // 512):
            for st in prenorm_steps(fb, l, 1, tt, fb.h, fb.Bh):
                st()
            for c4 in range(4):
                bank = rbank()
                for kc in range(KC):
                    K.mm(psb[bank][:, :], win[:, kc, c4 * 128:(c4 + 1) * 128], fb.h[:, kc, :], kc == 0, kc == KC - 1,
                         [Bwin, fb.Bh], [PB[bank]])
                g_, Bg_ = gst[ng % 2], Bgst[ng % 2]
                ng += 1
                K.actf(g_[:], psb[bank][:, :], ACT.Silu, [PB[bank]], [Bg_])
                K.dma(sp, gg_v[:, c4, tt * 512:(tt + 1) * 512], g_[:], reads=[Bg_], writes=[Bgg])
            if a2s == 2:
                continue
            K.ts(dve, pos[:], jpos[:], float(tt * 512), None, ALU.add, None, [Bjpos], [Bpos])
            for k2 in range(2):
                K.ts(dve, ai[:], pos[:], cf_[:, 0:1], fcst[:, k2:k2 + 1], ALU.mult, ALU.add, [Bpos, Bcf, Bfc], [Bai])
                K.op(dve, lambda e: e.tensor_scalar(ai[:], ai[:], icst[:, 3:4], None, ALU.bitwise_and), [Bai, Bic],
                     [Bai])
                K.actf(tab[:, k2, :], ai[:], ACT.Sin, [Bai], [Btab], scale=2.0 * np.pi / 65536.0, bias=-np.pi)
            if a2s == 3:
                continue
            for c2 in range(2):
                bank = rbank()
                for kc in range(KC):
                    K.mm(psb[bank][:, :], win[:, kc, 512 + c2 * 128:512 + (c2 + 1) * 128], fb.h[:, kc, :], kc == 0,
                         kc == KC - 1, [Bwin, fb.Bh], [PB[bank]])
                K.copy(act, cq[:, c2, :], psb[bank][:, :], [PB[bank]], [Bcq])
                i = fb.nsq % len(fb.sq)
                fb.nsq += 1
                K.actf(fb.sq[i][:], psb[bank][:, :], ACT.Square, [PB[bank]], [fb.Bsq[i]])
                K.mm(psb[6][:, :], onesb[:], fb.sq[i][:], c2 == 0, c2 == 1, [Bones, fb.Bsq[i]], [PB[6]])
            K.actf(fb.rstd[1][:], psb[6][:, :], ACT.Sqrt, [PB[6]], [fb.Brstd[1]], scale=1.0 / 256, bias=EPS)
            K.op(dve, lambda e: e.reciprocal(fb.rstd[1][:], fb.rstd[1][:]), [fb.Brstd[1]], [fb.Brstd[1]])
            for c2 in range(2):
                i = fb.ntmp % len(fb.tmp)
                fb.ntmp += 1
                K.stt(fb.tmp[i][:], cq[:, c2, :], qs, fb.rstd[1][:], ALU.mult, ALU.mult, [Bcq, fb.Brstd[1]],
                      [fb.Btmp[i]])
                K.actf(cqn[:, c2, :], fb.tmp[i][:], ACT.Identity, [fb.Btmp[i], Bqn, Bzb], [Bcqn],
                       scale=qn[:, c2:c2 + 1], bias=zb[:, 0:1])
            bank = rbank()
            for kc in range(KC):
                K.mm(psb[bank][:, :], win[:, kc, 768:896], fb.h[:, kc, :], kc == 0, kc == KC - 1, [Bwin, fb.Bh],
                     [PB[bank]])
            i = fb.nsq % len(fb.sq)
            fb.nsq += 1
            K.actf(fb.sq[i][:], psb[bank][:, :], ACT.Square, [PB[bank]], [fb.Bsq[i]])
            K.mm(psb[6][:, :], onesb[:], fb.sq[i][:], True, True, [Bones, fb.Bsq[i]], [PB[6]])
            K.actf(fb.rstd[1][:], psb[6][:, :], ACT.Sqrt, [PB[6]], [fb.Brstd[1]], scale=1.0 / 128, bias=EPS)
            K.op(dve, lambda e: e.reciprocal(fb.rstd[1][:], fb.rstd[1][:]), [fb.Brstd[1]], [fb.Brstd[1]])
            i = fb.ntmp % len(fb.tmp)
            fb.ntmp += 1
            K.tt(dve, fb.tmp[i][:], psb[bank][:, :], fb.rstd[1][:], ALU.mult, [PB[bank], fb.Brstd[1]], [fb.Btmp[i]])
            K.actf(ckvn[:], fb.tmp[i][:], ACT.Identity, [fb.Btmp[i], Bkvn, Bzb], [Bckvn], scale=kvn[:, 0:1],
                   bias=zb[:, 0:1])
            if a2s == 4:
                continue
            for kc in range(KC):
                K.mm(psb[2][0:96, :], win[:, kc, 896:992], fb.h[:, kc, :], kc == 0, kc == KC - 1, [Bwin, fb.Bh], [PB[2]])
            for kc in range(KC):
                K.mm(psb[3][0:96, :], win[:, kc, 992:1088], fb.h[:, kc, :], kc == 0, kc == KC - 1, [Bwin, fb.Bh],
                     [PB[3]])
            i = fb.ntmp % len(fb.tmp)
            fb.ntmp += 1
            K.tt(dve, fb.tmp[i][R, :], psb[2][R, :], tab[R, 0, :], ALU.mult, [PB[2], Btab], [fb.Btmp[i]])
            i2 = fb.ntmp % len(fb.tmp)
            fb.ntmp += 1
            K.tt(dve, fb.tmp[i2][R, :], psb[3][R, :], tab[R, 1, :], ALU.mult, [PB[3], Btab], [fb.Btmp[i2]])
            K.tt(dve, kro[R, :], fb.tmp[i][R, :], fb.tmp[i2][R, :], ALU.add, [fb.Btmp[i], fb.Btmp[i2]], [Bkro])
            if a2s == 5:
                continue
            for h in range(8):
                bank = rbank()
                K.mm(psb[bank][0:64, :], wkv[:, h * 64:(h + 1) * 64], ckvn[:], True, True, [Bwkv, Bckvn], [PB[bank]])
                K.copy(act, kst[0:64, h, :], psb[bank][0:64, :], [PB[bank]], [Bkst])
                K.copy(dve, kst[R, h, :], kro[R, :], [Bkro], [Bkst])
                for kc in range(2):
                    K.mm(psb[2][0:96, :], wqb[:, kc, h * 96:(h + 1) * 96], cqn[:, kc, :], kc == 0, kc == 1,
                         [Bwqb, Bcqn], [PB[2]])
                for kc in range(2):
                    K.mm(psb[3][0:96, :], wqb[:, kc, 768 + h * 96:768 + (h + 1) * 96], cqn[:, kc, :], kc == 0, kc == 1,
                         [Bwqb, Bcqn], [PB[3]])
                q_, Bq_ = qh[h % 2], Bqh[h % 2]
                K.copy(dve, q_[0:64, :], psb[2][0:64, :], [PB[2]], [Bq_])
                i = fb.ntmp % len(fb.tmp)
                fb.ntmp += 1
                K.tt(dve, fb.tmp[i][R, :], psb[2][R, :], tab[R, 0, :], ALU.mult, [PB[2], Btab], [fb.Btmp[i]])
                i2 = fb.ntmp % len(fb.tmp)
                fb.ntmp += 1
                K.tt(dve, fb.tmp[i2][R, :], psb[3][R, :], tab[R, 1, :], ALU.mult, [PB[3], Btab], [fb.Btmp[i2]])
                K.tt(dve, q_[R, :], fb.tmp[i][R, :], fb.tmp[i2][R, :], ALU.add, [fb.Btmp[i], fb.Btmp[i2]], [Bq_])
                K.dma(sp, Qd[h, :, tt * 512:(tt + 1) * 512], q_[:], reads=[Bq_], writes=[BQd])
            K.dma(sp, Kd_v[:, :, tt * 512:(tt + 1) * 512], kst[:], reads=[Bkst], writes=[BKd])
            if a2s == 6:
                continue
            for ts in range(4):
                tk = slice(ts * 128, (ts + 1) * 128)
                kt = tt * 4 + ts
                va, Bva = vaug[kt % 2], Bvaug[kt % 2]
                bank = rbank()
                K.mm(psb[bank][:, :], ckvn[:, tk], wkv[:, 512:1024], True, True, [Bckvn, Bwkv], [PB[bank]])
                K.copy(act if ts % 2 == 0 else dve, va[:, :, 0:64], psb[bank][:, :].rearrange("p (h e) -> p h e", h=8),
                       [PB[bank]], [Bva])
                K.dma(sp, Vd_v[:, :, kt, :], va[:], reads=[Bva], writes=[BVd])

    def even_b(S, nrank, Ksrc, BKs, Vsrc, BVs, stack):
        SK = S
        KB_ = min(1024, SK)
        nkt = KB_ // 128
        LOOK = 2
        NP = 4
        qt = [sb("b_qt%d" % i, [96, 512], BF16, stack) for i in range(2)]
        ktl = [sb("b_kt%d" % i, [96, KB_], BF16, stack) for i in range(3)]
        vtl = [sb("b_vt%d" % i, [128, nkt, 65], BF16, stack) for i in range(3)]
        pt = [sb("b_pt%d" % i, [128, 512], BF16, stack) for i in range(NP)]
        rc = sb("b_rc", [128, 512], F32, stack)
        osb = sb("b_osb", [64, 512], F32, stack)
        onb = [sb("b_on%d" % i, [64, 512], BF16, stack) for i in range(2)]
        Brc, Bosb = Buf(), Buf()
        Bqt, Bonb = ([Buf(), Buf()] for _ in range(2))
        Bktl, Bvtl = ([Buf(), Buf(), Buf()] for _ in range(2))
        Bpt = [Buf() for _ in range(NP)]
        nq = nk = ns = 0
        pend = []
        tails = []

        def flush_one():
            pend.pop(0)()

        for qb in range(S // 512):
            for h in range(8):
                q_, Bq_ = qt[nq % 2], Bqt[nq % 2]
                ob = 4 + nq % 2
                nq += 1
                K.dma(sp, q_[:], Qd[h, :, qb * 512:(qb + 1) * 512], reads=[BQd], writes=[Bq_])
                nblk = nrank * (SK // KB_)
                nstep = nblk * nkt
                si = 0
                for g in range(nrank):
                    for k0 in range(0, SK, KB_):
                        k_, Bk_ = ktl[nk % 3], Bktl[nk % 3]
                        v_, Bv_ = vtl[nk % 3], Bvtl[nk % 3]
                        nk += 1
                        K.dma(sp, k_[:], Ksrc(g, h, k0, KB_), reads=[BKs], writes=[Bk_])
                        K.dma(sp, v_[:], Vsrc(g, h, k0 // 128, nkt), reads=[BVs], writes=[Bv_])
                        for kt in range(nkt):
                            sbk = ns % 4
                            p_, Bp_ = pt[ns % NP], Bpt[ns % NP]
                            ns += 1
                            K.mm(psb[sbk][:, :], k_[:, kt * 128:(kt + 1) * 128], q_[:], True, True, [Bk_, Bq_],
                                 [PB[sbk]])
                            K.actf(p_[:], psb[sbk][:, :], ACT.Exp, [PB[sbk]], [Bp_])

                            def pv(v_=v_, Bv_=Bv_, p_=p_, Bp_=Bp_, kt=kt, first=(si == 0), last=(si == nstep - 1),
                                   ob=ob):
                                K.mm(psb[ob][0:65, :], v_[:, kt, :], p_[:], first, last, [Bv_, Bp_], [PB[ob]])
                            pend.append(pv)
                            si += 1
                            if len(pend) > LOOK:
                                flush_one()
                            if si == 4 and tails:
                                tails.pop(0)()
                while pend:
                    flush_one()
                while tails:
                    tails.pop(0)()
                K.op(dve, lambda e, ob=ob: e.reciprocal(rc[64:65, :], psb[ob][64:65, :]), [PB[ob]], [Brc])
                K.copy(dve, osb[:], psb[ob][0:64, :], [PB[ob]], [Bosb])

                def tail(h=h, qb=qb):
                    K.mm(psb[6][0:64, :], cst[64:65, 128:192], rc[64:65, :], True, True, [Bcst, Brc], [PB[6]])
                    o_, Bo_ = onb[h % 2], Bonb[h % 2]
                    K.tt(dve, o_[:], osb[:], psb[6][0:64, :], ALU.mult, [Bosb, PB[6]], [Bo_])
                    K.dma(sp, mixm[h, :, qb * 512:(qb + 1) * 512], o_[:], reads=[Bo_], writes=[Bmixm])
                tails.append(tail)
        while tails:
            tails.pop(0)()

    def even_phase_c(l, S, stack):
        i_ev = l // 2
        fb = mx_alloc(stack, with_h=False, with_y=True)
        wg = sb("c_wg", [128, 4, 1024], BF16, stack)
        wm = sb("c_wm", [64, 8, 1024], BF16, stack)
        og = [sb("c_og%d" % i, [128, 4, 512], BF16, stack) for i in range(2)]
        om = [sb("c_om%d" % i, [64, 8, 512], BF16, stack) for i in range(2)]
        Bwg, Bwm = Buf(), Buf()
        Bog, Bom = [Buf(), Buf()], [Buf(), Buf()]
        K.dma(sp, wg[:], ev_woutg_b[i_ev].rearrange("p (k n) -> p k n", k=4), reads=[Bevw], writes=[Bwg])
        K.dma(sp, wm[:], ev_woutm_b[i_ev].rearrange("p (k n) -> p k n", k=8), reads=[Bevw], writes=[Bwm])
        mixo_v = mixo.rearrange("(c p) s -> p c s", p=128)
        mixm_v = mixm.rearrange("h e s -> e h s")
        Cg = vec(l, 1, 2)
        pending = []
        for tt in range(S // 512):
            g_, Bg_ = og[tt % 2], Bog[tt % 2]
            m_, Bm_ = om[tt % 2], Bom[tt % 2]
            K.dma(sp, g_[:], mixo_v[:, 0:4, tt * 512:(tt + 1) * 512], reads=[Bmixo], writes=[Bg_])
            K.dma(sp, m_[:], mixm_v[:, :, tt * 512:(tt + 1) * 512], reads=[Bmixm], writes=[Bm_])

            def mm_oc(oc, bank, g_=g_, m_=m_, Bg_=Bg_, Bm_=Bm_):
                for ic in range(4):
                    K.mm(psb[bank][:, :], wg[:, ic, oc * 128:(oc + 1) * 128], g_[:, ic, :], ic == 0, False,
                         [Bwg, Bg_], [PB[bank]])
                for hh in range(8):
                    K.mm(psb[bank][:, :], wm[:, hh, oc * 128:(oc + 1) * 128], m_[:, hh, :], False, hh == 7,
                         [Bwm, Bm_], [PB[bank]])
            yphase(fb, tt, Cg, mm_oc, [], pending)
            while pending:
                pending.pop(0)()

    def even_mixer(l, S, is_sample):
        xg = is_sample and GRP > 1
        stop = cfg.ev_stop
        with ExitStack() as st:
            even_a1(l, S, st)
            K.barrier()
        if stop == 1:
            return
        with ExitStack() as st:
            Sin = even_exchange(S, st) if xg else None
            even_r(S, st, True, Sin)
            K.barrier()
        if stop == 2:
            return
        with ExitStack() as st:
            even_a2(l, S, is_sample, st)
            K.barrier()
        if stop == 3:
            return
        if xg:
            for h in range(8):
                K.op(pool, lambda e, h=h: e.collective_compute(
                    "AllGather", ALU.bypass, replica_groups=cfg.replica_groups,
                    ins=[Kd[h * 96:(h + 1) * 96, 0:S]], outs=[Kall[h]]), [BKd], [BKall])
                K.op(pool, lambda e, h=h: e.collective_compute(
                    "AllGather", ALU.bypass, replica_groups=cfg.replica_groups,
                    ins=[Vd[h * 128:(h + 1) * 128, 0:(S // 128) * 65]], outs=[Vall[h]]), [BVd], [BVall])
        with ExitStack() as st:
            even_o(l, S, st)
            K.barrier()
        if stop == 4:
            return
        if xg:
            K.barrier()
            kget = lambda g, h, k0, n: Kall[h, g * 96:(g + 1) * 96, k0:k0 + n]
            vget = lambda g, h, kt0, n: Vall[h].rearrange("(g p) (k e) -> g p k e", p=128, e=65)[g, :, kt0:kt0 + n, :]
            srcs = (GRP, kget, BKall, vget, BVall)
        else:
            kget = lambda g, h, k0, n: Kd[h * 96:(h + 1) * 96, k0:k0 + n]
            vget = lambda g, h, kt0, n: Vd[h * 128:(h + 1) * 128, :].rearrange("p (k e) -> p k e", e=65)[:, kt0:kt0 + n, :]
            srcs = (1, kget, BKd, vget, BVd)
        with ExitStack() as st:
            even_b(S, *srcs, st)
            K.barrier()
        if stop == 5:
            return
        with ExitStack() as st:
            even_phase_c(l, S, st)
            K.barrier()

    tok0 = 0
    for si, S in enumerate(cfg.seg_tokens):
        K.dma(sp, mv[:], modv[si], reads=[Bmodv], writes=[Bmv])
        with ExitStack() as st:
            load_segment(tok0, S, st)
            K.barrier()
        for l in range(L):
            with ExitStack() as st:
                fb = ffn_alloc(st)
                ffn_sublayer(fb, l, 0, S)
                K.barrier()
            if cfg.do_mixer and l % 2 == 1 and cfg.do_mixer & 2:
                odd_mixer(l, S, si == 2)
            if cfg.do_mixer and l % 2 == 0 and cfg.do_mixer & 1:
                even_mixer(l, S, si == 2)
            with ExitStack() as st:
                fb = ffn_alloc(st)
                ffn_sublayer(fb, l, 2, S)
                K.barrier()
        with ExitStack() as st:
            store_segment(tok0, S, st)
            K.barrier()
        tok0 += S
    K.barrier(full=True)
    es.close()
    return nc


def _fm(v):
    v = np.asarray(v)
    lead = v.shape[:-1]
    n = v.shape[-1] // 128
    v = v.reshape(lead + (n, 128))
    return np.ascontiguousarray(np.moveaxis(v, -1, 0))


def prep_shared(inp, cfg):
    L = cfg.depth
    sh = {}
    ident = np.eye(128, dtype=np.float32)
    sh["consts"] = np.ascontiguousarray(np.concatenate([ident, np.ones((128, 128), np.float32)], axis=1))
    aw = np.asarray(inp["ada_w"])[:L]
    aw = aw.reshape(L, KC, 128, 72, 128).transpose(0, 3, 2, 1, 4)
    sh["ada_w"] = np.ascontiguousarray(aw).reshape(L * 72, 128, KC, 128)
    sh["ada_b"] = _fm(np.asarray(inp["ada_b"])[:L]).reshape(128, L * 72)
    sh["npre"] = _fm(np.asarray(inp["norm_pre"])[:L]).reshape(128, L * 3 * KC)
    sh["npost"] = _fm(np.asarray(inp["norm_post"])[:L]).reshape(128, L * 3 * KC)
    w13 = np.asarray(inp["ffn_w13"])[:L].reshape(L * 2, KC, 128, 2, NFC, 128)
    sh["w13"] = np.ascontiguousarray(w13.transpose(0, 4, 2, 1, 3, 5)).reshape(L * 2, NFC, 128, KC * 256)
    w2 = np.asarray(inp["ffn_w2"])[:L].reshape(L * 2, NFC, 128, KC, 128)
    sh["w2"] = np.ascontiguousarray(w2.transpose(0, 3, 2, 1, 4)).reshape(L * 2, KC, 128, NFC * 128)
    NOD = L // 2
    if NOD:
        ow = np.asarray(inp["od_w_in"])[:NOD]
        sh["od_win"] = np.ascontiguousarray(ow.reshape(NOD, KC, 128, 1536).transpose(0, 2, 1, 3)).reshape(NOD, 128, KC * 1536)
        oo = np.asarray(inp["od_w_out"])[:NOD]
        sh["od_wout"] = np.ascontiguousarray(oo.reshape(NOD, KC, 128, 1024).transpose(0, 2, 1, 3)).reshape(NOD, 128, KC * 1024)
        ws = np.asarray(inp["sgu_w_s"])[:NOD]
        sh["sgu_wsT"] = np.ascontiguousarray(ws.transpose(0, 3, 1, 2)).reshape(NOD, 128, 512)
        sh["sgu_b"] = np.ascontiguousarray(np.asarray(inp["sgu_b"])[:NOD]).reshape(NOD, 1, 512)
        sh["sgu_nrm"] = np.ascontiguousarray(np.broadcast_to(np.asarray(inp["sgu_norm"])[:NOD, None, :], (NOD, 128, 512)))
    NEV = (L + 1) // 2
    if NEV:
        def kmaj(w, nk):
            n, _, cols = w.shape
            return np.ascontiguousarray(w.reshape(n, nk, 128, cols).transpose(0, 2, 1, 3)).reshape(n, 128, nk * cols)
        wi = np.asarray(inp["ev_w_in"])[:NEV]
        q, k, v, g, alr, cq, ckv, kr = (wi[:, :, a:b] for a, b in ((0, 256), (256, 512), (512, 1024), (1024, 1536),
                                                                  (1536, 1568), (1568, 1824), (1824, 1952), (1952, 1984)))
        sh["ev_win1"] = kmaj(np.concatenate([q, k, v, alr], axis=2), KC)
        fill = ckv[:, :, 0:64]
        krrot = np.concatenate([kr[:, :, 16:32], kr[:, :, 0:16]], axis=2)
        sh["ev_win2"] = kmaj(np.concatenate([g, cq, ckv, fill, kr, fill, krrot], axis=2), KC)
        wo = np.asarray(inp["ev_w_out"])[:NEV]
        sh["ev_woutg"] = kmaj(wo[:, 0:512], 4)
        sh["ev_woutm"] = np.ascontiguousarray(wo[:, 512:1024].reshape(NEV, 8, 64, 1024).transpose(0, 2, 1, 3)).reshape(NEV, 64, 8 * 1024)
        wa = np.asarray(inp["gla_w_alpha"])[:NEV]
        ba = np.asarray(inp["gla_b_alpha"])[:NEV]
        wal = np.zeros((NEV, 33, 512), np.float32)
        wal[:, 0:16, 0:256] = wa[:, 0]
        wal[:, 16:32, 256:512] = wa[:, 1]
        wal[:, 32, 0:256] = ba[:, 0]
        wal[:, 32, 256:512] = ba[:, 1]
        sh["gla_wal"] = wal
        sh["gla_nrm"] = np.ascontiguousarray(np.asarray(inp["gla_norm"])[:NEV].reshape(NEV, 128, 1))
        sh["mla_qn"] = np.ascontiguousarray(np.asarray(inp["mla_q_norm"])[:NEV].reshape(NEV, 2, 128).transpose(0, 2, 1))
        sh["mla_kvn"] = np.ascontiguousarray(np.asarray(inp["mla_kv_norm"])[:NEV].reshape(NEV, 128, 1))
        wq = np.asarray(inp["mla_w_q_b"])[:NEV].reshape(NEV, 256, 8, 96)
        wqr = np.concatenate([wq[..., 0:64], wq[..., 80:96], wq[..., 64:80]], axis=-1)
        sh["mla_wqb"] = kmaj(np.concatenate([wq.reshape(NEV, 256, 768), wqr.reshape(NEV, 256, 768)], axis=2), 2)
        wk = np.asarray(inp["mla_w_kv_b"])[:NEV].reshape(NEV, 128, 8, 128)
        sh["mla_wkvb"] = np.ascontiguousarray(np.concatenate([wk[..., 0:64].reshape(NEV, 128, 512),
                                                             wk[..., 64:128].reshape(NEV, 128, 512)], axis=2))
    t = np.arange(128)
    same = (t[:, None] // 64) == (t[None, :] // 64)
    Tfi = (same & (t[:, None] <= t[None, :])).astype(np.float32)
    Tbe = (same & (t[:, None] > t[None, :])).astype(np.float32)
    Tbi = (same & (t[:, None] >= t[None, :])).astype(np.float32)
    Tpe = (same & (t[:, None] < t[None, :])).astype(np.float32)
    Ind = np.stack([(t < 64), (t >= 64)], axis=1).astype(np.float32)
    sh["tconst"] = np.ascontiguousarray(np.concatenate([Tfi, Ind, Tbe, Tbi, Ind, Tpe], axis=1))
    return sh


def prep_core(inp, cfg, core, n_cores=8):
    xp = np.asarray(inp["x_prompt"])
    xs = np.asarray(inp["x_sample"])
    cp = np.asarray(inp["c_prompt"])
    cs = np.asarray(inp["c_sample"])
    SP = cfg.seg_tokens[0]
    SQ = cfg.seg_tokens[2]
    per_grp = n_cores // xs.shape[0]
    sb_, r = core // per_grp, core % per_grp
    xin = np.concatenate([xp[2 * core, :SP], xp[2 * core + 1, :SP], xs[sb_, r * SQ:(r + 1) * SQ]], axis=0)
    c = np.stack([cp[2 * core], cp[2 * core + 1], cs[sb_], np.zeros(D, np.float32)], axis=0)
    c3 = np.ascontiguousarray(c.T.reshape(KC, 128, 4).transpose(1, 0, 2))
    ic = np.zeros((128, 8), np.int32)
    stot = per_grp * SQ
    ic[:, 0] = SP - 1
    ic[:, 1] = stot - 1
    ic[:, 2] = 127
    ic[:, 3] = 65535
    ic[:, 4] = (SQ * r * np.arange(128)) % stot
    ic[:, 5] = r * SQ
    ic[:, 6] = 15
    fc = np.zeros((128, 64), np.float32)
    fc[:, 0] = 49152.0
    fc[80:96, 1] = 32768.0
    for r1 in range(min(per_grp, 4)):
        fc[:, 8 + r1] = 1.0 if r1 < r else 0.0
        fc[:, 12 + r1] = 1.0 if r1 > r else 0.0
        for r2 in range(min(per_grp, 4)):
            fc[:, 16 + r1 * 4 + r2] = 1.0 if r1 < r2 < r else 0.0
            fc[:, 32 + r1 * 4 + r2] = 1.0 if r < r2 < r1 else 0.0
    return {"xin": np.ascontiguousarray(xin), "c3": c3, "iconst": ic, "fconst": fc}


_CACHE = {}


def run(inp, cfg, n_cores=8, trace=False):
    key = (cfg.seg_tokens, cfg.depth, cfg.do_mixer, cfg.n_cores, cfg.group)
    if key not in _CACHE:
        _CACHE[key] = build(cfg)
    nc = _CACHE[key]
    sh = prep_shared(inp, cfg)
    in_maps = []
    for c in range(n_cores):
        m = dict(sh)
        m.update(prep_core(inp, cfg, c, n_cores))
        in_maps.append(m)
    res = run_bass_kernel_spmd(nc, in_maps, core_ids=list(range(n_cores)), trace=trace)
    return res


def kernel(**inputs):
    cfg = Cfg()
    res = run(inputs, cfg)
    SP, SQ = cfg.seg_tokens[0], cfg.seg_tokens[2]
    B, S = inputs["x_prompt"].shape[:2]
    DB, DS = inputs["x_sample"].shape[:2]
    yp = np.empty((B, S, D), np.float32)
    ys = np.empty((DB, DS, D), np.float32)
    per_grp = 8 // DB
    for c in range(8):
        y = res.results[c]["yout"]
        yp[2 * c] = y[0:SP]
        yp[2 * c + 1] = y[SP:2 * SP]
        ys[c // per_grp, (c % per_grp) * SQ:(c % per_grp + 1) * SQ] = y[2 * SP:2 * SP + SQ]
    return (yp, ys)
```

```python
import numpy as np
import concourse.bass as bass
import concourse.mybir as mybir
from concourse.bass_utils import run_bass_kernel_spmd
from contextlib import ExitStack

F32 = mybir.dt.float32
BF16 = mybir.dt.bfloat16
I32 = mybir.dt.int32
I16 = mybir.dt.int16
ACT = mybir.ActivationFunctionType
ALU = mybir.AluOpType

D = 1024
KC = 8
DFF = 2816
NFC = 22
EPS = 1e-6


class Buf:
    __slots__ = ("name", "w", "r", "wl")

    def __init__(self, name=""):
        self.name = name
        self.w = None
        self.r = {}
        self.wl = []


class EngW:
    def __init__(self, name, eng, sid, sem, inorder=False):
        self.name = name
        self.eng = eng
        self.sid = sid
        self.sem = sem
        self.cnt = 0
        self.known = {}
        self.inorder = inorder
        self.ring = []
        self.ring_pos = 0


class KB:
    def __init__(self, nc, nring=20):
        self.nc = nc
        self.es = ExitStack()
        self.sems = []
        self.semcnt = []
        self.engs = {}
        for name, eng, inorder in (("pe", nc.tensor, True), ("act", nc.scalar, False), ("dve", nc.vector, False),
                                   ("pool", nc.gpsimd, False), ("sp", nc.sync, False)):
            sid = self._newsem("c_" + name)
            self.engs[name] = EngW(name, eng, sid, self.sems[sid], inorder)
        for q in ("sp", "pool", "act"):
            E = self.engs[q]
            for i in range(nring):
                E.ring.append(self._newsem("d_%s%d" % (q, i)))
        self.pe, self.act, self.dve, self.pool, self.sp = (self.engs[n] for n in ("pe", "act", "dve", "pool", "sp"))
        self.bg_ring = [self._newsem("d_bg%d" % i) for i in range(24)]
        self.bg_pos = 0
        self.bg_sids = set(self.bg_ring)

    def _newsem(self, name):
        s = self.es.enter_context(self.nc.semaphore(name))
        self.sems.append(s)
        self.semcnt.append(0)
        return len(self.sems) - 1

    def _waits(self, E, reads, writes, extra=()):
        need = {}
        for b in reads:
            if b.w is not None and need.get(b.w[0], 0) < b.w[1]:
                need[b.w[0]] = b.w[1]
            for sid, val in b.wl:
                if need.get(sid, 0) < val:
                    need[sid] = val
        for b in writes:
            if b.w is not None and need.get(b.w[0], 0) < b.w[1]:
                need[b.w[0]] = b.w[1]
            for sid, val in b.wl:
                if need.get(sid, 0) < val:
                    need[sid] = val
            for sid, val in b.r.items():
                if need.get(sid, 0) < val:
                    need[sid] = val
        for sid, val in extra:
            if need.get(sid, 0) < val:
                need[sid] = val
        for sid, val in need.items():
            if sid == E.sid and E.inorder:
                continue
            if E.known.get(sid, 0) >= val:
                continue
            E.eng.wait_ge(self.sems[sid], val)
            E.known[sid] = val

    def op(self, E, emit, reads=(), writes=()):
        self._waits(E, reads, writes)
        ins = emit(E.eng)
        E.cnt += 1
        ins.then_inc(E.sem, 1)
        self.semcnt[E.sid] = E.cnt
        for b in reads:
            if b.r.get(E.sid, 0) < E.cnt:
                b.r[E.sid] = E.cnt
        for b in writes:
            b.w = (E.sid, E.cnt)
            b.r = {}

    def dma(self, Q, out, in_, reads=(), writes=(), bg=False, **kw):
        if bg:
            sid = self.bg_ring[self.bg_pos]
            self.bg_pos = (self.bg_pos + 1) % len(self.bg_ring)
        else:
            sid = Q.ring[Q.ring_pos]
            Q.ring_pos = (Q.ring_pos + 1) % len(Q.ring)
        prev = self.semcnt[sid]
        self._waits(Q, reads, writes, extra=((sid, prev),) if prev else ())
        ins = Q.eng.dma_start(out=out, in_=in_, **kw)
        self.semcnt[sid] = prev + 16
        ins.then_inc(self.sems[sid], 16)
        val = prev + 16
        for b in reads:
            if b.r.get(sid, 0) < val:
                b.r[sid] = val
        for b in writes:
            if bg:
                b.wl.append((sid, val))
            else:
                b.w = (sid, val)
                b.r = {}
                b.wl = []

    def barrier(self, full=False):
        for E in self.engs.values():
            for sid in range(len(self.sems)):
                val = self.semcnt[sid]
                if sid in self.bg_sids and not full:
                    continue
                if val and sid != E.sid and E.known.get(sid, 0) < val:
                    E.eng.wait_ge(self.sems[sid], val)
                    E.known[sid] = val
            if E.cnt and not E.inorder and E.known.get(E.sid, 0) < E.cnt:
                E.eng.wait_ge(E.sem, E.cnt)
                E.known[E.sid] = E.cnt

    def mm(self, out, lhsT, rhs, start, stop, reads, writes, **kw):
        self.op(self.pe, lambda e: e.matmul(out, lhsT, rhs, start=start, stop=stop, **kw), reads, writes)

    def actf(self, out, in_, func, reads, writes, **kw):
        self.op(self.act, lambda e: e.activation(out, in_, func, **kw), reads, writes)

    def tt(self, E, out, in0, in1, op, reads, writes):
        self.op(E, lambda e: e.tensor_tensor(out, in0, in1, op), reads, writes)

    def ts(self, E, out, in0, s1, s2, op0, op1, reads, writes):
        if op1 is None:
            self.op(E, lambda e: e.tensor_scalar(out, in0, s1, None, op0), reads, writes)
        else:
            self.op(E, lambda e: e.tensor_scalar(out, in0, s1, s2, op0, op1), reads, writes)

    def stt(self, out, in0, scalar, in1, op0, op1, reads, writes):
        self.op(self.dve, lambda e: e.scalar_tensor_tensor(out, in0, scalar, in1, op0, op1), reads, writes)

    def copy(self, E, out, in_, reads, writes):
        if E is self.act:
            self.op(E, lambda e: e.copy(out, in_), reads, writes)
        else:
            self.op(E, lambda e: e.tensor_copy(out, in_), reads, writes)


class Cfg:
    def __init__(self, seg_tokens=(4096, 4096, 4096), depth=4, do_mixer=True, n_cores=8, group=4):
        self.seg_tokens = tuple(seg_tokens)
        self.ntok = sum(seg_tokens)
        self.depth = depth
        self.do_mixer = 3 if do_mixer is True else int(do_mixer)
        self.nffn = depth * 2
        self.n_cores = n_cores
        self.ev_stop = 0
        self.a2_stop = 0
        self.no_xg = 0
        self.cc_max = 4 * 1024 * 1024
        self.group = group
        self.replica_groups = [list(range(g * group, (g + 1) * group)) for g in range(n_cores // group)]


def build(cfg):
    nc = bass.Bass("TRN2", target_bir_lowering=False)
    L = cfg.depth
    NF = cfg.nffn
    NT = cfg.ntok

    def din(name, shape, dt=F32):
        return nc.dram_tensor(name, list(shape), dt, kind="ExternalInput").ap()

    def dscr(name, shape, dt):
        return nc.dram_tensor(name, list(shape), dt, kind="Internal").ap()

    xin = din("xin", [NT, D])
    c3 = din("c3", [128, KC, 4])
    consts = din("consts", [128, 192])
    ada_w = din("ada_w", [L * 72, 128, KC, 128])
    ada_b = din("ada_b", [128, L * 72])
    npre = din("npre", [128, L * 3 * KC])
    npost = din("npost", [128, L * 3 * KC])
    w13 = din("w13", [NF, NFC, 128, KC * 256])
    w2 = din("w2", [NF, KC, 128, NFC * 128])
    yout = nc.dram_tensor("yout", [NT, D], F32, kind="ExternalOutput").ap()
    NOD = L // 2
    NEV = (L + 1) // 2
    SQ = cfg.seg_tokens[2]
    GRP = cfg.group
    iconst = din("iconst", [128, 8], I32)
    if NOD:
        od_win = din("od_win", [NOD, 128, KC * 1536])
        od_wout = din("od_wout", [NOD, 128, KC * 1024])
        sgu_wsT = din("sgu_wsT", [NOD, 128, 512])
        sgu_b = din("sgu_b", [NOD, 1, 512])
        sgu_nrm = din("sgu_nrm", [NOD, 128, 512])
        od_win_b = dscr("od_win_b", [NOD, 128, KC * 1536], BF16)
        od_wout_b = dscr("od_wout_b", [NOD, 128, KC * 1024], BF16)
        sgu_wsT_b = dscr("sgu_wsT_b", [NOD, 128, 512], BF16)
        sgu_b_b = dscr("sgu_b_b", [NOD, 1, 512], BF16)
    SMAXL = max(cfg.seg_tokens)
    tconst = din("tconst", [128, 516])
    fconst = din("fconst", [128, 64])
    if NEV:
        ev_win1 = din("ev_win1", [NEV, 128, KC * 1056])
        ev_win2 = din("ev_win2", [NEV, 128, KC * 1088])
        ev_woutg = din("ev_woutg", [NEV, 128, 4 * 1024])
        ev_woutm = din("ev_woutm", [NEV, 64, 8 * 1024])
        gla_wal = din("gla_wal", [NEV, 33, 512])
        gla_nrm = din("gla_nrm", [NEV, 128, 1])
        mla_qn = din("mla_qn", [NEV, 128, 2])
        mla_kvn = din("mla_kvn", [NEV, 128, 1])
        mla_wqb = din("mla_wqb", [NEV, 128, 2 * 1536])
        mla_wkvb = din("mla_wkvb", [NEV, 128, 1024])
        ev_win1_b = dscr("ev_win1_b", [NEV, 128, KC * 1056], BF16)
        ev_win2_b = dscr("ev_win2_b", [NEV, 128, KC * 1088], BF16)
        ev_woutg_b = dscr("ev_woutg_b", [NEV, 128, 4 * 1024], BF16)
        ev_woutm_b = dscr("ev_woutm_b", [NEV, 64, 8 * 1024], BF16)
        gla_wal_b = dscr("gla_wal_b", [NEV, 33, 512], BF16)
        mla_wqb_b = dscr("mla_wqb_b", [NEV, 128, 2 * 1536], BF16)
        mla_wkvb_b = dscr("mla_wkvb_b", [NEV, 128, 1024], BF16)
    NCH = SMAXL // 64
    gq = dscr("gq", [4, 2, 128, SMAXL], BF16)
    gvt = dscr("gvt", [SMAXL // 128, 128, 512], BF16)
    gkv = dscr("gkv", [2, NCH, 2, 128, 128], F32)
    gs = dscr("gs", [2, NCH, 2, 128, 128], BF16)
    gg = dscr("gg", [512, SMAXL], BF16)
    gsum = dscr("gsum", [4 * 128, 129], F32)
    gsum_all = dscr("gsum_all", [GRP * 4 * 128, 129], F32)
    Qd = dscr("Qd", [8, 96, SMAXL], BF16)
    Kd = dscr("Kd", [8 * 96, SMAXL], BF16)
    Kall = dscr("Kall", [8, GRP * 96, SQ], BF16)
    Vd = dscr("Vd", [8 * 128, (SMAXL // 128) * 65], BF16)
    Vall = dscr("Vall", [8, GRP * 128, (SQ // 128) * 65], BF16)
    mixm = dscr("mixm", [8, 64, SMAXL], BF16)
    Ud = dscr("Ud", [SMAXL, 1024], BF16)
    CC_MAX = cfg.cc_max
    RCU = min(SQ, max(128, (CC_MAX // (GRP * 2048)) // 128 * 128))
    NUC = SQ // RCU
    Uall = dscr("Uall", [NUC, GRP * RCU, 1024], BF16)
    mixo = dscr("mixo", [1024, SMAXL], BF16)

    w13b = dscr("w13b", [NF, NFC, 128, KC * 256], BF16)
    w2b = dscr("w2b", [NF, KC, 128, NFC * 128], BF16)
    modv = dscr("modv", [3, 128, L * 3 * 3 * KC], F32)

    K = KB(nc)
    es = K.es
    pe, act, dve, pool, sp = K.pe, K.act, K.dve, K.pool, K.sp

    uid = [0]

    def sb(name, shape, dt, stack=es):
        uid[0] += 1
        return stack.enter_context(nc.sbuf_tensor("%s_u%d" % (name, uid[0]), list(shape), dt))

    psb = [es.enter_context(nc.psum_tensor("ps%d" % i, [128, 512], F32)) for i in range(8)]
    PB = [Buf("ps%d" % i) for i in range(8)]

    SMAX = max(cfg.seg_tokens)
    xT = sb("xT", [128, KC, SMAX], F32)
    XB = [Buf("x%d" % i) for i in range(SMAX // 512)]
    cst = sb("cst", [128, 192], F32)
    onesb = sb("onesb", [128, 128], BF16)
    mv = sb("mv", [128, L * 3 * 3 * KC], F32)
    Bcst, Bones, Bmv = Buf("cst"), Buf("ones"), Buf("mv")
    ident = cst[:, 0:128]

    K.dma(sp, cst[:], consts, writes=[Bcst])
    K.op(dve, lambda e: e.memset(onesb[:], 1.0), [], [Bones])

    WB13 = [Buf("w13b%d" % f) for f in range(NF)]
    WB2 = [Buf("w2b%d" % f) for f in range(NF)]
    late_conv = []
    for f in range(NF):
        def cv(f=f):
            K.dma(pool, w13b[f], w13[f], writes=[WB13[f]], bg=True, max_dma_last_dim=4096)
            K.dma(pool, w2b[f], w2[f], writes=[WB2[f]], bg=True, max_dma_last_dim=4096)
        if f == 0:
            cv()
        else:
            late_conv.append(cv)

    Bodw = Buf("odw")
    if NOD:
        def cvo():
            for src, dst in ((od_win, od_win_b), (od_wout, od_wout_b), (sgu_wsT, sgu_wsT_b), (sgu_b, sgu_b_b)):
                K.dma(pool, dst, src, writes=[Bodw], bg=True, max_dma_last_dim=4096)
        late_conv.append(cvo)
    Bevw = Buf("evw")
    if NEV:
        for src, dst in ((ev_win1, ev_win1_b), (ev_win2, ev_win2_b), (ev_woutg, ev_woutg_b), (ev_woutm, ev_woutm_b),
                         (gla_wal, gla_wal_b), (mla_wqb, mla_wqb_b), (mla_wkvb, mla_wkvb_b)):
            K.dma(pool, dst, src, writes=[Bevw], bg=True, max_dma_last_dim=4096)
    icst = sb("icst", [128, 8], I32)
    Bic = Buf("icst")
    K.dma(sp, icst[:], iconst, writes=[Bic])

    Bmodv = Buf("modv")
    with ExitStack() as ps:
        ccT = sb("ccT", [128, KC, 4], F32, ps)
        adab = sb("adab", [128, L * 72], F32, ps)
        gpre = sb("gpre", [128, L * 3 * KC], F32, ps)
        gpost = sb("gpost", [128, L * 3 * KC], F32, ps)
        mfm = sb("mfm", [128, L * 72, 4], F32, ps)
        mvall = sb("mvall", [128, 3, L * 3 * 3 * KC], F32, ps)
        NAW = 4
        awt = [sb("awt%d" % i, [128, KC, 128], F32, ps) for i in range(NAW)]
        Bcc, Badab, Bgpre, Bgpost, Bmfm, Bmvall = (Buf(n) for n in ("cc", "adab", "gpre", "gpost", "mfm", "mvall"))
        Bawt = [Buf("awt%d" % i) for i in range(NAW)]
        K.dma(sp, ccT[:], c3, writes=[Bcc])
        K.dma(sp, adab[:], ada_b, writes=[Badab])
        K.dma(sp, gpre[:], npre, writes=[Bgpre])
        K.dma(sp, gpost[:], npost, writes=[Bgpost])
        K.actf(ccT[:], ccT[:], ACT.Silu, [Bcc], [Bcc])
        for t in range(L * 72):
            wt, Bw = awt[t % NAW], Bawt[t % NAW]
            K.dma(sp, wt[:], ada_w[t], writes=[Bw])
            bank = 7 - (t % 2)
            for kc in range(KC):
                K.mm(psb[bank][:, 0:4], wt[:, kc, :], ccT[:, kc, :], kc == 0, kc == KC - 1, [Bw, Bcc], [PB[bank]])
            K.ts(dve, mfm[:, t, :], psb[bank][:, 0:4], adab[:, t:t + 1], None, ALU.add, None,
                 [PB[bank], Badab], [Bmfm])
        gp4 = gpost[:].rearrange("p (l j c) -> p l j c", l=L, j=3)
        for j in (0, 2):
            K.ts(dve, gp4[:, :, j, :], gp4[:, :, j, :], 0.5, None, ALU.mult, None, [Bgpost], [Bgpost])
        mf5 = mfm[:].rearrange("p (l j t c) b -> p l j t c b", l=L, j=3, t=3)
        mv5 = mvall[:].rearrange("p b (l j v c) -> p b l j v c", l=L, j=3, v=3)
        gpr4 = gpre[:].rearrange("p (l j c) -> p l j c", l=L, j=3)
        for b in range(3):
            for l in range(L):
                for j in range(3):
                    K.stt(mv5[:, b, l, j, 0, :], mf5[:, l, j, 1, :, b], 1.0, gpr4[:, l, j, :], ALU.add, ALU.mult,
                          [Bmfm, Bgpre], [Bmvall])
                    K.copy(dve, mv5[:, b, l, j, 1, :], mf5[:, l, j, 0, :, b], [Bmfm], [Bmvall])
                    K.stt(mv5[:, b, l, j, 2, :], mf5[:, l, j, 2, :, b], 1.0, gp4[:, l, j, :], ALU.add, ALU.mult,
                          [Bmfm, Bgpost], [Bmvall])
        K.dma(sp, modv.rearrange("b p n -> p b n"), mvall[:], reads=[Bmvall], writes=[Bmodv])
        K.barrier()
    for cv in late_conv:
        cv()

    def vec(l, j, v):
        o = ((l * 3 + j) * 3 + v) * KC
        return mv[:, o:o + KC]

    def load_segment(tok0, S, stack):
        xtok = [sb("xtok%d" % i, [128, D], F32, stack) for i in range(2)]
        Bxt = [Buf("xtok%d" % i) for i in range(2)]
        for i in range(S // 128):
            xt_, Bx = xtok[i % 2], Bxt[i % 2]
            K.dma(sp, xt_[:], xin[tok0 + i * 128: tok0 + (i + 1) * 128, :], writes=[Bx])
            for hh in range(2):
                bank = (2 * i + hh) % 4
                for q in range(4):
                    kc = hh * 4 + q
                    K.op(pe, lambda e, kc=kc, q=q, bank=bank: e.transpose(psb[bank][:, q * 128:(q + 1) * 128],
                                                                           xt_[:, kc * 128:(kc + 1) * 128], ident),
                         [Bx, Bcst], [PB[bank]])
                dst = xT[:, hh * 4:(hh + 1) * 4, i * 128:(i + 1) * 128]
                src = psb[bank][:, :].rearrange("p (q t) -> p q t", q=4)
                K.copy(act if hh == 0 else dve, dst, src, [PB[bank]], [XB[i // 4]])

    def store_segment(tok0, S, stack):
        yt = [sb("ytok%d" % i, [128, D], F32, stack) for i in range(2)]
        Byt = [Buf("ytok%d" % i) for i in range(2)]
        for i in range(S // 128):
            y_, By = yt[i % 2], Byt[i % 2]
            for hh in range(2):
                bank = (2 * i + hh) % 4
                for q in range(4):
                    kc = hh * 4 + q
                    K.op(pe, lambda e, kc=kc, q=q, bank=bank: e.transpose(psb[bank][:, q * 128:(q + 1) * 128],
                                                                           xT[:, kc, i * 128:(i + 1) * 128], ident),
                         [XB[i // 4], Bcst], [PB[bank]])
                K.copy(act if hh == 0 else dve, y_[:, hh * 512:(hh + 1) * 512], psb[bank][:, :], [PB[bank]], [By])
            K.dma(sp, yout[tok0 + i * 128: tok0 + (i + 1) * 128, :], y_[:], reads=[By])

    class FfnBufs:
        pass

    def ffn_alloc(stack):
        fb = FfnBufs()
        fb.h = sb("f_h", [128, KC, 512], BF16, stack)
        fb.g = sb("f_g", [128, NFC, 512], BF16, stack)
        fb.y = sb("f_y", [128, KC, 512], F32, stack)
        fb.s = sb("f_s", [128, 512], F32, stack)
        fb.w13 = [sb("f_w13_%d" % i, [128, KC, 256], BF16, stack) for i in range(3)]
        fb.w2 = [sb("f_w2_%d" % i, [128, 11, 128], BF16, stack) for i in range(3)]
        fb.rstd = [sb("f_rstd%d" % i, [128, 512], F32, stack) for i in range(2)]
        fb.sq = [sb("f_sq%d" % i, [128, 512], BF16, stack) for i in range(2)]
        fb.tmp = [sb("f_tmp%d" % i, [128, 512], F32, stack) for i in range(1)]
        fb.Bh, fb.By, fb.Bs = Buf("h"), Buf("y"), Buf("s")
        fb.Bg = [Buf("g%d" % i) for i in range(NFC)]
        fb.Bw13 = [Buf() for _ in range(3)]
        fb.Bw2 = [Buf() for _ in range(3)]
        fb.Brstd = [Buf(), Buf()]
        fb.Bsq = [Buf(), Buf()]
        fb.Btmp = [Buf(), Buf()]
        fb.n13 = 0
        fb.n2 = 0
        fb.nsq = 0
        fb.ntmp = 0
        return fb

    def rstd_from_ss(fb, ri, bank):
        K.actf(fb.rstd[ri][:], psb[bank][:, :], ACT.Sqrt, [PB[bank]], [fb.Brstd[ri]], scale=1.0 / D, bias=EPS)
        K.op(dve, lambda e: e.reciprocal(fb.rstd[ri][:], fb.rstd[ri][:]), [fb.Brstd[ri]], [fb.Brstd[ri]])

    def prenorm_steps(fb, l, j, tt, hdst, Bh):
        tsl = slice(tt * 512, (tt + 1) * 512)
        A, Bv = vec(l, j, 0), vec(l, j, 1)
        steps = []

        def p0():
            for kc in range(KC):
                i = fb.nsq % len(fb.sq)
                fb.nsq += 1
                K.actf(fb.sq[i][:], xT[:, kc, tsl], ACT.Square, [XB[tt]], [fb.Bsq[i]])
                K.mm(psb[6][:, :], onesb[:], fb.sq[i][:], kc == 0, kc == KC - 1, [Bones, fb.Bsq[i]], [PB[6]])
        steps.append(p0)
        steps.append(lambda: rstd_from_ss(fb, 0, 6))
        for kc in range(KC):
            def pk(kc=kc):
                i = fb.ntmp % len(fb.tmp)
                fb.ntmp += 1
                K.tt(dve, fb.tmp[i][:], xT[:, kc, tsl], fb.rstd[0][:], ALU.mult, [XB[tt], fb.Brstd[0]], [fb.Btmp[i]])
                K.actf(hdst[:, kc, :], fb.tmp[i][:], ACT.Identity, [fb.Btmp[i], Bmv], [Bh],
                       scale=A[:, kc:kc + 1], bias=Bv[:, kc:kc + 1])
            steps.append(pk)
        return steps

    def yphase(fb, tt, Cg, mm_oc, nxt, pending_tail):
        tsl = slice(tt * 512, (tt + 1) * 512)
        prev_sq = None
        for oc in range(KC):
            bank = 4 + oc % 2
            mm_oc(oc, bank)
            if prev_sq is not None:
                po, pi = prev_sq
                K.mm(psb[7][:, :], onesb[:], fb.sq[pi][:], po == 0, False, [Bones, fb.Bsq[pi]], [PB[7]])
            K.copy(act, fb.y[:, oc, :], psb[bank][:, :], [PB[bank]], [fb.By])
            i = fb.nsq % len(fb.sq)
            fb.nsq += 1
            K.actf(fb.sq[i][:], psb[bank][:, :], ACT.Square, [PB[bank]], [fb.Bsq[i]])
            prev_sq = (oc, i)
            if nxt:
                nxt.pop(0)()
        po, pi = prev_sq
        K.mm(psb[7][:, :], onesb[:], fb.sq[pi][:], False, True, [Bones, fb.Bsq[pi]], [PB[7]])
        while nxt:
            nxt.pop(0)()
        pending_tail.append(lambda: rstd_from_ss(fb, 1, 7))
        for oc in range(KC):
            def tl(oc=oc):
                i = fb.ntmp % len(fb.tmp)
                fb.ntmp += 1
                K.tt(dve, fb.tmp[i][:], fb.y[:, oc, :], fb.rstd[1][:], ALU.mult, [fb.By, fb.Brstd[1]],
                     [fb.Btmp[i]])
                K.stt(xT[:, oc, tsl], fb.tmp[i][:], Cg[:, oc:oc + 1], xT[:, oc, tsl], ALU.mult, ALU.add,
                      [fb.Btmp[i], Bmv, XB[tt]], [XB[tt]])
            pending_tail.append(tl)

    def ffn_sublayer(fb, l, j, S):
        f = l * 2 + (0 if j == 0 else 1)
        ntile = S // 512
        Cg = vec(l, j, 2)
        pending_tail = []
        for st in prenorm_steps(fb, l, j, 0, fb.h, fb.Bh):
            st()
        for tt in range(ntile):
            tsl = slice(tt * 512, (tt + 1) * 512)
            for fc in range(NFC):
                r = fb.n13 % 3
                fb.n13 += 1
                K.dma(sp, fb.w13[r][:], w13b[f, fc].rearrange("p (k n) -> p k n", k=KC), reads=[WB13[f]],
                      writes=[fb.Bw13[r]])
                ba, bb = fc % 2, 2 + fc % 2
                for half, bank in ((0, ba), (1, bb)):
                    for kc in range(KC):
                        K.mm(psb[bank][:, :], fb.w13[r][:, kc, half * 128:(half + 1) * 128], fb.h[:, kc, :],
                             kc == 0, kc == KC - 1, [fb.Bw13[r], fb.Bh], [PB[bank]])
                K.actf(fb.s[:], psb[ba][:, :], ACT.Silu, [PB[ba]], [fb.Bs])
                K.tt(dve, fb.g[:, fc, :], fb.s[:], psb[bb][:, :], ALU.mult, [fb.Bs, PB[bb]], [fb.Bg[fc]])
                if pending_tail:
                    pending_tail.pop(0)()
            while pending_tail:
                pending_tail.pop(0)()
            nxt = prenorm_steps(fb, l, j, tt + 1, fb.h, fb.Bh) if tt + 1 < ntile else []
            if nxt:
                nxt.pop(0)()
            def mm_oc(oc, bank, f=f):
                for hf in range(2):
                    r = fb.n2 % 3
                    fb.n2 += 1
                    K.dma(sp, fb.w2[r][:],
                          w2b[f, oc].rearrange("p (k n) -> p k n", k=NFC)[:, hf * 11:(hf + 1) * 11, :],
                          reads=[WB2[f]], writes=[fb.Bw2[r]])
                    for q in range(11):
                        fc = hf * 11 + q
                        K.mm(psb[bank][:, :], fb.w2[r][:, q, :], fb.g[:, fc, :], fc == 0, fc == NFC - 1,
                             [fb.Bw2[r], fb.Bg[fc]], [PB[bank]])
            yphase(fb, tt, Cg, mm_oc, nxt, pending_tail)
        while pending_tail:
            pending_tail.pop(0)()


    def mx_alloc(stack, with_h=True, with_y=False, nrstd=2):
        fb = FfnBufs()
        if with_h:
            fb.h = sb("m_h", [128, KC, 512], BF16, stack)
        if with_y:
            fb.y = sb("m_y", [128, KC, 512], F32, stack)
        fb.rstd = [sb("m_rstd%d" % i, [128, 512], F32, stack) for i in range(nrstd)]
        fb.sq = [sb("m_sq%d" % i, [128, 512], BF16, stack) for i in range(2)]
        fb.tmp = [sb("m_tmp%d" % i, [128, 512], F32, stack) for i in range(2)]
        fb.Bh, fb.By = Buf("h"), Buf("y")
        fb.Brstd = [Buf(), Buf()]
        fb.Bsq = [Buf(), Buf()]
        fb.Btmp = [Buf(), Buf()]
        fb.nsq = 0
        fb.ntmp = 0
        return fb

    class Gen:
        pass

    def gen_alloc(stack, mask_col, use_pjx, nA=2, blocks=True):
        G = Gen()
        G.PJ = sb("g_pj", [128, 512], I32, stack)
        G.A = [sb("g_a%d" % i, [128, 512], I32, stack) for i in range(nA)]
        G.BPJ = Buf("pj")
        G.BA = [Buf() for _ in range(nA)]
        G.n = 0
        G.mask = icst[:, mask_col:mask_col + 1]
        K.op(pool, lambda e: e.iota(G.PJ[:], [[1, 512]], base=0, channel_multiplier=0), [], [G.BPJ])
        K.op(pool, lambda e: e.iota(G.A[0][:], [[0, 512]], base=0, channel_multiplier=1), [], [G.BA[0]])
        if blocks:
            G.Jf = sb("g_jf", [128, 512], I32, stack)
            G.BJf = Buf()
            K.copy(dve, G.Jf[:], G.PJ[:], [G.BPJ], [G.BJf])
        K.op(pool, lambda e: e.tensor_tensor(G.PJ[:], G.PJ[:], G.A[0][:], ALU.mult), [G.BPJ, G.BA[0]], [G.BPJ])
        if blocks:
            G.Pf = sb("g_pf", [128, 512], I32, stack)
            G.PJ0 = sb("g_pj0", [128, 512], I32, stack)
            G.BPf, G.BPJ0 = Buf(), Buf()
            K.copy(dve, G.Pf[:], G.A[0][:], [G.BA[0]], [G.BPf])
        if use_pjx:
            K.op(pool, lambda e: e.iota(G.A[0][:], [[0, 512]], base=0, channel_multiplier=0), [], [G.BA[0]])
            K.op(pool, lambda e: e.tensor_scalar(G.A[0][:], G.A[0][:], icst[:, 4:5], None, ALU.add),
                 [G.BA[0], Bic], [G.BA[0]])
            K.op(pool, lambda e: e.tensor_tensor(G.PJ[:], G.PJ[:], G.A[0][:], ALU.add), [G.BPJ, G.BA[0]], [G.BPJ])
        if blocks:
            K.copy(dve, G.PJ0[:], G.PJ[:], [G.BPJ], [G.BPJ0])
        return G

    def gen_block(G, S_tot, sp0):
        K.op(pool, lambda e: e.tensor_scalar(G.PJ[:], G.Pf[:], int(sp0 % S_tot), None, ALU.mult), [G.BPf], [G.BPJ])
        K.op(pool, lambda e: e.tensor_tensor(G.PJ[:], G.PJ[:], G.PJ0[:], ALU.add), [G.BPJ, G.BPJ0], [G.BPJ])

    def gen_tile(G, dst, Bdst, S_tot, s0, sp0, off):
        base = (s0 * sp0 + off) % S_tot
        step = s0 % S_tot
        i = G.n % len(G.A)
        G.n += 1
        A = G.A[i]
        if step == 0:
            K.op(pool, lambda e: e.iota(A[:], [[0, 512]], base=base, channel_multiplier=0), [], [G.BA[i]])
        else:
            K.op(pool, lambda e: e.tensor_scalar(A[:], G.Jf[:], int(step), int(base), ALU.mult, ALU.add), [G.BJf],
                 [G.BA[i]])
        K.op(pool, lambda e: e.tensor_tensor(A[:], A[:], G.PJ[:], ALU.add), [G.BA[i], G.BPJ], [G.BA[i]])
        K.op(dve, lambda e: e.tensor_scalar(A[:], A[:], G.mask, None, ALU.bitwise_and), [G.BA[i], Bic], [G.BA[i]])
        K.actf(dst, A[:], ACT.Sin, [G.BA[i]], [Bdst], scale=2.0 * np.pi / S_tot, bias=-np.pi)

    BUd, BUall, Bmixo = Buf("Ud"), Buf("Uall"), Buf("mixo")

    def odd_phase_a(l, S, stack):
        i_od = l // 2
        fb = mx_alloc(stack, nrstd=1)
        win = sb("o_win", [128, KC, 1536], BF16, stack)
        wsT = sb("o_wsT", [128, 512], BF16, stack)
        sgb = sb("o_sgb", [1, 512], BF16, stack)
        nrm = sb("o_nrm", [128, 512], F32, stack)
        ccsc = sb("o_ccsc", [128, 256], BF16, stack)
        gtmp = sb("o_gtmp", [128, 512], BF16, stack)
        zcT = sb("o_zcT", [128, 4, 512], BF16, stack)
        usb = [sb("o_usb%d" % i, [128, 1024], BF16, stack) for i in range(2)]
        uT = sb("o_uT", [128, 4, 512], F32, stack)
        gv = sb("o_gv", [128, 512], F32, stack)
        vtok = [sb("o_vtok%d" % i, [128, 512], BF16, stack) for i in range(2)]
        odT = sb("o_odT", [128, 4, 512], BF16, stack)
        ssq = sb("o_ssq", [128, 2], F32, stack)
        Bwin, BwsT, Bsgb, Bnrm, Bccsc, Bgtmp, BzcT, BuT, Bgv, BodT, Bssq = (Buf() for _ in range(11))
        Busb = [Buf(), Buf()]
        Bvtok = [Buf(), Buf()]
        K.dma(sp, win[:], od_win_b[i_od].rearrange("p (k n) -> p k n", k=KC), reads=[Bodw], writes=[Bwin])
        K.dma(sp, wsT[:], sgu_wsT_b[i_od], reads=[Bodw], writes=[BwsT])
        K.dma(sp, sgb[:], sgu_b_b[i_od], reads=[Bodw], writes=[Bsgb])
        K.dma(sp, nrm[:], sgu_nrm[i_od], writes=[Bnrm])
        G = gen_alloc(stack, 2, False, nA=1, blocks=False)
        gen_tile(G, gtmp[:], Bgtmp, 128, 0, 0, 96)
        K.copy(dve, ccsc[:, 0:128], gtmp[:, 0:128], [Bgtmp], [Bccsc])
        gen_tile(G, gtmp[:], Bgtmp, 128, 0, 0, 0)
        K.copy(dve, ccsc[:, 128:256], gtmp[:, 0:128], [Bgtmp], [Bccsc])
        mixo_v = mixo.rearrange("(c p) s -> p c s", p=128)
        nb = [0]

        def bank2():
            nb[0] += 1
            return nb[0] % 2

        for tt in range(S // 512):
            for st in prenorm_steps(fb, l, 1, tt, fb.h, fb.Bh):
                st()
            for g in range(4):
                bank = bank2()
                for kc in range(KC):
                    K.mm(psb[bank][:, :], win[:, kc, g * 128:(g + 1) * 128], fb.h[:, kc, :], kc == 0, kc == KC - 1,
                         [Bwin, fb.Bh], [PB[bank]])
                K.copy(act if g % 2 == 0 else dve, zcT[:, g, :], psb[bank][:, :], [PB[bank]], [BzcT])
            for g in range(4):
                bank = bank2()
                for kc in range(KC):
                    K.mm(psb[bank][:, :], win[:, kc, 512 + g * 128:512 + (g + 1) * 128], fb.h[:, kc, :], kc == 0,
                         kc == KC - 1, [Bwin, fb.Bh], [PB[bank]])
                K.actf(uT[:, g, :], psb[bank][:, :], ACT.Gelu, [PB[bank]], [BuT])
            for ts in range(4):
                tk = slice(ts * 128, (ts + 1) * 128)
                ub, Bub = usb[ts % 2], Busb[ts % 2]
                for gp in range(2):
                    bank = 2 + gp
                    for gg in range(2):
                        g = gp * 2 + gg
                        K.mm(psb[bank][:, gg * 256:(gg + 1) * 256], zcT[:, g, tk], ccsc[:], True, True,
                             [BzcT, Bccsc], [PB[bank]])
                    K.copy(act if gp == 0 else dve, ub[:, gp * 512:(gp + 1) * 512], psb[bank][:, :], [PB[bank]],
                           [Bub])
                K.dma(sp, Ud[tt * 512 + ts * 128: tt * 512 + (ts + 1) * 128, :], ub[:], reads=[Bub], writes=[BUd])
                bank = bank2()
                for kc in range(KC):
                    K.mm(psb[bank][:, :], fb.h[:, kc, tk], win[:, kc, 1024:1536], kc == 0, kc == KC - 1,
                         [Bwin, fb.Bh], [PB[bank]])
                K.actf(gv[:], psb[bank][:, :], ACT.Gelu, [PB[bank]], [Bgv])
                vt, Bvt = vtok[ts % 2], Bvtok[ts % 2]
                K.actf(vt[:], gv[:], ACT.Square, [Bgv], [Bvt, Bssq], accum_out=ssq[:, 0:1])
                K.actf(ssq[:, 1:2], ssq[:, 0:1], ACT.Sqrt, [Bssq], [Bssq], scale=1.0 / 512, bias=EPS)
                K.op(dve, lambda e: e.reciprocal(ssq[:, 1:2], ssq[:, 1:2]), [Bssq], [Bssq])
                K.stt(vt[:], gv[:], ssq[:, 1:2], nrm[:], ALU.mult, ALU.mult, [Bgv, Bssq, Bnrm], [Bvt])
                bank = 6
                for hd in range(4):
                    hs = slice(hd * 128, (hd + 1) * 128)
                    K.mm(psb[bank][:, hs], vt[:, hs], wsT[:, hs], True, False, [Bvt, BwsT], [PB[bank]])
                    K.mm(psb[bank][:, hs], onesb[0:1, :], sgb[0:1, hs], False, True, [Bones, Bsgb], [PB[bank]])
                K.tt(dve, odT[:, :, tk], uT[:, :, tk], psb[bank][:, :].rearrange("p (h i) -> p h i", h=4), ALU.mult,
                     [BuT, PB[bank]], [BodT])
            K.dma(sp, mixo_v[:, 4:8, tt * 512:(tt + 1) * 512], odT[:], reads=[BodT], writes=[Bmixo])

    def odd_phase_b(S, S_keys, Usrc, BUsrc, is_sample, stack):
        G = gen_alloc(stack, 1 if is_sample else 0, is_sample)
        ct = [sb("b_ct%d" % i, [128, 512], BF16, stack) for i in range(2)]
        stl = [sb("b_st%d" % i, [128, 512], BF16, stack) for i in range(2)]
        ut = [sb("b_ut%d" % i, [128, 1024], BF16, stack) for i in range(3)]
        fcs = sb("b_fcs", [128, 4, 512], BF16, stack)
        Rc = [sb("b_rc%d" % i, [128, 512], I16, stack) for i in range(2)]
        Rs = [sb("b_rs%d" % i, [128, 512], I16, stack) for i in range(2)]
        R32 = sb("b_r32", [128, 512], I32, stack)
        Di = sb("b_di", [128, 512], I32, stack)
        D16 = sb("b_d16", [128, 512], I16, stack)
        m16 = sb("b_m16", [128, 1], I16, stack)
        Bct, Bst = [Buf(), Buf()], [Buf(), Buf()]
        BRc, BRs = [Buf(), Buf()], [Buf(), Buf()]
        But = [Buf(), Buf(), Buf()]
        Bfcs, BDi, BD16, BR32, Bm16 = Buf(), Buf(), Buf(), Buf(), Buf()
        mixo_v = mixo.rearrange("(c p) s -> p c s", p=128)
        scale = 1.0 / float(np.sqrt(S_keys * 128.0))
        na = S_keys // 128
        sc_sin = 2.0 * np.pi / S_keys
        n = 0
        for bq in range(S // 512):
            sp0 = bq * 512
            gen_block(G, S_keys, sp0)
            K.op(pool, lambda e: e.tensor_scalar(Di[:], G.Jf[:], 128, int((128 * sp0) % S_keys), ALU.mult, ALU.add),
                 [G.BJf], [BDi])
            K.op(dve, lambda e: e.tensor_scalar(Di[:], Di[:], G.mask, None, ALU.bitwise_and), [BDi, Bic], [BDi])
            K.copy(dve, D16[:], Di[:], [BDi], [BD16])
            for R0, BR0, off in ((Rc[0], BRc[0], (3 * S_keys) // 4), (Rs[0], BRs[0], S_keys // 2)):
                K.op(pool, lambda e, off=off: e.tensor_scalar(R32[:], G.PJ[:], int(off), None, ALU.add), [G.BPJ],
                     [BR32])
                K.op(dve, lambda e: e.tensor_scalar(R32[:], R32[:], G.mask, None, ALU.bitwise_and), [BR32, Bic],
                     [BR32])
                K.copy(dve, R0[:], R32[:], [BR32], [BR0])
            for a in range(na):
                s0 = a * 128
                i2, i3 = n % 2, n % 3
                n += 1
                cur, nxt = a % 2, (a + 1) % 2
                K.actf(ct[i2][:], Rc[cur][:], ACT.Sin, [BRc[cur]], [Bct[i2]], scale=sc_sin, bias=-np.pi)
                K.actf(stl[i2][:], Rs[cur][:], ACT.Sin, [BRs[cur]], [Bst[i2]], scale=sc_sin, bias=-np.pi)
                if a + 1 < na:
                    for R_, BR_ in ((Rc, BRc), (Rs, BRs)):
                        K.tt(dve, R_[nxt][:], R_[cur][:], D16[:], ALU.add, [BR_[cur], BD16], [BR_[nxt]])
                        K.op(dve, lambda e, R_=R_, nxt=nxt: e.tensor_scalar(R_[nxt][:], R_[nxt][:], G.mask, None,
                                                                            ALU.bitwise_and), [BR_[nxt], Bic],
                             [BR_[nxt]])
                K.dma(sp, ut[i3][:], Usrc(s0), reads=[BUsrc], writes=[But[i3]])
                for g in range(4):
                    K.mm(psb[g][:, :], ut[i3][:, g * 256:g * 256 + 128], ct[i2][:], a == 0, False,
                         [But[i3], Bct[i2]], [PB[g]])
                    K.mm(psb[g][:, :], ut[i3][:, g * 256 + 128:g * 256 + 256], stl[i2][:], False, a == na - 1,
                         [But[i3], Bst[i2]], [PB[g]])
            for g in range(4):
                if g % 2 == 0:
                    K.actf(fcs[:, g, :], psb[g][:, :], ACT.Copy, [PB[g]], [Bfcs], scale=scale)
                else:
                    K.ts(dve, fcs[:, g, :], psb[g][:, :], scale, None, ALU.mult, None, [PB[g]], [Bfcs])
            K.dma(sp, mixo_v[:, 0:4, bq * 512:(bq + 1) * 512], fcs[:], reads=[Bfcs], writes=[Bmixo])

    def mixer_phase_c(l, S, wout_dram, Bw_dram, stack):
        fb = mx_alloc(stack, with_h=False, with_y=True)
        wo = sb("c_wo", [128, KC, 1024], BF16, stack)
        ot = [sb("c_ot%d" % i, [128, KC, 512], BF16, stack) for i in range(2)]
        Bwo = Buf()
        Bot = [Buf(), Buf()]
        K.dma(sp, wo[:], wout_dram.rearrange("p (k n) -> p k n", k=KC), reads=[Bw_dram], writes=[Bwo])
        mixo_v = mixo.rearrange("(c p) s -> p c s", p=128)
        Cg = vec(l, 1, 2)
        pending = []
        for tt in range(S // 512):
            o_, Bo = ot[tt % 2], Bot[tt % 2]
            K.dma(sp, o_[:], mixo_v[:, :, tt * 512:(tt + 1) * 512], reads=[Bmixo], writes=[Bo])

            def mm_oc(oc, bank, o_=o_, Bo=Bo):
                for ic in range(KC):
                    K.mm(psb[bank][:, :], wo[:, ic, oc * 128:(oc + 1) * 128], o_[:, ic, :], ic == 0, ic == KC - 1,
                         [Bwo, Bo], [PB[bank]])
            yphase(fb, tt, Cg, mm_oc, [], pending)
            while pending:
                pending.pop(0)()

    def odd_mixer(l, S, is_sample):
        i_od = l // 2
        with ExitStack() as st:
            odd_phase_a(l, S, st)
            K.barrier()
        if is_sample and GRP > 1 and not cfg.no_xg:
            for ci in range(NUC):
                K.op(pool, lambda e, ci=ci: e.collective_compute(
                    "AllGather", ALU.bypass, replica_groups=cfg.replica_groups,
                    ins=[Ud[ci * RCU:(ci + 1) * RCU, :]], outs=[Uall[ci]]), [BUd], [BUall])
            K.barrier()

            def usrc(s0):
                g, i = s0 // S, s0 % S
                ci, w = i // RCU, i % RCU
                return Uall[ci, g * RCU + w:g * RCU + w + 128, :]
            Usrc, BUsrc, S_keys = usrc, BUall, GRP * S
        else:
            Usrc, BUsrc, S_keys = (lambda s0: Ud[s0:s0 + 128, :]), BUd, S
        with ExitStack() as st:
            odd_phase_b(S, S_keys, Usrc, BUsrc, is_sample and GRP > 1 and not cfg.no_xg, st)
            K.barrier()
        with ExitStack() as st:
            mixer_phase_c(l, S, od_wout_b[i_od], Bodw, st)
            K.barrier()

    fcst = sb("fcst", [128, 64], F32)
    Bfc = Buf("fcst")
    K.dma(sp, fcst[:], fconst, writes=[Bfc])
    NCHL = SMAXL // 64
    decs = sb("decs", [128, 2, 2, NCHL], F32)
    Bdecs = Buf("decs")
    Bgq, Bgvt, Bgkv, Bgs, Bgg, Bgsum, Bgsall = (Buf() for _ in range(7))
    BQd, BKd, BKall, BVd, BVall, Bmixm = (Buf() for _ in range(6))
    rr = [0]

    def rbank(lo=0, n=2):
        rr[0] += 1
        return lo + rr[0] % n

    def even_a1(l, S, stack):
        i_ev = l // 2
        fb = mx_alloc(stack)
        win = sb("a_win", [128, KC, 1056], BF16, stack)
        wal = sb("a_wal", [33, 512], BF16, stack)
        tc = sb("a_tc", [128, 516], F32, stack)
        alr = sb("a_alr", [33, 512], BF16, stack)
        qk = sb("a_qk", [128, 4, 512], F32, stack)
        spt = sb("a_spt", [128, 512], F32, stack)
        E = sb("a_E", [128, 4, 2, 128], F32, stack)
        ekd = sb("a_ekd", [128, 512], F32, stack)
        kd = sb("a_kd", [128, 512], BF16, stack)
        vtok = [sb("a_vtok%d" % i, [128, 512], BF16, stack) for i in range(2)]
        kvst = sb("a_kvst", [128, 2, 4, 128], F32, stack)
        qst = sb("a_qst", [128, 4, 2, 512], BF16, stack)
        Bwin, Bwal, Btc, Balr, Bqk, Bspt, BE, Bekd, Bkd, Bkvst, Bqst = (Buf() for _ in range(11))
        Bvtok = [Buf(), Buf()]
        K.dma(sp, win[:], ev_win1_b[i_ev].rearrange("p (k n) -> p k n", k=KC), reads=[Bevw], writes=[Bwin])
        K.dma(sp, wal[:], gla_wal_b[i_ev], reads=[Bevw], writes=[Bwal])
        K.dma(sp, tc[:], tconst, writes=[Btc])
        K.op(dve, lambda e: e.memset(alr[32:33, :], 1.0), [], [Balr])
        gq_v = gq.rearrange("k r p s -> p k r s")
        for tt in range(S // 512):
            for st in prenorm_steps(fb, l, 1, tt, fb.h, fb.Bh):
                st()
            for c4 in range(4):
                bank = rbank()
                for kc in range(KC):
                    K.mm(psb[bank][:, :], win[:, kc, c4 * 128:(c4 + 1) * 128], fb.h[:, kc, :], kc == 0, kc == KC - 1,
                         [Bwin, fb.Bh], [PB[bank]])
                K.copy(act if c4 % 2 == 0 else dve, qk[:, c4, :], psb[bank][:, :], [PB[bank]], [Bqk])
            bank = rbank()
            for kc in range(KC):
                K.mm(psb[bank][0:32, :], win[:, kc, 1024:1056], fb.h[:, kc, :], kc == 0, kc == KC - 1,
                     [Bwin, fb.Bh], [PB[bank]])
            K.copy(act, alr[0:32, :], psb[bank][0:32, :], [PB[bank]], [Balr])
            for ts in range(4):
                tk = slice(ts * 128, (ts + 1) * 128)
                n = tt * 4 + ts
                vt, Bvt = vtok[n % 2], Bvtok[n % 2]
                for kc in range(KC):
                    K.mm(psb[2][:, 0:256], fb.h[:, kc, tk], win[:, kc, 256:512], kc == 0, kc == KC - 1,
                         [Bwin, fb.Bh], [PB[2]])
                for kc in range(KC):
                    K.mm(psb[3][:, :], fb.h[:, kc, tk], win[:, kc, 512:1024], kc == 0, kc == KC - 1,
                         [Bwin, fb.Bh], [PB[3]])
                K.copy(act, vt[:], psb[3][:, :], [PB[3]], [Bvt])
                K.dma(sp, gvt[n], vt[:], reads=[Bvt], writes=[Bgvt])
                K.mm(psb[4][:, :], alr[0:33, tk], wal[0:33, :], True, True, [Balr, Bwal], [PB[4]])
                K.actf(spt[:], psb[4][:, :], ACT.Exp, [PB[4]], [Bspt], scale=-1.0)
                K.actf(spt[:], spt[:], ACT.Ln, [Bspt], [Bspt], bias=1.0)
                for pr in range(2):
                    K.mm(psb[5][:, pr * 130:pr * 130 + 130], spt[:, pr * 128:(pr + 1) * 128], tc[:, 0:130], True, True,
                         [Bspt, Btc], [PB[5]])
                for pr in range(2):
                    K.mm(psb[6 + pr][:, 0:258], spt[:, 256 + pr * 128:256 + (pr + 1) * 128], tc[:, 130:388], True,
                         True, [Bspt, Btc], [PB[6 + pr]])
                K.mm(psb[4][:, 0:256], tc[:, 130:258], spt[:, 0:256], True, True, [Bspt, Btc], [PB[4]])
                K.mm(psb[4][:, 256:512], tc[:, 388:516], spt[:, 256:512], True, True, [Bspt, Btc], [PB[4]])
                sc = 1.0 / 16.0
                for pr in range(2):
                    K.actf(E[:, 0, pr, :], psb[5][:, pr * 130:pr * 130 + 128], ACT.Exp, [PB[5]], [BE], scale=-sc)
                    K.actf(E[:, 1, pr, :], psb[5][:, pr * 130:pr * 130 + 128], ACT.Exp, [PB[5]], [BE], scale=sc)
                    K.actf(decs[:, 0, pr, 2 * n:2 * n + 2], psb[5][:, pr * 130 + 128:pr * 130 + 130], ACT.Exp,
                           [PB[5]], [Bdecs], scale=-sc)
                    K.actf(E[:, 2, pr, :], psb[6 + pr][:, 0:128], ACT.Exp, [PB[6 + pr]], [BE], scale=-sc)
                    K.actf(E[:, 3, pr, :], psb[6 + pr][:, 128:256], ACT.Exp, [PB[6 + pr]], [BE], scale=sc)
                    K.actf(decs[:, 1, pr, 2 * n:2 * n + 2], psb[6 + pr][:, 256:258], ACT.Exp, [PB[6 + pr]], [Bdecs],
                           scale=-sc)
                K.actf(ekd[:], psb[4][:, :], ACT.Exp, [PB[4]], [Bekd], scale=-sc)
                K.tt(dve, kd[:, 0:256], psb[2][:, 0:256], ekd[:, 0:256], ALU.mult, [PB[2], Bekd], [Bkd])
                K.tt(dve, kd[:, 256:512], psb[2][:, 0:256], ekd[:, 256:512], ALU.mult, [PB[2], Bekd], [Bkd])
                for pr in range(2):
                    K.stt(qst[:, 0, pr, tk], qk[:, pr, tk], 0.125, E[:, 0, pr, :], ALU.mult, ALU.mult, [Bqk, BE], [Bqst])
                    K.stt(qst[:, 1, pr, tk], qk[:, pr, tk], 0.125, E[:, 2, pr, :], ALU.mult, ALU.mult, [Bqk, BE], [Bqst])
                    K.tt(pool, qst[:, 2, pr, tk], qk[:, 2 + pr, tk], E[:, 1, pr, :], ALU.mult, [Bqk, BE], [Bqst])
                    K.tt(pool, qst[:, 3, pr, tk], qk[:, 2 + pr, tk], E[:, 3, pr, :], ALU.mult, [Bqk, BE], [Bqst])
                for c in range(2):
                    for dr in range(2):
                        for h in range(4):
                            hb = (h % 2) * 64
                            col = (dr * 2 + h // 2) * 128
                            K.mm(psb[c][hb:hb + 64, col:col + 128],
                                 kd[c * 64:(c + 1) * 64, dr * 256 + h * 64:dr * 256 + (h + 1) * 64],
                                 vt[c * 64:(c + 1) * 64, h * 128:(h + 1) * 128], True, True, [Bkd, Bvt], [PB[c]])
                    K.copy(act if c == 0 else dve, kvst[:, :, c * 2:c * 2 + 2, :],
                           psb[c][:, :].rearrange("p (d r v) -> p d r v", d=2, r=2), [PB[c]], [Bkvst])
                for dr in range(2):
                    K.dma(sp, gkv[dr, 2 * n:2 * n + 2].rearrange("c r p v -> p c r v"),
                          kvst[:, dr, :, :].rearrange("p (c r) v -> p c r v", c=2), reads=[Bkvst], writes=[Bgkv])
            K.dma(sp, gq_v[:, :, :, tt * 512:(tt + 1) * 512], qst[:], reads=[Bqst], writes=[Bgq])

    def even_r(S, stack, store, Sin=None):
        nch = S // 64
        CB = min(8, nch)
        St = [[sb("r_st%d%d" % (d_, p_), [128, 128], F32, stack) for p_ in range(2)] for d_ in range(2)]
        BSt = [[Buf(), Buf()], [Buf(), Buf()]]
        kvb = [sb("r_kvb%d" % i, [128, CB, 2, 128], F32, stack) for i in range(2)]
        stb = [sb("r_stb%d" % i, [128, CB, 2, 128], BF16, stack) for i in range(2)]
        Bkvb, Bstb = [Buf(), Buf()], [Buf(), Buf()]
        nb = 0
        for dr in range(2):
            for pr in range(2):
                if Sin is None:
                    K.op(dve, lambda e, dr=dr, pr=pr: e.memset(St[dr][pr][:], 0.0), [], [BSt[dr][pr]])
                else:
                    K.copy(dve, St[dr][pr][:], Sin[dr][pr][0][:], [Sin[dr][pr][1]], [BSt[dr][pr]])
            batches = list(range(0, nch, CB))
            if dr == 1:
                batches = batches[::-1]
            for c0 in batches:
                kb, Bk = kvb[nb % 2], Bkvb[nb % 2]
                sbf, Bs_ = stb[nb % 2], Bstb[nb % 2]
                nb += 1
                K.dma(sp, kb[:], gkv[dr, c0:c0 + CB].rearrange("c r p v -> p c r v"), reads=[Bgkv], writes=[Bk])
                cis = list(range(CB))
                if dr == 1:
                    cis = cis[::-1]
                for ci in cis:
                    c = c0 + ci
                    for pr in range(2):
                        if store:
                            K.copy(act, sbf[:, ci, pr, :], St[dr][pr][:], [BSt[dr][pr]], [Bs_])
                        K.stt(St[dr][pr][:], St[dr][pr][:], decs[:, dr, pr, c:c + 1], kb[:, ci, pr, :], ALU.mult,
                              ALU.add, [BSt[dr][pr], Bdecs, Bk], [BSt[dr][pr]])
                if store:
                    K.dma(sp, gs[dr, c0:c0 + CB].rearrange("c r p v -> p c r v"), sbf[:], reads=[Bs_], writes=[Bgs])
        return St, BSt

    def even_exchange(S, stack):
        nch = S // 64
        St, BSt = even_r(S, stack, False)
        pk = sb("x_pk", [128, 4, 129], F32, stack)
        Bpk = Buf()
        for dr in range(2):
            for pr in range(2):
                k4 = dr * 2 + pr
                K.copy(dve, pk[:, k4, 0:128], St[dr][pr][:], [BSt[dr][pr]], [Bpk])
                K.copy(dve, pk[:, k4, 128:129], decs[:, dr, pr, 0:1], [Bdecs], [Bpk])
                for c in range(1, nch):
                    K.tt(dve, pk[:, k4, 128:129], pk[:, k4, 128:129], decs[:, dr, pr, c:c + 1], ALU.mult,
                         [Bpk, Bdecs], [Bpk])
        K.dma(sp, gsum.rearrange("(k p) v -> p k v", p=128), pk[:], reads=[Bpk], writes=[Bgsum])
        K.barrier()
        K.op(pool, lambda e: e.collective_compute("AllGather", ALU.bypass, replica_groups=cfg.replica_groups,
                                                  ins=[gsum], outs=[gsum_all]), [Bgsum], [Bgsall])
        K.barrier()
        pa = sb("x_pa", [128, GRP, 4, 129], F32, stack)
        Bpa = Buf()
        K.dma(sp, pa[:], gsum_all.rearrange("(g k p) v -> p g k v", g=GRP, p=128), reads=[Bgsall], writes=[Bpa])
        Sin = [[None, None], [None, None]]
        cf = sb("x_cf", [128, 2], F32, stack)
        Bcf = Buf()
        for dr in range(2):
            for pr in range(2):
                k4 = dr * 2 + pr
                t_ = sb("x_sin%d" % k4, [128, 128], F32, stack)
                Bt = Buf()
                K.op(dve, lambda e, t_=t_: e.memset(t_[:], 0.0), [], [Bt])
                for r1 in range(GRP):
                    K.copy(dve, cf[:, 0:1], fcst[:, 8 + dr * 4 + r1:9 + dr * 4 + r1], [Bfc], [Bcf])
                    for r2 in range(GRP):
                        ic_ = 16 + dr * 16 + r1 * 4 + r2
                        K.ts(dve, cf[:, 1:2], pa[:, r2, k4, 128:129], -1.0, fcst[:, ic_:ic_ + 1], ALU.add, ALU.mult,
                             [Bpa, Bfc], [Bcf])
                        K.stt(cf[:, 0:1], cf[:, 1:2], 1.0, cf[:, 0:1], ALU.add, ALU.mult, [Bcf], [Bcf])
                    K.stt(t_[:], pa[:, r1, k4, 0:128], cf[:, 0:1], t_[:], ALU.mult, ALU.add, [Bpa, Bcf, Bt], [Bt])
                Sin[dr][pr] = (t_, Bt)
        return Sin

    def even_o(l, S, stack):
        i_ev = l // 2
        tcm = sb("o_tcm", [128, 256], F32, stack)
        gn = sb("o_gn", [128, 1], F32, stack)
        qt = [sb("o_qt%d" % i, [128, 4, 2, 512], BF16, stack) for i in range(2)]
        vtl = [sb("o_vt%d" % i, [128, 4, 512], BF16, stack) for i in range(2)]
        gt = [sb("o_gt%d" % i, [128, 4, 512], BF16, stack) for i in range(2)]
        sf = [sb("o_sf%d" % i, [128, 8, 2, 128], BF16, stack) for i in range(2)]
        sbw = [sb("o_sb%d" % i, [128, 8, 2, 128], BF16, stack) for i in range(2)]
        am = [sb("o_am%d" % i, [128, 256], BF16, stack) for i in range(2)]
        sq = sb("o_sq", [128, 512], BF16, stack)
        rs = sb("o_rs", [128, 512], F32, stack)
        on = sb("o_on", [128, 512], F32, stack)
        ost = sb("o_ost", [128, 4, 512], BF16, stack)
        Btcm, Bgn, Bsq, Brs, Bon, Bost = (Buf() for _ in range(6))
        Bqt, Bvtl, Bgt, Bsf, Bsbw, Bam = ([Buf(), Buf()] for _ in range(6))
        K.dma(sp, tcm[:, 0:128], tconst[:, 0:128], writes=[Btcm])
        K.dma(sp, tcm[:, 128:256], tconst[:, 130:258], writes=[Btcm])
        K.dma(sp, gn[:], gla_nrm[i_ev], writes=[Bgn])
        gq_v = gq.rearrange("k r p s -> p k r s")
        gg_v = gg.rearrange("(c p) s -> p c s", p=128)
        mixo_v = mixo.rearrange("(c p) s -> p c s", p=128)
        na = 0
        for tt in range(S // 512):
            i2 = tt % 2
            K.dma(sp, qt[i2][:], gq_v[:, :, :, tt * 512:(tt + 1) * 512], reads=[Bgq], writes=[Bqt[i2]])
            K.dma(sp, vtl[i2][:], gvt[tt * 4:(tt + 1) * 4].rearrange("n p v -> p n v"), reads=[Bgvt], writes=[Bvtl[i2]])
            K.dma(sp, gt[i2][:], gg_v[:, :, tt * 512:(tt + 1) * 512], reads=[Bgg], writes=[Bgt[i2]])
            K.dma(sp, sf[i2][:], gs[0, tt * 8:(tt + 1) * 8].rearrange("c r p v -> p c r v"), reads=[Bgs],
                  writes=[Bsf[i2]])
            K.dma(sp, sbw[i2][:], gs[1, tt * 8:(tt + 1) * 8].rearrange("c r p v -> p c r v"), reads=[Bgs],
                  writes=[Bsbw[i2]])
            q_ = qt[i2]
            for ts in range(4):
                tk = slice(ts * 128, (ts + 1) * 128)
                for h in range(4):
                    pr, hb = h // 2, (h % 2) * 64
                    rows = slice(hb, hb + 64)
                    ab = h % 2
                    ob = 2 + h % 2
                    K.mm(psb[ab][:, 0:128], q_[rows, 2, pr, tk], q_[rows, 0, pr, tk], True, True, [Bqt[i2]], [PB[ab]])
                    K.mm(psb[ab][:, 128:256], q_[rows, 3, pr, tk], q_[rows, 1, pr, tk], True, True, [Bqt[i2]],
                         [PB[ab]])
                    a_, Ba = am[na % 2], Bam[na % 2]
                    na += 1
                    K.tt(dve, a_[:], psb[ab][:, 0:256], tcm[:], ALU.mult, [PB[ab], Btcm], [Ba])
                    o0 = pr * 128
                    oc_ = slice(o0, o0 + 128)
                    K.mm(psb[ob][:, oc_], vtl[i2][:, ts, h * 128:(h + 1) * 128], a_[:, 0:128], True, False,
                         [Bvtl[i2], Ba], [PB[ob]])
                    K.mm(psb[ob][:, oc_], vtl[i2][:, ts, h * 128:(h + 1) * 128], a_[:, 128:256], False, False,
                         [Bvtl[i2], Ba], [PB[ob]])
                    for c in range(2):
                        ci = ts * 2 + c
                        cs = slice(o0 + c * 64, o0 + (c + 1) * 64)
                        tks = slice(ts * 128 + c * 64, ts * 128 + (c + 1) * 64)
                        K.mm(psb[ob][:, cs], sf[i2][rows, ci, pr, :], q_[rows, 0, pr, tks], False, False,
                             [Bsf[i2], Bqt[i2]], [PB[ob]])
                        K.mm(psb[ob][:, cs], sbw[i2][rows, ci, pr, :], q_[rows, 1, pr, tks], False, c == 1,
                             [Bsbw[i2], Bqt[i2]], [PB[ob]])
                for par in range(2):
                    ob = 2 + par
                    hsel = slice(par, 4, 2)
                    K.actf(sq[:, 0:256], psb[ob][:, 0:256], ACT.Square, [PB[ob]], [Bsq])
                    K.mm(psb[4][:, 0:256], onesb[:], sq[:, 0:256], True, True, [Bones, Bsq], [PB[4]])
                    K.actf(rs[:, 0:256], psb[4][:, 0:256], ACT.Sqrt, [PB[4]], [Brs], scale=1.0 / 128, bias=EPS)
                    K.op(dve, lambda e: e.reciprocal(rs[:, 0:256], rs[:, 0:256]), [Brs], [Brs])
                    K.tt(dve, on[:, 0:256], psb[ob][:, 0:256], rs[:, 0:256], ALU.mult, [PB[ob], Brs], [Bon])
                    K.stt(ost[:, hsel, tk], on[:, 0:256].rearrange("p (h i) -> p h i", h=2), gn[:, 0:1],
                          gt[i2][:, hsel, tk], ALU.mult, ALU.mult, [Bon, Bgn, Bgt[i2]], [Bost])
            K.dma(sp, mixo_v[:, 0:4, tt * 512:(tt + 1) * 512], ost[:], reads=[Bost], writes=[Bmixo])

    def even_a2(l, S, is_sample, stack):
        i_ev = l // 2
        fb = mx_alloc(stack)
        win = sb("m_win", [128, KC, 1088], BF16, stack)
        wqb = sb("m_wqb", [128, 2, 1536], BF16, stack)
        wkv = sb("m_wkv", [128, 1024], BF16, stack)
        qn = sb("m_qn", [128, 2], F32, stack)
        kvn = sb("m_kvn", [128, 1], F32, stack)
        cq = sb("m_cq", [128, 2, 512], F32, stack)
        cqn = sb("m_cqn", [128, 2, 512], BF16, stack)
        ckvn = sb("m_ckvn", [128, 512], BF16, stack)
        cf_ = sb("m_cf", [128, 4], F32, stack)
        zb = sb("m_zb", [128, 1], F32, stack)
        Bzb = Buf()
        K.op(dve, lambda e: e.memset(zb[:], 0.0), [], [Bzb])
        ai = sb("m_ai", [128, 512], I32, stack)
        pos = sb("m_pos", [128, 512], F32, stack)
        tab = sb("m_tab", [128, 2, 512], F32, stack)
        gst = [sb("m_gst%d" % i, [128, 512], BF16, stack) for i in range(2)]
        qh = [sb("m_qh%d" % i, [96, 512], BF16, stack) for i in range(2)]
        kst = sb("m_kst", [96, 8, 512], BF16, stack)
        kro = sb("m_kro", [96, 512], BF16, stack)
        vaug = [sb("m_vaug%d" % i, [128, 8, 65], BF16, stack) for i in range(2)]
        (Bwin, Bwqb, Bwkv, Bqn, Bkvn, Bcq, Bcqn, Bckvn, Bcf, Bai, Bpos, Btab, Bkst, Bkro) = (Buf() for _ in range(14))
        Bgst, Bqh, Bvaug = ([Buf(), Buf()] for _ in range(3))
        K.dma(sp, win[:], ev_win2_b[i_ev].rearrange("p (k n) -> p k n", k=KC), reads=[Bevw], writes=[Bwin])
        K.dma(sp, wqb[:], mla_wqb_b[i_ev].rearrange("p (k n) -> p k n", k=2), reads=[Bevw], writes=[Bwqb])
        K.dma(sp, wkv[:], mla_wkvb_b[i_ev], reads=[Bevw], writes=[Bwkv])
        K.dma(sp, qn[:], mla_qn[i_ev], writes=[Bqn])
        K.dma(sp, kvn[:], mla_kvn[i_ev], writes=[Bkvn])
        for i in range(2):
            K.op(dve, lambda e, i=i: e.memset(vaug[i][:], 1.0), [], [Bvaug[i]])
        K.op(pool, lambda e: e.iota(ai[:], [[0, 512]], base=0, channel_multiplier=1), [], [Bai])
        K.op(dve, lambda e: e.tensor_scalar(ai[:, 0:1], ai[:, 0:1], icst[:, 6:7], None, ALU.bitwise_and), [Bai, Bic],
             [Bai])
        K.copy(dve, cf_[:, 1:2], ai[:, 0:1], [Bai], [Bcf])
        K.actf(cf_[:, 0:1], cf_[:, 1:2], ACT.Exp, [Bcf], [Bcf], scale=-float(np.log(10000.0)) / 16.0)
        K.ts(dve, cf_[:, 0:1], cf_[:, 0:1], 65536.0 / (2.0 * np.pi), None, ALU.mult, None, [Bcf], [Bcf])
        jpos = sb("m_jpos", [128, 512], I32, stack)
        Bjpos = Buf()
        K.op(pool, lambda e: e.iota(jpos[:], [[1, 512]], base=0, channel_multiplier=0), [], [Bjpos])
        if is_sample:
            K.op(pool, lambda e: e.tensor_scalar(jpos[:], jpos[:], icst[:, 5:6], None, ALU.add), [Bjpos, Bic], [Bjpos])
        gg_v = gg.rearrange("(c p) s -> p c s", p=128)
        Kd_v = Kd.rearrange("(h r) s -> r h s", h=8)
        Vd_v = Vd.rearrange("(h p) (k e) -> p h k e", h=8, e=65)
        qs = float(96.0 ** -0.5)
        R = slice(64, 96)
        ng = 0
        a2s = cfg.a2_stop
        if a2s == 1:
            return
        for tt in range(S // 512):
            for st in prenorm_steps(fb, l, 1, tt, fb.h, fb.Bh):
                st()
            for c4 in range(4):
                bank = rbank()
                for kc in range(KC):
                    K.mm(psb[bank][:, :], win[:, kc, c4 * 128:(c4 + 1) * 128], fb.h[:, kc, :], kc == 0, kc == KC - 1,
                         [Bwin, fb.Bh], [PB[bank]])
                g_, Bg_ = gst[ng % 2], Bgst[ng % 2]
                ng += 1
                K.actf(g_[:], psb[bank][:, :], ACT.Silu, [PB[bank]], [Bg_])
                K.dma(sp, gg_v[:, c4, tt * 512:(tt + 1) * 512], g_[:], reads=[Bg_], writes=[Bgg])
            if a2s == 2:
                continue
            K.ts(dve, pos[:], jpos[:], float(tt * 512), None, ALU.add, None, [Bjpos], [Bpos])
            for k2 in range(2):
                K.ts(dve, ai[:], pos[:], cf_[:, 0:1], fcst[:, k2:k2 + 1], ALU.mult, ALU.add, [Bpos, Bcf, Bfc], [Bai])
                K.op(dve, lambda e: e.tensor_scalar(ai[:], ai[:], icst[:, 3:4], None, ALU.bitwise_and), [Bai, Bic],
                     [Bai])
                K.actf(tab[:, k2, :], ai[:], ACT.Sin, [Bai], [Btab], scale=2.0 * np.pi / 65536.0, bias=-np.pi)
            if a2s == 3:
                continue
            for c2 in range(2):
                bank = rbank()
                for kc in range(KC):
                    K.mm(psb[bank][:, :], win[:, kc, 512 + c2 * 128:512 + (c2 + 1) * 128], fb.h[:, kc, :], kc == 0,
                         kc == KC - 1, [Bwin, fb.Bh], [PB[bank]])
                K.copy(act, cq[:, c2, :], psb[bank][:, :], [PB[bank]], [Bcq])
                i = fb.nsq % len(fb.sq)
                fb.nsq += 1
                K.actf(fb.sq[i][:], psb[bank][:, :], ACT.Square, [PB[bank]], [fb.Bsq[i]])
                K.mm(psb[6][:, :], onesb[:], fb.sq[i][:], c2 == 0, c2 == 1, [Bones, fb.Bsq[i]], [PB[6]])
            K.actf(fb.rstd[1][:], psb[6][:, :], ACT.Sqrt, [PB[6]], [fb.Brstd[1]], scale=1.0 / 256, bias=EPS)
            K.op(dve, lambda e: e.reciprocal(fb.rstd[1][:], fb.rstd[1][:]), [fb.Brstd[1]], [fb.Brstd[1]])
            for c2 in range(2):
                i = fb.ntmp % len(fb.tmp)
                fb.ntmp += 1
                K.stt(fb.tmp[i][:], cq[:, c2, :], qs, fb.rstd[1][:], ALU.mult, ALU.mult, [Bcq, fb.Brstd[1]],
                      [fb.Btmp[i]])
                K.actf(cqn[:, c2, :], fb.tmp[i][:], ACT.Identity, [fb.Btmp[i], Bqn, Bzb], [Bcqn],
                       scale=qn[:, c2:c2 + 1], bias=zb[:, 0:1])
            bank = rbank()
            for kc in range(KC):
                K.mm(psb[bank][:, :], win[:, kc, 768:896], fb.h[:, kc, :], kc == 0, kc == KC - 1, [Bwin, fb.Bh],
                     [PB[bank]])
            i = fb.nsq % len(fb.sq)
            fb.nsq += 1
            K.actf(fb.sq[i][:], psb[bank][:, :], ACT.Square, [PB[bank]], [fb.Bsq[i]])
            K.mm(psb[6][:, :], onesb[:], fb.sq[i][:], True, True, [Bones, fb.Bsq[i]], [PB[6]])
            K.actf(fb.rstd[1][:], psb[6][:, :], ACT.Sqrt, [PB[6]], [fb.Brstd[1]], scale=1.0 / 128, bias=EPS)
            K.op(dve, lambda e: e.reciprocal(fb.rstd[1][:], fb.rstd[1][:]), [fb.Brstd[1]], [fb.Brstd[1]])
            i = fb.ntmp % len(fb.tmp)
            fb.ntmp += 1
            K.tt(dve, fb.tmp[i][:], psb[bank][:, :], fb.rstd[1][:], ALU.mult, [PB[bank], fb.Brstd[1]], [fb.Btmp[i]])
            K.actf(ckvn[:], fb.tmp[i][:], ACT.Identity, [fb.Btmp[i], Bkvn, Bzb], [Bckvn], scale=kvn[:, 0:1],
                   bias=zb[:, 0:1])
            if a2s == 4:
                continue
            for kc in range(KC):
                K.mm(psb[2][0:96, :], win[:, kc, 896:992], fb.h[:, kc, :], kc == 0, kc == KC - 1, [Bwin, fb.Bh], [PB[2]])
            for kc in range(KC):
                K.mm(psb[3][0:96, :], win[:, kc, 992:1088], fb.h[:, kc, :], kc == 0, kc == KC - 1, [Bwin, fb.Bh],
                     [PB[3]])
            i = fb.ntmp % len(fb.tmp)
            fb.ntmp += 1
            K.tt(dve, fb.tmp[i][R, :], psb[2][R, :], tab[R, 0, :], ALU.mult, [PB[2], Btab], [fb.Btmp[i]])
            i2 = fb.ntmp % len(fb.tmp)
            fb.ntmp += 1
            K.tt(dve, fb.tmp[i2][R, :], psb[3][R, :], tab[R, 1, :], ALU.mult, [PB[3], Btab], [fb.Btmp[i2]])
            K.tt(dve, kro[R, :], fb.tmp[i][R, :], fb.tmp[i2][R, :], ALU.add, [fb.Btmp[i], fb.Btmp[i2]], [Bkro])
            if a2s == 5:
                continue
            for h in range(8):
                bank = rbank()
                K.mm(psb[bank][0:64, :], wkv[:, h * 64:(h + 1) * 64], ckvn[:], True, True, [Bwkv, Bckvn], [PB[bank]])
                K.copy(act, kst[0:64, h, :], psb[bank][0:64, :], [PB[bank]], [Bkst])
                K.copy(dve, kst[R, h, :], kro[R, :], [Bkro], [Bkst])
                for kc in range(2):
                    K.mm(psb[2][0:96, :], wqb[:, kc, h * 96:(h + 1) * 96], cqn[:, kc, :], kc == 0, kc == 1,
                         [Bwqb, Bcqn], [PB[2]])
                for kc in range(2):
                    K.mm(psb[3][0:96, :], wqb[:, kc, 768 + h * 96:768 + (h + 1) * 96], cqn[:, kc, :], kc == 0, kc == 1,
                         [Bwqb, Bcqn], [PB[3]])
                q_, Bq_ = qh[h % 2], Bqh[h % 2]
                K.copy(dve, q_[0:64, :], psb[2][0:64, :], [PB[2]], [Bq_])
                i = fb.ntmp % len(fb.tmp)
                fb.ntmp += 1
                K.tt(dve, fb.tmp[i][R, :], psb[2][R, :], tab[R, 0, :], ALU.mult, [PB[2], Btab], [fb.Btmp[i]])
                i2 = fb.ntmp % len(fb.tmp)
                fb.ntmp += 1
                K.tt(dve, fb.tmp[i2][R, :], psb[3][R, :], tab[R, 1, :], ALU.mult, [PB[3], Btab], [fb.Btmp[i2]])
                K.tt(dve, q_[R, :], fb.tmp[i][R, :], fb.tmp[i2][R, :], ALU.add, [fb.Btmp[i], fb.Btmp[i2]], [Bq_])
                K.dma(sp, Qd[h, :, tt * 512:(tt + 1) * 512], q_[:], reads=[Bq_], writes=[BQd])
            K.dma(sp, Kd_v[:, :, tt * 512:(tt + 1) * 512], kst[:], reads=[Bkst], writes=[BKd])
            if a2s == 6:
                continue
            for ts in range(4):
                tk = slice(ts * 128, (ts + 1) * 128)
                kt = tt * 4 + ts
                va, Bva = vaug[kt % 2], Bvaug[kt % 2]
                bank = rbank()
                K.mm(psb[bank][:, :], ckvn[:, tk], wkv[:, 512:1024], True, True, [Bckvn, Bwkv], [PB[bank]])
                K.copy(act if ts % 2 == 0 else dve, va[:, :, 0:64], psb[bank][:, :].rearrange("p (h e) -> p h e", h=8),
                       [PB[bank]], [Bva])
                K.dma(sp, Vd_v[:, :, kt, :], va[:], reads=[Bva], writes=[BVd])

    def even_b(S, nrank, Ksrc, BKs, Vsrc, BVs, stack):
        SK = S
        KB_ = min(1024, SK)
        nkt = KB_ // 128
        LOOK = 2
        NP = 4
        qt = [sb("b_qt%d" % i, [96, 512], BF16, stack) for i in range(2)]
        ktl = [sb("b_kt%d" % i, [96, KB_], BF16, stack) for i in range(3)]
        vtl = [sb("b_vt%d" % i, [128, nkt, 65], BF16, stack) for i in range(3)]
        pt = [sb("b_pt%d" % i, [128, 512], BF16, stack) for i in range(NP)]
        rc = sb("b_rc", [128, 512], F32, stack)
        osb = sb("b_osb", [64, 512], F32, stack)
        onb = [sb("b_on%d" % i, [64, 512], BF16, stack) for i in range(2)]
        Brc, Bosb = Buf(), Buf()
        Bqt, Bonb = ([Buf(), Buf()] for _ in range(2))
        Bktl, Bvtl = ([Buf(), Buf(), Buf()] for _ in range(2))
        Bpt = [Buf() for _ in range(NP)]
        nq = nk = ns = 0
        pend = []
        tails = []

        def flush_one():
            pend.pop(0)()

        for qb in range(S // 512):
            for h in range(8):
                q_, Bq_ = qt[nq % 2], Bqt[nq % 2]
                ob = 4 + nq % 2
                nq += 1
                K.dma(sp, q_[:], Qd[h, :, qb * 512:(qb + 1) * 512], reads=[BQd], writes=[Bq_])
                nblk = nrank * (SK // KB_)
                nstep = nblk * nkt
                si = 0
                for g in range(nrank):
                    for k0 in range(0, SK, KB_):
                        k_, Bk_ = ktl[nk % 3], Bktl[nk % 3]
                        v_, Bv_ = vtl[nk % 3], Bvtl[nk % 3]
                        nk += 1
                        K.dma(sp, k_[:], Ksrc(g, h, k0, KB_), reads=[BKs], writes=[Bk_])
                        K.dma(sp, v_[:], Vsrc(g, h, k0 // 128, nkt), reads=[BVs], writes=[Bv_])
                        for kt in range(nkt):
                            sbk = ns % 4
                            p_, Bp_ = pt[ns % NP], Bpt[ns % NP]
                            ns += 1
                            K.mm(psb[sbk][:, :], k_[:, kt * 128:(kt + 1) * 128], q_[:], True, True, [Bk_, Bq_],
                                 [PB[sbk]])
                            K.actf(p_[:], psb[sbk][:, :], ACT.Exp, [PB[sbk]], [Bp_])

                            def pv(v_=v_, Bv_=Bv_, p_=p_, Bp_=Bp_, kt=kt, first=(si == 0), last=(si == nstep - 1),
                                   ob=ob):
                                K.mm(psb[ob][0:65, :], v_[:, kt, :], p_[:], first, last, [Bv_, Bp_], [PB[ob]])
                            pend.append(pv)
                            si += 1
                            if len(pend) > LOOK:
                                flush_one()
                            if si == 4 and tails:
                                tails.pop(0)()
                while pend:
                    flush_one()
                while tails:
                    tails.pop(0)()
                K.op(dve, lambda e, ob=ob: e.reciprocal(rc[64:65, :], psb[ob][64:65, :]), [PB[ob]], [Brc])
                K.copy(dve, osb[:], psb[ob][0:64, :], [PB[ob]], [Bosb])

                def tail(h=h, qb=qb):
                    K.mm(psb[6][0:64, :], cst[64:65, 128:192], rc[64:65, :], True, True, [Bcst, Brc], [PB[6]])
                    o_, Bo_ = onb[h % 2], Bonb[h % 2]
                    K.tt(dve, o_[:], osb[:], psb[6][0:64, :], ALU.mult, [Bosb, PB[6]], [Bo_])
                    K.dma(sp, mixm[h, :, qb * 512:(qb + 1) * 512], o_[:], reads=[Bo_], writes=[Bmixm])
                tails.append(tail)
        while tails:
            tails.pop(0)()

    def even_phase_c(l, S, stack):
        i_ev = l // 2
        fb = mx_alloc(stack, with_h=False, with_y=True)
        wg = sb("c_wg", [128, 4, 1024], BF16, stack)
        wm = sb("c_wm", [64, 8, 1024], BF16, stack)
        og = [sb("c_og%d" % i, [128, 4, 512], BF16, stack) for i in range(2)]
        om = [sb("c_om%d" % i, [64, 8, 512], BF16, stack) for i in range(2)]
        Bwg, Bwm = Buf(), Buf()
        Bog, Bom = [Buf(), Buf()], [Buf(), Buf()]
        K.dma(sp, wg[:], ev_woutg_b[i_ev].rearrange("p (k n) -> p k n", k=4), reads=[Bevw], writes=[Bwg])
        K.dma(sp, wm[:], ev_woutm_b[i_ev].rearrange("p (k n) -> p k n", k=8), reads=[Bevw], writes=[Bwm])
        mixo_v = mixo.rearrange("(c p) s -> p c s", p=128)
        mixm_v = mixm.rearrange("h e s -> e h s")
        Cg = vec(l, 1, 2)
        pending = []
        for tt in range(S // 512):
            g_, Bg_ = og[tt % 2], Bog[tt % 2]
            m_, Bm_ = om[tt % 2], Bom[tt % 2]
            K.dma(sp, g_[:], mixo_v[:, 0:4, tt * 512:(tt + 1) * 512], reads=[Bmixo], writes=[Bg_])
            K.dma(sp, m_[:], mixm_v[:, :, tt * 512:(tt + 1) * 512], reads=[Bmixm], writes=[Bm_])

            def mm_oc(oc, bank, g_=g_, m_=m_, Bg_=Bg_, Bm_=Bm_):
                for ic in range(4):
                    K.mm(psb[bank][:, :], wg[:, ic, oc * 128:(oc + 1) * 128], g_[:, ic, :], ic == 0, False,
                         [Bwg, Bg_], [PB[bank]])
                for hh in range(8):
                    K.mm(psb[bank][:, :], wm[:, hh, oc * 128:(oc + 1) * 128], m_[:, hh, :], False, hh == 7,
                         [Bwm, Bm_], [PB[bank]])
            yphase(fb, tt, Cg, mm_oc, [], pending)
            while pending:
                pending.pop(0)()

    def even_mixer(l, S, is_sample):
        xg = is_sample and GRP > 1
        stop = cfg.ev_stop
        with ExitStack() as st:
            even_a1(l, S, st)
            K.barrier()
        if stop == 1:
            return
        with ExitStack() as st:
            Sin = even_exchange(S, st) if xg else None
            even_r(S, st, True, Sin)
            K.barrier()
        if stop == 2:
            return
        with ExitStack() as st:
            even_a2(l, S, is_sample, st)
            K.barrier()
        if stop == 3:
            return
        if xg:
            for h in range(8):
                K.op(pool, lambda e, h=h: e.collective_compute(
                    "AllGather", ALU.bypass, replica_groups=cfg.replica_groups,
                    ins=[Kd[h * 96:(h + 1) * 96, 0:S]], outs=[Kall[h]]), [BKd], [BKall])
                K.op(pool, lambda e, h=h: e.collective_compute(
                    "AllGather", ALU.bypass, replica_groups=cfg.replica_groups,
                    ins=[Vd[h * 128:(h + 1) * 128, 0:(S // 128) * 65]], outs=[Vall[h]]), [BVd], [BVall])
        with ExitStack() as st:
            even_o(l, S, st)
            K.barrier()
        if stop == 4:
            return
        if xg:
            K.barrier()
            kget = lambda g, h, k0, n: Kall[h, g * 96:(g + 1) * 96, k0:k0 + n]
            vget = lambda g, h, kt0, n: Vall[h].rearrange("(g p) (k e) -> g p k e", p=128, e=65)[g, :, kt0:kt0 + n, :]
            srcs = (GRP, kget, BKall, vget, BVall)
        else:
            kget = lambda g, h, k0, n: Kd[h * 96:(h + 1) * 96, k0:k0 + n]
            vget = lambda g, h, kt0, n: Vd[h * 128:(h + 1) * 128, :].rearrange("p (k e) -> p k e", e=65)[:, kt0:kt0 + n, :]
            srcs = (1, kget, BKd, vget, BVd)
        with ExitStack() as st:
            even_b(S, *srcs, st)
            K.barrier()
        if stop == 5:
            return
        with ExitStack() as st:
            even_phase_c(l, S, st)
            K.barrier()

    tok0 = 0
    for si, S in enumerate(cfg.seg_tokens):
        K.dma(sp, mv[:], modv[si], reads=[Bmodv], writes=[Bmv])
        with ExitStack() as st:
            load_segment(tok0, S, st)
            K.barrier()
        for l in range(L):
            with ExitStack() as st:
                fb = ffn_alloc(st)
                ffn_sublayer(fb, l, 0, S)
                K.barrier()
            if cfg.do_mixer and l % 2 == 1 and cfg.do_mixer & 2:
                odd_mixer(l, S, si == 2)
            if cfg.do_mixer and l % 2 == 0 and cfg.do_mixer & 1:
                even_mixer(l, S, si == 2)
            with ExitStack() as st:
                fb = ffn_alloc(st)
                ffn_sublayer(fb, l, 2, S)
                K.barrier()
        with ExitStack() as st:
            store_segment(tok0, S, st)
            K.barrier()
        tok0 += S
    K.barrier(full=True)
    es.close()
    return nc


def _fm(v):
    v = np.asarray(v)
    lead = v.shape[:-1]
    n = v.shape[-1] // 128
    v = v.reshape(lead + (n, 128))
    return np.ascontiguousarray(np.moveaxis(v, -1, 0))


def prep_shared(inp, cfg):
    L = cfg.depth
    sh = {}
    ident = np.eye(128, dtype=np.float32)
    sh["consts"] = np.ascontiguousarray(np.concatenate([ident, np.ones((128, 64), np.float32)], axis=1))
    aw = np.asarray(inp["ada_w"])[:L]
    aw = aw.reshape(L, KC, 128, 72, 128).transpose(0, 3, 2, 1, 4)
    sh["ada_w"] = np.ascontiguousarray(aw).reshape(L * 72, 128, KC, 128)
    sh["ada_b"] = _fm(np.asarray(inp["ada_b"])[:L]).reshape(128, L * 72)
    sh["npre"] = _fm(np.asarray(inp["norm_pre"])[:L]).reshape(128, L * 3 * KC)
    sh["npost"] = _fm(np.asarray(inp["norm_post"])[:L]).reshape(128, L * 3 * KC)
    w13 = np.asarray(inp["ffn_w13"])[:L].reshape(L * 2, KC, 128, 2, NFC, 128)
    sh["w13"] = np.ascontiguousarray(w13.transpose(0, 4, 2, 1, 3, 5)).reshape(L * 2, NFC, 128, KC * 256)
    w2 = np.asarray(inp["ffn_w2"])[:L].reshape(L * 2, NFC, 128, KC, 128)
    sh["w2"] = np.ascontiguousarray(w2.transpose(0, 3, 2, 1, 4)).reshape(L * 2, KC, 128, NFC * 128)
    NOD = L // 2
    if NOD:
        ow = np.asarray(inp["od_w_in"])[:NOD]
        sh["od_win"] = np.ascontiguousarray(ow.reshape(NOD, KC, 128, 1536).transpose(0, 2, 1, 3)).reshape(NOD, 128, KC * 1536)
        oo = np.asarray(inp["od_w_out"])[:NOD]
        sh["od_wout"] = np.ascontiguousarray(oo.reshape(NOD, KC, 128, 1024).transpose(0, 2, 1, 3)).reshape(NOD, 128, KC * 1024)
        ws = np.asarray(inp["sgu_w_s"])[:NOD]
        sh["sgu_wsT"] = np.ascontiguousarray(ws.transpose(0, 3, 1, 2)).reshape(NOD, 128, 512)
        sh["sgu_b"] = np.ascontiguousarray(np.asarray(inp["sgu_b"])[:NOD]).reshape(NOD, 1, 512)
        sh["sgu_nrm"] = np.ascontiguousarray(np.broadcast_to(np.asarray(inp["sgu_norm"])[:NOD, None, :], (NOD, 128, 512)))
    NEV = (L + 1) // 2
    if NEV:
        def kmaj(w, nk):
            n, _, cols = w.shape
            return np.ascontiguousarray(w.reshape(n, nk, 128, cols).transpose(0, 2, 1, 3)).reshape(n, 128, nk * cols)
        wi = np.asarray(inp["ev_w_in"])[:NEV]
        q, k, v, g, alr, cq, ckv, kr = (wi[:, :, a:b] for a, b in ((0, 256), (256, 512), (512, 1024), (1024, 1536),
                                                                  (1536, 1568), (1568, 1824), (1824, 1952), (1952, 1984)))
        sh["ev_win1"] = kmaj(np.concatenate([q, k, v, alr], axis=2), KC)
        fill = ckv[:, :, 0:64]
        krrot = np.concatenate([kr[:, :, 16:32], kr[:, :, 0:16]], axis=2)
        sh["ev_win2"] = kmaj(np.concatenate([g, cq, ckv, fill, kr, fill, krrot], axis=2), KC)
        wo = np.asarray(inp["ev_w_out"])[:NEV]
        sh["ev_woutg"] = kmaj(wo[:, 0:512], 4)
        sh["ev_woutm"] = np.ascontiguousarray(wo[:, 512:1024].reshape(NEV, 8, 64, 1024).transpose(0, 2, 1, 3)).reshape(NEV, 64, 8 * 1024)
        wa = np.asarray(inp["gla_w_alpha"])[:NEV]
        ba = np.asarray(inp["gla_b_alpha"])[:NEV]
        wal = np.zeros((NEV, 33, 512), np.float32)
        wal[:, 0:16, 0:256] = wa[:, 0]
        wal[:, 16:32, 256:512] = wa[:, 1]
        wal[:, 32, 0:256] = ba[:, 0]
        wal[:, 32, 256:512] = ba[:, 1]
        sh["gla_wal"] = wal
        sh["gla_nrm"] = np.ascontiguousarray(np.asarray(inp["gla_norm"])[:NEV].reshape(NEV, 128, 1))
        sh["mla_qn"] = np.ascontiguousarray(np.asarray(inp["mla_q_norm"])[:NEV].reshape(NEV, 2, 128).transpose(0, 2, 1))
        sh["mla_kvn"] = np.ascontiguousarray(np.asarray(inp["mla_kv_norm"])[:NEV].reshape(NEV, 128, 1))
        wq = np.asarray(inp["mla_w_q_b"])[:NEV].reshape(NEV, 256, 8, 96)
        wqr = np.concatenate([wq[..., 0:64], wq[..., 80:96], wq[..., 64:80]], axis=-1)
        sh["mla_wqb"] = kmaj(np.concatenate([wq.reshape(NEV, 256, 768), wqr.reshape(NEV, 256, 768)], axis=2), 2)
        wk = np.asarray(inp["mla_w_kv_b"])[:NEV].reshape(NEV, 128, 8, 128)
        sh["mla_wkvb"] = np.ascontiguousarray(np.concatenate([wk[..., 0:64].reshape(NEV, 128, 512),
                                                             wk[..., 64:128].reshape(NEV, 128, 512)], axis=2))
    t = np.arange(128)
    same = (t[:, None] // 64) == (t[None, :] // 64)
    Tfi = (same & (t[:, None] <= t[None, :])).astype(np.float32)
    Tbe = (same & (t[:, None] > t[None, :])).astype(np.float32)
    Tbi = (same & (t[:, None] >= t[None, :])).astype(np.float32)
    Tpe = (same & (t[:, None] < t[None, :])).astype(np.float32)
    Ind = np.stack([(t < 64), (t >= 64)], axis=1).astype(np.float32)
    sh["tconst"] = np.ascontiguousarray(np.concatenate([Tfi, Ind, Tbe, Tbi, Ind, Tpe], axis=1))
    return sh


def prep_core(inp, cfg, core, n_cores=8):
    xp = np.asarray(inp["x_prompt"])
    xs = np.asarray(inp["x_sample"])
    cp = np.asarray(inp["c_prompt"])
    cs = np.asarray(inp["c_sample"])
    SP = cfg.seg_tokens[0]
    SQ = cfg.seg_tokens[2]
    per_grp = n_cores // xs.shape[0]
    sb_, r = core // per_grp, core % per_grp
    xin = np.concatenate([xp[2 * core, :SP], xp[2 * core + 1, :SP], xs[sb_, r * SQ:(r + 1) * SQ]], axis=0)
    c = np.stack([cp[2 * core], cp[2 * core + 1], cs[sb_], np.zeros(D, np.float32)], axis=0)
    c3 = np.ascontiguousarray(c.T.reshape(KC, 128, 4).transpose(1, 0, 2))
    ic = np.zeros((128, 8), np.int32)
    stot = per_grp * SQ
    ic[:, 0] = SP - 1
    ic[:, 1] = stot - 1
    ic[:, 2] = 127
    ic[:, 3] = 65535
    ic[:, 4] = (SQ * r * np.arange(128)) % stot
    ic[:, 5] = r * SQ
    ic[:, 6] = 15
    fc = np.zeros((128, 64), np.float32)
    fc[:, 0] = 49152.0
    fc[80:96, 1] = 32768.0
    for r1 in range(min(per_grp, 4)):
        fc[:, 8 + r1] = 1.0 if r1 < r else 0.0
        fc[:, 12 + r1] = 1.0 if r1 > r else 0.0
        for r2 in range(min(per_grp, 4)):
            fc[:, 16 + r1 * 4 + r2] = 1.0 if r1 < r2 < r else 0.0
            fc[:, 32 + r1 * 4 + r2] = 1.0 if r < r2 < r1 else 0.0
    return {"xin": np.ascontiguousarray(xin), "c3": c3, "iconst": ic, "fconst": fc}


_CACHE = {}


def run(inp, cfg, n_cores=8, trace=False):
    key = (cfg.seg_tokens, cfg.depth, cfg.do_mixer, cfg.n_cores, cfg.group)
    if key not in _CACHE:
        _CACHE[key] = build(cfg)
    nc = _CACHE[key]
    sh = prep_shared(inp, cfg)
    in_maps = []
    for c in range(n_cores):
        m = dict(sh)
        m.update(prep_core(inp, cfg, c, n_cores))
        in_maps.append(m)
    res = run_bass_kernel_spmd(nc, in_maps, core_ids=list(range(n_cores)), trace=trace)
    return res


def kernel(**inputs):
    cfg = Cfg()
    res = run(inputs, cfg)
    SP, SQ = cfg.seg_tokens[0], cfg.seg_tokens[2]
    B, S = inputs["x_prompt"].shape[:2]
    DB, DS = inputs["x_sample"].shape[:2]
    yp = np.empty((B, S, D), np.float32)
    ys = np.empty((DB, DS, D), np.float32)
    per_grp = 8 // DB
    for c in range(8):
        y = res.results[c]["yout"]
        yp[2 * c] = y[0:SP]
        yp[2 * c + 1] = y[SP:2 * SP]
        ys[c // per_grp, (c % per_grp) * SQ:(c % per_grp + 1) * SQ] = y[2 * SP:2 * SP + SQ]
    return (yp, ys)
```
